# Optimizing a Trainium2 kernel written in Bass

```python
import numpy as np
import jax, jax.numpy as jnp
from jax import lax

D_MODEL = 2048
BATCH = 2
SEQ = 4096
DEPTH = 4

N_MIXERS = 4
HEAD_DIM = 128
N_HEADS = D_MODEL // HEAD_DIM
ROT_DIM = HEAD_DIM // 4
ROPE_THETA = 500000.0
NORM_EPS = 1e-6
D_FF = 4 * D_MODEL
PLE_DIM = 256

MOBA_BLOCK = 256
MOBA_TOPK = 3
MOBA_QCHUNK = 32

POOL_WINDOWS = (2, 4, 8, 16)
POOL_GROUP = D_MODEL // len(POOL_WINDOWS)

NSA_KV_GROUPS = 4
NSA_Q_PER_KV = N_HEADS // NSA_KV_GROUPS
NSA_CMP_LEN = 32
NSA_CMP_STRIDE = 16
NSA_SLC_LEN = 64
NSA_SLC_TOPK = 16
NSA_WINDOW = 512
NSA_QCHUNK = 64
NSA_KV_WIDTH = NSA_KV_GROUPS * HEAD_DIM

CONV_WIDTH = 3

N_LAYERS_MOBA = (DEPTH + 3) // 4
N_LAYERS_POOL = (DEPTH + 2) // 4
N_LAYERS_NSA = (DEPTH + 1) // 4
N_LAYERS_CONV = DEPTH // 4

kernel_name = "hybrid_moba_pool_nsa_conv_trunk"


def rmsnorm(x, gain):
    xf = x.astype(jnp.float32)
    y = xf * lax.rsqrt(jnp.mean(xf * xf, axis=-1, keepdims=True) + NORM_EPS)
    return (y * gain.astype(jnp.float32)).astype(x.dtype)


def partial_rope(x, positions):
    half = ROT_DIM // 2
    freqs = jnp.float32(ROPE_THETA) ** (-jnp.arange(half, dtype=jnp.float32) * 2.0 / ROT_DIM)
    ang = positions.astype(jnp.float32)[..., None] * freqs
    cos = jnp.cos(ang)[:, :, None, :]
    sin = jnp.sin(ang)[:, :, None, :]
    xr = x[..., :ROT_DIM].astype(jnp.float32)
    x1, x2 = xr[..., :half], xr[..., half:]
    rot = jnp.concatenate([x1 * cos - x2 * sin, x2 * cos + x1 * sin], axis=-1).astype(x.dtype)
    return jnp.concatenate([rot, x[..., ROT_DIM:]], axis=-1)


def masked_softmax(logits, mask):
    s = jnp.where(mask, logits.astype(jnp.float32), -jnp.inf)
    m = jnp.max(s, axis=-1, keepdims=True)
    m = jnp.where(jnp.isfinite(m), m, 0.0)
    e = jnp.where(mask, jnp.exp(s - m), 0.0)
    return e / jnp.maximum(jnp.sum(e, axis=-1, keepdims=True), jnp.finfo(jnp.float32).tiny)


def moba_mixer(xn, positions, w_qkv, q_gain, k_gain, w_o):
    B, S, _ = xn.shape
    H, hd, L = N_HEADS, HEAD_DIM, MOBA_BLOCK
    qkv = (xn @ w_qkv).reshape(B, S, 3, H, hd)
    q = partial_rope(rmsnorm(qkv[:, :, 0], q_gain), positions)
    k = partial_rope(rmsnorm(qkv[:, :, 1], k_gain), positions)
    v = qkv[:, :, 2]
    q, k, v = (t.transpose(0, 2, 1, 3) for t in (q, k, v))
    nb = -(-S // L)
    pad = nb * L - S
    kp = jnp.pad(k, ((0, 0), (0, 0), (0, pad), (0, 0)))
    vp = jnp.pad(v, ((0, 0), (0, 0), (0, pad), (0, 0)))
    kb = kp.reshape(B, H, nb, L, hd)
    vb = vp.reshape(B, H, nb, L, hd)
    k_mean = jnp.mean(kb.astype(jnp.float32), axis=3).astype(k.dtype)
    n_sel = min(MOBA_TOPK, nb)
    scale = HEAD_DIM ** -0.5
    bi = jnp.arange(B)[:, None, None, None]
    hi = jnp.arange(H)[None, :, None, None]
    blk_ids = jnp.arange(nb)
    in_blk = jnp.arange(L)
    Qc = MOBA_QCHUNK

    def chunk(c):
        q0 = c * Qc
        t = q0 + jnp.arange(Qc)
        own = q0 // L
        qc = lax.dynamic_slice_in_dim(q, q0, Qc, axis=2)
        gate = jnp.einsum('bhqd,bhnd->bhqn', qc, k_mean).astype(jnp.float32)
        gate = jnp.where(blk_ids < own, gate, -jnp.inf)
        _, sel = lax.top_k(gate, n_sel)
        sel_ok = jnp.arange(n_sel) < own
        k_sel = kb[bi, hi, sel]
        v_sel = vb[bi, hi, sel]
        s_sel = jnp.einsum('bhqd,bhqnld->bhqnl', qc, k_sel).astype(jnp.float32) * scale
        k_own = lax.dynamic_slice_in_dim(kp, own * L, L, axis=2)
        v_own = lax.dynamic_slice_in_dim(vp, own * L, L, axis=2)
        s_own = jnp.einsum('bhqd,bhld->bhql', qc, k_own).astype(jnp.float32) * scale
        own_ok = (own * L + in_blk)[None, :] <= t[:, None]
        logits = jnp.concatenate([s_sel.reshape(B, H, Qc, n_sel * L), s_own], axis=-1)
        sel_mask = jnp.broadcast_to(jnp.repeat(sel_ok, L)[None, :], (Qc, n_sel * L))
        mask = jnp.concatenate([sel_mask, own_ok], axis=-1)
        probs = masked_softmax(logits, mask).astype(v.dtype)
        p_sel = probs[..., :n_sel * L].reshape(B, H, Qc, n_sel, L)
        p_own = probs[..., n_sel * L:]
        return (jnp.einsum('bhqnl,bhqnld->bhqd', p_sel, v_sel)
                + jnp.einsum('bhql,bhld->bhqd', p_own, v_own))

    o = lax.map(chunk, jnp.arange(S // Qc))
    o = o.transpose(1, 0, 3, 2, 4).reshape(B, S, H * hd)
    return o @ w_o


def pool_mixer(xn, w_groups, scale):
    B, S, D = xn.shape
    xf = xn.astype(jnp.float32)
    cs = jnp.concatenate([jnp.zeros((B, 1, D), jnp.float32), lax.cumsum(xf, axis=1)], axis=1)
    hi = jnp.arange(1, S + 1)
    outs = []
    for g, w in enumerate(POOL_WINDOWS):
        sl = slice(g * POOL_GROUP, (g + 1) * POOL_GROUP)
        lo = jnp.maximum(hi - w, 0)
        cnt = (hi - lo).astype(jnp.float32)[None, :, None]
        mean = (cs[:, hi, sl] - cs[:, lo, sl]) / cnt
        outs.append((mean - xf[:, :, sl]).astype(xn.dtype) @ w_groups[g])
    return jnp.concatenate(outs, axis=-1) * scale


def nsa_mixer(xn, positions, w_q, w_kv, q_gain, k_gain, cmp_pos, cmp_w1, cmp_w2, w_gate, w_o):
    B, S, _ = xn.shape
    G, R, hd, H = NSA_KV_GROUPS, NSA_Q_PER_KV, HEAD_DIM, N_HEADS
    scale = HEAD_DIM ** -0.5
    q = rmsnorm((xn @ w_q).reshape(B, S, H, hd), q_gain)
    q_cmp = q.reshape(B, S, G, R, hd)
    q_rot = partial_rope(q, positions).reshape(B, S, G, R, hd)
    kv = (xn @ w_kv).reshape(B, S, 6, G, hd)
    k_slc = partial_rope(rmsnorm(kv[:, :, 2], k_gain[1]), positions)
    v_slc = kv[:, :, 3]
    k_win = partial_rope(rmsnorm(kv[:, :, 4], k_gain[2]), positions)
    v_win = kv[:, :, 5]

    n_cmp = (S - NSA_CMP_LEN) // NSA_CMP_STRIDE + 1
    starts = np.arange(n_cmp) * NSA_CMP_STRIDE
    idx = starts[:, None] + np.arange(NSA_CMP_LEN)[None, :]

    def compress(t, pos, w1, w2):
        blk = t[:, idx] + pos[None, None, :, None, :]
        blk = blk.transpose(0, 1, 3, 2, 4).reshape(B, n_cmp, G, NSA_CMP_LEN * hd)
        return jax.nn.gelu(blk @ w1) @ w2

    k_cmp = rmsnorm(compress(kv[:, :, 0], cmp_pos[0], cmp_w1[0], cmp_w2[0]), k_gain[0])
    v_cmp = compress(kv[:, :, 1], cmp_pos[1], cmp_w1[1], cmp_w2[1])
    t_all = jnp.arange(S)
    cmp_ok = jnp.asarray(starts + NSA_CMP_LEN - 1)[None, :] <= t_all[:, None]
    s_cmp = jnp.einsum('bsgrd,bngd->bgrsn', q_cmp, k_cmp).astype(jnp.float32) * scale
    p_cmp = masked_softmax(s_cmp, cmp_ok)
    o_cmp = jnp.einsum('bgrsn,bngd->bsgrd', p_cmp.astype(v_cmp.dtype), v_cmp)

    n_slc = S // NSA_SLC_LEN
    slc_lo = np.arange(n_slc) * NSA_SLC_LEN
    overlap = ((starts[:, None] < (slc_lo + NSA_SLC_LEN)[None, :])
               & (slc_lo[None, :] < (starts + NSA_CMP_LEN)[:, None])).astype(np.float32)
    imp = jnp.einsum('bgrsn,nj->bgsj', p_cmp, jnp.asarray(overlap))
    cur = t_all // NSA_SLC_LEN
    jb = jnp.arange(n_slc)
    forced = (jb[None, :] == cur[:, None]) | (jb[None, :] == 0)
    imp = jnp.where(forced, jnp.inf, jnp.where(jb[None, :] <= cur[:, None], imp, -jnp.inf))
    n_top = min(NSA_SLC_TOPK, n_slc)
    top_val, top_idx = lax.top_k(imp, n_top)
    top_ok = top_val > -jnp.inf

    k_slc_b = k_slc.reshape(B, n_slc, NSA_SLC_LEN, G, hd).transpose(0, 3, 1, 2, 4)
    v_slc_b = v_slc.reshape(B, n_slc, NSA_SLC_LEN, G, hd).transpose(0, 3, 1, 2, 4)
    W = NSA_WINDOW
    k_win_p = jnp.pad(k_win, ((0, 0), (W, 0), (0, 0), (0, 0)))
    v_win_p = jnp.pad(v_win, ((0, 0), (W, 0), (0, 0), (0, 0)))
    bi = jnp.arange(B)[:, None, None, None]
    gi = jnp.arange(G)[None, :, None, None]
    in_slc = jnp.arange(NSA_SLC_LEN)
    Qc = NSA_QCHUNK
    win_off = jnp.arange(Qc + W)

    def chunk(c):
        q0 = c * Qc
        t = q0 + jnp.arange(Qc)
        qc = lax.dynamic_slice_in_dim(q_rot, q0, Qc, axis=1)
        idx_c = lax.dynamic_slice_in_dim(top_idx, q0, Qc, axis=2)
        ok_c = lax.dynamic_slice_in_dim(top_ok, q0, Qc, axis=2)
        k_sel = k_slc_b[bi, gi, idx_c]
        v_sel = v_slc_b[bi, gi, idx_c]
        s = jnp.einsum('bqgrd,bgqnld->bgrqnl', qc, k_sel).astype(jnp.float32) * scale
        kpos = idx_c[..., None] * NSA_SLC_LEN + in_slc
        m = ok_c[..., None] & (kpos <= t[None, None, :, None, None])
        p = masked_softmax(s.reshape(B, G, R, Qc, n_top * NSA_SLC_LEN),
                           m[:, :, None].reshape(B, G, 1, Qc, n_top * NSA_SLC_LEN))
        o_s = jnp.einsum('bgrqnl,bgqnld->bqgrd',
                         p.reshape(B, G, R, Qc, n_top, NSA_SLC_LEN).astype(v_sel.dtype), v_sel)
        kw = lax.dynamic_slice_in_dim(k_win_p, q0, Qc + W, axis=1)
        vw = lax.dynamic_slice_in_dim(v_win_p, q0, Qc + W, axis=1)
        kpos_w = q0 - W + win_off
        dist = t[:, None] - kpos_w[None, :]
        wm = (kpos_w[None, :] >= 0) & (dist >= 0) & (dist < W)
        sw = jnp.einsum('bqgrd,bkgd->bgrqk', qc, kw).astype(jnp.float32) * scale
        pw = masked_softmax(sw, wm)
        o_w = jnp.einsum('bgrqk,bkgd->bqgrd', pw.astype(vw.dtype), vw)
        return o_s, o_w

    o_slc, o_win = lax.map(chunk, jnp.arange(S // Qc))
    o_slc = o_slc.transpose(1, 0, 2, 3, 4, 5).reshape(B, S, H, hd)
    o_win = o_win.transpose(1, 0, 2, 3, 4, 5).reshape(B, S, H, hd)
    gates = jax.nn.sigmoid(xn @ w_gate).reshape(B, S, H, 3)
    o = (gates[..., 0:1] * o_cmp.reshape(B, S, H, hd)
         + gates[..., 1:2] * o_slc + gates[..., 2:3] * o_win)
    return o.reshape(B, S, H * hd) @ w_o


def conv_mixer(xn, w_in, conv_w, conv_b, w_o):
    D = xn.shape[-1]
    bch = xn @ w_in
    b_gate, c_gate, h = bch[..., :D], bch[..., D:2 * D], bch[..., 2 * D:]
    u = c_gate * h
    conv = lax.conv_general_dilated(u, conv_w[:, None, :], window_strides=(1,),
                                    padding=[(CONV_WIDTH - 1, 0)],
                                    dimension_numbers=('NWC', 'WIO', 'NWC'),
                                    feature_group_count=D) + conv_b
    return (b_gate * conv) @ w_o


def squared_relu_mlp(xn, w1, w2):
    return jnp.square(jax.nn.relu(xn @ w1)) @ w2


def setup_inputs(seed: int = 0) -> dict:
    key = jax.random.key(seed)
    ks = iter(jax.random.split(key, 48))

    def nrm(shape, scale):
        return jax.random.normal(next(ks), shape, jnp.float32) * scale

    def gain(shape):
        return 1.0 + 0.02 * jax.random.normal(next(ks), shape, jnp.float32)

    D, hd = D_MODEL, HEAD_DIM
    NA, NP, NN, NC = N_LAYERS_MOBA, N_LAYERS_POOL, N_LAYERS_NSA, N_LAYERS_CONV
    return {
        "x": nrm((BATCH, SEQ, D), 1.0),
        "p": nrm((DEPTH, BATCH, SEQ, PLE_DIM), 1.0),
        "positions": jnp.broadcast_to(jnp.arange(SEQ, dtype=jnp.int32), (BATCH, SEQ)),
        "mixer_norm": gain((DEPTH, D)),
        "mlp_norm": gain((DEPTH, D)),
        "mlp_w1": nrm((DEPTH, D, D_FF), D ** -0.5),
        "mlp_w2": nrm((DEPTH, D_FF, D), 0.5 * D_FF ** -0.5),
        "ple_norm": gain((DEPTH, D)),
        "ple_gate": nrm((DEPTH, D, D), D ** -0.5),
        "ple_proj": nrm((DEPTH, PLE_DIM, D), 0.5 * PLE_DIM ** -0.5),
        "moba_w_qkv": nrm((NA, D, 3 * D), D ** -0.5),
        "moba_q_gain": gain((NA, hd)),
        "moba_k_gain": gain((NA, hd)),
        "moba_w_o": nrm((NA, D, D), D ** -0.5),
        "pool_w": nrm((NP, len(POOL_WINDOWS), POOL_GROUP, POOL_GROUP), POOL_GROUP ** -0.5),
        "pool_scale": gain((NP, D)),
        "nsa_w_q": nrm((NN, D, D), D ** -0.5),
        "nsa_w_kv": nrm((NN, D, 6 * NSA_KV_WIDTH), D ** -0.5),
        "nsa_q_gain": gain((NN, hd)),
        "nsa_k_gain": gain((NN, 3, hd)),
        "nsa_cmp_pos": nrm((NN, 2, NSA_CMP_LEN, hd), 0.1),
        "nsa_cmp_w1": nrm((NN, 2, NSA_CMP_LEN * hd, hd), (NSA_CMP_LEN * hd) ** -0.5),
        "nsa_cmp_w2": nrm((NN, 2, hd, hd), hd ** -0.5),
        "nsa_w_gate": nrm((NN, D, 3 * N_HEADS), D ** -0.5),
        "nsa_w_o": nrm((NN, D, D), D ** -0.5),
        "conv_w_in": nrm((NC, D, 3 * D), D ** -0.5),
        "conv_w": nrm((NC, CONV_WIDTH, D), CONV_WIDTH ** -0.5),
        "conv_b": nrm((NC, D), 0.01),
        "conv_w_o": nrm((NC, D, D), D ** -0.5),
    }


def reference(x, p, positions, mixer_norm, mlp_norm, mlp_w1, mlp_w2, ple_norm, ple_gate, ple_proj,
              moba_w_qkv, moba_q_gain, moba_k_gain, moba_w_o, pool_w, pool_scale,
              nsa_w_q, nsa_w_kv, nsa_q_gain, nsa_k_gain, nsa_cmp_pos, nsa_cmp_w1, nsa_cmp_w2,
              nsa_w_gate, nsa_w_o, conv_w_in, conv_w, conv_b, conv_w_o):
    h = x
    for i in range(DEPTH):
        kind, j = i % N_MIXERS, i // N_MIXERS
        xn = rmsnorm(h, mixer_norm[i])
        if kind == 0:
            mix = moba_mixer(xn, positions, moba_w_qkv[j], moba_q_gain[j], moba_k_gain[j], moba_w_o[j])
        elif kind == 1:
            mix = pool_mixer(xn, pool_w[j], pool_scale[j])
        elif kind == 2:
            mix = nsa_mixer(xn, positions, nsa_w_q[j], nsa_w_kv[j], nsa_q_gain[j], nsa_k_gain[j],
                            nsa_cmp_pos[j], nsa_cmp_w1[j], nsa_cmp_w2[j], nsa_w_gate[j], nsa_w_o[j])
        else:
            mix = conv_mixer(xn, conv_w_in[j], conv_w[j], conv_b[j], conv_w_o[j])
        h = h + mix
        h = h + squared_relu_mlp(rmsnorm(h, mlp_norm[i]), mlp_w1[i], mlp_w2[i])
        gate = jax.nn.sigmoid(rmsnorm(h, ple_norm[i]) @ ple_gate[i])
        h = h + gate * (p[i] @ ple_proj[i])
    return h
```

```python
import contextlib
import numpy as np
import ml_dtypes
import concourse.bass as bass
import concourse.mybir as mybir
from concourse.bass_utils import run_bass_kernel_spmd

F32 = mybir.dt.float32
BF16 = mybir.dt.bfloat16
I32 = mybir.dt.int32
AF = mybir.ActivationFunctionType
ALU = mybir.AluOpType
AX = mybir.AxisListType

D = 2048
NCH = 16
S = 4096
B = 2
NT = 1024
TB = 512
NTB = NT // TB
DFF = 8192
EPS = 1e-6
NCORES = 8
HD = 128
NH = 16
WSLOT = 4096
NWSLOT = 4
NEG = -30000.0


def core_chunks(c):
    j = c % 4
    return [j, 7 - j]


def core_tokens(c):
    return np.concatenate([np.arange(k * TB, (k + 1) * TB) for k in core_chunks(c)])


class Fake:
    def __getitem__(self, k):
        return self

    def rearrange(self, *a, **k):
        return self

    def ap(self):
        return self


class Ref:
    __slots__ = ("sem", "val", "eng")

    def __init__(self, sem, val, eng):
        self.sem, self.val, self.eng = sem, val, eng


class Tok:
    __slots__ = ("w", "rs")

    def __init__(self):
        self.w = None
        self.rs = {}


COMPUTE = ("pe", "act", "dve", "pool")
NDSEM = 8


class Prog:
    def __init__(self):
        self.nc = bass.Bass("TRN2", target_bir_lowering=False)
        self.es = contextlib.ExitStack()
        self.streams = {e: [] for e in COMPUTE + ("sp",)}
        self.cnt = {e: 0 for e in COMPUTE}
        self.seen = {e: {} for e in COMPUTE + ("sp",)}
        self.esem = {e: self.es.enter_context(self.nc.semaphore("sem_" + e)) for e in COMPUTE}
        self.dsem = {q: [self.es.enter_context(self.nc.semaphore("dsem_%s%d" % (q, i))) for i in range(NDSEM)]
                     for q in ("sp", "pool")}
        self.dcnt = {"sp": 0, "pool": 0}
        self.toks = {}
        self.dry = False
        self.wplan = []
        self.wi = 0
        self.wissued = 0
        self.banks = None
        self.bi = 0
        self.rots = {}
        self.dram = {}
        self.out_refs = []
        self.phase_es = None
        self.nrot = 8
        self.nwslot = NWSLOT
        self.wlive = 2

    def din(self, name, shape, dtype):
        t = self.nc.dram_tensor(name, list(shape), dtype, kind="ExternalInput").ap()
        self.dram[name] = t
        return t

    def dout(self, name, shape, dtype):
        t = self.nc.dram_tensor(name, list(shape), dtype, kind="ExternalOutput").ap()
        self.dram[name] = t
        return t

    def sb(self, name, shape, dtype, phase=False):
        if self.dry:
            return Fake()
        es = self.phase_es if (phase and self.phase_es is not None) else self.es
        self.uid = getattr(self, "uid", 0) + 1
        return es.enter_context(self.nc.sbuf_tensor("s%d_%s" % (self.uid, name), list(shape), dtype))

    def begin_phase(self):
        if self.dry:
            return
        self.barrier()
        self.phase_es = contextlib.ExitStack()
        self.rots = {k: v for k, v in self.rots.items() if not v[3]}

    def end_phase(self):
        if self.dry:
            return
        self.barrier()
        self.phase_es.close()
        self.phase_es = None
        self.rots = {k: v for k, v in self.rots.items() if not v[3]}

    def setup_psum(self):
        if self.dry:
            self.banks = [Fake() for _ in range(8)]
        else:
            self.banks = [self.es.enter_context(self.nc.psum_tensor("bank%d" % i, [128, 512], F32)) for i in range(8)]

    def bank(self):
        i = self.bi
        self.bi = (self.bi + 1) % self.nrot
        return self.banks[i], self.tok(("bank", i))

    def accbank(self, i):
        assert self.nrot + i < 8
        return self.banks[self.nrot + i], self.tok(("bank", self.nrot + i))

    def rot(self, name, n, shape, dtype, phase=True):
        if name not in self.rots:
            self.rots[name] = ([self.sb("%s_%d" % (name, i), shape, dtype, phase=phase) for i in range(n)], 0, n, phase)
        tiles, i, n_, ph = self.rots[name]
        self.rots[name] = (tiles, (i + 1) % n_, n_, ph)
        return tiles[i], self.tok((name, i))

    def tok(self, key):
        t = self.toks.get(key)
        if t is None:
            t = self.toks[key] = Tok()
        return t

    def _deps(self, eng, r, w):
        waits = {}

        def add(ref):
            if ref is None:
                return
            if ref.eng == eng and eng == "pe":
                return
            k = id(ref.sem)
            if k not in waits or waits[k][1] < ref.val:
                waits[k] = (ref.sem, ref.val)
        for t in r:
            add(t.w)
        for t in w:
            add(t.w)
            for ref in t.rs.values():
                add(ref)
        out = []
        seen = self.seen[eng]
        for k, (sem, val) in waits.items():
            if seen.get(k, 0) < val:
                seen[k] = val
                out.append((sem, val))
        return out

    def _commit(self, ref, r, w):
        for t in r:
            k = id(ref.sem)
            old = t.rs.get(k)
            if old is None or old.val < ref.val:
                t.rs[k] = ref
        for t in w:
            t.w = ref
            t.rs = {}

    def op(self, eng, fn, r=(), w=()):
        if self.dry:
            return
        waits = self._deps(eng, r, w)
        self.cnt[eng] += 1
        ref = Ref(self.esem[eng], self.cnt[eng], eng)
        self.streams[eng].append((waits, fn, (ref.sem, 1)))
        self._commit(ref, r, w)

    def dma(self, q, out, in_, r=(), w=(), is_out=False):
        if self.dry:
            return
        n = self.dcnt[q]
        self.dcnt[q] += 1
        sem = self.dsem[q][n % NDSEM]
        val = 16 * (n // NDSEM + 1)
        waits = self._deps(q, r, w)
        if n >= NDSEM:
            k = id(sem)
            if self.seen[q].get(k, 0) < val - 16:
                self.seen[q][k] = val - 16
                waits.append((sem, val - 16))
        ref = Ref(sem, val, "dma_" + q)
        self.streams[q].append((waits, (lambda e, o=out, i=in_: e.dma_start(out=o, in_=i)), (sem, 16)))
        self._commit(ref, r, w)
        if is_out:
            self.out_refs.append(ref)

    def barrier(self):
        if self.dry:
            return
        allw = [(self.esem[e], self.cnt[e]) for e in COMPUTE if self.cnt[e] > 0]
        for q in ("sp", "pool"):
            n = self.dcnt[q]
            for i in range(min(n, NDSEM)):
                last = ((n - 1 - i) // NDSEM) * NDSEM + i
                allw.append((self.dsem[q][i], 16 * (last // NDSEM + 1)))
        for e in COMPUTE + ("sp",):
            seen = self.seen[e]
            ws = []
            for sem, val in allw:
                if seen.get(id(sem), 0) < val:
                    seen[id(sem)] = val
                    ws.append((sem, val))
            if ws:
                self.streams[e].append((ws, None, None))

    def mm(self, out, lhsT, rhs, start, stop, r, w):
        self.op("pe", lambda e: e.matmul(out, lhsT, rhs, start=start, stop=stop), r, w)

    def act(self, out, in_, func, r, w, **kw):
        self.op("act", lambda e: e.activation(out=out, in_=in_, func=func, **kw), r, w)

    def tt(self, eng, out, in0, in1, op, r, w):
        self.op(eng, lambda e: e.tensor_tensor(out=out, in0=in0, in1=in1, op=op), r, w)

    def ts(self, eng, out, in0, s1, s2, op0, op1, r, w):
        if s2 is None:
            self.op(eng, lambda e: e.tensor_scalar(out=out, in0=in0, scalar1=s1, scalar2=None, op0=op0), r, w)
        else:
            self.op(eng, lambda e: e.tensor_scalar(out=out, in0=in0, scalar1=s1, scalar2=s2, op0=op0, op1=op1), r, w)

    def stt(self, out, in0, scalar, in1, op0, op1, r, w):
        self.op("dve", lambda e: e.scalar_tensor_tensor(out=out, in0=in0, scalar=scalar, in1=in1, op0=op0, op1=op1), r, w)

    def copy(self, eng, out, in_, r, w):
        if eng == "act":
            self.op("act", lambda e: e.copy(out=out, in_=in_), r, w)
        else:
            self.op(eng, lambda e: e.tensor_copy(out=out, in_=in_), r, w)

    def setup_wslots(self):
        self.wslots = [self.sb("wslot%d" % i, [128, WSLOT], BF16) for i in range(self.nwslot)]

    def wnext(self, wap, k0, kc, n0, ncols):
        assert kc * ncols <= WSLOT
        if self.dry:
            self.wplan.append((wap, k0, kc, n0, ncols))
            return Fake(), None
        i = self.wi
        self.wi += 1
        assert self.wplan[i][1:] == (k0, kc, n0, ncols), (self.wplan[i][1:], (k0, kc, n0, ncols))
        while self.wissued < min(len(self.wplan), i + self.nwslot - self.wlive + 1):
            jj = self.wissued
            wap_, k0_, kc_, n0_, nc_ = self.wplan[jj]
            slot = self.wslots[jj % self.nwslot]
            dst = slot[:, 0:kc_ * nc_].rearrange("p (k n) -> p k n", k=kc_)
            src = wap_[k0_:k0_ + kc_ * 128, n0_:n0_ + nc_].rearrange("(k p) n -> p k n", p=128)
            self.dma("pool", dst, src, r=(), w=(self.tok(("wslot", jj % self.nwslot)),))
            self.wissued += 1
        slot = self.wslots[i % self.nwslot]
        return slot[:, 0:kc * ncols].rearrange("p (k n) -> p k n", k=kc), self.tok(("wslot", i % self.nwslot))

    def finish(self):
        ws = []
        seen = self.seen["sp"]
        best = {}
        for ref in self.out_refs:
            k = id(ref.sem)
            if k not in best or best[k][1] < ref.val:
                best[k] = (ref.sem, ref.val)
        for k, (sem, val) in best.items():
            if seen.get(k, 0) < val:
                ws.append((sem, val))
        self.barrier()
        self.streams["sp"].append((ws, None, None))
        nc = self.nc
        regs = {"pe": "tensor", "act": "scalar", "dve": "vector", "pool": "gpsimd", "sp": "sync"}
        with nc.Block() as block:
            for eng, attr in regs.items():
                stream = self.streams[eng]

                def f(e, stream=stream):
                    for waits, fn, inc in stream:
                        for s_, v_ in waits:
                            e.wait_ge(s_, v_)
                        if fn is not None:
                            ins = fn(e)
                            if inc is not None:
                                ins.then_inc(inc[0], inc[1])
                getattr(block, attr)(f)
        self.es.close()
        return nc


def emit_rstd(P, h, cs, tbw, src_toks):
    bk, bkt = P.bank()
    for c in range(NCH):
        sq, sqt = P.rot("sq", 3, [128, TB], BF16)
        P.tt("pool", sq[:, :tbw], h[:, c, cs], h[:, c, cs], ALU.mult, r=(src_toks[c],), w=(sqt,))
        P.mm(bk[:, :tbw], P.ones[:, :], sq[:, :tbw], c == 0, c == NCH - 1, r=(sqt, P.tok("ones")), w=(bkt,))
    rstd, rt = P.rot("rstd", 2, [128, TB], F32)
    P.act(rstd[:, :tbw], bk[:, :tbw], AF.Sqrt, r=(bkt, P.tok("consts")), w=(rt,), scale=1.0 / D, bias=P.epsc[:, 0:1])
    P.op("dve", lambda e, o=rstd[:, :tbw]: e.reciprocal(out=o, in_=o), r=(rt,), w=(rt,))
    return rstd, rt


def emit_norm(P, h, gain_col, dst, dst_tokf, ntb=NTB, src_tokf=None, tbw=TB):
    if src_tokf is None:
        src_tokf = lambda c, tb: P.tok(("h", c, tb))
    for tb in range(ntb):
        cs = slice(tb * tbw, (tb + 1) * tbw)
        rstd, rt = emit_rstd(P, h, cs, tbw, [src_tokf(c, tb) for c in range(NCH)])
        for c in range(NCH):
            P.stt(dst[:, c, cs], h[:, c, cs], gain_col[:, c:c + 1], rstd[:, :tbw], ALU.mult, ALU.mult,
                  r=(src_tokf(c, tb), rt, P.tok("vecs")), w=(dst_tokf(c, tb),))


DBG = {"mlp": True, "ple": True, "mix": True}


def emit_mlp_ple(P, L, w1, w2, wg, wp, pT_dram, vec):
    h = P.h
    P.begin_phase()
    xn = P.sb("xn", [128, NCH, NT], BF16, phase=True)
    hid = P.sb("hid", [128, 8, NT], BF16, phase=True)
    pT = P.sb("pT", [128, 2, NT], BF16, phase=True)
    xt = lambda c, tb: P.tok(("xn", c, tb))
    ht = lambda c, tb: P.tok(("h", c, tb))
    hidt = lambda c, tb: P.tok(("hid", c, tb))
    P.dma("pool", pT[:, :, :], pT_dram[:, :, :], r=(), w=(P.tok("pT"),))
    if DBG["mlp"]:
        emit_norm(P, h, vec("mlp_norm"), xn, xt)
    for fb in range(8 if DBG["mlp"] else 0):
        for j in range(4):
            wt, wtk = P.wnext(w1, 0, 16, fb * 1024 + j * 256, 256)
            for m in range(2):
                for tb in range(NTB):
                    cs = slice(tb * TB, (tb + 1) * TB)
                    bk, bkt = P.bank()
                    for kc in range(NCH):
                        P.mm(bk[:, :], wt[:, kc, m * 128:(m + 1) * 128], xn[:, kc, cs], kc == 0, kc == NCH - 1,
                             r=(wtk, xt(kc, tb)), w=(bkt,))
                    rl, rlt = P.rot("relu", 3, [128, TB], F32)
                    P.act(rl[:, :], bk[:, :], AF.Relu, r=(bkt,), w=(rlt,))
                    P.tt("pool", hid[:, j * 2 + m, cs], rl[:, :], rl[:, :], ALU.mult, r=(rlt,), w=(hidt(j * 2 + m, tb),))
        for j in range(4):
            wt, wtk = P.wnext(w2, fb * 1024, 8, j * 512, 512)
            for m in range(4):
                for tb in range(NTB):
                    cs = slice(tb * TB, (tb + 1) * TB)
                    bk, bkt = P.bank()
                    for kc in range(8):
                        P.mm(bk[:, :], wt[:, kc, m * 128:(m + 1) * 128], hid[:, kc, cs], kc == 0, kc == 7,
                             r=(wtk, hidt(kc, tb)), w=(bkt,))
                    c = j * 4 + m
                    P.tt("dve", h[:, c, cs], h[:, c, cs], bk[:, :], ALU.add, r=(bkt, ht(c, tb)), w=(ht(c, tb),))
    if DBG["ple"]:
        emit_norm(P, h, vec("ple_norm"), xn, xt)
    for j in range(8 if DBG["ple"] else 0):
        wt, wtk = P.wnext(wg, 0, 16, j * 256, 256)
        wq, wqk = P.wnext(wp, 0, 2, j * 256, 256)
        for m in range(2):
            for tb in range(NTB):
                cs = slice(tb * TB, (tb + 1) * TB)
                c = j * 2 + m
                bk, bkt = P.bank()
                for kc in range(NCH):
                    P.mm(bk[:, :], wt[:, kc, m * 128:(m + 1) * 128], xn[:, kc, cs], kc == 0, kc == NCH - 1,
                         r=(wtk, xt(kc, tb)), w=(bkt,))
                g, gt = P.rot("gate", 3, [128, TB], F32)
                P.act(g[:, :], bk[:, :], AF.Sigmoid, r=(bkt,), w=(gt,))
                bk2, bk2t = P.bank()
                for kc in range(2):
                    P.mm(bk2[:, :], wq[:, kc, m * 128:(m + 1) * 128], pT[:, kc, cs], kc == 0, kc == 1,
                         r=(wqk, P.tok("pT")), w=(bk2t,))
                P.tt("dve", g[:, :], g[:, :], bk2[:, :], ALU.mult, r=(gt, bk2t), w=(gt,))
                P.tt("dve", h[:, c, cs], h[:, c, cs], g[:, :], ALU.add, r=(gt, ht(c, tb)), w=(ht(c, tb),))
    P.end_phase()


def common_setup(P, nvec, alloc_h=True):
    P.setup_psum()
    P.setup_wslots()
    if alloc_h:
        P.h = P.sb("hT", [128, NCH, NT], F32)
    P.ones = P.sb("ones", [128, 128], BF16)
    P.epsc = P.sb("epsc", [128, 1], F32)
    P.vecs = P.sb("vecs", [128, nvec], F32)
    P.op("pool", lambda e: e.memset(P.ones[:, :], 1.0), r=(), w=(P.tok("ones"),))
    P.op("pool", lambda e: e.memset(P.epsc[:, :], EPS), r=(), w=(P.tok("consts"),))


def load_h(P, hT_dram):
    for c in range(NCH):
        P.dma("sp", P.h[:, c, :], hT_dram[:, c, :], r=(), w=tuple(P.tok(("h", c, tb)) for tb in range(NTB)))


def store_h(P, out_dram):
    for c in range(NCH):
        P.dma("sp", out_dram[:, c, :], P.h[:, c, :], r=tuple(P.tok(("h", c, tb)) for tb in range(NTB)), w=(), is_out=True)


class Vecs:
    def __init__(self, names):
        self.names = list(names)

    def n(self):
        return 16 * len(self.names)

    def ap(self, P, name):
        i = self.names.index(name)
        return P.vecs[:, 16 * i:16 * (i + 1)]

    def host(self, arrs):
        return np.ascontiguousarray(np.concatenate([np.asarray(arrs[n], np.float32).reshape(16, 128).T for n in self.names], axis=1))


def run_two_pass(P, body):
    P.dry = True
    body()
    P.dry = False
    P.bi = 0
    P.rots = {}
    P.toks = {}
    body()
    assert P.wi == len(P.wplan), (P.wi, len(P.wplan))
    return P.finish()


POOL_W = (2, 4, 8, 16)


def emit_pool(P, wpool, hhalo_dram, fac_dram, vec):
    h = P.h
    P.begin_phase()
    hx = P.sb("hx", [128, NCH, 32], F32, phase=True)
    xh = P.sb("xh", [128, NCH, 32], F32, phase=True)
    fac = P.sb("fac", [128, 4, 2, 16], F32, phase=True)
    P.dma("sp", hx[:, :, :], hhalo_dram[:, :, :], r=(), w=(P.tok("hx"),))
    P.dma("sp", fac[:, :, :, :], fac_dram[:, :, :, :], r=(), w=(P.tok("fac"),))
    gain = vec("mixer_norm")
    scale = vec("pool_scale")
    emit_norm(P, hx, gain, xh, lambda c, tb: P.tok(("xh", c)), ntb=1, src_tokf=lambda c, tb: P.tok("hx"), tbw=32)
    ht = lambda c, tb: P.tok(("h", c, tb))
    for s in range(NTB):
        cs = slice(s * TB, (s + 1) * TB)
        rstd, rt = emit_rstd(P, h, cs, TB, [ht(c, s) for c in range(NCH)])
        for g in range(4):
            w = POOL_W[g]
            dg, dgt = P.rot("diffg", 2, [128, 4, TB], BF16)
            for cc in range(4):
                c = 4 * g + cc
                xe, xet = P.rot("xe", 2, [128, 16 + TB], F32)
                P.copy("pool", xe[:, 0:16], xh[:, c, s * 16:(s + 1) * 16], r=(P.tok(("xh", c)),), w=(xet,))
                P.stt(xe[:, 16:16 + TB], h[:, c, cs], gain[:, c:c + 1], rstd[:, :], ALU.mult, ALU.mult,
                      r=(ht(c, s), rt, P.tok("vecs"), xet), w=(xet,))
                cur, curt = xe, xet
                shift = 1
                for st in range(g + 1):
                    nx, nxt = P.rot("ss", 3, [128, 16 + TB], F32)
                    lo = 2 * shift - 1
                    P.tt("dve", nx[:, lo:16 + TB], cur[:, lo:16 + TB], cur[:, lo - shift:16 + TB - shift], ALU.add,
                         r=(curt,), w=(nxt,))
                    cur, curt = nx, nxt
                    shift *= 2
                P.stt(dg[:, cc, :], cur[:, 16:16 + TB], 1.0 / w, xe[:, 16:16 + TB], ALU.mult, ALU.subtract,
                      r=(curt, xet), w=(dgt,))
                t16, t16t = P.rot("t16", 2, [128, 16], F32)
                P.tt("dve", t16[:, :], cur[:, 16:32], fac[:, g, s, :], ALU.mult, r=(curt, P.tok("fac")), w=(t16t,))
                P.tt("dve", dg[:, cc, 0:16], t16[:, :], xe[:, 16:32], ALU.subtract, r=(t16t, xet, dgt), w=(dgt,))
            wt, wtk = P.wnext(wpool, g * 512, 4, 0, 512)
            for oc in range(4):
                bk, bkt = P.bank()
                for kc in range(4):
                    P.mm(bk[:, :], wt[:, kc, oc * 128:(oc + 1) * 128], dg[:, kc, :], kc == 0, kc == 3, r=(wtk, dgt), w=(bkt,))
                c = 4 * g + oc
                P.stt(h[:, c, cs], bk[:, :], scale[:, c:c + 1], h[:, c, cs], ALU.mult, ALU.add,
                      r=(bkt, ht(c, s), P.tok("vecs")), w=(ht(c, s),))
    P.end_phase()


V1 = Vecs(["mixer_norm", "pool_scale", "mlp_norm", "ple_norm"])


def build_L1():
    P = Prog()
    hT = P.din("hT", [128, NCH, NT], F32)
    hhalo = P.din("hhalo", [128, NCH, 32], F32)
    fac = P.din("poolfac", [128, 4, 2, 16], F32)
    pT = P.din("pT", [128, 2, NT], F32)
    vecs = P.din("vecs", [128, V1.n()], F32)
    wpool = P.din("pool_w", [2048, 512], F32)
    w1 = P.din("mlp_w1", [D, DFF], F32)
    w2 = P.din("mlp_w2", [DFF, D], F32)
    wg = P.din("ple_gate", [D, D], F32)
    wp = P.din("ple_proj", [256, D], F32)
    out = P.dout("hT_out", [128, NCH, NT], F32)

    def body():
        common_setup(P, V1.n())
        P.dma("sp", P.vecs[:, :], vecs[:, :], r=(), w=(P.tok("vecs"),))
        load_h(P, hT)
        vec = lambda n: V1.ap(P, n)
        if DBG["mix"]:
            emit_pool(P, wpool, hhalo, fac, vec)
        emit_mlp_ple(P, 1, w1, w2, wg, wp, pT, vec)
        store_h(P, out)
    return run_two_pass(P, body)


def to_fm(a):
    ntok, nf = a.shape
    return np.ascontiguousarray(a.T.reshape(nf // 128, 128, ntok).transpose(1, 0, 2))


def from_fm(a):
    p, nch, ntok = a.shape
    return np.ascontiguousarray(a.transpose(1, 0, 2).reshape(nch * 128, ntok).T)


def halo_rows(hfull_b, c, n):
    out = np.zeros((2 * n, hfull_b.shape[1]), np.float32)
    for s, k in enumerate(core_chunks(c)):
        if k > 0:
            out[s * n:(s + 1) * n] = hfull_b[k * TB - n:k * TB]
    return out


def pool_fac(c):
    f = np.zeros((128, 4, 2, 16), np.float32)
    for g, w in enumerate(POOL_W):
        for s, k in enumerate(core_chunks(c)):
            for t in range(16):
                cnt = min(w, t + 1) if k == 0 else w
                f[:, g, s, t] = 1.0 / cnt
    return f


def run_prog(nc, in_maps):
    res = run_bass_kernel_spmd(nc, in_maps, core_ids=list(range(NCORES)))
    return res.results


def layer1(inp, h0):
    nc = build_L1()
    vec_arrs = {"mixer_norm": inp["mixer_norm"][1], "pool_scale": inp["pool_scale"][0], "mlp_norm": inp["mlp_norm"][1],
                "ple_norm": inp["ple_norm"][1]}
    vecs = V1.host(vec_arrs)
    in_maps = []
    for c in range(NCORES):
        b = c // 4
        tk = core_tokens(c)
        in_maps.append({
            "hT": to_fm(h0[b][tk]), "hhalo": to_fm(halo_rows(h0[b], c, 16)), "poolfac": pool_fac(c),
            "pT": to_fm(inp["p"][1, b][tk]), "vecs": vecs,
            "pool_w": np.ascontiguousarray(inp["pool_w"][0].reshape(2048, 512)),
            "mlp_w1": inp["mlp_w1"][1], "mlp_w2": inp["mlp_w2"][1], "ple_gate": inp["ple_gate"][1], "ple_proj": inp["ple_proj"][1],
        })
    res = run_prog(nc, in_maps)
    h1 = np.zeros_like(h0)
    for c in range(NCORES):
        h1[c // 4][core_tokens(c)] = from_fm(res[c]["hT_out"])
    return h1


def emit_conv(P, w_in, w_o, hhalo_dram, vec):
    h = P.h
    P.begin_phase()
    hx = P.sb("hx", [128, NCH, 32], F32, phase=True)
    xh = P.sb("xh", [128, NCH, 32], BF16, phase=True)
    xn = P.sb("xn", [128, NCH, NT], BF16, phase=True)
    gT = P.sb("gT", [128, NCH, NT], BF16, phase=True)
    P.dma("sp", hx[:, :, :], hhalo_dram[:, :, :], r=(), w=(P.tok("hx"),))
    gain = vec("mixer_norm")
    xht = lambda c, tb: P.tok(("xh", c))
    xt = lambda c, tb: P.tok(("xn", c, tb))
    ht = lambda c, tb: P.tok(("h", c, tb))
    gt = lambda c, tb: P.tok(("gT", c, tb))
    emit_norm(P, hx, gain, xh, xht, ntb=1, src_tokf=lambda c, tb: P.tok("hx"), tbw=32)
    emit_norm(P, h, gain, xn, xt)
    w0, w1, w2, cb = vec("conv_w0"), vec("conv_w1"), vec("conv_w2"), vec("conv_b")
    vt = P.tok("vecs")

    def proj(wt, wtk, m, tb):
        bk, bkt = P.bank()
        cs = slice(tb * TB, (tb + 1) * TB)
        for kc in range(NCH):
            P.mm(bk[:, :], wt[:, kc, m * 128:(m + 1) * 128], xn[:, kc, cs], kc == 0, kc == NCH - 1, r=(wtk, xt(kc, tb)), w=(bkt,))
        return bk, bkt

    def projh(wt, wtk, m):
        bk, bkt = P.bank()
        for kc in range(NCH):
            P.mm(bk[:, 0:32], wt[:, kc, m * 128:(m + 1) * 128], xh[:, kc, :], kc == 0, kc == NCH - 1, r=(wtk, xht(kc, 0)), w=(bkt,))
        return bk, bkt

    for dcp in range(8):
        wt, wtk = P.wnext(w_in, 0, 16, 2048 + dcp * 256, 256)
        cg = {}
        for m in range(2):
            for tb in range(NTB):
                bk, bkt = proj(wt, wtk, m, tb)
                t, tt_ = P.rot("cg", 4, [128, TB], F32)
                P.copy("act", t[:, :], bk[:, :], r=(bkt,), w=(tt_,))
                cg[(m, tb)] = (t, tt_)
            bk, bkt = projh(wt, wtk, m)
            t, tt_ = P.rot("cgh", 2, [128, 32], F32)
            P.copy("act", t[:, :], bk[:, 0:32], r=(bkt,), w=(tt_,))
            cg[(m, "h")] = (t, tt_)
        wt, wtk = P.wnext(w_in, 0, 16, 4096 + dcp * 256, 256)
        cv = {}
        for m in range(2):
            c = dcp * 2 + m
            us = {}
            for tb in range(NTB):
                bk, bkt = proj(wt, wtk, m, tb)
                u, ut = P.rot("u", 4, [128, 16 + TB], F32)
                P.tt("dve", u[:, 16:16 + TB], cg[(m, tb)][0][:, :], bk[:, :], ALU.mult, r=(cg[(m, tb)][1], bkt), w=(ut,))
                us[tb] = (u, ut)
            bk, bkt = projh(wt, wtk, m)
            for tb in range(NTB):
                u, ut = us[tb]
                P.tt("dve", u[:, 0:16], cg[(m, "h")][0][:, tb * 16:(tb + 1) * 16], bk[:, tb * 16:(tb + 1) * 16], ALU.mult,
                     r=(cg[(m, "h")][1], bkt, ut), w=(ut,))
                v, vtk = P.rot("cv", 4, [128, TB], F32)
                P.ts("dve", v[:, :], u[:, 14:14 + TB], w0[:, c:c + 1], cb[:, c:c + 1], ALU.mult, ALU.add, r=(ut, vt), w=(vtk,))
                P.stt(v[:, :], u[:, 15:15 + TB], w1[:, c:c + 1], v[:, :], ALU.mult, ALU.add, r=(ut, vt, vtk), w=(vtk,))
                P.stt(v[:, :], u[:, 16:16 + TB], w2[:, c:c + 1], v[:, :], ALU.mult, ALU.add, r=(ut, vt, vtk), w=(vtk,))
                cv[(m, tb)] = (v, vtk)
        wt, wtk = P.wnext(w_in, 0, 16, dcp * 256, 256)
        for m in range(2):
            c = dcp * 2 + m
            for tb in range(NTB):
                bk, bkt = proj(wt, wtk, m, tb)
                P.tt("dve", gT[:, c, tb * TB:(tb + 1) * TB], bk[:, :], cv[(m, tb)][0][:, :], ALU.mult, r=(bkt, cv[(m, tb)][1]), w=(gt(c, tb),))
    emit_outproj(P, w_o, gT, gt)
    P.end_phase()


def emit_outproj(P, w_o, src, src_tokf):
    h = P.h
    ht = lambda c, tb: P.tok(("h", c, tb))
    for j in range(8):
        wt, wtk = P.wnext(w_o, 0, 16, j * 256, 256)
        for m in range(2):
            c = j * 2 + m
            for tb in range(NTB):
                cs = slice(tb * TB, (tb + 1) * TB)
                bk, bkt = P.bank()
                for kc in range(NCH):
                    P.mm(bk[:, :], wt[:, kc, m * 128:(m + 1) * 128], src[:, kc, cs], kc == 0, kc == NCH - 1,
                         r=(wtk, src_tokf(kc, tb)), w=(bkt,))
                P.tt("dve", h[:, c, cs], h[:, c, cs], bk[:, :], ALU.add, r=(bkt, ht(c, tb)), w=(ht(c, tb),))


V3 = Vecs(["mixer_norm", "conv_w0", "conv_w1", "conv_w2", "conv_b", "mlp_norm", "ple_norm"])


def build_L3():
    P = Prog()
    hT = P.din("hT", [128, NCH, NT], F32)
    hhalo = P.din("hhalo", [128, NCH, 32], F32)
    pT = P.din("pT", [128, 2, NT], F32)
    vecs = P.din("vecs", [128, V3.n()], F32)
    w_in = P.din("conv_w_in", [D, 3 * D], F32)
    w_o = P.din("conv_w_o", [D, D], F32)
    w1 = P.din("mlp_w1", [D, DFF], F32)
    w2 = P.din("mlp_w2", [DFF, D], F32)
    wg = P.din("ple_gate", [D, D], F32)
    wp = P.din("ple_proj", [256, D], F32)
    out = P.dout("hT_out", [128, NCH, NT], F32)

    def body():
        common_setup(P, V3.n())
        P.dma("sp", P.vecs[:, :], vecs[:, :], r=(), w=(P.tok("vecs"),))
        load_h(P, hT)
        vec = lambda n: V3.ap(P, n)
        if DBG["mix"]:
            emit_conv(P, w_in, w_o, hhalo, vec)
        emit_mlp_ple(P, 3, w1, w2, wg, wp, pT, vec)
        store_h(P, out)
    return run_two_pass(P, body)


def layer3(inp, h2):
    nc = build_L3()
    vec_arrs = {"mixer_norm": inp["mixer_norm"][3], "conv_w0": inp["conv_w"][0, 0], "conv_w1": inp["conv_w"][0, 1],
                "conv_w2": inp["conv_w"][0, 2], "conv_b": inp["conv_b"][0], "mlp_norm": inp["mlp_norm"][3], "ple_norm": inp["ple_norm"][3]}
    vecs = V3.host(vec_arrs)
    in_maps = []
    for c in range(NCORES):
        b = c // 4
        tk = core_tokens(c)
        in_maps.append({
            "hT": to_fm(h2[b][tk]), "hhalo": to_fm(halo_rows(h2[b], c, 16)),
            "pT": to_fm(inp["p"][3, b][tk]), "vecs": vecs,
            "conv_w_in": inp["conv_w_in"][0], "conv_w_o": inp["conv_w_o"][0],
            "mlp_w1": inp["mlp_w1"][3], "mlp_w2": inp["mlp_w2"][3], "ple_gate": inp["ple_gate"][3], "ple_proj": inp["ple_proj"][3],
        })
    res = run_prog(nc, in_maps)
    h3 = np.zeros_like(h2)
    for c in range(NCORES):
        h3[c // 4][core_tokens(c)] = from_fm(res[c]["hT_out"])
    return h3


ROPE_THETA = 500000.0
PI = float(np.pi)


def rope_consts():
    freq = np.zeros((128, 1), np.float32)
    fr = (np.float32(ROPE_THETA) ** (-np.arange(16, dtype=np.float32) * np.float32(2.0) / np.float32(32))).astype(np.float32)
    freq[0:16, 0] = fr
    freq[16:32, 0] = fr
    rotT = np.zeros((128, 128), np.float32)
    for i in range(16):
        rotT[i + 16, i] = -1.0
        rotT[i, i + 16] = 1.0
    return freq, rotT.astype(ml_dtypes.bfloat16)


def emit_rope_tables(P, posb_dram, hc):
    P.cosF = P.sb("cosF", [128, NT], F32)
    P.sinF = P.sb("sinF", [128, NT], F32)
    P.begin_phase()
    posi = P.sb("posi", [128, NT], I32, phase=True)
    ang = P.sb("ang", [128, NT], F32, phase=True)
    tk = P.tok("rope")
    P.dma("sp", posi[:, :], posb_dram[:, :], r=(), w=(tk,))
    P.copy("dve", ang[:, :], posi[:, :], r=(tk,), w=(tk,))
    P.ts("dve", ang[:, :], ang[:, :], hc[:, 2:3], None, ALU.mult, None, r=(tk, P.tok("hc")), w=(tk,))
    ki = P.sb("rope_ki", [128, NT], I32, phase=True)
    kf = P.sb("rope_kf", [128, NT], F32, phase=True)
    msk = P.sb("rope_m", [128, NT], F32, phase=True)
    for dst, shift in ((P.sinF, 0.0), (P.cosF, 0.5 * PI)):
        P.ts("dve", dst[:, :], ang[:, :], shift, None, ALU.add, None, r=(tk,), w=(tk,))
        P.ts("dve", kf[:, :], dst[:, :], 1.0 / (2.0 * PI), None, ALU.mult, None, r=(tk,), w=(tk,))
        P.copy("dve", ki[:, :], kf[:, :], r=(tk,), w=(tk,))
        P.copy("dve", kf[:, :], ki[:, :], r=(tk,), w=(tk,))
        P.stt(dst[:, :], kf[:, :], -2.0 * PI, dst[:, :], ALU.mult, ALU.add, r=(tk,), w=(tk,))
        P.ts("dve", msk[:, :], dst[:, :], PI, None, ALU.is_gt, None, r=(tk,), w=(tk,))
        P.stt(dst[:, :], msk[:, :], -2.0 * PI, dst[:, :], ALU.mult, ALU.add, r=(tk,), w=(tk,))
        P.ts("dve", msk[:, :], dst[:, :], -PI, None, ALU.is_lt, None, r=(tk,), w=(tk,))
        P.stt(dst[:, :], msk[:, :], 2.0 * PI, dst[:, :], ALU.mult, ALU.add, r=(tk,), w=(tk,))
        P.act(dst[:, :], dst[:, :], AF.Sin, r=(tk,), w=(tk,))
    P.end_phase()


def emit_headnorm(P, bk, bkt, gain1, cs, rope, hc):
    sq, sqt = P.rot("hsq", 3, [128, TB], BF16)
    P.act(sq[:, :], bk[:, :], AF.Square, r=(bkt,), w=(sqt,))
    b2, b2t = P.bank()
    P.mm(b2[:, :], P.ones[:, :], sq[:, :], True, True, r=(sqt, P.tok("ones")), w=(b2t,))
    rs, rst = P.rot("hrstd", 2, [128, TB], F32)
    P.act(rs[:, :], b2[:, :], AF.Sqrt, r=(b2t, P.tok("consts")), w=(rst,), scale=1.0 / HD, bias=P.epsc[:, 0:1])
    P.op("dve", lambda e, o=rs[:, :]: e.reciprocal(out=o, in_=o), r=(rst,), w=(rst,))
    qn, qnt = P.rot("qn", 3, [128, TB], BF16)
    P.stt(qn[:, :], bk[:, :], gain1, rs[:, :], ALU.mult, ALU.mult, r=(bkt, rst, P.tok("hc")), w=(qnt,))
    if not rope:
        return qn, qnt
    if rope == "both":
        P.qn_last = (qn, qnt)
    b3, b3t = P.bank()
    P.mm(b3[:, :], P.rotT[:, :], qn[:, :], True, True, r=(qnt, P.tok("hc")), w=(b3t,))
    t1, t1t = P.rot("rp1", 2, [128, TB], F32)
    P.tt("dve", t1[:, :], qn[:, :], P.cosF[:, cs], ALU.mult, r=(qnt, P.tok("rope")), w=(t1t,))
    t2, t2t = P.rot("rp2", 2, [128, TB], F32)
    P.tt("dve", t2[:, :], b3[:, :], P.sinF[:, cs], ALU.mult, r=(b3t, P.tok("rope")), w=(t2t,))
    P.tt("pool", t1[:, :], t1[:, :], t2[:, :], ALU.add, r=(t1t, t2t), w=(t1t,))
    return t1, t1t


def load_headconsts(P, hc_dram, rotT_dram, ncol):
    P.hc = P.sb("hc", [128, ncol], F32)
    P.rotT = P.sb("rotT", [128, 128], BF16)
    P.dma("sp", P.hc[:, :], hc_dram[:, :], r=(), w=(P.tok("hc"),))
    P.dma("sp", P.rotT[:, :], rotT_dram[:, :], r=(), w=(P.tok("hc"),))


V0A = Vecs(["mixer_norm"])


def build_L0a():
    P = Prog()
    xT = P.din("hT", [128, NCH, NT], F32)
    vecs = P.din("vecs", [128, V0A.n()], F32)
    hc_d = P.din("hc", [128, 4], F32)
    rotT_d = P.din("rotT", [128, 128], BF16)
    posb = P.din("posb", [128, NT], I32)
    wqkv = P.din("w_qkv", [D, 3 * D], F32)
    qT_o = P.dout("qT", [NH, 128, NT], BF16)
    kT_o = P.dout("kT", [NH, 128, NT], BF16)
    v_o = P.dout("v", [NT, D], BF16)
    km_o = P.dout("kmean", [128, NH, 4], BF16)

    def body():
        common_setup(P, V0A.n())
        P.dma("sp", P.vecs[:, :], vecs[:, :], r=(), w=(P.tok("vecs"),))
        load_headconsts(P, hc_d, rotT_d, 4)
        load_h(P, xT)
        emit_rope_tables(P, posb, P.hc)
        xn = P.sb("xn", [128, NCH, NT], BF16)
        km = P.sb("km", [128, NH, 4], BF16)
        xt = lambda c, tb: P.tok(("xn", c, tb))
        emit_norm(P, P.h, V0A.ap(P, "mixer_norm"), xn, xt)
        for part in range(2):
            for j in range(8):
                wt, wtk = P.wnext(wqkv, 0, 16, part * D + j * 256, 256)
                for m in range(2):
                    hd = 2 * j + m
                    stg, stgt = P.rot("stage", 2, [128, NT], BF16, phase=False)
                    for tb in range(NTB):
                        cs = slice(tb * TB, (tb + 1) * TB)
                        bk, bkt = P.bank()
                        for kc in range(NCH):
                            P.mm(bk[:, :], wt[:, kc, m * 128:(m + 1) * 128], xn[:, kc, cs], kc == 0, kc == NCH - 1, r=(wtk, xt(kc, tb)), w=(bkt,))
                        qf, qft = emit_headnorm(P, bk, bkt, P.hc[:, part:part + 1], cs, True, P.hc)
                        P.copy("act", stg[:, cs], qf[:, :], r=(qft, stgt), w=(stgt,))
                        if part == 1:
                            kr, krt = P.rot("kred", 2, [128, 2], F32, phase=False)
                            P.op("dve", lambda e, o=kr[:, :], i=qf[:, :].rearrange("p (b t) -> p b t", b=2): e.tensor_reduce(out=o, in_=i, axis=AX.X, op=ALU.add),
                                 r=(qft,), w=(krt,))
                            P.ts("dve", km[:, hd, tb * 2:(tb + 1) * 2], kr[:, :], 1.0 / 256.0, None, ALU.mult, None, r=(krt, P.tok("km")), w=(P.tok("km"),))
                    P.dma("sp", (qT_o if part == 0 else kT_o)[hd], stg[:, :], r=(stgt,), w=(), is_out=True)
        for j in range(8):
            wt, wtk = P.wnext(wqkv, 0, 16, 2 * D + j * 256, 256)
            vs, vst = P.rot("vstage", 2, [128, 8, 256], BF16, phase=False)
            for tt_ in range(NT // 128):
                bk, bkt = P.bank()
                for kc in range(NCH):
                    P.mm(bk[:, 0:256], xn[:, kc, tt_ * 128:(tt_ + 1) * 128], wt[:, kc, :], kc == 0, kc == NCH - 1,
                         r=(wtk, xt(kc, tt_ // 4)), w=(bkt,))
                P.copy("act", vs[:, tt_, :], bk[:, 0:256], r=(bkt, vst), w=(vst,))
            P.dma("sp", v_o[:, j * 256:(j + 1) * 256].rearrange("(t p) n -> p t n", p=128), vs[:, :, :], r=(vst,), w=(), is_out=True)
        P.dma("sp", km_o[:, :, :], km[:, :, :], r=(P.tok("km"),), w=(), is_out=True)
    return run_two_pass(P, body)


def head_consts(qg, kg):
    freq, rotT = rope_consts()
    hc = np.zeros((128, 4), np.float32)
    hc[:, 0] = qg
    hc[:, 1] = kg
    hc[:, 2] = freq[:, 0]
    hc[:, 3] = -PI
    return hc, rotT


def layer0a(inp, x=None):
    x = inp["x"] if x is None else x
    nc = build_L0a()
    vecs = V0A.host({"mixer_norm": inp["mixer_norm"][0]})
    hc, rotT = head_consts(inp["moba_q_gain"][0], inp["moba_k_gain"][0])
    in_maps = []
    for c in range(NCORES):
        b = c // 4
        tk = core_tokens(c)
        in_maps.append({"hT": to_fm(x[b][tk]), "vecs": vecs, "hc": hc, "rotT": rotT,
                        "posb": np.ascontiguousarray(np.broadcast_to(inp["positions"][b][tk].astype(np.int32)[None, :], (128, NT))),
                        "w_qkv": inp["moba_w_qkv"][0]})
    res = run_prog(nc, in_maps)
    bf = ml_dtypes.bfloat16
    q = np.zeros((B, S, NH, HD), bf)
    k = np.zeros((B, S, NH, HD), bf)
    v = np.zeros((B, S, D), bf)
    km = np.zeros((B, S // 256, NH, HD), bf)
    for c in range(NCORES):
        b = c // 4
        tk = core_tokens(c)
        q[b, tk] = res[c]["qT"].transpose(2, 0, 1)
        k[b, tk] = res[c]["kT"].transpose(2, 0, 1)
        v[b, tk] = res[c]["v"]
        kmc = res[c]["kmean"]
        for s_, ch in enumerate(core_chunks(c)):
            km[b, 2 * ch:2 * ch + 2] = kmc[:, :, 2 * s_:2 * s_ + 2].transpose(2, 1, 0)
    return {"q": q, "k": k, "v": v, "kmean": km}


def slot_lists(c):
    j = c % 4
    return [[None] * (3 - j) + list(range(0, j + 1)), [None] * j + list(range(0, 8 - j))]


def attn_consts():
    tri = np.zeros((128, 4, TB), np.float32)
    kk = np.arange(128)[:, None]
    qq = np.arange(TB)[None, :]
    for jj in range(4):
        tri[:, jj, :] = np.where(128 * jj + kk <= qq, 0.0, NEG)
    boh = np.zeros((16, 16, 128), np.float32)
    for b_ in range(16):
        boh[b_, b_, :] = 1.0
    ident = np.eye(128, dtype=np.float32)
    bf = ml_dtypes.bfloat16
    return tri.astype(bf), boh.astype(bf), ident.astype(bf), ident


def moba_masks(c):
    m1 = np.full((128, 2, 2, 16), NEG, np.float32)
    notown = np.ones((128, 2, 2, 16), np.float32)
    for s_, lst in enumerate(slot_lists(c)):
        nblk = 2 * len(lst)
        for half in range(2):
            own = nblk - 2 + half
            for blk in range(nblk):
                valid = lst[blk // 2] is not None
                m1[:, s_, half, blk] = 0.0 if (valid and blk < own) else NEG
            notown[:, s_, half, own] = 0.0
    return m1, notown


def emit_moba_attn(P, qT_d, kT_d, v_d, kmT_d, m1_d, notown_d, tri_d, boh_d, identb_d, identf_d, w_o):
    P.begin_phase()
    P.nrot = 6
    OT = P.sb("OT", [128, NCH, NT], BF16, phase=True)
    kmT = P.sb("kmT", [128, NH, 24], BF16, phase=True)
    m1 = P.sb("m1", [128, 2, 2, 16], F32, phase=True)
    notown = P.sb("notown", [128, 2, 2, 16], F32, phase=True)
    tri = P.sb("tri", [128, 4, TB], BF16, phase=True)
    boh = P.sb("boh", [16, 16, 128], BF16, phase=True)
    identb = P.sb("identb", [128, 128], BF16, phase=True)
    identf = P.sb("identf", [128, 128], F32, phase=True)
    ac = P.tok("ac")
    for dst, src, nd in ((kmT, kmT_d, 3), (m1, m1_d, 4), (notown, notown_d, 4), (tri, tri_d, 3), (boh, boh_d, 3), (identb, identb_d, 2), (identf, identf_d, 2)):
        P.dma("sp", dst[(slice(None),) * nd], src, r=(), w=(ac,))
    scale = HD ** -0.5
    for h in range(NH):
        kT, kTt = P.rot("kT", 2, [128, 48 * 128], BF16)
        vv, vvt = P.rot("vv", 2, [128, 48, 128], BF16)
        qh, qht = P.rot("qh", 2, [128, NT], BF16)
        P.dma("sp", qh[:, :], qT_d[h], r=(), w=(qht,))
        P.dma("sp", kT[:, :], kT_d[h], r=(), w=(kTt,))
        P.dma("sp", vv[:, :, :], v_d[h], r=(), w=(vvt,))
        for s_ in range(2):
            nblk = 8 if s_ == 0 else 16
            boff = 0 if s_ == 0 else 8
            nkt = 2 * nblk
            ktoff = 0 if s_ == 0 else 16
            qs = slice(s_ * TB, (s_ + 1) * TB)
            gb, gbt = P.bank()
            for qt in range(4):
                P.mm(gb[:, qt * 16:qt * 16 + nblk], qh[:, s_ * TB + qt * 128:s_ * TB + (qt + 1) * 128], kmT[:, h, boff:boff + nblk],
                     True, True, r=(qht, ac), w=(gbt,))
            tbk, tbkt = P.bank()
            for qt in range(4):
                gm, gmt = P.rot("gm", 2, [128, 16], F32)
                P.tt("dve", gm[:, 0:nblk], gb[:, qt * 16:qt * 16 + nblk], m1[:, s_, qt // 2, 0:nblk], ALU.add, r=(gbt, ac), w=(gmt,))
                mx, mxt = P.rot("mx", 2, [128, 8], F32)
                P.op("dve", lambda e, o=mx[:, :], i=gm[:, 0:nblk]: e.max(out=o, in_=i), r=(gmt,), w=(mxt,))
                sb1, sb1t = P.rot("sb1", 2, [128, 16], F32)
                P.ts("dve", sb1[:, 0:nblk], gm[:, 0:nblk], mx[:, 2:3], -NEG, ALU.is_ge, ALU.mult, r=(gmt, mxt), w=(sb1t,))
                P.stt(sb1[:, 0:nblk], sb1[:, 0:nblk], NEG, m1[:, s_, qt // 2, 0:nblk], ALU.add, ALU.add, r=(sb1t, ac), w=(sb1t,))
                P.tt("dve", sb1[:, 0:nblk], sb1[:, 0:nblk], notown[:, s_, qt // 2, 0:nblk], ALU.mult, r=(sb1t, ac), w=(sb1t,))
                P.op("pe", lambda e, o=tbk[0:nblk, qt * 128:(qt + 1) * 128], i=sb1[:, 0:nblk], idn=identf[:, :]: e.transpose(out=o, in_=i, identity=idn),
                     r=(sb1t, ac), w=(tbkt,))
            selT, selTt = P.rot("selT", 2, [16, TB], BF16)
            P.copy("act", selT[0:nblk, :], tbk[0:nblk, :], r=(tbkt,), w=(selTt,))
            ob, obt = P.accbank(0)
            db, dbt = P.accbank(1)
            for kt in range(nkt):
                g = ktoff + kt
                sbk, sbkt = P.bank()
                diag = kt >= nkt - 4
                P.mm(sbk[:, :], kT[:, g * 128:(g + 1) * 128], qh[:, qs], True, False, r=(kTt, qht), w=(sbkt,))
                P.mm(sbk[:, :], boh[0:nblk, kt // 2, :], selT[0:nblk, :], False, not diag, r=(ac, selTt), w=(sbkt,))
                if diag:
                    P.mm(sbk[:, :], identb[:, :], tri[:, kt - (nkt - 4), :], False, True, r=(ac,), w=(sbkt,))
                E, Et = P.rot("E", 3, [128, TB], BF16)
                P.act(E[:, :], sbk[:, :], AF.Exp, r=(sbkt,), w=(Et,), scale=scale)
                P.mm(ob[:, :], vv[:, g, :], E[:, :], kt == 0, kt == nkt - 1, r=(vvt, Et), w=(obt,))
                P.mm(db[:, :], P.ones[:, :], E[:, :], kt == 0, kt == nkt - 1, r=(P.tok("ones"), Et), w=(dbt,))
            rden, rdent = P.rot("rden", 2, [128, TB], F32)
            P.op("dve", lambda e, o=rden[:, :], i=db[:, :]: e.reciprocal(out=o, in_=i), r=(dbt,), w=(rdent,))
            P.tt("dve", OT[:, h, qs], ob[:, :], rden[:, :], ALU.mult, r=(obt, rdent), w=(P.tok(("OT", h, s_)),))
    emit_outproj(P, w_o, OT, lambda kc, tb: P.tok(("OT", kc, tb)))
    P.end_phase()


V0B = Vecs(["mlp_norm", "ple_norm"])


def build_L0b():
    P = Prog()
    xT = P.din("hT", [128, NCH, NT], F32)
    pT = P.din("pT", [128, 2, NT], F32)
    vecs = P.din("vecs", [128, V0B.n()], F32)
    qT_d = P.din("qT", [NH, 128, NT], BF16)
    kT_d = P.din("kT", [NH, 128, 48 * 128], BF16)
    v_d = P.din("v", [NH, 128, 48, 128], BF16)
    kmT_d = P.din("kmT", [128, NH, 24], BF16)
    m1_d = P.din("m1", [128, 2, 2, 16], F32)
    notown_d = P.din("notown", [128, 2, 2, 16], F32)
    tri_d = P.din("tri", [128, 4, TB], BF16)
    boh_d = P.din("boh", [16, 16, 128], BF16)
    identb_d = P.din("identb", [128, 128], BF16)
    identf_d = P.din("identf", [128, 128], F32)
    w_o = P.din("w_o", [D, D], F32)
    w1 = P.din("mlp_w1", [D, DFF], F32)
    w2 = P.din("mlp_w2", [DFF, D], F32)
    wg = P.din("ple_gate", [D, D], F32)
    wp = P.din("ple_proj", [256, D], F32)
    out = P.dout("hT_out", [128, NCH, NT], F32)

    def body():
        common_setup(P, V0B.n())
        P.dma("sp", P.vecs[:, :], vecs[:, :], r=(), w=(P.tok("vecs"),))
        load_h(P, xT)
        vec = lambda n: V0B.ap(P, n)
        if DBG["mix"]:
            emit_moba_attn(P, qT_d, kT_d, v_d, kmT_d, m1_d, notown_d, tri_d, boh_d, identb_d, identf_d, w_o)
        emit_mlp_ple(P, 0, w1, w2, wg, wp, pT, vec)
        store_h(P, out)
    return run_two_pass(P, body)


def list_tokens(lst):
    return np.concatenate([(np.arange(k * TB, (k + 1) * TB) if k is not None else np.full(TB, -1)) for k in lst])


def gather_rows(a, idx):
    out = a[np.maximum(idx, 0)]
    out[idx < 0] = 0
    return out


def layer0b(inp, x, qkv):
    nc = build_L0b()
    vecs = V0B.host({"mlp_norm": inp["mlp_norm"][0], "ple_norm": inp["ple_norm"][0]})
    tri, boh, identb, identf = attn_consts()
    in_maps = []
    for c in range(NCORES):
        b = c // 4
        tk = core_tokens(c)
        lists = slot_lists(c)
        idx = np.concatenate([list_tokens(l) for l in lists])
        kl = gather_rows(qkv["k"][b], idx)
        vl = gather_rows(qkv["v"][b], idx).reshape(48, 128, NH, HD)
        bidx = np.concatenate([np.repeat(np.array([(-1 if k is None else k) for k in l]), 2) * 2 + np.tile([0, 1], len(l)) for l in lists])
        bidx = np.where(bidx < 0, -1, bidx)
        kml = gather_rows(qkv["kmean"][b], bidx)
        m1, notown = moba_masks(c)
        in_maps.append({
            "hT": to_fm(x[b][tk]), "pT": to_fm(inp["p"][0, b][tk]), "vecs": vecs,
            "qT": np.ascontiguousarray(qkv["q"][b][tk].transpose(1, 2, 0)),
            "kT": np.ascontiguousarray(kl.transpose(1, 2, 0)),
            "v": np.ascontiguousarray(vl.transpose(2, 1, 0, 3)),
            "kmT": np.ascontiguousarray(kml.transpose(2, 1, 0)),
            "m1": m1, "notown": notown, "tri": tri, "boh": boh, "identb": identb, "identf": identf,
            "w_o": inp["moba_w_o"][0],
            "mlp_w1": inp["mlp_w1"][0], "mlp_w2": inp["mlp_w2"][0], "ple_gate": inp["ple_gate"][0], "ple_proj": inp["ple_proj"][0],
        })
    res = run_prog(nc, in_maps)
    h0 = np.zeros_like(x)
    for c in range(NCORES):
        h0[c // 4][core_tokens(c)] = from_fm(res[c]["hT_out"])
    return h0


V2A = Vecs(["mixer_norm"])
GELU_C = float(2.0 * np.sqrt(2.0 / np.pi))


def build_L2a():
    P = Prog()
    P.nwslot = 5
    P.wlive = 4
    hT = P.din("hT", [128, NCH, NT], F32)
    hhalo = P.din("hhalo", [128, NCH, 32], F32)
    vecs = P.din("vecs", [128, V2A.n()], F32)
    hc_d = P.din("hc", [128, 8], F32)
    rotT_d = P.din("rotT", [128, 128], BF16)
    identf_d = P.din("identf", [128, 128], F32)
    posb = P.din("posb", [128, NT], I32)
    posT_d = P.din("cmp_posT", [128, 2, 32], F32)
    wq = P.din("w_q", [D, D], F32)
    wkv = P.din("w_kv", [D, 3072], F32)
    wgate = P.din("w_gate", [D, 48], F32)
    cw1 = P.din("cmp_w1", [2 * 4096, 128], F32)
    cw2 = P.din("cmp_w2", [2 * 128, 128], F32)
    qc_o = P.dout("qcT", [NH, 128, NT], BF16)
    qr_o = P.dout("qrT", [NH, 128, NT], BF16)
    ks_o = P.dout("kslcT", [4, 128, NT], BF16)
    kw_o = P.dout("kwinT", [4, 128, NT], BF16)
    vs_o = P.dout("vslc", [NT, 512], BF16)
    vw_o = P.dout("vwin", [NT, 512], BF16)
    kc_o = P.dout("kcmpT", [4, 128, 64], BF16)
    vc_o = P.dout("vcmpT", [4, 128, 64], BF16)
    g_o = P.dout("gT", [48, NT], BF16)

    def body():
        common_setup(P, V2A.n())
        P.dma("sp", P.vecs[:, :], vecs[:, :], r=(), w=(P.tok("vecs"),))
        load_headconsts(P, hc_d, rotT_d, 8)
        identf = P.sb("identf", [128, 128], F32)
        posT = P.sb("posT", [128, 2, 32], BF16)
        P.dma("sp", identf[:, :], identf_d[:, :], r=(), w=(P.tok("hc"),))
        P.dma("pool", posT[:, :, :], posT_d[:, :, :], r=(), w=(P.tok("hc"),))
        load_h(P, hT)
        hx = P.sb("hx", [128, NCH, 32], F32)
        xh = P.sb("xh", [128, NCH, 32], BF16)
        P.dma("sp", hx[:, :, :], hhalo[:, :, :], r=(), w=(P.tok("hx"),))
        emit_rope_tables(P, posb, P.hc)
        xn = P.sb("xn", [128, NCH, NT], BF16)
        xt = lambda c, tb: P.tok(("xn", c, tb))
        xht = lambda c, tb: P.tok(("xh", c))
        gain = V2A.ap(P, "mixer_norm")
        emit_norm(P, hx, gain, xh, xht, ntb=1, src_tokf=lambda c, tb: P.tok("hx"), tbw=32)
        emit_norm(P, P.h, gain, xn, xt)
        hct = P.tok("hc")

        def proj(wt, wtk, m, tb):
            bk, bkt = P.bank()
            cs = slice(tb * TB, (tb + 1) * TB)
            for kc in range(NCH):
                P.mm(bk[:, :], wt[:, kc, m * 128:(m + 1) * 128], xn[:, kc, cs], kc == 0, kc == NCH - 1, r=(wtk, xt(kc, tb)), w=(bkt,))
            return bk, bkt

        for j in range(8):
            wt, wtk = P.wnext(wq, 0, 16, j * 256, 256)
            for m in range(2):
                hd = 2 * j + m
                sc, sct = P.rot("stage", 2, [128, NT], BF16, phase=False)
                sr, srt = P.rot("stage2", 2, [128, NT], BF16, phase=False)
                for tb in range(NTB):
                    cs = slice(tb * TB, (tb + 1) * TB)
                    bk, bkt = proj(wt, wtk, m, tb)
                    qf, qft = emit_headnorm(P, bk, bkt, P.hc[:, 0:1], cs, "both", P.hc)
                    qn, qnt = P.qn_last
                    P.copy("pool", sc[:, cs], qn[:, :], r=(qnt, sct), w=(sct,))
                    P.copy("act", sr[:, cs], qf[:, :], r=(qft, srt), w=(srt,))
                P.dma("sp", qc_o[hd], sc[:, :], r=(sct,), w=(), is_out=True)
                P.dma("sp", qr_o[hd], sr[:, :], r=(srt,), w=(), is_out=True)
        for idx, gcol, outd in ((2, 1, ks_o), (4, 5, kw_o)):
            for j in range(2):
                wt, wtk = P.wnext(wkv, 0, 16, idx * 512 + j * 256, 256)
                for m in range(2):
                    g = 2 * j + m
                    sr, srt = P.rot("stage2", 2, [128, NT], BF16, phase=False)
                    for tb in range(NTB):
                        cs = slice(tb * TB, (tb + 1) * TB)
                        bk, bkt = proj(wt, wtk, m, tb)
                        kf, kft = emit_headnorm(P, bk, bkt, P.hc[:, gcol:gcol + 1], cs, True, P.hc)
                        P.copy("act", sr[:, cs], kf[:, :], r=(kft, srt), w=(srt,))
                    P.dma("sp", outd[g], sr[:, :], r=(srt,), w=(), is_out=True)
        for idx, outd in ((3, vs_o), (5, vw_o)):
            for j in range(2):
                wt, wtk = P.wnext(wkv, 0, 16, idx * 512 + j * 256, 256)
                vs, vst = P.rot("vstage", 2, [128, 8, 256], BF16, phase=False)
                for tt_ in range(NT // 128):
                    bk, bkt = P.bank()
                    for kc in range(NCH):
                        P.mm(bk[:, 0:256], xn[:, kc, tt_ * 128:(tt_ + 1) * 128], wt[:, kc, :], kc == 0, kc == NCH - 1,
                             r=(wtk, xt(kc, tt_ // 4)), w=(bkt,))
                    P.copy("act", vs[:, tt_, :], bk[:, 0:256], r=(bkt, vst), w=(vst,))
                P.dma("sp", outd[:, j * 256:(j + 1) * 256].rearrange("(t p) n -> p t n", p=128), vs[:, :, :], r=(vst,), w=(), is_out=True)
        wt, wtk = P.wnext(wgate, 0, 16, 0, 48)
        gts = P.sb("gts", [48, NT], BF16)
        for tt_ in range(NT // 128):
            bk, bkt = P.bank()
            for kc in range(NCH):
                P.mm(bk[:, 0:48], xn[:, kc, tt_ * 128:(tt_ + 1) * 128], wt[:, kc, :], kc == 0, kc == NCH - 1, r=(wtk, xt(kc, tt_ // 4)), w=(bkt,))
            gs, gst = P.rot("gsig", 2, [128, 48], F32, phase=False)
            P.act(gs[:, :], bk[:, 0:48], AF.Sigmoid, r=(bkt,), w=(gst,))
            b2, b2t = P.bank()
            P.op("pe", lambda e, o=b2[0:48, 0:128], i=gs[:, :], idn=identf[:, :]: e.transpose(out=o, in_=i, identity=idn), r=(gst, hct), w=(b2t,))
            P.copy("act", gts[:, tt_ * 128:(tt_ + 1) * 128], b2[0:48, 0:128], r=(b2t, P.tok("gts")), w=(P.tok("gts"),))
        P.dma("sp", g_o[:, :], gts[:, :], r=(P.tok("gts"),), w=(), is_out=True)
        kcs = P.sb("kcs", [128, 4, 64], BF16)
        vcs = P.sb("vcs", [128, 4, 64], BF16)
        for idx in range(2):
            w1t, w1k = P.wnext(cw1, idx * 4096, 32, 0, 128)
            w2t, w2k = P.wnext(cw2, idx * 128, 1, 0, 128)
            pbk, pbt = P.bank()
            for l in range(32):
                P.mm(pbk[:, 0:1], w1t[:, l, :], posT[:, idx, l:l + 1], l == 0, l == 31, r=(w1k, hct), w=(pbt,))
            pb, pbst = P.rot("posb", 2, [128, 1], F32, phase=False)
            P.copy("act", pb[:, :], pbk[:, 0:1], r=(pbt,), w=(pbst,))
            for j in range(2):
                wt, wtk = P.wnext(wkv, 0, 16, idx * 512 + j * 256, 256)
                for m in range(2):
                    g = 2 * j + m
                    for s_ in range(NTB):
                        raw, rawt = P.rot("craw", 2, [128, 16 + TB], BF16, phase=False)
                        bk, bkt = proj(wt, wtk, m, s_)
                        P.copy("act", raw[:, 16:16 + TB], bk[:, :], r=(bkt, rawt), w=(rawt,))
                        bh, bht = P.bank()
                        for kc in range(NCH):
                            P.mm(bh[:, 0:16], wt[:, kc, m * 128:(m + 1) * 128], xh[:, kc, s_ * 16:(s_ + 1) * 16], kc == 0, kc == NCH - 1,
                                 r=(wtk, xht(kc, 0)), w=(bht,))
                        P.copy("act", raw[:, 0:16], bh[:, 0:16], r=(bht, rawt), w=(rawt,))
                        cb, cbt = P.bank()
                        for l in range(32):
                            P.mm(cb[:, 0:32], w1t[:, l, :], raw[:, l:l + 16 * 31 + 1:16], l == 0, l == 31, r=(w1k, rawt), w=(cbt,))
                        x, xt_ = P.rot("cx", 2, [128, 32], F32, phase=False)
                        P.ts("dve", x[:, :], cb[:, 0:32], pb[:, 0:1], None, ALU.add, None, r=(cbt, pbst), w=(xt_,))
                        u, ut = P.rot("cu", 2, [128, 32], F32, phase=False)
                        P.tt("dve", u[:, :], x[:, :], x[:, :], ALU.mult, r=(xt_,), w=(ut,))
                        P.ts("dve", u[:, :], u[:, :], 0.044715, 1.0, ALU.mult, ALU.add, r=(ut,), w=(ut,))
                        P.tt("dve", u[:, :], u[:, :], x[:, :], ALU.mult, r=(ut, xt_), w=(ut,))
                        P.act(u[:, :], u[:, :], AF.Sigmoid, r=(ut,), w=(ut,), scale=GELU_C)
                        ge, get = P.rot("cge", 2, [128, 32], BF16, phase=False)
                        P.tt("dve", ge[:, :], u[:, :], x[:, :], ALU.mult, r=(ut, xt_), w=(get,))
                        ob, obt = P.bank()
                        P.mm(ob[:, 0:32], w2t[:, 0, :], ge[:, :], True, True, r=(w2k, get), w=(obt,))
                        if idx == 1:
                            P.copy("act", vcs[:, g, s_ * 32:(s_ + 1) * 32], ob[:, 0:32], r=(obt, P.tok("vcs")), w=(P.tok("vcs"),))
                        else:
                            sq, sqt = P.rot("csq", 2, [128, 32], BF16, phase=False)
                            P.act(sq[:, :], ob[:, 0:32], AF.Square, r=(obt,), w=(sqt,))
                            b2, b2t = P.bank()
                            P.mm(b2[:, 0:32], P.ones[:, :], sq[:, :], True, True, r=(sqt, P.tok("ones")), w=(b2t,))
                            rs, rst = P.rot("crs", 2, [128, 32], F32, phase=False)
                            P.act(rs[:, :], b2[:, 0:32], AF.Sqrt, r=(b2t, P.tok("consts")), w=(rst,), scale=1.0 / HD, bias=P.epsc[:, 0:1])
                            P.op("dve", lambda e, o=rs[:, :]: e.reciprocal(out=o, in_=o), r=(rst,), w=(rst,))
                            P.stt(kcs[:, g, s_ * 32:(s_ + 1) * 32], ob[:, 0:32], P.hc[:, 4:5], rs[:, :], ALU.mult, ALU.mult,
                                  r=(obt, rst, hct, P.tok("kcs")), w=(P.tok("kcs"),))
        for g in range(4):
            P.dma("sp", kc_o[g], kcs[:, g, :], r=(P.tok("kcs"),), w=(), is_out=True)
            P.dma("sp", vc_o[g], vcs[:, g, :], r=(P.tok("vcs"),), w=(), is_out=True)
    return run_two_pass(P, body)


def layer2a(inp, h1):
    nc = build_L2a()
    vecs = V2A.host({"mixer_norm": inp["mixer_norm"][2]})
    freq, rotT = rope_consts()
    hc = np.zeros((128, 8), np.float32)
    hc[:, 0] = inp["nsa_q_gain"][0]
    hc[:, 1] = inp["nsa_k_gain"][0, 1]
    hc[:, 2] = freq[:, 0]
    hc[:, 3] = -PI
    hc[:, 4] = inp["nsa_k_gain"][0, 0]
    hc[:, 5] = inp["nsa_k_gain"][0, 2]
    identf = np.eye(128, dtype=np.float32)
    posT = np.ascontiguousarray(inp["nsa_cmp_pos"][0].transpose(2, 0, 1))
    in_maps = []
    for c in range(NCORES):
        b = c // 4
        tk = core_tokens(c)
        in_maps.append({"hT": to_fm(h1[b][tk]), "hhalo": to_fm(halo_rows(h1[b], c, 16)), "vecs": vecs, "hc": hc, "rotT": rotT, "identf": identf,
                        "posb": np.ascontiguousarray(np.broadcast_to(inp["positions"][b][tk].astype(np.int32)[None, :], (128, NT))),
                        "cmp_posT": posT, "w_q": inp["nsa_w_q"][0], "w_kv": inp["nsa_w_kv"][0], "w_gate": inp["nsa_w_gate"][0],
                        "cmp_w1": np.ascontiguousarray(inp["nsa_cmp_w1"][0].reshape(2 * 4096, 128)),
                        "cmp_w2": np.ascontiguousarray(inp["nsa_cmp_w2"][0].reshape(2 * 128, 128))})
    res = run_prog(nc, in_maps)
    bf = ml_dtypes.bfloat16
    o = {"qc": np.zeros((B, S, NH, HD), bf), "qr": np.zeros((B, S, NH, HD), bf),
         "kslc": np.zeros((B, S, 4, HD), bf), "kwin": np.zeros((B, S, 4, HD), bf),
         "vslc": np.zeros((B, S, 512), bf), "vwin": np.zeros((B, S, 512), bf),
         "kcmp": np.zeros((B, 8, 32, 4, HD), bf), "vcmp": np.zeros((B, 8, 32, 4, HD), bf),
         "gT": np.zeros((B, S, 48), bf)}
    for c in range(NCORES):
        b = c // 4
        tk = core_tokens(c)
        r = res[c]
        o["qc"][b, tk] = r["qcT"].transpose(2, 0, 1)
        o["qr"][b, tk] = r["qrT"].transpose(2, 0, 1)
        o["kslc"][b, tk] = r["kslcT"].transpose(2, 0, 1)
        o["kwin"][b, tk] = r["kwinT"].transpose(2, 0, 1)
        o["vslc"][b, tk] = r["vslc"]
        o["vwin"][b, tk] = r["vwin"]
        o["gT"][b, tk] = r["gT"].T
        for s_, ch in enumerate(core_chunks(c)):
            o["kcmp"][b, ch] = r["kcmpT"][:, :, s_ * 32:(s_ + 1) * 32].transpose(2, 0, 1)
            o["vcmp"][b, ch] = r["vcmpT"][:, :, s_ * 32:(s_ + 1) * 32].transpose(2, 0, 1)
    return o


BIG = 1.0e30


def nsa_consts():
    bf = ml_dtypes.bfloat16
    kk = np.arange(128)[:, None]
    qq = np.arange(TB)[None, :]
    band = np.zeros((128, 4, TB), np.float32)
    for jj in range(4):
        band[:, jj, :] = np.where(128 * jj + kk > qq, 0.0, NEG)
    cmpdiag = np.zeros((128, TB), np.float32)
    for i in range(32):
        cmpdiag[96 + i, :] = np.where(16 * i + 15 <= np.arange(TB), 0.0, NEG)
    boh2 = np.zeros((64, 32, 128), np.float32)
    for kt in range(32):
        boh2[2 * kt, kt, 0:64] = 1.0
        boh2[2 * kt + 1, kt, 64:128] = 1.0
    sel48 = np.zeros((48, 48, 128), np.float32)
    for i in range(48):
        sel48[i, i, :] = 1.0
    ovl = np.zeros((128, 3, 64), np.float32)
    for s_, nch in enumerate((4, 8)):
        for e in range(32 * nch):
            ci, i = divmod(e, 32)
            n0, n1 = 512 * ci - 16 + 16 * i, 512 * ci + 16 + 16 * i
            for jl in range(8 * nch):
                j0, j1 = 64 * jl, 64 * jl + 64
                if n0 < j1 and j0 < n1:
                    tile = 0 if s_ == 0 else 1 + e // 128
                    ovl[e % 128, tile, jl] = 1.0
    return band.astype(bf), cmpdiag.astype(bf), boh2.astype(bf), sel48.astype(bf), ovl.astype(bf)


def nsa_core_consts(c):
    lists = slot_lists(c)
    cmppad = np.zeros((128, 3), np.float32)
    slcm = np.zeros((128, 2, 4, 3, 64), np.float32)
    winpad = np.zeros((128, 2), np.float32)
    for s_, lst in enumerate(lists):
        for e in range(32 * len(lst)):
            ci, i = divmod(e, 32)
            pad = lst[ci] is None or (lst[ci] == 0 and i == 0)
            tile = 0 if s_ == 0 else 1 + e // 128
            cmppad[e % 128, tile] = NEG if pad else 0.0
        nsel = 8 * len(lst)
        npad = sum(1 for k in lst if k is None)
        first = 8 * npad
        for qt in range(4):
            for p in range(128):
                cur = nsel - 8 + (qt * 128 + p) // 64
                for jl in range(64):
                    if jl >= nsel or lst[jl // 8] is None or jl > cur:
                        slcm[p, s_, qt, 0, jl], slcm[p, s_, qt, 1, jl], slcm[p, s_, qt, 2, jl] = 0.0, -BIG, NEG
                    elif jl == cur or jl == first:
                        slcm[p, s_, qt, 0, jl], slcm[p, s_, qt, 1, jl] = 0.0, BIG
                    else:
                        slcm[p, s_, qt, 0, jl] = 1.0
        winpad[:, s_] = NEG if core_chunks(c)[s_] == 0 else 0.0
    return cmppad, slcm, winpad


def emit_nsa_attn(P, dd, OT):
    P.begin_phase()
    P.nrot = 6
    ph = dict(phase=True)
    tri = P.sb("tri", [128, 4, TB], BF16, **ph)
    band = P.sb("band", [128, 4, TB], BF16, **ph)
    cmpdiag = P.sb("cmpdiag", [128, TB], BF16, **ph)
    boh2 = P.sb("boh2", [64, 32, 128], BF16, **ph)
    sel48 = P.sb("sel48", [48, 48, 128], BF16, **ph)
    ovl = P.sb("ovl", [128, 3, 64], BF16, **ph)
    identb = P.sb("identb", [128, 128], BF16, **ph)
    identf = P.sb("identf", [128, 128], F32, **ph)
    cmppad = P.sb("cmppad", [128, 3], F32, **ph)
    slcm = P.sb("slcm", [128, 2, 4, 3, 64], F32, **ph)
    winpad = P.sb("winpad", [128, 2], F32, **ph)
    gT = P.sb("gTs", [48, NT], BF16, **ph)
    ac = P.tok("ac")
    for dst, name, nd in ((tri, "tri", 3), (band, "band", 3), (cmpdiag, "cmpdiag", 2), (boh2, "boh2", 3), (sel48, "sel48", 3), (ovl, "ovl", 3),
                          (identb, "identb", 2), (identf, "identf", 2), (cmppad, "cmppad", 2), (slcm, "slcm", 5), (winpad, "winpad", 2), (gT, "gT", 2)):
        P.dma("sp", dst[(slice(None),) * nd], dd[name], r=(), w=(ac,))
    scale = HD ** -0.5
    onest = P.tok("ones")
    oacc = [P.sb("oacc%d" % r, [128, TB], F32, **ph) for r in range(4)]
    psum_ = [P.sb("psumT%d" % t, [128, TB], F32, **ph) for t in range(2)]
    pb = [P.sb("pb%d" % t, [128, TB], BF16, **ph) for t in range(2)]

    def finish_branch(h, br, r, qs, ob, obt, db, dbt, first, guard):
        rden, rdent = P.rot("rden", 2, [128, TB], F32)
        if guard:
            P.ts("dve", rden[:, :], db[:, :], 1e-30, None, ALU.max, None, r=(dbt,), w=(rdent,))
            P.op("dve", lambda e, o=rden[:, :]: e.reciprocal(out=o, in_=o), r=(rdent,), w=(rdent,))
        else:
            P.op("dve", lambda e, o=rden[:, :], i=db[:, :]: e.reciprocal(out=o, in_=i), r=(dbt,), w=(rdent,))
        gb, gbt = P.bank()
        P.mm(gb[:, :], sel48[0:48, h * 3 + br, :], gT[0:48, qs], True, True, r=(ac,), w=(gbt,))
        cf, cft = P.rot("coef", 2, [128, TB], F32)
        P.tt("dve", cf[:, :], gb[:, :], rden[:, :], ALU.mult, r=(gbt, rdent), w=(cft,))
        oat = P.tok(("oacc", r))
        if first:
            P.tt("dve", oacc[r][:, :], ob[:, :], cf[:, :], ALU.mult, r=(obt, cft), w=(oat,))
        else:
            tm, tmt = P.rot("otmp", 2, [128, TB], F32)
            P.tt("dve", tm[:, :], ob[:, :], cf[:, :], ALU.mult, r=(obt, cft), w=(tmt,))
            P.tt("pool", oacc[r][:, :], oacc[r][:, :], tm[:, :], ALU.add, r=(tmt, oat), w=(oat,))
        return rden, rdent

    for g in range(4):
        ks, kst = P.rot("ks", 1, [128, 48 * 128], BF16)
        vs, vst = P.rot("vs", 1, [128, 48, 128], BF16)
        kw, kwt = P.rot("kw", 1, [128, 16 * 128], BF16)
        vw, vwt = P.rot("vw", 1, [128, 16, 128], BF16)
        kc, kct = P.rot("kc", 1, [128, 384], BF16)
        vc, vct = P.rot("vc", 1, [128, 3, 128], BF16)
        P.dma("sp", kc[:, :], dd["kcmpT"][g], r=(), w=(kct,))
        P.dma("sp", vc[:, :, :], dd["vcmp"][g], r=(), w=(vct,))
        P.dma("sp", ks[:, :], dd["kslcT"][g], r=(), w=(kst,))
        P.dma("sp", vs[:, :, :], dd["vslc"][g], r=(), w=(vst,))
        P.dma("sp", kw[:, :], dd["kwinT"][g], r=(), w=(kwt,))
        P.dma("sp", vw[:, :, :], dd["vwin"][g], r=(), w=(vwt,))
        qcs, qrs = [], []
        for r in range(4):
            h = 4 * g + r
            qc, qct = P.rot("qc", 4, [128, NT], BF16)
            qr, qrt = P.rot("qr", 4, [128, NT], BF16)
            P.dma("sp", qc[:, :], dd["qcT"][h], r=(), w=(qct,))
            P.dma("sp", qr[:, :], dd["qrT"][h], r=(), w=(qrt,))
            qcs.append((qc, qct))
            qrs.append((qr, qrt))
        for s_ in range(2):
            qs = slice(s_ * TB, (s_ + 1) * TB)
            nch = 4 if s_ == 0 else 8
            nsel = 8 * nch
            ctiles = [0] if s_ == 0 else [1, 2]
            for r in range(4):
                h = 4 * g + r
                qc, qct = qcs[r]
                ob, obt = P.accbank(0)
                db, dbt = P.accbank(1)
                Es = []
                for ti, ct in enumerate(ctiles):
                    sbk, sbkt = P.bank()
                    last = ti == len(ctiles) - 1
                    P.mm(sbk[:, :], kc[:, ct * 128:(ct + 1) * 128], qc[:, qs], True, not last, r=(kct, qct), w=(sbkt,))
                    if last:
                        P.mm(sbk[:, :], identb[:, :], cmpdiag[:, :], False, True, r=(ac,), w=(sbkt,))
                    E, Et = P.rot("Ec", 2, [128, TB], BF16)
                    P.act(E[:, :], sbk[:, :], AF.Exp, r=(sbkt, ac), w=(Et,), scale=scale, bias=cmppad[:, ct:ct + 1])
                    P.mm(ob[:, :], vc[:, ct, :], E[:, :], ti == 0, last, r=(vct, Et), w=(obt,))
                    P.mm(db[:, :], P.ones[:, :], E[:, :], ti == 0, last, r=(onest, Et), w=(dbt,))
                    Es.append((E, Et))
                rden, rdent = finish_branch(h, 0, r, qs, ob, obt, db, dbt, True, True)
                for ti, (E, Et) in enumerate(Es):
                    pst = P.tok(("psumT", ti))
                    if r == 0:
                        P.tt("dve", psum_[ti][:, :], E[:, :], rden[:, :], ALU.mult, r=(Et, rdent), w=(pst,))
                    else:
                        tm, tmt = P.rot("otmp", 2, [128, TB], F32)
                        P.tt("dve", tm[:, :], E[:, :], rden[:, :], ALU.mult, r=(Et, rdent), w=(tmt,))
                        P.tt("pool", psum_[ti][:, :], psum_[ti][:, :], tm[:, :], ALU.add, r=(tmt, pst), w=(pst,))
            for ti in range(len(ctiles)):
                P.copy("act", pb[ti][:, :], psum_[ti][:, :], r=(P.tok(("psumT", ti)),), w=(P.tok(("pb", ti)),))
            tbk, tbkt = P.bank()
            for qt in range(4):
                ib, ibt = P.bank()
                for ti, ct in enumerate(ctiles):
                    P.mm(ib[:, 0:nsel], pb[ti][:, qt * 128:(qt + 1) * 128], ovl[:, ct, 0:nsel], ti == 0, ti == len(ctiles) - 1,
                         r=(P.tok(("pb", ti)), ac), w=(ibt,))
                im, imt = P.rot("impm", 2, [128, 64], F32)
                P.tt("dve", im[:, 0:nsel], ib[:, 0:nsel], slcm[:, s_, qt, 0, 0:nsel], ALU.mult, r=(ibt, ac), w=(imt,))
                P.tt("dve", im[:, 0:nsel], im[:, 0:nsel], slcm[:, s_, qt, 1, 0:nsel], ALU.add, r=(imt, ac), w=(imt,))
                mx, mxt = P.rot("mx", 2, [128, 8], F32)
                P.op("dve", lambda e, o=mx[:, :], i=im[:, 0:nsel]: e.max(out=o, in_=i), r=(imt,), w=(mxt,))
                rp, rpt = P.rot("rep", 2, [128, 64], F32)
                P.op("dve", lambda e, o=rp[:, 0:nsel], a=mx[:, :], i=im[:, 0:nsel]: e.match_replace(out=o, in_to_replace=a, in_values=i, imm_value=-2.0 * BIG),
                     r=(imt, mxt), w=(rpt,))
                mx2, mx2t = P.rot("mx2", 2, [128, 8], F32)
                P.op("dve", lambda e, o=mx2[:, :], i=rp[:, 0:nsel]: e.max(out=o, in_=i), r=(rpt,), w=(mx2t,))
                sb1, sb1t = P.rot("sb1", 2, [128, 64], F32)
                P.ts("dve", sb1[:, 0:nsel], im[:, 0:nsel], mx2[:, 7:8], -NEG, ALU.is_ge, ALU.mult, r=(imt, mx2t), w=(sb1t,))
                P.stt(sb1[:, 0:nsel], sb1[:, 0:nsel], NEG, slcm[:, s_, qt, 2, 0:nsel], ALU.add, ALU.add, r=(sb1t, ac), w=(sb1t,))
                P.op("pe", lambda e, o=tbk[0:nsel, qt * 128:(qt + 1) * 128], i=sb1[:, 0:nsel], idn=identf[:, :]: e.transpose(out=o, in_=i, identity=idn),
                     r=(sb1t, ac), w=(tbkt,))
            selT, selTt = P.rot("selT", 2, [64, TB], BF16)
            P.copy("act", selT[0:nsel, :], tbk[0:nsel, :], r=(tbkt,), w=(selTt,))
            nkt = 4 * nch
            ktoff = 0 if s_ == 0 else 16
            for r in range(4):
                h = 4 * g + r
                qr, qrt = qrs[r]
                ob, obt = P.accbank(0)
                db, dbt = P.accbank(1)
                for kt in range(nkt):
                    gk = ktoff + kt
                    sbk, sbkt = P.bank()
                    diag = kt >= nkt - 4
                    P.mm(sbk[:, :], ks[:, gk * 128:(gk + 1) * 128], qr[:, qs], True, False, r=(kst, qrt), w=(sbkt,))
                    P.mm(sbk[:, :], boh2[0:nsel, kt, :], selT[0:nsel, :], False, not diag, r=(ac, selTt), w=(sbkt,))
                    if diag:
                        P.mm(sbk[:, :], identb[:, :], tri[:, kt - (nkt - 4), :], False, True, r=(ac,), w=(sbkt,))
                    E, Et = P.rot("E", 3, [128, TB], BF16)
                    P.act(E[:, :], sbk[:, :], AF.Exp, r=(sbkt,), w=(Et,), scale=scale)
                    P.mm(ob[:, :], vs[:, gk, :], E[:, :], kt == 0, kt == nkt - 1, r=(vst, Et), w=(obt,))
                    P.mm(db[:, :], P.ones[:, :], E[:, :], kt == 0, kt == nkt - 1, r=(onest, Et), w=(dbt,))
                finish_branch(h, 1, r, qs, ob, obt, db, dbt, False, False)
            for r in range(4):
                h = 4 * g + r
                qr, qrt = qrs[r]
                ob, obt = P.accbank(0)
                db, dbt = P.accbank(1)
                for kt in range(8):
                    gk = s_ * 8 + kt
                    sbk, sbkt = P.bank()
                    P.mm(sbk[:, :], kw[:, gk * 128:(gk + 1) * 128], qr[:, qs], True, False, r=(kwt, qrt), w=(sbkt,))
                    msk = band[:, kt, :] if kt < 4 else tri[:, kt - 4, :]
                    P.mm(sbk[:, :], identb[:, :], msk, False, True, r=(ac,), w=(sbkt,))
                    E, Et = P.rot("E", 3, [128, TB], BF16)
                    if kt < 4:
                        P.act(E[:, :], sbk[:, :], AF.Exp, r=(sbkt, ac), w=(Et,), scale=scale, bias=winpad[:, s_:s_ + 1])
                    else:
                        P.act(E[:, :], sbk[:, :], AF.Exp, r=(sbkt,), w=(Et,), scale=scale)
                    P.mm(ob[:, :], vw[:, gk, :], E[:, :], kt == 0, kt == 7, r=(vwt, Et), w=(obt,))
                    P.mm(db[:, :], P.ones[:, :], E[:, :], kt == 0, kt == 7, r=(onest, Et), w=(dbt,))
                finish_branch(h, 2, r, qs, ob, obt, db, dbt, False, False)
                P.copy("act", OT[:, h, qs], oacc[r][:, :], r=(P.tok(("oacc", r)),), w=(P.tok(("OT", h, s_)),))
    P.end_phase()


V2B = Vecs(["mlp_norm", "ple_norm"])
L2B_IN = [("qcT", [NH, 128, NT], BF16), ("qrT", [NH, 128, NT], BF16), ("gT", [48, NT], BF16),
          ("kslcT", [4, 128, 48 * 128], BF16), ("vslc", [4, 128, 48, 128], BF16), ("kwinT", [4, 128, 16 * 128], BF16), ("vwin", [4, 128, 16, 128], BF16),
          ("kcmpT", [4, 128, 384], BF16), ("vcmp", [4, 128, 3, 128], BF16),
          ("tri", [128, 4, TB], BF16), ("band", [128, 4, TB], BF16), ("cmpdiag", [128, TB], BF16), ("boh2", [64, 32, 128], BF16),
          ("sel48", [48, 48, 128], BF16), ("ovl", [128, 3, 64], BF16), ("identb", [128, 128], BF16), ("identf", [128, 128], F32),
          ("cmppad", [128, 3], F32), ("slcm", [128, 2, 4, 3, 64], F32), ("winpad", [128, 2], F32)]


def build_L2b():
    P = Prog()
    hT = P.din("hT", [128, NCH, NT], F32)
    pT = P.din("pT", [128, 2, NT], F32)
    vecs = P.din("vecs", [128, V2B.n()], F32)
    dd = {name: P.din(name, shape, dt) for name, shape, dt in L2B_IN}
    w_o = P.din("w_o", [D, D], F32)
    w1 = P.din("mlp_w1", [D, DFF], F32)
    w2 = P.din("mlp_w2", [DFF, D], F32)
    wg = P.din("ple_gate", [D, D], F32)
    wp = P.din("ple_proj", [256, D], F32)
    out = P.dout("hT_out", [128, NCH, NT], F32)

    def body():
        common_setup(P, V2B.n(), alloc_h=False)
        P.dma("sp", P.vecs[:, :], vecs[:, :], r=(), w=(P.tok("vecs"),))
        OT = P.sb("OT", [128, NCH, NT], BF16)
        vec = lambda n: V2B.ap(P, n)
        if DBG["mix"]:
            emit_nsa_attn(P, dd, OT)
        P.h = P.sb("hT", [128, NCH, NT], F32)
        load_h(P, hT)
        if DBG["mix"]:
            P.begin_phase()
            emit_outproj(P, w_o, OT, lambda kc, tb: P.tok(("OT", kc, tb)))
            P.end_phase()
        emit_mlp_ple(P, 2, w1, w2, wg, wp, pT, vec)
        store_h(P, out)
    return run_two_pass(P, body)


def layer2b(inp, h1, a):
    nc = build_L2b()
    vecs = V2B.host({"mlp_norm": inp["mlp_norm"][2], "ple_norm": inp["ple_norm"][2]})
    tri, boh, identb, identf = attn_consts()
    band, cmpdiag, boh2, sel48, ovl = nsa_consts()
    in_maps = []
    for c in range(NCORES):
        b = c // 4
        tk = core_tokens(c)
        lists = slot_lists(c)
        idx = np.concatenate([list_tokens(l) for l in lists])
        ksl = gather_rows(a["kslc"][b], idx)
        vsl = gather_rows(a["vslc"][b], idx).reshape(48, 128, 4, HD)
        widx = np.concatenate([list_tokens([(k - 1) if k > 0 else None, k]) for k in core_chunks(c)])
        kwl = gather_rows(a["kwin"][b], widx)
        vwl = gather_rows(a["vwin"][b], widx).reshape(16, 128, 4, HD)
        cidx = np.concatenate([np.array([(-1 if k is None else k)]) for l in lists for k in l])
        kcl = gather_rows(a["kcmp"][b], cidx).reshape(384, 4, HD)
        vcl = gather_rows(a["vcmp"][b], cidx).reshape(3, 128, 4, HD)
        cmppad, slcm, winpad = nsa_core_consts(c)
        m = {
            "hT": to_fm(h1[b][tk]), "pT": to_fm(inp["p"][2, b][tk]), "vecs": vecs,
            "qcT": np.ascontiguousarray(a["qc"][b][tk].transpose(1, 2, 0)), "qrT": np.ascontiguousarray(a["qr"][b][tk].transpose(1, 2, 0)),
            "gT": np.ascontiguousarray(a["gT"][b][tk].T),
            "kslcT": np.ascontiguousarray(ksl.transpose(1, 2, 0)), "vslc": np.ascontiguousarray(vsl.transpose(2, 1, 0, 3)),
            "kwinT": np.ascontiguousarray(kwl.transpose(1, 2, 0)), "vwin": np.ascontiguousarray(vwl.transpose(2, 1, 0, 3)),
            "kcmpT": np.ascontiguousarray(kcl.transpose(1, 2, 0)), "vcmp": np.ascontiguousarray(vcl.transpose(2, 1, 0, 3)),
            "tri": tri, "band": band, "cmpdiag": cmpdiag, "boh2": boh2, "sel48": sel48, "ovl": ovl, "identb": identb, "identf": identf,
            "cmppad": cmppad, "slcm": slcm, "winpad": winpad,
            "w_o": inp["nsa_w_o"][0],
            "mlp_w1": inp["mlp_w1"][2], "mlp_w2": inp["mlp_w2"][2], "ple_gate": inp["ple_gate"][2], "ple_proj": inp["ple_proj"][2],
        }
        in_maps.append(m)
    res = run_prog(nc, in_maps)
    h2 = np.zeros_like(h1)
    for c in range(NCORES):
        h2[c // 4][core_tokens(c)] = from_fm(res[c]["hT_out"])
    return h2


def kernel(**inputs):
    inp = {k: np.asarray(v) for k, v in inputs.items()}
    x = np.ascontiguousarray(inp["x"], dtype=np.float32)
    qkv = layer0a(inp, x)
    h0 = layer0b(inp, x, qkv)
    h1 = layer1(inp, h0)
    a = layer2a(inp, h1)
    h2 = layer2b(inp, h1, a)
    h3 = layer3(inp, h2)
    return h3.astype(np.float32)
```

```python
import contextlib
import numpy as np
import ml_dtypes
import concourse.bass as bass
import concourse.mybir as mybir
from concourse.bass_utils import run_bass_kernel_spmd

F32 = mybir.dt.float32
BF16 = mybir.dt.bfloat16
I32 = mybir.dt.int32
AF = mybir.ActivationFunctionType
ALU = mybir.AluOpType
AX = mybir.AxisListType

D = 2048
NCH = 16
S = 4096
B = 2
NT = 1024
TB = 512
NTB = NT // TB
DFF = 8192
EPS = 1e-6
NCORES = 8
HD = 128
NH = 16
WSLOT = 4096
NWSLOT = 4
NEG = -30000.0


def core_chunks(c):
    j = c % 4
    return [j, 7 - j]


def core_tokens(c):
    return np.concatenate([np.arange(k * TB, (k + 1) * TB) for k in core_chunks(c)])


class Fake:
    def __getitem__(self, k):
        return self

    def rearrange(self, *a, **k):
        return self

    def ap(self):
        return self


class Ref:
    __slots__ = ("sem", "val", "eng")

    def __init__(self, sem, val, eng):
        self.sem, self.val, self.eng = sem, val, eng


class Tok:
    __slots__ = ("w", "rs")

    def __init__(self):
        self.w = None
        self.rs = {}


COMPUTE = ("pe", "act", "dve", "pool")
NDSEM = 8


class Prog:
    def __init__(self):
        self.nc = bass.Bass("TRN2", target_bir_lowering=False)
        self.es = contextlib.ExitStack()
        self.streams = {e: [] for e in COMPUTE + ("sp",)}
        self.cnt = {e: 0 for e in COMPUTE}
        self.seen = {e: {} for e in COMPUTE + ("sp",)}
        self.esem = {e: self.es.enter_context(self.nc.semaphore("sem_" + e)) for e in COMPUTE}
        self.dsem = {q: [self.es.enter_context(self.nc.semaphore("dsem_%s%d" % (q, i))) for i in range(NDSEM)]
                     for q in ("sp", "pool")}
        self.dcnt = {"sp": 0, "pool": 0}
        self.toks = {}
        self.dry = False
        self.wplan = []
        self.wi = 0
        self.wissued = 0
        self.banks = None
        self.bi = 0
        self.rots = {}
        self.dram = {}
        self.out_refs = []
        self.phase_es = None
        self.nrot = 8
        self.nwslot = NWSLOT
        self.wlive = 2

    def din(self, name, shape, dtype):
        t = self.nc.dram_tensor(name, list(shape), dtype, kind="ExternalInput").ap()
        self.dram[name] = t
        return t

    def dout(self, name, shape, dtype):
        t = self.nc.dram_tensor(name, list(shape), dtype, kind="ExternalOutput").ap()
        self.dram[name] = t
        return t

    def sb(self, name, shape, dtype, phase=False):
        if self.dry:
            return Fake()
        es = self.phase_es if (phase and self.phase_es is not None) else self.es
        self.uid = getattr(self, "uid", 0) + 1
        return es.enter_context(self.nc.sbuf_tensor("s%d_%s" % (self.uid, name), list(shape), dtype))

    def begin_phase(self):
        if self.dry:
            return
        self.barrier()
        self.phase_es = contextlib.ExitStack()
        self.rots = {k: v for k, v in self.rots.items() if not v[3]}

    def end_phase(self):
        if self.dry:
            return
        self.barrier()
        self.phase_es.close()
        self.phase_es = None
        self.rots = {k: v for k, v in self.rots.items() if not v[3]}

    def setup_psum(self):
        if self.dry:
            self.banks = [Fake() for _ in range(8)]
        else:
            self.banks = [self.es.enter_context(self.nc.psum_tensor("bank%d" % i, [128, 512], F32)) for i in range(8)]

    def bank(self):
        i = self.bi
        self.bi = (self.bi + 1) % self.nrot
        return self.banks[i], self.tok(("bank", i))

    def accbank(self, i):
        assert self.nrot + i < 8
        return self.banks[self.nrot + i], self.tok(("bank", self.nrot + i))

    def rot(self, name, n, shape, dtype, phase=True):
        if name not in self.rots:
            self.rots[name] = ([self.sb("%s_%d" % (name, i), shape, dtype, phase=phase) for i in range(n)], 0, n, phase)
        tiles, i, n_, ph = self.rots[name]
        self.rots[name] = (tiles, (i + 1) % n_, n_, ph)
        return tiles[i], self.tok((name, i))

    def tok(self, key):
        t = self.toks.get(key)
        if t is None:
            t = self.toks[key] = Tok()
        return t

    def _deps(self, eng, r, w):
        waits = {}

        def add(ref):
            if ref is None:
                return
            if ref.eng == eng and eng == "pe":
                return
            k = id(ref.sem)
            if k not in waits or waits[k][1] < ref.val:
                waits[k] = (ref.sem, ref.val)
        for t in r:
            add(t.w)
        for t in w:
            add(t.w)
            for ref in t.rs.values():
                add(ref)
        out = []
        seen = self.seen[eng]
        for k, (sem, val) in waits.items():
            if seen.get(k, 0) < val:
                seen[k] = val
                out.append((sem, val))
        return out

    def _commit(self, ref, r, w):
        for t in r:
            k = id(ref.sem)
            old = t.rs.get(k)
            if old is None or old.val < ref.val:
                t.rs[k] = ref
        for t in w:
            t.w = ref
            t.rs = {}

    def op(self, eng, fn, r=(), w=()):
        if self.dry:
            return
        waits = self._deps(eng, r, w)
        self.cnt[eng] += 1
        ref = Ref(self.esem[eng], self.cnt[eng], eng)
        self.streams[eng].append((waits, fn, (ref.sem, 1)))
        self._commit(ref, r, w)

    def dma(self, q, out, in_, r=(), w=(), is_out=False):
        if self.dry:
            return
        n = self.dcnt[q]
        self.dcnt[q] += 1
        sem = self.dsem[q][n % NDSEM]
        val = 16 * (n // NDSEM + 1)
        waits = self._deps(q, r, w)
        if n >= NDSEM:
            k = id(sem)
            if self.seen[q].get(k, 0) < val - 16:
                self.seen[q][k] = val - 16
                waits.append((sem, val - 16))
        ref = Ref(sem, val, "dma_" + q)
        self.streams[q].append((waits, (lambda e, o=out, i=in_: e.dma_start(out=o, in_=i)), (sem, 16)))
        self._commit(ref, r, w)
        if is_out:
            self.out_refs.append(ref)

    def barrier(self):
        if self.dry:
            return
        allw = [(self.esem[e], self.cnt[e]) for e in COMPUTE if self.cnt[e] > 0]
        for q in ("sp", "pool"):
            n = self.dcnt[q]
            for i in range(min(n, NDSEM)):
                last = ((n - 1 - i) // NDSEM) * NDSEM + i
                allw.append((self.dsem[q][i], 16 * (last // NDSEM + 1)))
        for e in COMPUTE + ("sp",):
            seen = self.seen[e]
            ws = []
            for sem, val in allw:
                if seen.get(id(sem), 0) < val:
                    seen[id(sem)] = val
                    ws.append((sem, val))
            if ws:
                self.streams[e].append((ws, None, None))

    def mm(self, out, lhsT, rhs, start, stop, r, w):
        self.op("pe", lambda e: e.matmul(out, lhsT, rhs, start=start, stop=stop), r, w)

    def act(self, out, in_, func, r, w, **kw):
        self.op("act", lambda e: e.activation(out=out, in_=in_, func=func, **kw), r, w)

    def tt(self, eng, out, in0, in1, op, r, w):
        self.op(eng, lambda e: e.tensor_tensor(out=out, in0=in0, in1=in1, op=op), r, w)

    def ts(self, eng, out, in0, s1, s2, op0, op1, r, w):
        if s2 is None:
            self.op(eng, lambda e: e.tensor_scalar(out=out, in0=in0, scalar1=s1, scalar2=None, op0=op0), r, w)
        else:
            self.op(eng, lambda e: e.tensor_scalar(out=out, in0=in0, scalar1=s1, scalar2=s2, op0=op0, op1=op1), r, w)

    def stt(self, out, in0, scalar, in1, op0, op1, r, w):
        self.op("dve", lambda e: e.scalar_tensor_tensor(out=out, in0=in0, scalar=scalar, in1=in1, op0=op0, op1=op1), r, w)

    def copy(self, eng, out, in_, r, w):
        if eng == "act":
            self.op("act", lambda e: e.copy(out=out, in_=in_), r, w)
        else:
            self.op(eng, lambda e: e.tensor_copy(out=out, in_=in_), r, w)

    def setup_wslots(self):
        self.wslots = [self.sb("wslot%d" % i, [128, WSLOT], BF16) for i in range(self.nwslot)]

    def wnext(self, wap, k0, kc, n0, ncols):
        assert kc * ncols <= WSLOT
        if self.dry:
            self.wplan.append((wap, k0, kc, n0, ncols))
            return Fake(), None
        i = self.wi
        self.wi += 1
        assert self.wplan[i][1:] == (k0, kc, n0, ncols), (self.wplan[i][1:], (k0, kc, n0, ncols))
        while self.wissued < min(len(self.wplan), i + self.nwslot - self.wlive + 1):
            jj = self.wissued
            wap_, k0_, kc_, n0_, nc_ = self.wplan[jj]
            slot = self.wslots[jj % self.nwslot]
            dst = slot[:, 0:kc_ * nc_].rearrange("p (k n) -> p k n", k=kc_)
            src = wap_[k0_:k0_ + kc_ * 128, n0_:n0_ + nc_].rearrange("(k p) n -> p k n", p=128)
            self.dma("pool", dst, src, r=(), w=(self.tok(("wslot", jj % self.nwslot)),))
            self.wissued += 1
        slot = self.wslots[i % self.nwslot]
        return slot[:, 0:kc * ncols].rearrange("p (k n) -> p k n", k=kc), self.tok(("wslot", i % self.nwslot))

    def finish(self):
        ws = []
        seen = self.seen["sp"]
        best = {}
        for ref in self.out_refs:
            k = id(ref.sem)
            if k not in best or best[k][1] < ref.val:
                best[k] = (ref.sem, ref.val)
        for k, (sem, val) in best.items():
            if seen.get(k, 0) < val:
                ws.append((sem, val))
        self.barrier()
        self.streams["sp"].append((ws, None, None))
        nc = self.nc
        regs = {"pe": "tensor", "act": "scalar", "dve": "vector", "pool": "gpsimd", "sp": "sync"}
        with nc.Block() as block:
            for eng, attr in regs.items():
                stream = self.streams[eng]

                def f(e, stream=stream):
                    for waits, fn, inc in stream:
                        for s_, v_ in waits:
                            e.wait_ge(s_, v_)
                        if fn is not None:
                            ins = fn(e)
                            if inc is not None:
                                ins.then_inc(inc[0], inc[1])
                getattr(block, attr)(f)
        self.es.close()
        return nc


def emit_rstd(P, h, cs, tbw, src_toks):
    bk, bkt = P.bank()
    for c in range(NCH):
        sq, sqt = P.rot("sq", 3, [128, TB], BF16)
        P.tt("pool", sq[:, :tbw], h[:, c, cs], h[:, c, cs], ALU.mult, r=(src_toks[c],), w=(sqt,))
        P.mm(bk[:, :tbw], P.ones[:, :], sq[:, :tbw], c == 0, c == NCH - 1, r=(sqt, P.tok("ones")), w=(bkt,))
    rstd, rt = P.rot("rstd", 2, [128, TB], F32)
    P.act(rstd[:, :tbw], bk[:, :tbw], AF.Sqrt, r=(bkt, P.tok("consts")), w=(rt,), scale=1.0 / D, bias=P.epsc[:, 0:1])
    P.op("dve", lambda e, o=rstd[:, :tbw]: e.reciprocal(out=o, in_=o), r=(rt,), w=(rt,))
    return rstd, rt


def emit_norm(P, h, gain_col, dst, dst_tokf, ntb=NTB, src_tokf=None, tbw=TB):
    if src_tokf is None:
        src_tokf = lambda c, tb: P.tok(("h", c, tb))
    for tb in range(ntb):
        cs = slice(tb * tbw, (tb + 1) * tbw)
        rstd, rt = emit_rstd(P, h, cs, tbw, [src_tokf(c, tb) for c in range(NCH)])
        for c in range(NCH):
            P.stt(dst[:, c, cs], h[:, c, cs], gain_col[:, c:c + 1], rstd[:, :tbw], ALU.mult, ALU.mult,
                  r=(src_tokf(c, tb), rt, P.tok("vecs")), w=(dst_tokf(c, tb),))


DBG = {"mlp": True, "ple": True, "mix": True}


def pipeline(items, lags):
    n = len(items)
    tot = n + max(lags)
    for step in range(tot):
        for k, lag in enumerate(lags):
            i = step - lag
            if 0 <= i < n:
                items[i][k]()


def emit_mlp_ple(P, L, w1, w2, wg, wp, pT_dram, vec):
    h = P.h
    P.begin_phase()
    xn = P.sb("xn", [128, NCH, NT], BF16, phase=True)
    hid = P.sb("hid", [128, 8, NT], BF16, phase=True)
    pT = P.sb("pT", [128, 2, NT], BF16, phase=True)
    xt = lambda c, tb: P.tok(("xn", c, tb))
    ht = lambda c, tb: P.tok(("h", c, tb))
    hidt = lambda c, tb: P.tok(("hid", c, tb))
    P.dma("pool", pT[:, :, :], pT_dram[:, :, :], r=(), w=(P.tok("pT"),))
    if DBG["mlp"]:
        emit_norm(P, h, vec("mlp_norm"), xn, xt)
    for fb in range(8 if DBG["mlp"] else 0):
        for j in range(4):
            wt, wtk = P.wnext(w1, 0, 16, fb * 1024 + j * 256, 256)
            for m in range(2):
                for tb in range(NTB):
                    cs = slice(tb * TB, (tb + 1) * TB)
                    bk, bkt = P.bank()
                    for kc in range(NCH):
                        P.mm(bk[:, :], wt[:, kc, m * 128:(m + 1) * 128], xn[:, kc, cs], kc == 0, kc == NCH - 1,
                             r=(wtk, xt(kc, tb)), w=(bkt,))
                    rl, rlt = P.rot("relu", 3, [128, TB], F32)
                    P.act(rl[:, :], bk[:, :], AF.Relu, r=(bkt,), w=(rlt,))
                    P.tt("pool", hid[:, j * 2 + m, cs], rl[:, :], rl[:, :], ALU.mult, r=(rlt,), w=(hidt(j * 2 + m, tb),))
        for j in range(4):
            wt, wtk = P.wnext(w2, fb * 1024, 8, j * 512, 512)
            for m in range(4):
                for tb in range(NTB):
                    cs = slice(tb * TB, (tb + 1) * TB)
                    bk, bkt = P.bank()
                    for kc in range(8):
                        P.mm(bk[:, :], wt[:, kc, m * 128:(m + 1) * 128], hid[:, kc, cs], kc == 0, kc == 7,
                             r=(wtk, hidt(kc, tb)), w=(bkt,))
                    c = j * 4 + m
                    P.tt("dve", h[:, c, cs], h[:, c, cs], bk[:, :], ALU.add, r=(bkt, ht(c, tb)), w=(ht(c, tb),))
    if DBG["ple"]:
        emit_norm(P, h, vec("ple_norm"), xn, xt)
    for j in range(8 if DBG["ple"] else 0):
        wt, wtk = P.wnext(wg, 0, 16, j * 256, 256)
        wq, wqk = P.wnext(wp, 0, 2, j * 256, 256)
        for m in range(2):
            for tb in range(NTB):
                cs = slice(tb * TB, (tb + 1) * TB)
                c = j * 2 + m
                bk, bkt = P.bank()
                for kc in range(NCH):
                    P.mm(bk[:, :], wt[:, kc, m * 128:(m + 1) * 128], xn[:, kc, cs], kc == 0, kc == NCH - 1,
                         r=(wtk, xt(kc, tb)), w=(bkt,))
                g, gt = P.rot("gate", 3, [128, TB], F32)
                P.act(g[:, :], bk[:, :], AF.Sigmoid, r=(bkt,), w=(gt,))
                bk2, bk2t = P.bank()
                for kc in range(2):
                    P.mm(bk2[:, :], wq[:, kc, m * 128:(m + 1) * 128], pT[:, kc, cs], kc == 0, kc == 1,
                         r=(wqk, P.tok("pT")), w=(bk2t,))
                P.tt("dve", g[:, :], g[:, :], bk2[:, :], ALU.mult, r=(gt, bk2t), w=(gt,))
                P.tt("dve", h[:, c, cs], h[:, c, cs], g[:, :], ALU.add, r=(gt, ht(c, tb)), w=(ht(c, tb),))
    P.end_phase()


def common_setup(P, nvec, alloc_h=True):
    P.setup_psum()
    P.setup_wslots()
    if alloc_h:
        P.h = P.sb("hT", [128, NCH, NT], F32)
    P.ones = P.sb("ones", [128, 128], BF16)
    P.epsc = P.sb("epsc", [128, 1], F32)
    P.vecs = P.sb("vecs", [128, nvec], F32)
    P.op("pool", lambda e: e.memset(P.ones[:, :], 1.0), r=(), w=(P.tok("ones"),))
    P.op("pool", lambda e: e.memset(P.epsc[:, :], EPS), r=(), w=(P.tok("consts"),))


def load_h(P, hT_dram):
    for c in range(NCH):
        P.dma("sp", P.h[:, c, :], hT_dram[:, c, :], r=(), w=tuple(P.tok(("h", c, tb)) for tb in range(NTB)))


def store_h(P, out_dram):
    for c in range(NCH):
        P.dma("sp", out_dram[:, c, :], P.h[:, c, :], r=tuple(P.tok(("h", c, tb)) for tb in range(NTB)), w=(), is_out=True)


class Vecs:
    def __init__(self, names):
        self.names = list(names)

    def n(self):
        return 16 * len(self.names)

    def ap(self, P, name):
        i = self.names.index(name)
        return P.vecs[:, 16 * i:16 * (i + 1)]

    def host(self, arrs):
        return np.ascontiguousarray(np.concatenate([np.asarray(arrs[n], np.float32).reshape(16, 128).T for n in self.names], axis=1))


def run_two_pass(P, body):
    P.dry = True
    body()
    P.dry = False
    P.bi = 0
    P.rots = {}
    P.toks = {}
    body()
    assert P.wi == len(P.wplan), (P.wi, len(P.wplan))
    return P.finish()


POOL_W = (2, 4, 8, 16)


def emit_pool(P, wpool, hhalo_dram, fac_dram, vec):
    h = P.h
    P.begin_phase()
    hx = P.sb("hx", [128, NCH, 32], F32, phase=True)
    xh = P.sb("xh", [128, NCH, 32], F32, phase=True)
    fac = P.sb("fac", [128, 4, 2, 16], F32, phase=True)
    P.dma("sp", hx[:, :, :], hhalo_dram[:, :, :], r=(), w=(P.tok("hx"),))
    P.dma("sp", fac[:, :, :, :], fac_dram[:, :, :, :], r=(), w=(P.tok("fac"),))
    gain = vec("mixer_norm")
    scale = vec("pool_scale")
    emit_norm(P, hx, gain, xh, lambda c, tb: P.tok(("xh", c)), ntb=1, src_tokf=lambda c, tb: P.tok("hx"), tbw=32)
    ht = lambda c, tb: P.tok(("h", c, tb))
    for s in range(NTB):
        cs = slice(s * TB, (s + 1) * TB)
        rstd, rt = emit_rstd(P, h, cs, TB, [ht(c, s) for c in range(NCH)])
        for g in range(4):
            w = POOL_W[g]
            dg, dgt = P.rot("diffg", 2, [128, 4, TB], BF16)
            for cc in range(4):
                c = 4 * g + cc
                xe, xet = P.rot("xe", 2, [128, 16 + TB], F32)
                P.copy("pool", xe[:, 0:16], xh[:, c, s * 16:(s + 1) * 16], r=(P.tok(("xh", c)),), w=(xet,))
                P.stt(xe[:, 16:16 + TB], h[:, c, cs], gain[:, c:c + 1], rstd[:, :], ALU.mult, ALU.mult,
                      r=(ht(c, s), rt, P.tok("vecs"), xet), w=(xet,))
                cur, curt = xe, xet
                shift = 1
                for st in range(g + 1):
                    nx, nxt = P.rot("ss", 3, [128, 16 + TB], F32)
                    lo = 2 * shift - 1
                    P.tt("dve", nx[:, lo:16 + TB], cur[:, lo:16 + TB], cur[:, lo - shift:16 + TB - shift], ALU.add,
                         r=(curt,), w=(nxt,))
                    cur, curt = nx, nxt
                    shift *= 2
                P.stt(dg[:, cc, :], cur[:, 16:16 + TB], 1.0 / w, xe[:, 16:16 + TB], ALU.mult, ALU.subtract,
                      r=(curt, xet), w=(dgt,))
                t16, t16t = P.rot("t16", 2, [128, 16], F32)
                P.tt("dve", t16[:, :], cur[:, 16:32], fac[:, g, s, :], ALU.mult, r=(curt, P.tok("fac")), w=(t16t,))
                P.tt("dve", dg[:, cc, 0:16], t16[:, :], xe[:, 16:32], ALU.subtract, r=(t16t, xet, dgt), w=(dgt,))
            wt, wtk = P.wnext(wpool, g * 512, 4, 0, 512)
            for oc in range(4):
                bk, bkt = P.bank()
                for kc in range(4):
                    P.mm(bk[:, :], wt[:, kc, oc * 128:(oc + 1) * 128], dg[:, kc, :], kc == 0, kc == 3, r=(wtk, dgt), w=(bkt,))
                c = 4 * g + oc
                P.stt(h[:, c, cs], bk[:, :], scale[:, c:c + 1], h[:, c, cs], ALU.mult, ALU.add,
                      r=(bkt, ht(c, s), P.tok("vecs")), w=(ht(c, s),))
    P.end_phase()


V1 = Vecs(["mixer_norm", "pool_scale", "mlp_norm", "ple_norm"])


def build_L1():
    P = Prog()
    hT = P.din("hT", [128, NCH, NT], F32)
    hhalo = P.din("hhalo", [128, NCH, 32], F32)
    fac = P.din("poolfac", [128, 4, 2, 16], F32)
    pT = P.din("pT", [128, 2, NT], F32)
    vecs = P.din("vecs", [128, V1.n()], F32)
    wpool = P.din("pool_w", [2048, 512], F32)
    w1 = P.din("mlp_w1", [D, DFF], F32)
    w2 = P.din("mlp_w2", [DFF, D], F32)
    wg = P.din("ple_gate", [D, D], F32)
    wp = P.din("ple_proj", [256, D], F32)
    out = P.dout("hT_out", [128, NCH, NT], F32)

    def body():
        common_setup(P, V1.n())
        P.dma("sp", P.vecs[:, :], vecs[:, :], r=(), w=(P.tok("vecs"),))
        load_h(P, hT)
        vec = lambda n: V1.ap(P, n)
        if DBG["mix"]:
            emit_pool(P, wpool, hhalo, fac, vec)
        emit_mlp_ple(P, 1, w1, w2, wg, wp, pT, vec)
        store_h(P, out)
    return run_two_pass(P, body)


def to_fm(a):
    ntok, nf = a.shape
    return np.ascontiguousarray(a.T.reshape(nf // 128, 128, ntok).transpose(1, 0, 2))


def from_fm(a):
    p, nch, ntok = a.shape
    return np.ascontiguousarray(a.transpose(1, 0, 2).reshape(nch * 128, ntok).T)


def halo_rows(hfull_b, c, n):
    out = np.zeros((2 * n, hfull_b.shape[1]), np.float32)
    for s, k in enumerate(core_chunks(c)):
        if k > 0:
            out[s * n:(s + 1) * n] = hfull_b[k * TB - n:k * TB]
    return out


def pool_fac(c):
    f = np.zeros((128, 4, 2, 16), np.float32)
    for g, w in enumerate(POOL_W):
        for s, k in enumerate(core_chunks(c)):
            for t in range(16):
                cnt = min(w, t + 1) if k == 0 else w
                f[:, g, s, t] = 1.0 / cnt
    return f


def run_prog(nc, in_maps):
    if DBG.get("trace"):
        res = run_bass_kernel_spmd(nc, in_maps, core_ids=list(range(NCORES)), trace=True)
        print("EXEC_TIME_NS", res.exec_time_ns, flush=True)
        DBG["last_res"] = res
        return res.results
    res = run_bass_kernel_spmd(nc, in_maps, core_ids=list(range(NCORES)))
    return res.results


def layer1(inp, h0):
    nc = build_L1()
    vec_arrs = {"mixer_norm": inp["mixer_norm"][1], "pool_scale": inp["pool_scale"][0], "mlp_norm": inp["mlp_norm"][1],
                "ple_norm": inp["ple_norm"][1]}
    vecs = V1.host(vec_arrs)
    in_maps = []
    for c in range(NCORES):
        b = c // 4
        tk = core_tokens(c)
        in_maps.append({
            "hT": to_fm(h0[b][tk]), "hhalo": to_fm(halo_rows(h0[b], c, 16)), "poolfac": pool_fac(c),
            "pT": to_fm(inp["p"][1, b][tk]), "vecs": vecs,
            "pool_w": np.ascontiguousarray(inp["pool_w"][0].reshape(2048, 512)),
            "mlp_w1": inp["mlp_w1"][1], "mlp_w2": inp["mlp_w2"][1], "ple_gate": inp["ple_gate"][1], "ple_proj": inp["ple_proj"][1],
        })
    res = run_prog(nc, in_maps)
    h1 = np.zeros_like(h0)
    for c in range(NCORES):
        h1[c // 4][core_tokens(c)] = from_fm(res[c]["hT_out"])
    return h1


def emit_conv(P, w_in, w_o, hhalo_dram, vec):
    h = P.h
    P.begin_phase()
    hx = P.sb("hx", [128, NCH, 32], F32, phase=True)
    xh = P.sb("xh", [128, NCH, 32], BF16, phase=True)
    xn = P.sb("xn", [128, NCH, NT], BF16, phase=True)
    gT = P.sb("gT", [128, NCH, NT], BF16, phase=True)
    P.dma("sp", hx[:, :, :], hhalo_dram[:, :, :], r=(), w=(P.tok("hx"),))
    gain = vec("mixer_norm")
    xht = lambda c, tb: P.tok(("xh", c))
    xt = lambda c, tb: P.tok(("xn", c, tb))
    ht = lambda c, tb: P.tok(("h", c, tb))
    gt = lambda c, tb: P.tok(("gT", c, tb))
    emit_norm(P, hx, gain, xh, xht, ntb=1, src_tokf=lambda c, tb: P.tok("hx"), tbw=32)
    emit_norm(P, h, gain, xn, xt)
    w0, w1, w2, cb = vec("conv_w0"), vec("conv_w1"), vec("conv_w2"), vec("conv_b")
    vt = P.tok("vecs")

    def proj(wt, wtk, m, tb):
        bk, bkt = P.bank()
        cs = slice(tb * TB, (tb + 1) * TB)
        for kc in range(NCH):
            P.mm(bk[:, :], wt[:, kc, m * 128:(m + 1) * 128], xn[:, kc, cs], kc == 0, kc == NCH - 1, r=(wtk, xt(kc, tb)), w=(bkt,))
        return bk, bkt

    def projh(wt, wtk, m):
        bk, bkt = P.bank()
        for kc in range(NCH):
            P.mm(bk[:, 0:32], wt[:, kc, m * 128:(m + 1) * 128], xh[:, kc, :], kc == 0, kc == NCH - 1, r=(wtk, xht(kc, 0)), w=(bkt,))
        return bk, bkt

    for dcp in range(8):
        wt, wtk = P.wnext(w_in, 0, 16, 2048 + dcp * 256, 256)
        cg = {}
        for m in range(2):
            for tb in range(NTB):
                bk, bkt = proj(wt, wtk, m, tb)
                t, tt_ = P.rot("cg", 4, [128, TB], F32)
                P.copy("act", t[:, :], bk[:, :], r=(bkt,), w=(tt_,))
                cg[(m, tb)] = (t, tt_)
            bk, bkt = projh(wt, wtk, m)
            t, tt_ = P.rot("cgh", 2, [128, 32], F32)
            P.copy("act", t[:, :], bk[:, 0:32], r=(bkt,), w=(tt_,))
            cg[(m, "h")] = (t, tt_)
        wt, wtk = P.wnext(w_in, 0, 16, 4096 + dcp * 256, 256)
        cv = {}
        for m in range(2):
            c = dcp * 2 + m
            us = {}
            for tb in range(NTB):
                bk, bkt = proj(wt, wtk, m, tb)
                u, ut = P.rot("u", 4, [128, 16 + TB], F32)
                P.tt("dve", u[:, 16:16 + TB], cg[(m, tb)][0][:, :], bk[:, :], ALU.mult, r=(cg[(m, tb)][1], bkt), w=(ut,))
                us[tb] = (u, ut)
            bk, bkt = projh(wt, wtk, m)
            for tb in range(NTB):
                u, ut = us[tb]
                P.tt("dve", u[:, 0:16], cg[(m, "h")][0][:, tb * 16:(tb + 1) * 16], bk[:, tb * 16:(tb + 1) * 16], ALU.mult,
                     r=(cg[(m, "h")][1], bkt, ut), w=(ut,))
                v, vtk = P.rot("cv", 4, [128, TB], F32)
                P.ts("dve", v[:, :], u[:, 14:14 + TB], w0[:, c:c + 1], cb[:, c:c + 1], ALU.mult, ALU.add, r=(ut, vt), w=(vtk,))
                P.stt(v[:, :], u[:, 15:15 + TB], w1[:, c:c + 1], v[:, :], ALU.mult, ALU.add, r=(ut, vt, vtk), w=(vtk,))
                P.stt(v[:, :], u[:, 16:16 + TB], w2[:, c:c + 1], v[:, :], ALU.mult, ALU.add, r=(ut, vt, vtk), w=(vtk,))
                cv[(m, tb)] = (v, vtk)
        wt, wtk = P.wnext(w_in, 0, 16, dcp * 256, 256)
        for m in range(2):
            c = dcp * 2 + m
            for tb in range(NTB):
                bk, bkt = proj(wt, wtk, m, tb)
                P.tt("dve", gT[:, c, tb * TB:(tb + 1) * TB], bk[:, :], cv[(m, tb)][0][:, :], ALU.mult, r=(bkt, cv[(m, tb)][1]), w=(gt(c, tb),))
    emit_outproj(P, w_o, gT, gt)
    P.end_phase()


def emit_outproj(P, w_o, src, src_tokf):
    h = P.h
    ht = lambda c, tb: P.tok(("h", c, tb))
    for j in range(8):
        wt, wtk = P.wnext(w_o, 0, 16, j * 256, 256)
        for m in range(2):
            c = j * 2 + m
            for tb in range(NTB):
                cs = slice(tb * TB, (tb + 1) * TB)
                bk, bkt = P.bank()
                for kc in range(NCH):
                    P.mm(bk[:, :], wt[:, kc, m * 128:(m + 1) * 128], src[:, kc, cs], kc == 0, kc == NCH - 1,
                         r=(wtk, src_tokf(kc, tb)), w=(bkt,))
                P.tt("dve", h[:, c, cs], h[:, c, cs], bk[:, :], ALU.add, r=(bkt, ht(c, tb)), w=(ht(c, tb),))


V3 = Vecs(["mixer_norm", "conv_w0", "conv_w1", "conv_w2", "conv_b", "mlp_norm", "ple_norm"])


def build_L3():
    P = Prog()
    hT = P.din("hT", [128, NCH, NT], F32)
    hhalo = P.din("hhalo", [128, NCH, 32], F32)
    pT = P.din("pT", [128, 2, NT], F32)
    vecs = P.din("vecs", [128, V3.n()], F32)
    w_in = P.din("conv_w_in", [D, 3 * D], F32)
    w_o = P.din("conv_w_o", [D, D], F32)
    w1 = P.din("mlp_w1", [D, DFF], F32)
    w2 = P.din("mlp_w2", [DFF, D], F32)
    wg = P.din("ple_gate", [D, D], F32)
    wp = P.din("ple_proj", [256, D], F32)
    out = P.dout("hT_out", [128, NCH, NT], F32)

    def body():
        common_setup(P, V3.n())
        P.dma("sp", P.vecs[:, :], vecs[:, :], r=(), w=(P.tok("vecs"),))
        load_h(P, hT)
        vec = lambda n: V3.ap(P, n)
        if DBG["mix"]:
            emit_conv(P, w_in, w_o, hhalo, vec)
        emit_mlp_ple(P, 3, w1, w2, wg, wp, pT, vec)
        store_h(P, out)
    return run_two_pass(P, body)


def layer3(inp, h2):
    nc = build_L3()
    vec_arrs = {"mixer_norm": inp["mixer_norm"][3], "conv_w0": inp["conv_w"][0, 0], "conv_w1": inp["conv_w"][0, 1],
                "conv_w2": inp["conv_w"][0, 2], "conv_b": inp["conv_b"][0], "mlp_norm": inp["mlp_norm"][3], "ple_norm": inp["ple_norm"][3]}
    vecs = V3.host(vec_arrs)
    in_maps = []
    for c in range(NCORES):
        b = c // 4
        tk = core_tokens(c)
        in_maps.append({
            "hT": to_fm(h2[b][tk]), "hhalo": to_fm(halo_rows(h2[b], c, 16)),
            "pT": to_fm(inp["p"][3, b][tk]), "vecs": vecs,
            "conv_w_in": inp["conv_w_in"][0], "conv_w_o": inp["conv_w_o"][0],
            "mlp_w1": inp["mlp_w1"][3], "mlp_w2": inp["mlp_w2"][3], "ple_gate": inp["ple_gate"][3], "ple_proj": inp["ple_proj"][3],
        })
    res = run_prog(nc, in_maps)
    h3 = np.zeros_like(h2)
    for c in range(NCORES):
        h3[c // 4][core_tokens(c)] = from_fm(res[c]["hT_out"])
    return h3


ROPE_THETA = 500000.0
PI = float(np.pi)


def rope_consts():
    freq = np.zeros((128, 1), np.float32)
    fr = (np.float32(ROPE_THETA) ** (-np.arange(16, dtype=np.float32) * np.float32(2.0) / np.float32(32))).astype(np.float32)
    freq[0:16, 0] = fr
    freq[16:32, 0] = fr
    rotT = np.zeros((128, 128), np.float32)
    for i in range(16):
        rotT[i + 16, i] = -1.0
        rotT[i, i + 16] = 1.0
    return freq, rotT.astype(ml_dtypes.bfloat16)


def emit_rope_tables(P, posb_dram, hc):
    P.cosF = P.sb("cosF", [128, NT], F32)
    P.sinF = P.sb("sinF", [128, NT], F32)
    P.begin_phase()
    posi = P.sb("posi", [128, NT], I32, phase=True)
    ang = P.sb("ang", [128, NT], F32, phase=True)
    tk = P.tok("rope")
    P.dma("sp", posi[:, :], posb_dram[:, :], r=(), w=(tk,))
    P.copy("dve", ang[:, :], posi[:, :], r=(tk,), w=(tk,))
    P.ts("dve", ang[:, :], ang[:, :], hc[:, 2:3], None, ALU.mult, None, r=(tk, P.tok("hc")), w=(tk,))
    ki = P.sb("rope_ki", [128, NT], I32, phase=True)
    kf = P.sb("rope_kf", [128, NT], F32, phase=True)
    msk = P.sb("rope_m", [128, NT], F32, phase=True)
    for dst, shift in ((P.sinF, 0.0), (P.cosF, 0.5 * PI)):
        P.ts("dve", dst[:, :], ang[:, :], shift, None, ALU.add, None, r=(tk,), w=(tk,))
        P.ts("dve", kf[:, :], dst[:, :], 1.0 / (2.0 * PI), None, ALU.mult, None, r=(tk,), w=(tk,))
        P.copy("dve", ki[:, :], kf[:, :], r=(tk,), w=(tk,))
        P.copy("dve", kf[:, :], ki[:, :], r=(tk,), w=(tk,))
        P.stt(dst[:, :], kf[:, :], -2.0 * PI, dst[:, :], ALU.mult, ALU.add, r=(tk,), w=(tk,))
        P.ts("dve", msk[:, :], dst[:, :], PI, None, ALU.is_gt, None, r=(tk,), w=(tk,))
        P.stt(dst[:, :], msk[:, :], -2.0 * PI, dst[:, :], ALU.mult, ALU.add, r=(tk,), w=(tk,))
        P.ts("dve", msk[:, :], dst[:, :], -PI, None, ALU.is_lt, None, r=(tk,), w=(tk,))
        P.stt(dst[:, :], msk[:, :], 2.0 * PI, dst[:, :], ALU.mult, ALU.add, r=(tk,), w=(tk,))
        P.act(dst[:, :], dst[:, :], AF.Sin, r=(tk,), w=(tk,))
    P.end_phase()


def emit_headnorm(P, bk, bkt, gain1, cs, rope, hc):
    sq, sqt = P.rot("hsq", 3, [128, TB], BF16)
    P.act(sq[:, :], bk[:, :], AF.Square, r=(bkt,), w=(sqt,))
    b2, b2t = P.bank()
    P.mm(b2[:, :], P.ones[:, :], sq[:, :], True, True, r=(sqt, P.tok("ones")), w=(b2t,))
    rs, rst = P.rot("hrstd", 2, [128, TB], F32)
    P.act(rs[:, :], b2[:, :], AF.Sqrt, r=(b2t, P.tok("consts")), w=(rst,), scale=1.0 / HD, bias=P.epsc[:, 0:1])
    P.op("dve", lambda e, o=rs[:, :]: e.reciprocal(out=o, in_=o), r=(rst,), w=(rst,))
    qn, qnt = P.rot("qn", 3, [128, TB], BF16)
    P.stt(qn[:, :], bk[:, :], gain1, rs[:, :], ALU.mult, ALU.mult, r=(bkt, rst, P.tok("hc")), w=(qnt,))
    if not rope:
        return qn, qnt
    if rope == "both":
        P.qn_last = (qn, qnt)
    b3, b3t = P.bank()
    P.mm(b3[:, :], P.rotT[:, :], qn[:, :], True, True, r=(qnt, P.tok("hc")), w=(b3t,))
    t1, t1t = P.rot("rp1", 2, [128, TB], F32)
    P.tt("dve", t1[:, :], qn[:, :], P.cosF[:, cs], ALU.mult, r=(qnt, P.tok("rope")), w=(t1t,))
    t2, t2t = P.rot("rp2", 2, [128, TB], F32)
    P.tt("dve", t2[:, :], b3[:, :], P.sinF[:, cs], ALU.mult, r=(b3t, P.tok("rope")), w=(t2t,))
    P.tt("pool", t1[:, :], t1[:, :], t2[:, :], ALU.add, r=(t1t, t2t), w=(t1t,))
    return t1, t1t


def load_headconsts(P, hc_dram, rotT_dram, ncol):
    P.hc = P.sb("hc", [128, ncol], F32)
    P.rotT = P.sb("rotT", [128, 128], BF16)
    P.dma("sp", P.hc[:, :], hc_dram[:, :], r=(), w=(P.tok("hc"),))
    P.dma("sp", P.rotT[:, :], rotT_dram[:, :], r=(), w=(P.tok("hc"),))


V0A = Vecs(["mixer_norm"])


def build_L0a():
    P = Prog()
    xT = P.din("hT", [128, NCH, NT], F32)
    vecs = P.din("vecs", [128, V0A.n()], F32)
    hc_d = P.din("hc", [128, 4], F32)
    rotT_d = P.din("rotT", [128, 128], BF16)
    posb = P.din("posb", [128, NT], I32)
    wqkv = P.din("w_qkv", [D, 3 * D], F32)
    qT_o = P.dout("qT", [NH, 128, NT], BF16)
    kT_o = P.dout("kT", [NH, 128, NT], BF16)
    v_o = P.dout("v", [NT, D], BF16)
    km_o = P.dout("kmean", [128, NH, 4], BF16)

    def body():
        common_setup(P, V0A.n())
        P.dma("sp", P.vecs[:, :], vecs[:, :], r=(), w=(P.tok("vecs"),))
        load_headconsts(P, hc_d, rotT_d, 4)
        load_h(P, xT)
        emit_rope_tables(P, posb, P.hc)
        xn = P.sb("xn", [128, NCH, NT], BF16)
        km = P.sb("km", [128, NH, 4], BF16)
        xt = lambda c, tb: P.tok(("xn", c, tb))
        emit_norm(P, P.h, V0A.ap(P, "mixer_norm"), xn, xt)
        for part in range(2):
            for j in range(8):
                wt, wtk = P.wnext(wqkv, 0, 16, part * D + j * 256, 256)
                for m in range(2):
                    hd = 2 * j + m
                    stg, stgt = P.rot("stage", 2, [128, NT], BF16, phase=False)
                    for tb in range(NTB):
                        cs = slice(tb * TB, (tb + 1) * TB)
                        bk, bkt = P.bank()
                        for kc in range(NCH):
                            P.mm(bk[:, :], wt[:, kc, m * 128:(m + 1) * 128], xn[:, kc, cs], kc == 0, kc == NCH - 1, r=(wtk, xt(kc, tb)), w=(bkt,))
                        qf, qft = emit_headnorm(P, bk, bkt, P.hc[:, part:part + 1], cs, True, P.hc)
                        P.copy("act", stg[:, cs], qf[:, :], r=(qft, stgt), w=(stgt,))
                        if part == 1:
                            kr, krt = P.rot("kred", 2, [128, 2], F32, phase=False)
                            P.op("dve", lambda e, o=kr[:, :], i=qf[:, :].rearrange("p (b t) -> p b t", b=2): e.tensor_reduce(out=o, in_=i, axis=AX.X, op=ALU.add),
                                 r=(qft,), w=(krt,))
                            P.ts("dve", km[:, hd, tb * 2:(tb + 1) * 2], kr[:, :], 1.0 / 256.0, None, ALU.mult, None, r=(krt, P.tok("km")), w=(P.tok("km"),))
                    P.dma("sp", (qT_o if part == 0 else kT_o)[hd], stg[:, :], r=(stgt,), w=(), is_out=True)
        for j in range(8):
            wt, wtk = P.wnext(wqkv, 0, 16, 2 * D + j * 256, 256)
            vs, vst = P.rot("vstage", 2, [128, 8, 256], BF16, phase=False)
            for tt_ in range(NT // 128):
                bk, bkt = P.bank()
                for kc in range(NCH):
                    P.mm(bk[:, 0:256], xn[:, kc, tt_ * 128:(tt_ + 1) * 128], wt[:, kc, :], kc == 0, kc == NCH - 1,
                         r=(wtk, xt(kc, tt_ // 4)), w=(bkt,))
                P.copy("act", vs[:, tt_, :], bk[:, 0:256], r=(bkt, vst), w=(vst,))
            P.dma("sp", v_o[:, j * 256:(j + 1) * 256].rearrange("(t p) n -> p t n", p=128), vs[:, :, :], r=(vst,), w=(), is_out=True)
        P.dma("sp", km_o[:, :, :], km[:, :, :], r=(P.tok("km"),), w=(), is_out=True)
    return run_two_pass(P, body)


def head_consts(qg, kg):
    freq, rotT = rope_consts()
    hc = np.zeros((128, 4), np.float32)
    hc[:, 0] = qg
    hc[:, 1] = kg
    hc[:, 2] = freq[:, 0]
    hc[:, 3] = -PI
    return hc, rotT


def layer0a(inp, x=None):
    x = inp["x"] if x is None else x
    nc = build_L0a()
    vecs = V0A.host({"mixer_norm": inp["mixer_norm"][0]})
    hc, rotT = head_consts(inp["moba_q_gain"][0], inp["moba_k_gain"][0])
    in_maps = []
    for c in range(NCORES):
        b = c // 4
        tk = core_tokens(c)
        in_maps.append({"hT": to_fm(x[b][tk]), "vecs": vecs, "hc": hc, "rotT": rotT,
                        "posb": np.ascontiguousarray(np.broadcast_to(inp["positions"][b][tk].astype(np.int32)[None, :], (128, NT))),
                        "w_qkv": inp["moba_w_qkv"][0]})
    res = run_prog(nc, in_maps)
    bf = ml_dtypes.bfloat16
    q = np.zeros((B, S, NH, HD), bf)
    k = np.zeros((B, S, NH, HD), bf)
    v = np.zeros((B, S, D), bf)
    km = np.zeros((B, S // 256, NH, HD), bf)
    for c in range(NCORES):
        b = c // 4
        tk = core_tokens(c)
        q[b, tk] = res[c]["qT"].transpose(2, 0, 1)
        k[b, tk] = res[c]["kT"].transpose(2, 0, 1)
        v[b, tk] = res[c]["v"]
        kmc = res[c]["kmean"]
        for s_, ch in enumerate(core_chunks(c)):
            km[b, 2 * ch:2 * ch + 2] = kmc[:, :, 2 * s_:2 * s_ + 2].transpose(2, 1, 0)
    return {"q": q, "k": k, "v": v, "kmean": km}


def slot_lists(c):
    j = c % 4
    return [[None] * (3 - j) + list(range(0, j + 1)), [None] * j + list(range(0, 8 - j))]


def attn_consts():
    tri = np.zeros((128, 4, TB), np.float32)
    kk = np.arange(128)[:, None]
    qq = np.arange(TB)[None, :]
    for jj in range(4):
        tri[:, jj, :] = np.where(128 * jj + kk <= qq, 0.0, NEG)
    boh = np.zeros((16, 16, 128), np.float32)
    for b_ in range(16):
        boh[b_, b_, :] = 1.0
    ident = np.eye(128, dtype=np.float32)
    bf = ml_dtypes.bfloat16
    return tri.astype(bf), boh.astype(bf), ident.astype(bf), ident


def moba_masks(c):
    m1 = np.full((128, 2, 2, 16), NEG, np.float32)
    notown = np.ones((128, 2, 2, 16), np.float32)
    for s_, lst in enumerate(slot_lists(c)):
        nblk = 2 * len(lst)
        for half in range(2):
            own = nblk - 2 + half
            for blk in range(nblk):
                valid = lst[blk // 2] is not None
                m1[:, s_, half, blk] = 0.0 if (valid and blk < own) else NEG
            notown[:, s_, half, own] = 0.0
    return m1, notown


def emit_moba_attn(P, qT_d, kT_d, v_d, kmT_d, m1_d, notown_d, tri_d, boh_d, identb_d, identf_d, w_o):
    P.begin_phase()
    P.nrot = 6
    OT = P.sb("OT", [128, NCH, NT], BF16, phase=True)
    kmT = P.sb("kmT", [128, NH, 24], BF16, phase=True)
    m1 = P.sb("m1", [128, 2, 2, 16], F32, phase=True)
    notown = P.sb("notown", [128, 2, 2, 16], F32, phase=True)
    tri = P.sb("tri", [128, 4, TB], BF16, phase=True)
    boh = P.sb("boh", [16, 16, 128], BF16, phase=True)
    identb = P.sb("identb", [128, 128], BF16, phase=True)
    identf = P.sb("identf", [128, 128], F32, phase=True)
    ac = P.tok("ac")
    for dst, src, nd in ((kmT, kmT_d, 3), (m1, m1_d, 4), (notown, notown_d, 4), (tri, tri_d, 3), (boh, boh_d, 3), (identb, identb_d, 2), (identf, identf_d, 2)):
        P.dma("sp", dst[(slice(None),) * nd], src, r=(), w=(ac,))
    scale = HD ** -0.5
    units = [(h, s_) for h in range(NH) for s_ in range(2)]
    hbuf = {}
    ust = {}

    def geom(s_):
        nblk = 8 if s_ == 0 else 16
        return nblk, (0 if s_ == 0 else 8), 2 * nblk, (0 if s_ == 0 else 16)

    def pre0(u):
        h, s_ = units[u]
        if s_ == 0:
            kT, kTt = P.rot("kT", 2, [128, 48 * 128], BF16)
            vv, vvt = P.rot("vv", 2, [128, 48, 128], BF16)
            qh, qht = P.rot("qh", 2, [128, NT], BF16)
            P.dma("sp", qh[:, :], qT_d[h], r=(), w=(qht,))
            P.dma("sp", kT[:, :], kT_d[h], r=(), w=(kTt,))
            P.dma("sp", vv[:, :, :], v_d[h], r=(), w=(vvt,))
            hbuf[h] = (kT, kTt, vv, vvt, qh, qht)
        kT, kTt, vv, vvt, qh, qht = hbuf[h]
        nblk, boff, nkt, ktoff = geom(s_)
        gb, gbt = P.bank()
        for qt in range(4):
            P.mm(gb[:, qt * 16:qt * 16 + nblk], qh[:, s_ * TB + qt * 128:s_ * TB + (qt + 1) * 128], kmT[:, h, boff:boff + nblk],
                 True, True, r=(qht, ac), w=(gbt,))
        sbs = []
        for qt in range(4):
            gm, gmt = P.rot("gm", 4, [128, 16], F32)
            P.tt("dve", gm[:, 0:nblk], gb[:, qt * 16:qt * 16 + nblk], m1[:, s_, qt // 2, 0:nblk], ALU.add, r=(gbt, ac), w=(gmt,))
            mx, mxt = P.rot("mx", 4, [128, 8], F32)
            P.op("dve", lambda e, o=mx[:, :], i=gm[:, 0:nblk]: e.max(out=o, in_=i), r=(gmt,), w=(mxt,))
            sb1, sb1t = P.rot("sb1", 8, [128, 16], F32)
            P.ts("dve", sb1[:, 0:nblk], gm[:, 0:nblk], mx[:, 2:3], -NEG, ALU.is_ge, ALU.mult, r=(gmt, mxt), w=(sb1t,))
            P.stt(sb1[:, 0:nblk], sb1[:, 0:nblk], NEG, m1[:, s_, qt // 2, 0:nblk], ALU.add, ALU.add, r=(sb1t, ac), w=(sb1t,))
            P.tt("dve", sb1[:, 0:nblk], sb1[:, 0:nblk], notown[:, s_, qt // 2, 0:nblk], ALU.mult, r=(sb1t, ac), w=(sb1t,))
            sbs.append((sb1, sb1t))
        ust[u] = {"sbs": sbs}

    def pre1(u):
        h, s_ = units[u]
        nblk, boff, nkt, ktoff = geom(s_)
        tbk, tbkt = P.bank()
        for qt in range(4):
            sb1, sb1t = ust[u]["sbs"][qt]
            P.op("pe", lambda e, o=tbk[0:nblk, qt * 128:(qt + 1) * 128], i=sb1[:, 0:nblk], idn=identf[:, :]: e.transpose(out=o, in_=i, identity=idn),
                 r=(sb1t, ac), w=(tbkt,))
        selT, selTt = P.rot("selT", 2, [16, TB], BF16)
        P.copy("act", selT[0:nblk, :], tbk[0:nblk, :], r=(tbkt,), w=(selTt,))
        ust[u]["selT"] = (selT, selTt)

    def main(u):
        h, s_ = units[u]
        kT, kTt, vv, vvt, qh, qht = hbuf[h]
        nblk, boff, nkt, ktoff = geom(s_)
        selT, selTt = ust[u]["selT"]
        qs = slice(s_ * TB, (s_ + 1) * TB)
        ob, obt = P.accbank(0)
        db, dbt = P.accbank(1)
        items = []
        for kt in range(nkt):
            st = {}

            def sA(kt=kt, st=st):
                g = ktoff + kt
                if u + 1 < len(units):
                    if kt == 2:
                        pre0(u + 1)
                    if kt == nkt // 2 + 2:
                        pre1(u + 1)
                sbk, sbkt = P.bank()
                diag = kt >= nkt - 4
                P.mm(sbk[:, :], kT[:, g * 128:(g + 1) * 128], qh[:, qs], True, False, r=(kTt, qht), w=(sbkt,))
                P.mm(sbk[:, :], boh[0:nblk, kt // 2, :], selT[0:nblk, :], False, not diag, r=(ac, selTt), w=(sbkt,))
                if diag:
                    P.mm(sbk[:, :], identb[:, :], tri[:, kt - (nkt - 4), :], False, True, r=(ac,), w=(sbkt,))
                st["s"] = (sbk, sbkt)

            def sB(kt=kt, st=st):
                g = ktoff + kt
                sbk, sbkt = st["s"]
                E, Et = P.rot("E", 4, [128, TB], BF16)
                P.act(E[:, :], sbk[:, :], AF.Exp, r=(sbkt,), w=(Et,), scale=scale)
                P.mm(ob[:, :], vv[:, g, :], E[:, :], kt == 0, kt == nkt - 1, r=(vvt, Et), w=(obt,))
                P.mm(db[:, :], P.ones[:, :], E[:, :], kt == 0, kt == nkt - 1, r=(P.tok("ones"), Et), w=(dbt,))
            items.append((sA, sB))
        pipeline(items, (0, 2))
        rden, rdent = P.rot("rden", 2, [128, TB], F32)
        P.op("dve", lambda e, o=rden[:, :], i=db[:, :]: e.reciprocal(out=o, in_=i), r=(dbt,), w=(rdent,))
        P.tt("dve", OT[:, h, qs], ob[:, :], rden[:, :], ALU.mult, r=(obt, rdent), w=(P.tok(("OT", h, s_)),))
        del ust[u]

    pre0(0)
    pre1(0)
    for u in range(len(units)):
        main(u)
    emit_outproj(P, w_o, OT, lambda kc, tb: P.tok(("OT", kc, tb)))
    P.end_phase()


V0B = Vecs(["mlp_norm", "ple_norm"])


def build_L0b():
    P = Prog()
    xT = P.din("hT", [128, NCH, NT], F32)
    pT = P.din("pT", [128, 2, NT], F32)
    vecs = P.din("vecs", [128, V0B.n()], F32)
    qT_d = P.din("qT", [NH, 128, NT], BF16)
    kT_d = P.din("kT", [NH, 128, 48 * 128], BF16)
    v_d = P.din("v", [NH, 128, 48, 128], BF16)
    kmT_d = P.din("kmT", [128, NH, 24], BF16)
    m1_d = P.din("m1", [128, 2, 2, 16], F32)
    notown_d = P.din("notown", [128, 2, 2, 16], F32)
    tri_d = P.din("tri", [128, 4, TB], BF16)
    boh_d = P.din("boh", [16, 16, 128], BF16)
    identb_d = P.din("identb", [128, 128], BF16)
    identf_d = P.din("identf", [128, 128], F32)
    w_o = P.din("w_o", [D, D], F32)
    w1 = P.din("mlp_w1", [D, DFF], F32)
    w2 = P.din("mlp_w2", [DFF, D], F32)
    wg = P.din("ple_gate", [D, D], F32)
    wp = P.din("ple_proj", [256, D], F32)
    out = P.dout("hT_out", [128, NCH, NT], F32)

    def body():
        common_setup(P, V0B.n())
        P.dma("sp", P.vecs[:, :], vecs[:, :], r=(), w=(P.tok("vecs"),))
        load_h(P, xT)
        vec = lambda n: V0B.ap(P, n)
        if DBG["mix"]:
            emit_moba_attn(P, qT_d, kT_d, v_d, kmT_d, m1_d, notown_d, tri_d, boh_d, identb_d, identf_d, w_o)
        emit_mlp_ple(P, 0, w1, w2, wg, wp, pT, vec)
        store_h(P, out)
    return run_two_pass(P, body)


def list_tokens(lst):
    return np.concatenate([(np.arange(k * TB, (k + 1) * TB) if k is not None else np.full(TB, -1)) for k in lst])


def gather_rows(a, idx):
    out = a[np.maximum(idx, 0)]
    out[idx < 0] = 0
    return out


def layer0b(inp, x, qkv):
    nc = build_L0b()
    vecs = V0B.host({"mlp_norm": inp["mlp_norm"][0], "ple_norm": inp["ple_norm"][0]})
    tri, boh, identb, identf = attn_consts()
    in_maps = []
    for c in range(NCORES):
        b = c // 4
        tk = core_tokens(c)
        lists = slot_lists(c)
        idx = np.concatenate([list_tokens(l) for l in lists])
        kl = gather_rows(qkv["k"][b], idx)
        vl = gather_rows(qkv["v"][b], idx).reshape(48, 128, NH, HD)
        bidx = np.concatenate([np.repeat(np.array([(-1 if k is None else k) for k in l]), 2) * 2 + np.tile([0, 1], len(l)) for l in lists])
        bidx = np.where(bidx < 0, -1, bidx)
        kml = gather_rows(qkv["kmean"][b], bidx)
        m1, notown = moba_masks(c)
        in_maps.append({
            "hT": to_fm(x[b][tk]), "pT": to_fm(inp["p"][0, b][tk]), "vecs": vecs,
            "qT": np.ascontiguousarray(qkv["q"][b][tk].transpose(1, 2, 0)),
            "kT": np.ascontiguousarray(kl.transpose(1, 2, 0)),
            "v": np.ascontiguousarray(vl.transpose(2, 1, 0, 3)),
            "kmT": np.ascontiguousarray(kml.transpose(2, 1, 0)),
            "m1": m1, "notown": notown, "tri": tri, "boh": boh, "identb": identb, "identf": identf,
            "w_o": inp["moba_w_o"][0],
            "mlp_w1": inp["mlp_w1"][0], "mlp_w2": inp["mlp_w2"][0], "ple_gate": inp["ple_gate"][0], "ple_proj": inp["ple_proj"][0],
        })
    res = run_prog(nc, in_maps)
    h0 = np.zeros_like(x)
    for c in range(NCORES):
        h0[c // 4][core_tokens(c)] = from_fm(res[c]["hT_out"])
    return h0


V2A = Vecs(["mixer_norm"])
GELU_C = float(2.0 * np.sqrt(2.0 / np.pi))


def build_L2a():
    P = Prog()
    P.nwslot = 5
    P.wlive = 4
    hT = P.din("hT", [128, NCH, NT], F32)
    hhalo = P.din("hhalo", [128, NCH, 32], F32)
    vecs = P.din("vecs", [128, V2A.n()], F32)
    hc_d = P.din("hc", [128, 8], F32)
    rotT_d = P.din("rotT", [128, 128], BF16)
    identf_d = P.din("identf", [128, 128], F32)
    posb = P.din("posb", [128, NT], I32)
    posT_d = P.din("cmp_posT", [128, 2, 32], F32)
    wq = P.din("w_q", [D, D], F32)
    wkv = P.din("w_kv", [D, 3072], F32)
    wgate = P.din("w_gate", [D, 48], F32)
    cw1 = P.din("cmp_w1", [2 * 4096, 128], F32)
    cw2 = P.din("cmp_w2", [2 * 128, 128], F32)
    qc_o = P.dout("qcT", [NH, 128, NT], BF16)
    qr_o = P.dout("qrT", [NH, 128, NT], BF16)
    ks_o = P.dout("kslcT", [4, 128, NT], BF16)
    kw_o = P.dout("kwinT", [4, 128, NT], BF16)
    vs_o = P.dout("vslc", [NT, 512], BF16)
    vw_o = P.dout("vwin", [NT, 512], BF16)
    kc_o = P.dout("kcmpT", [4, 128, 64], BF16)
    vc_o = P.dout("vcmpT", [4, 128, 64], BF16)
    g_o = P.dout("gT", [48, NT], BF16)

    def body():
        common_setup(P, V2A.n())
        P.dma("sp", P.vecs[:, :], vecs[:, :], r=(), w=(P.tok("vecs"),))
        load_headconsts(P, hc_d, rotT_d, 8)
        identf = P.sb("identf", [128, 128], F32)
        posT = P.sb("posT", [128, 2, 32], BF16)
        P.dma("sp", identf[:, :], identf_d[:, :], r=(), w=(P.tok("hc"),))
        P.dma("pool", posT[:, :, :], posT_d[:, :, :], r=(), w=(P.tok("hc"),))
        load_h(P, hT)
        hx = P.sb("hx", [128, NCH, 32], F32)
        xh = P.sb("xh", [128, NCH, 32], BF16)
        P.dma("sp", hx[:, :, :], hhalo[:, :, :], r=(), w=(P.tok("hx"),))
        emit_rope_tables(P, posb, P.hc)
        xn = P.sb("xn", [128, NCH, NT], BF16)
        xt = lambda c, tb: P.tok(("xn", c, tb))
        xht = lambda c, tb: P.tok(("xh", c))
        gain = V2A.ap(P, "mixer_norm")
        emit_norm(P, hx, gain, xh, xht, ntb=1, src_tokf=lambda c, tb: P.tok("hx"), tbw=32)
        emit_norm(P, P.h, gain, xn, xt)
        hct = P.tok("hc")

        def proj(wt, wtk, m, tb):
            bk, bkt = P.bank()
            cs = slice(tb * TB, (tb + 1) * TB)
            for kc in range(NCH):
                P.mm(bk[:, :], wt[:, kc, m * 128:(m + 1) * 128], xn[:, kc, cs], kc == 0, kc == NCH - 1, r=(wtk, xt(kc, tb)), w=(bkt,))
            return bk, bkt

        for j in range(8):
            wt, wtk = P.wnext(wq, 0, 16, j * 256, 256)
            for m in range(2):
                hd = 2 * j + m
                sc, sct = P.rot("stage", 2, [128, NT], BF16, phase=False)
                sr, srt = P.rot("stage2", 2, [128, NT], BF16, phase=False)
                for tb in range(NTB):
                    cs = slice(tb * TB, (tb + 1) * TB)
                    bk, bkt = proj(wt, wtk, m, tb)
                    qf, qft = emit_headnorm(P, bk, bkt, P.hc[:, 0:1], cs, "both", P.hc)
                    qn, qnt = P.qn_last
                    P.copy("pool", sc[:, cs], qn[:, :], r=(qnt, sct), w=(sct,))
                    P.copy("act", sr[:, cs], qf[:, :], r=(qft, srt), w=(srt,))
                P.dma("sp", qc_o[hd], sc[:, :], r=(sct,), w=(), is_out=True)
                P.dma("sp", qr_o[hd], sr[:, :], r=(srt,), w=(), is_out=True)
        for idx, gcol, outd in ((2, 1, ks_o), (4, 5, kw_o)):
            for j in range(2):
                wt, wtk = P.wnext(wkv, 0, 16, idx * 512 + j * 256, 256)
                for m in range(2):
                    g = 2 * j + m
                    sr, srt = P.rot("stage2", 2, [128, NT], BF16, phase=False)
                    for tb in range(NTB):
                        cs = slice(tb * TB, (tb + 1) * TB)
                        bk, bkt = proj(wt, wtk, m, tb)
                        kf, kft = emit_headnorm(P, bk, bkt, P.hc[:, gcol:gcol + 1], cs, True, P.hc)
                        P.copy("act", sr[:, cs], kf[:, :], r=(kft, srt), w=(srt,))
                    P.dma("sp", outd[g], sr[:, :], r=(srt,), w=(), is_out=True)
        for idx, outd in ((3, vs_o), (5, vw_o)):
            for j in range(2):
                wt, wtk = P.wnext(wkv, 0, 16, idx * 512 + j * 256, 256)
                vs, vst = P.rot("vstage", 2, [128, 8, 256], BF16, phase=False)
                for tt_ in range(NT // 128):
                    bk, bkt = P.bank()
                    for kc in range(NCH):
                        P.mm(bk[:, 0:256], xn[:, kc, tt_ * 128:(tt_ + 1) * 128], wt[:, kc, :], kc == 0, kc == NCH - 1,
                             r=(wtk, xt(kc, tt_ // 4)), w=(bkt,))
                    P.copy("act", vs[:, tt_, :], bk[:, 0:256], r=(bkt, vst), w=(vst,))
                P.dma("sp", outd[:, j * 256:(j + 1) * 256].rearrange("(t p) n -> p t n", p=128), vs[:, :, :], r=(vst,), w=(), is_out=True)
        wt, wtk = P.wnext(wgate, 0, 16, 0, 48)
        gts = P.sb("gts", [48, NT], BF16)
        for tt_ in range(NT // 128):
            bk, bkt = P.bank()
            for kc in range(NCH):
                P.mm(bk[:, 0:48], xn[:, kc, tt_ * 128:(tt_ + 1) * 128], wt[:, kc, :], kc == 0, kc == NCH - 1, r=(wtk, xt(kc, tt_ // 4)), w=(bkt,))
            gs, gst = P.rot("gsig", 2, [128, 48], F32, phase=False)
            P.act(gs[:, :], bk[:, 0:48], AF.Sigmoid, r=(bkt,), w=(gst,))
            b2, b2t = P.bank()
            P.op("pe", lambda e, o=b2[0:48, 0:128], i=gs[:, :], idn=identf[:, :]: e.transpose(out=o, in_=i, identity=idn), r=(gst, hct), w=(b2t,))
            P.copy("act", gts[:, tt_ * 128:(tt_ + 1) * 128], b2[0:48, 0:128], r=(b2t, P.tok("gts")), w=(P.tok("gts"),))
        P.dma("sp", g_o[:, :], gts[:, :], r=(P.tok("gts"),), w=(), is_out=True)
        kcs = P.sb("kcs", [128, 4, 64], BF16)
        vcs = P.sb("vcs", [128, 4, 64], BF16)
        for idx in range(2):
            w1t, w1k = P.wnext(cw1, idx * 4096, 32, 0, 128)
            w2t, w2k = P.wnext(cw2, idx * 128, 1, 0, 128)
            pbk, pbt = P.bank()
            for l in range(32):
                P.mm(pbk[:, 0:1], w1t[:, l, :], posT[:, idx, l:l + 1], l == 0, l == 31, r=(w1k, hct), w=(pbt,))
            pb, pbst = P.rot("posb", 2, [128, 1], F32, phase=False)
            P.copy("act", pb[:, :], pbk[:, 0:1], r=(pbt,), w=(pbst,))
            for j in range(2):
                wt, wtk = P.wnext(wkv, 0, 16, idx * 512 + j * 256, 256)
                for m in range(2):
                    g = 2 * j + m
                    for s_ in range(NTB):
                        raw, rawt = P.rot("craw", 2, [128, 16 + TB], BF16, phase=False)
                        bk, bkt = proj(wt, wtk, m, s_)
                        P.copy("act", raw[:, 16:16 + TB], bk[:, :], r=(bkt, rawt), w=(rawt,))
                        bh, bht = P.bank()
                        for kc in range(NCH):
                            P.mm(bh[:, 0:16], wt[:, kc, m * 128:(m + 1) * 128], xh[:, kc, s_ * 16:(s_ + 1) * 16], kc == 0, kc == NCH - 1,
                                 r=(wtk, xht(kc, 0)), w=(bht,))
                        P.copy("act", raw[:, 0:16], bh[:, 0:16], r=(bht, rawt), w=(rawt,))
                        cb, cbt = P.bank()
                        for l in range(32):
                            P.mm(cb[:, 0:32], w1t[:, l, :], raw[:, l:l + 16 * 31 + 1:16], l == 0, l == 31, r=(w1k, rawt), w=(cbt,))
                        x, xt_ = P.rot("cx", 2, [128, 32], F32, phase=False)
                        P.ts("dve", x[:, :], cb[:, 0:32], pb[:, 0:1], None, ALU.add, None, r=(cbt, pbst), w=(xt_,))
                        u, ut = P.rot("cu", 2, [128, 32], F32, phase=False)
                        P.tt("dve", u[:, :], x[:, :], x[:, :], ALU.mult, r=(xt_,), w=(ut,))
                        P.ts("dve", u[:, :], u[:, :], 0.044715, 1.0, ALU.mult, ALU.add, r=(ut,), w=(ut,))
                        P.tt("dve", u[:, :], u[:, :], x[:, :], ALU.mult, r=(ut, xt_), w=(ut,))
                        P.act(u[:, :], u[:, :], AF.Sigmoid, r=(ut,), w=(ut,), scale=GELU_C)
                        ge, get = P.rot("cge", 2, [128, 32], BF16, phase=False)
                        P.tt("dve", ge[:, :], u[:, :], x[:, :], ALU.mult, r=(ut, xt_), w=(get,))
                        ob, obt = P.bank()
                        P.mm(ob[:, 0:32], w2t[:, 0, :], ge[:, :], True, True, r=(w2k, get), w=(obt,))
                        if idx == 1:
                            P.copy("act", vcs[:, g, s_ * 32:(s_ + 1) * 32], ob[:, 0:32], r=(obt, P.tok("vcs")), w=(P.tok("vcs"),))
                        else:
                            sq, sqt = P.rot("csq", 2, [128, 32], BF16, phase=False)
                            P.act(sq[:, :], ob[:, 0:32], AF.Square, r=(obt,), w=(sqt,))
                            b2, b2t = P.bank()
                            P.mm(b2[:, 0:32], P.ones[:, :], sq[:, :], True, True, r=(sqt, P.tok("ones")), w=(b2t,))
                            rs, rst = P.rot("crs", 2, [128, 32], F32, phase=False)
                            P.act(rs[:, :], b2[:, 0:32], AF.Sqrt, r=(b2t, P.tok("consts")), w=(rst,), scale=1.0 / HD, bias=P.epsc[:, 0:1])
                            P.op("dve", lambda e, o=rs[:, :]: e.reciprocal(out=o, in_=o), r=(rst,), w=(rst,))
                            P.stt(kcs[:, g, s_ * 32:(s_ + 1) * 32], ob[:, 0:32], P.hc[:, 4:5], rs[:, :], ALU.mult, ALU.mult,
                                  r=(obt, rst, hct, P.tok("kcs")), w=(P.tok("kcs"),))
        for g in range(4):
            P.dma("sp", kc_o[g], kcs[:, g, :], r=(P.tok("kcs"),), w=(), is_out=True)
            P.dma("sp", vc_o[g], vcs[:, g, :], r=(P.tok("vcs"),), w=(), is_out=True)
    return run_two_pass(P, body)


def layer2a(inp, h1):
    nc = build_L2a()
    vecs = V2A.host({"mixer_norm": inp["mixer_norm"][2]})
    freq, rotT = rope_consts()
    hc = np.zeros((128, 8), np.float32)
    hc[:, 0] = inp["nsa_q_gain"][0]
    hc[:, 1] = inp["nsa_k_gain"][0, 1]
    hc[:, 2] = freq[:, 0]
    hc[:, 3] = -PI
    hc[:, 4] = inp["nsa_k_gain"][0, 0]
    hc[:, 5] = inp["nsa_k_gain"][0, 2]
    identf = np.eye(128, dtype=np.float32)
    posT = np.ascontiguousarray(inp["nsa_cmp_pos"][0].transpose(2, 0, 1))
    in_maps = []
    for c in range(NCORES):
        b = c // 4
        tk = core_tokens(c)
        in_maps.append({"hT": to_fm(h1[b][tk]), "hhalo": to_fm(halo_rows(h1[b], c, 16)), "vecs": vecs, "hc": hc, "rotT": rotT, "identf": identf,
                        "posb": np.ascontiguousarray(np.broadcast_to(inp["positions"][b][tk].astype(np.int32)[None, :], (128, NT))),
                        "cmp_posT": posT, "w_q": inp["nsa_w_q"][0], "w_kv": inp["nsa_w_kv"][0], "w_gate": inp["nsa_w_gate"][0],
                        "cmp_w1": np.ascontiguousarray(inp["nsa_cmp_w1"][0].reshape(2 * 4096, 128)),
                        "cmp_w2": np.ascontiguousarray(inp["nsa_cmp_w2"][0].reshape(2 * 128, 128))})
    res = run_prog(nc, in_maps)
    bf = ml_dtypes.bfloat16
    o = {"qc": np.zeros((B, S, NH, HD), bf), "qr": np.zeros((B, S, NH, HD), bf),
         "kslc": np.zeros((B, S, 4, HD), bf), "kwin": np.zeros((B, S, 4, HD), bf),
         "vslc": np.zeros((B, S, 512), bf), "vwin": np.zeros((B, S, 512), bf),
         "kcmp": np.zeros((B, 8, 32, 4, HD), bf), "vcmp": np.zeros((B, 8, 32, 4, HD), bf),
         "gT": np.zeros((B, S, 48), bf)}
    for c in range(NCORES):
        b = c // 4
        tk = core_tokens(c)
        r = res[c]
        o["qc"][b, tk] = r["qcT"].transpose(2, 0, 1)
        o["qr"][b, tk] = r["qrT"].transpose(2, 0, 1)
        o["kslc"][b, tk] = r["kslcT"].transpose(2, 0, 1)
        o["kwin"][b, tk] = r["kwinT"].transpose(2, 0, 1)
        o["vslc"][b, tk] = r["vslc"]
        o["vwin"][b, tk] = r["vwin"]
        o["gT"][b, tk] = r["gT"].T
        for s_, ch in enumerate(core_chunks(c)):
            o["kcmp"][b, ch] = r["kcmpT"][:, :, s_ * 32:(s_ + 1) * 32].transpose(2, 0, 1)
            o["vcmp"][b, ch] = r["vcmpT"][:, :, s_ * 32:(s_ + 1) * 32].transpose(2, 0, 1)
    return o


BIG = 1.0e30


def nsa_consts():
    bf = ml_dtypes.bfloat16
    kk = np.arange(128)[:, None]
    qq = np.arange(TB)[None, :]
    band = np.zeros((128, 4, TB), np.float32)
    for jj in range(4):
        band[:, jj, :] = np.where(128 * jj + kk > qq, 0.0, NEG)
    cmpdiag = np.zeros((128, TB), np.float32)
    for i in range(32):
        cmpdiag[96 + i, :] = np.where(16 * i + 15 <= np.arange(TB), 0.0, NEG)
    boh2 = np.zeros((64, 32, 128), np.float32)
    for kt in range(32):
        boh2[2 * kt, kt, 0:64] = 1.0
        boh2[2 * kt + 1, kt, 64:128] = 1.0
    sel48 = np.zeros((48, 48, 128), np.float32)
    for i in range(48):
        sel48[i, i, :] = 1.0
    ovl = np.zeros((128, 3, 64), np.float32)
    for s_, nch in enumerate((4, 8)):
        for e in range(32 * nch):
            ci, i = divmod(e, 32)
            n0, n1 = 512 * ci - 16 + 16 * i, 512 * ci + 16 + 16 * i
            for jl in range(8 * nch):
                j0, j1 = 64 * jl, 64 * jl + 64
                if n0 < j1 and j0 < n1:
                    tile = 0 if s_ == 0 else 1 + e // 128
                    ovl[e % 128, tile, jl] = 1.0
    return band.astype(bf), cmpdiag.astype(bf), boh2.astype(bf), sel48.astype(bf), ovl.astype(bf)


def nsa_core_consts(c):
    lists = slot_lists(c)
    cmppad = np.zeros((128, 3), np.float32)
    slcm = np.zeros((128, 2, 4, 3, 64), np.float32)
    winpad = np.zeros((128, 2), np.float32)
    for s_, lst in enumerate(lists):
        for e in range(32 * len(lst)):
            ci, i = divmod(e, 32)
            pad = lst[ci] is None or (lst[ci] == 0 and i == 0)
            tile = 0 if s_ == 0 else 1 + e // 128
            cmppad[e % 128, tile] = NEG if pad else 0.0
        nsel = 8 * len(lst)
        npad = sum(1 for k in lst if k is None)
        first = 8 * npad
        for qt in range(4):
            for p in range(128):
                cur = nsel - 8 + (qt * 128 + p) // 64
                for jl in range(64):
                    if jl >= nsel or lst[jl // 8] is None or jl > cur:
                        slcm[p, s_, qt, 0, jl], slcm[p, s_, qt, 1, jl], slcm[p, s_, qt, 2, jl] = 0.0, -BIG, NEG
                    elif jl == cur or jl == first:
                        slcm[p, s_, qt, 0, jl], slcm[p, s_, qt, 1, jl] = 0.0, BIG
                    else:
                        slcm[p, s_, qt, 0, jl] = 1.0
        winpad[:, s_] = NEG if core_chunks(c)[s_] == 0 else 0.0
    return cmppad, slcm, winpad


def emit_nsa_attn(P, dd, OT):
    P.begin_phase()
    P.nrot = 6
    ph = dict(phase=True)
    tri = P.sb("tri", [128, 4, TB], BF16, **ph)
    band = P.sb("band", [128, 4, TB], BF16, **ph)
    cmpdiag = P.sb("cmpdiag", [128, TB], BF16, **ph)
    boh2 = P.sb("boh2", [64, 32, 128], BF16, **ph)
    sel48 = P.sb("sel48", [48, 48, 128], BF16, **ph)
    ovl = P.sb("ovl", [128, 3, 64], BF16, **ph)
    identb = P.sb("identb", [128, 128], BF16, **ph)
    identf = P.sb("identf", [128, 128], F32, **ph)
    cmppad = P.sb("cmppad", [128, 3], F32, **ph)
    slcm = P.sb("slcm", [128, 2, 4, 3, 64], F32, **ph)
    winpad = P.sb("winpad", [128, 2], F32, **ph)
    gT = P.sb("gTs", [48, NT], BF16, **ph)
    ac = P.tok("ac")
    for dst, name, nd in ((tri, "tri", 3), (band, "band", 3), (cmpdiag, "cmpdiag", 2), (boh2, "boh2", 3), (sel48, "sel48", 3), (ovl, "ovl", 3),
                          (identb, "identb", 2), (identf, "identf", 2), (cmppad, "cmppad", 2), (slcm, "slcm", 5), (winpad, "winpad", 2), (gT, "gT", 2)):
        P.dma("sp", dst[(slice(None),) * nd], dd[name], r=(), w=(ac,))
    scale = HD ** -0.5
    onest = P.tok("ones")
    oacc = [P.sb("oacc%d" % r, [128, TB], F32, **ph) for r in range(4)]
    psum_ = [P.sb("psumT%d" % t, [128, TB], F32, **ph) for t in range(2)]
    pb = [P.sb("pb%d" % t, [128, TB], BF16, **ph) for t in range(2)]

    def finish_branch(h, br, r, qs, ob, obt, db, dbt, first, guard):
        rden, rdent = P.rot("rden", 2, [128, TB], F32)
        if guard:
            P.ts("dve", rden[:, :], db[:, :], 1e-30, None, ALU.max, None, r=(dbt,), w=(rdent,))
            P.op("dve", lambda e, o=rden[:, :]: e.reciprocal(out=o, in_=o), r=(rdent,), w=(rdent,))
        else:
            P.op("dve", lambda e, o=rden[:, :], i=db[:, :]: e.reciprocal(out=o, in_=i), r=(dbt,), w=(rdent,))
        gb, gbt = P.bank()
        P.mm(gb[:, :], sel48[0:48, h * 3 + br, :], gT[0:48, qs], True, True, r=(ac,), w=(gbt,))
        cf, cft = P.rot("coef", 2, [128, TB], F32)
        P.tt("dve", cf[:, :], gb[:, :], rden[:, :], ALU.mult, r=(gbt, rdent), w=(cft,))
        oat = P.tok(("oacc", r))
        if first:
            P.tt("dve", oacc[r][:, :], ob[:, :], cf[:, :], ALU.mult, r=(obt, cft), w=(oat,))
        else:
            tm, tmt = P.rot("otmp", 2, [128, TB], F32)
            P.tt("dve", tm[:, :], ob[:, :], cf[:, :], ALU.mult, r=(obt, cft), w=(tmt,))
            P.tt("pool", oacc[r][:, :], oacc[r][:, :], tm[:, :], ALU.add, r=(tmt, oat), w=(oat,))
        return rden, rdent

    for g in range(4):
        ks, kst = P.rot("ks", 1, [128, 48 * 128], BF16)
        vs, vst = P.rot("vs", 1, [128, 48, 128], BF16)
        kw, kwt = P.rot("kw", 1, [128, 16 * 128], BF16)
        vw, vwt = P.rot("vw", 1, [128, 16, 128], BF16)
        kc, kct = P.rot("kc", 1, [128, 384], BF16)
        vc, vct = P.rot("vc", 1, [128, 3, 128], BF16)
        P.dma("sp", kc[:, :], dd["kcmpT"][g], r=(), w=(kct,))
        P.dma("sp", vc[:, :, :], dd["vcmp"][g], r=(), w=(vct,))
        P.dma("sp", ks[:, :], dd["kslcT"][g], r=(), w=(kst,))
        P.dma("sp", vs[:, :, :], dd["vslc"][g], r=(), w=(vst,))
        P.dma("sp", kw[:, :], dd["kwinT"][g], r=(), w=(kwt,))
        P.dma("sp", vw[:, :, :], dd["vwin"][g], r=(), w=(vwt,))
        qcs, qrs = [], []
        for r in range(4):
            h = 4 * g + r
            qc, qct = P.rot("qc", 4, [128, NT], BF16)
            qr, qrt = P.rot("qr", 4, [128, NT], BF16)
            P.dma("sp", qc[:, :], dd["qcT"][h], r=(), w=(qct,))
            P.dma("sp", qr[:, :], dd["qrT"][h], r=(), w=(qrt,))
            qcs.append((qc, qct))
            qrs.append((qr, qrt))
        for s_ in range(2):
            qs = slice(s_ * TB, (s_ + 1) * TB)
            nch = 4 if s_ == 0 else 8
            nsel = 8 * nch
            ctiles = [0] if s_ == 0 else [1, 2]
            for r in range(4):
                h = 4 * g + r
                qc, qct = qcs[r]
                ob, obt = P.accbank(0)
                db, dbt = P.accbank(1)
                Es = []
                for ti, ct in enumerate(ctiles):
                    sbk, sbkt = P.bank()
                    last = ti == len(ctiles) - 1
                    P.mm(sbk[:, :], kc[:, ct * 128:(ct + 1) * 128], qc[:, qs], True, not last, r=(kct, qct), w=(sbkt,))
                    if last:
                        P.mm(sbk[:, :], identb[:, :], cmpdiag[:, :], False, True, r=(ac,), w=(sbkt,))
                    E, Et = P.rot("Ec", 2, [128, TB], BF16)
                    P.act(E[:, :], sbk[:, :], AF.Exp, r=(sbkt, ac), w=(Et,), scale=scale, bias=cmppad[:, ct:ct + 1])
                    P.mm(ob[:, :], vc[:, ct, :], E[:, :], ti == 0, last, r=(vct, Et), w=(obt,))
                    P.mm(db[:, :], P.ones[:, :], E[:, :], ti == 0, last, r=(onest, Et), w=(dbt,))
                    Es.append((E, Et))
                rden, rdent = finish_branch(h, 0, r, qs, ob, obt, db, dbt, True, True)
                for ti, (E, Et) in enumerate(Es):
                    pst = P.tok(("psumT", ti))
                    if r == 0:
                        P.tt("dve", psum_[ti][:, :], E[:, :], rden[:, :], ALU.mult, r=(Et, rdent), w=(pst,))
                    else:
                        tm, tmt = P.rot("otmp", 2, [128, TB], F32)
                        P.tt("dve", tm[:, :], E[:, :], rden[:, :], ALU.mult, r=(Et, rdent), w=(tmt,))
                        P.tt("pool", psum_[ti][:, :], psum_[ti][:, :], tm[:, :], ALU.add, r=(tmt, pst), w=(pst,))
            selst = {}

            def selA():
                for ti in range(len(ctiles)):
                    P.copy("act", pb[ti][:, :], psum_[ti][:, :], r=(P.tok(("psumT", ti)),), w=(P.tok(("pb", ti)),))
                sbs = []
                for qt in range(4):
                    ib, ibt = P.bank()
                    for ti, ct in enumerate(ctiles):
                        P.mm(ib[:, 0:nsel], pb[ti][:, qt * 128:(qt + 1) * 128], ovl[:, ct, 0:nsel], ti == 0, ti == len(ctiles) - 1,
                             r=(P.tok(("pb", ti)), ac), w=(ibt,))
                    im, imt = P.rot("impm", 4, [128, 64], F32)
                    P.tt("dve", im[:, 0:nsel], ib[:, 0:nsel], slcm[:, s_, qt, 0, 0:nsel], ALU.mult, r=(ibt, ac), w=(imt,))
                    P.tt("dve", im[:, 0:nsel], im[:, 0:nsel], slcm[:, s_, qt, 1, 0:nsel], ALU.add, r=(imt, ac), w=(imt,))
                    mx, mxt = P.rot("mx", 4, [128, 8], F32)
                    P.op("dve", lambda e, o=mx[:, :], i=im[:, 0:nsel]: e.max(out=o, in_=i), r=(imt,), w=(mxt,))
                    rp, rpt = P.rot("rep", 4, [128, 64], F32)
                    P.op("dve", lambda e, o=rp[:, 0:nsel], a=mx[:, :], i=im[:, 0:nsel]: e.match_replace(out=o, in_to_replace=a, in_values=i, imm_value=-2.0 * BIG),
                         r=(imt, mxt), w=(rpt,))
                    mx2, mx2t = P.rot("mx2", 4, [128, 8], F32)
                    P.op("dve", lambda e, o=mx2[:, :], i=rp[:, 0:nsel]: e.max(out=o, in_=i), r=(rpt,), w=(mx2t,))
                    sb1, sb1t = P.rot("sb1", 8, [128, 64], F32)
                    P.ts("dve", sb1[:, 0:nsel], im[:, 0:nsel], mx2[:, 7:8], -NEG, ALU.is_ge, ALU.mult, r=(imt, mx2t), w=(sb1t,))
                    P.stt(sb1[:, 0:nsel], sb1[:, 0:nsel], NEG, slcm[:, s_, qt, 2, 0:nsel], ALU.add, ALU.add, r=(sb1t, ac), w=(sb1t,))
                    sbs.append((sb1, sb1t))
                selst["sbs"] = sbs

            def selB():
                tbk, tbkt = P.bank()
                for qt in range(4):
                    sb1, sb1t = selst["sbs"][qt]
                    P.op("pe", lambda e, o=tbk[0:nsel, qt * 128:(qt + 1) * 128], i=sb1[:, 0:nsel], idn=identf[:, :]: e.transpose(out=o, in_=i, identity=idn),
                         r=(sb1t, ac), w=(tbkt,))
                selT, selTt = P.rot("selT", 2, [64, TB], BF16)
                P.copy("act", selT[0:nsel, :], tbk[0:nsel, :], r=(tbkt,), w=(selTt,))
                selst["selT"] = (selT, selTt)
            nkt = 4 * nch
            ktoff = 0 if s_ == 0 else 16
            def slc_branch(r):
                selT, selTt = selst["selT"]
                h = 4 * g + r
                qr, qrt = qrs[r]
                ob, obt = P.accbank(0)
                db, dbt = P.accbank(1)
                items = []
                for kt in range(nkt):
                    st = {}

                    def sA(kt=kt, st=st, qr=qr, qrt=qrt):
                        gk = ktoff + kt
                        sbk, sbkt = P.bank()
                        diag = kt >= nkt - 4
                        P.mm(sbk[:, :], ks[:, gk * 128:(gk + 1) * 128], qr[:, qs], True, False, r=(kst, qrt), w=(sbkt,))
                        P.mm(sbk[:, :], boh2[0:nsel, kt, :], selT[0:nsel, :], False, not diag, r=(ac, selTt), w=(sbkt,))
                        if diag:
                            P.mm(sbk[:, :], identb[:, :], tri[:, kt - (nkt - 4), :], False, True, r=(ac,), w=(sbkt,))
                        st["s"] = (sbk, sbkt)

                    def sB(kt=kt, st=st, ob=ob, obt=obt, db=db, dbt=dbt):
                        gk = ktoff + kt
                        sbk, sbkt = st["s"]
                        E, Et = P.rot("E", 4, [128, TB], BF16)
                        P.act(E[:, :], sbk[:, :], AF.Exp, r=(sbkt,), w=(Et,), scale=scale)
                        P.mm(ob[:, :], vs[:, gk, :], E[:, :], kt == 0, kt == nkt - 1, r=(vst, Et), w=(obt,))
                        P.mm(db[:, :], P.ones[:, :], E[:, :], kt == 0, kt == nkt - 1, r=(onest, Et), w=(dbt,))
                    items.append((sA, sB))
                pipeline(items, (0, 2))
                finish_branch(h, 1, r, qs, ob, obt, db, dbt, False, False)
                P.copy("act", OT[:, h, qs], oacc[r][:, :], r=(P.tok(("oacc", r)),), w=(P.tok(("OT", h, s_)),))
            def win_branch(r):
                h = 4 * g + r
                qr, qrt = qrs[r]
                ob, obt = P.accbank(0)
                db, dbt = P.accbank(1)
                items = []
                for kt in range(8):
                    st = {}

                    def sA(kt=kt, st=st, qr=qr, qrt=qrt):
                        gk = s_ * 8 + kt
                        sbk, sbkt = P.bank()
                        P.mm(sbk[:, :], kw[:, gk * 128:(gk + 1) * 128], qr[:, qs], True, False, r=(kwt, qrt), w=(sbkt,))
                        msk = band[:, kt, :] if kt < 4 else tri[:, kt - 4, :]
                        P.mm(sbk[:, :], identb[:, :], msk, False, True, r=(ac,), w=(sbkt,))
                        st["s"] = (sbk, sbkt)

                    def sB(kt=kt, st=st, ob=ob, obt=obt, db=db, dbt=dbt):
                        gk = s_ * 8 + kt
                        sbk, sbkt = st["s"]
                        E, Et = P.rot("E", 4, [128, TB], BF16)
                        if kt < 4:
                            P.act(E[:, :], sbk[:, :], AF.Exp, r=(sbkt, ac), w=(Et,), scale=scale, bias=winpad[:, s_:s_ + 1])
                        else:
                            P.act(E[:, :], sbk[:, :], AF.Exp, r=(sbkt,), w=(Et,), scale=scale)
                        P.mm(ob[:, :], vw[:, gk, :], E[:, :], kt == 0, kt == 7, r=(vwt, Et), w=(obt,))
                        P.mm(db[:, :], P.ones[:, :], E[:, :], kt == 0, kt == 7, r=(onest, Et), w=(dbt,))
                    items.append((sA, sB))
                pipeline(items, (0, 2))
                finish_branch(h, 2, r, qs, ob, obt, db, dbt, False, False)
            selA()
            win_branch(0)
            win_branch(1)
            selB()
            win_branch(2)
            win_branch(3)
            for r in range(4):
                slc_branch(r)
    P.end_phase()


V2B = Vecs(["mlp_norm", "ple_norm"])
L2B_IN = [("qcT", [NH, 128, NT], BF16), ("qrT", [NH, 128, NT], BF16), ("gT", [48, NT], BF16),
          ("kslcT", [4, 128, 48 * 128], BF16), ("vslc", [4, 128, 48, 128], BF16), ("kwinT", [4, 128, 16 * 128], BF16), ("vwin", [4, 128, 16, 128], BF16),
          ("kcmpT", [4, 128, 384], BF16), ("vcmp", [4, 128, 3, 128], BF16),
          ("tri", [128, 4, TB], BF16), ("band", [128, 4, TB], BF16), ("cmpdiag", [128, TB], BF16), ("boh2", [64, 32, 128], BF16),
          ("sel48", [48, 48, 128], BF16), ("ovl", [128, 3, 64], BF16), ("identb", [128, 128], BF16), ("identf", [128, 128], F32),
          ("cmppad", [128, 3], F32), ("slcm", [128, 2, 4, 3, 64], F32), ("winpad", [128, 2], F32)]


def build_L2b():
    P = Prog()
    hT = P.din("hT", [128, NCH, NT], F32)
    pT = P.din("pT", [128, 2, NT], F32)
    vecs = P.din("vecs", [128, V2B.n()], F32)
    dd = {name: P.din(name, shape, dt) for name, shape, dt in L2B_IN}
    w_o = P.din("w_o", [D, D], F32)
    w1 = P.din("mlp_w1", [D, DFF], F32)
    w2 = P.din("mlp_w2", [DFF, D], F32)
    wg = P.din("ple_gate", [D, D], F32)
    wp = P.din("ple_proj", [256, D], F32)
    out = P.dout("hT_out", [128, NCH, NT], F32)

    def body():
        common_setup(P, V2B.n(), alloc_h=False)
        P.dma("sp", P.vecs[:, :], vecs[:, :], r=(), w=(P.tok("vecs"),))
        OT = P.sb("OT", [128, NCH, NT], BF16)
        vec = lambda n: V2B.ap(P, n)
        if DBG["mix"]:
            emit_nsa_attn(P, dd, OT)
        P.h = P.sb("hT", [128, NCH, NT], F32)
        load_h(P, hT)
        if DBG["mix"]:
            P.begin_phase()
            emit_outproj(P, w_o, OT, lambda kc, tb: P.tok(("OT", kc, tb)))
            P.end_phase()
        emit_mlp_ple(P, 2, w1, w2, wg, wp, pT, vec)
        store_h(P, out)
    return run_two_pass(P, body)


def layer2b(inp, h1, a):
    nc = build_L2b()
    vecs = V2B.host({"mlp_norm": inp["mlp_norm"][2], "ple_norm": inp["ple_norm"][2]})
    tri, boh, identb, identf = attn_consts()
    band, cmpdiag, boh2, sel48, ovl = nsa_consts()
    in_maps = []
    for c in range(NCORES):
        b = c // 4
        tk = core_tokens(c)
        lists = slot_lists(c)
        idx = np.concatenate([list_tokens(l) for l in lists])
        ksl = gather_rows(a["kslc"][b], idx)
        vsl = gather_rows(a["vslc"][b], idx).reshape(48, 128, 4, HD)
        widx = np.concatenate([list_tokens([(k - 1) if k > 0 else None, k]) for k in core_chunks(c)])
        kwl = gather_rows(a["kwin"][b], widx)
        vwl = gather_rows(a["vwin"][b], widx).reshape(16, 128, 4, HD)
        cidx = np.concatenate([np.array([(-1 if k is None else k)]) for l in lists for k in l])
        kcl = gather_rows(a["kcmp"][b], cidx).reshape(384, 4, HD)
        vcl = gather_rows(a["vcmp"][b], cidx).reshape(3, 128, 4, HD)
        cmppad, slcm, winpad = nsa_core_consts(c)
        m = {
            "hT": to_fm(h1[b][tk]), "pT": to_fm(inp["p"][2, b][tk]), "vecs": vecs,
            "qcT": np.ascontiguousarray(a["qc"][b][tk].transpose(1, 2, 0)), "qrT": np.ascontiguousarray(a["qr"][b][tk].transpose(1, 2, 0)),
            "gT": np.ascontiguousarray(a["gT"][b][tk].T),
            "kslcT": np.ascontiguousarray(ksl.transpose(1, 2, 0)), "vslc": np.ascontiguousarray(vsl.transpose(2, 1, 0, 3)),
            "kwinT": np.ascontiguousarray(kwl.transpose(1, 2, 0)), "vwin": np.ascontiguousarray(vwl.transpose(2, 1, 0, 3)),
            "kcmpT": np.ascontiguousarray(kcl.transpose(1, 2, 0)), "vcmp": np.ascontiguousarray(vcl.transpose(2, 1, 0, 3)),
            "tri": tri, "band": band, "cmpdiag": cmpdiag, "boh2": boh2, "sel48": sel48, "ovl": ovl, "identb": identb, "identf": identf,
            "cmppad": cmppad, "slcm": slcm, "winpad": winpad,
            "w_o": inp["nsa_w_o"][0],
            "mlp_w1": inp["mlp_w1"][2], "mlp_w2": inp["mlp_w2"][2], "ple_gate": inp["ple_gate"][2], "ple_proj": inp["ple_proj"][2],
        }
        in_maps.append(m)
    res = run_prog(nc, in_maps)
    h2 = np.zeros_like(h1)
    for c in range(NCORES):
        h2[c // 4][core_tokens(c)] = from_fm(res[c]["hT_out"])
    return h2


def kernel(**inputs):
    inp = {k: np.asarray(v) for k, v in inputs.items()}
    x = np.ascontiguousarray(inp["x"], dtype=np.float32)
    qkv = layer0a(inp, x)
    h0 = layer0b(inp, x, qkv)
    h1 = layer1(inp, h0)
    a = layer2a(inp, h1)
    h2 = layer2b(inp, h1, a)
    h3 = layer3(inp, h2)
    return h3.astype(np.float32)
```

```python
import contextlib
import numpy as np
import ml_dtypes
import concourse.bass as bass
import concourse.mybir as mybir
from concourse.bass_utils import run_bass_kernel_spmd

F32 = mybir.dt.float32
BF16 = mybir.dt.bfloat16
I32 = mybir.dt.int32
AF = mybir.ActivationFunctionType
ALU = mybir.AluOpType
AX = mybir.AxisListType

D = 2048
NCH = 16
S = 4096
B = 2
NT = 1024
TB = 512
NTB = NT // TB
DFF = 8192
EPS = 1e-6
NCORES = 8
HD = 128
NH = 16
WSLOT = 4096
NWSLOT = 4
NEG = -30000.0
FUSE_WAIT = False


def core_chunks(c):
    j = c % 4
    return [j, 7 - j]


def core_tokens(c):
    return np.concatenate([np.arange(k * TB, (k + 1) * TB) for k in core_chunks(c)])


class Fake:
    def __getitem__(self, k):
        return self

    def rearrange(self, *a, **k):
        return self

    def ap(self):
        return self


class Ref:
    __slots__ = ("sem", "val", "eng")

    def __init__(self, sem, val, eng):
        self.sem, self.val, self.eng = sem, val, eng


class Tok:
    __slots__ = ("w", "rs")

    def __init__(self):
        self.w = None
        self.rs = {}


COMPUTE = ("pe", "act", "dve", "pool")
NDSEM = 8


class Prog:
    def __init__(self):
        self.nc = bass.Bass("TRN2", target_bir_lowering=False)
        self.es = contextlib.ExitStack()
        self.streams = {e: [] for e in COMPUTE + ("sp",)}
        self.cnt = {e: 0 for e in COMPUTE}
        self.seen = {e: {} for e in COMPUTE + ("sp",)}
        self.esem = {e: self.es.enter_context(self.nc.semaphore("sem_" + e)) for e in COMPUTE}
        self.dsem = {q: [self.es.enter_context(self.nc.semaphore("dsem_%s%d" % (q, i))) for i in range(NDSEM)]
                     for q in ("sp", "pool")}
        self.dcnt = {"sp": 0, "pool": 0}
        self.toks = {}
        self.dry = False
        self.wplan = []
        self.wi = 0
        self.wissued = 0
        self.banks = None
        self.bi = 0
        self.rots = {}
        self.dram = {}
        self.out_refs = []
        self.phase_es = None
        self.nrot = 8
        self.nwslot = NWSLOT
        self.wlive = 2

    def din(self, name, shape, dtype):
        t = self.nc.dram_tensor(name, list(shape), dtype, kind="ExternalInput").ap()
        self.dram[name] = t
        return t

    def dout(self, name, shape, dtype):
        t = self.nc.dram_tensor(name, list(shape), dtype, kind="ExternalOutput").ap()
        self.dram[name] = t
        return t

    def sb(self, name, shape, dtype, phase=False):
        if self.dry:
            return Fake()
        es = self.phase_es if (phase and self.phase_es is not None) else self.es
        self.uid = getattr(self, "uid", 0) + 1
        return es.enter_context(self.nc.sbuf_tensor("s%d_%s" % (self.uid, name), list(shape), dtype))

    def begin_phase(self):
        if self.dry:
            return
        self.barrier()
        self.phase_es = contextlib.ExitStack()
        self.rots = {k: v for k, v in self.rots.items() if not v[3]}

    def end_phase(self):
        if self.dry:
            return
        self.barrier()
        self.phase_es.close()
        self.phase_es = None
        self.rots = {k: v for k, v in self.rots.items() if not v[3]}

    def setup_psum(self):
        if self.dry:
            self.banks = [Fake() for _ in range(8)]
        else:
            self.banks = [self.es.enter_context(self.nc.psum_tensor("bank%d" % i, [128, 512], F32)) for i in range(8)]

    def bank(self):
        i = self.bi
        self.bi = (self.bi + 1) % self.nrot
        return self.banks[i], self.tok(("bank", i))

    def accbank(self, i):
        assert self.nrot + i < 8
        return self.banks[self.nrot + i], self.tok(("bank", self.nrot + i))

    def rot(self, name, n, shape, dtype, phase=True):
        if name not in self.rots:
            self.rots[name] = ([self.sb("%s_%d" % (name, i), shape, dtype, phase=phase) for i in range(n)], 0, n, phase)
        tiles, i, n_, ph = self.rots[name]
        self.rots[name] = (tiles, (i + 1) % n_, n_, ph)
        return tiles[i], self.tok((name, i))

    def tok(self, key):
        t = self.toks.get(key)
        if t is None:
            t = self.toks[key] = Tok()
        return t

    def _deps(self, eng, r, w):
        waits = {}

        def add(ref):
            if ref is None:
                return
            if ref.eng == eng and eng == "pe":
                return
            k = id(ref.sem)
            if k not in waits or waits[k][1] < ref.val:
                waits[k] = (ref.sem, ref.val)
        for t in r:
            add(t.w)
        for t in w:
            add(t.w)
            for ref in t.rs.values():
                add(ref)
        out = []
        seen = self.seen[eng]
        for k, (sem, val) in waits.items():
            if seen.get(k, 0) < val:
                seen[k] = val
                out.append((sem, val))
        return out

    def _commit(self, ref, r, w):
        for t in r:
            k = id(ref.sem)
            old = t.rs.get(k)
            if old is None or old.val < ref.val:
                t.rs[k] = ref
        for t in w:
            t.w = ref
            t.rs = {}

    def op(self, eng, fn, r=(), w=()):
        if self.dry:
            return
        waits = self._deps(eng, r, w)
        self.cnt[eng] += 1
        ref = Ref(self.esem[eng], self.cnt[eng], eng)
        self.streams[eng].append((waits, fn, (ref.sem, 1)))
        self._commit(ref, r, w)

    def dma(self, q, out, in_, r=(), w=(), is_out=False):
        if self.dry:
            return
        n = self.dcnt[q]
        self.dcnt[q] += 1
        sem = self.dsem[q][n % NDSEM]
        val = 16 * (n // NDSEM + 1)
        waits = self._deps(q, r, w)
        if n >= NDSEM:
            k = id(sem)
            if self.seen[q].get(k, 0) < val - 16:
                self.seen[q][k] = val - 16
                waits.append((sem, val - 16))
        ref = Ref(sem, val, "dma_" + q)
        self.streams[q].append((waits, (lambda e, o=out, i=in_: e.dma_start(out=o, in_=i)), (sem, 16)))
        self._commit(ref, r, w)
        if is_out:
            self.out_refs.append(ref)

    def barrier(self):
        if self.dry:
            return
        allw = [(self.esem[e], self.cnt[e]) for e in COMPUTE if self.cnt[e] > 0]
        for q in ("sp", "pool"):
            n = self.dcnt[q]
            for i in range(min(n, NDSEM)):
                last = ((n - 1 - i) // NDSEM) * NDSEM + i
                allw.append((self.dsem[q][i], 16 * (last // NDSEM + 1)))
        for e in COMPUTE + ("sp",):
            seen = self.seen[e]
            ws = []
            for sem, val in allw:
                if seen.get(id(sem), 0) < val:
                    seen[id(sem)] = val
                    ws.append((sem, val))
            if ws:
                self.streams[e].append((ws, None, None))

    def mm(self, out, lhsT, rhs, start, stop, r, w):
        self.op("pe", lambda e: e.matmul(out, lhsT, rhs, start=start, stop=stop), r, w)

    def act(self, out, in_, func, r, w, **kw):
        self.op("act", lambda e: e.activation(out=out, in_=in_, func=func, **kw), r, w)

    def tt(self, eng, out, in0, in1, op, r, w):
        self.op(eng, lambda e: e.tensor_tensor(out=out, in0=in0, in1=in1, op=op), r, w)

    def ts(self, eng, out, in0, s1, s2, op0, op1, r, w):
        if s2 is None:
            self.op(eng, lambda e: e.tensor_scalar(out=out, in0=in0, scalar1=s1, scalar2=None, op0=op0), r, w)
        else:
            self.op(eng, lambda e: e.tensor_scalar(out=out, in0=in0, scalar1=s1, scalar2=s2, op0=op0, op1=op1), r, w)

    def stt(self, out, in0, scalar, in1, op0, op1, r, w):
        self.op("dve", lambda e: e.scalar_tensor_tensor(out=out, in0=in0, scalar=scalar, in1=in1, op0=op0, op1=op1), r, w)

    def copy(self, eng, out, in_, r, w):
        if eng == "act":
            self.op("act", lambda e: e.copy(out=out, in_=in_), r, w)
        else:
            self.op(eng, lambda e: e.tensor_copy(out=out, in_=in_), r, w)

    def setup_wslots(self):
        self.wslots = [self.sb("wslot%d" % i, [128, WSLOT], BF16) for i in range(self.nwslot)]

    def wnext(self, wap, k0, kc, n0, ncols):
        assert kc * ncols <= WSLOT
        if self.dry:
            self.wplan.append((wap, k0, kc, n0, ncols))
            return Fake(), None
        i = self.wi
        self.wi += 1
        assert self.wplan[i][1:] == (k0, kc, n0, ncols), (self.wplan[i][1:], (k0, kc, n0, ncols))
        while self.wissued < min(len(self.wplan), i + self.nwslot - self.wlive + 1):
            jj = self.wissued
            wap_, k0_, kc_, n0_, nc_ = self.wplan[jj]
            slot = self.wslots[jj % self.nwslot]
            dst = slot[:, 0:kc_ * nc_].rearrange("p (k n) -> p k n", k=kc_)
            src = wap_[k0_:k0_ + kc_ * 128, n0_:n0_ + nc_].rearrange("(k p) n -> p k n", p=128)
            self.dma("pool", dst, src, r=(), w=(self.tok(("wslot", jj % self.nwslot)),))
            self.wissued += 1
        slot = self.wslots[i % self.nwslot]
        return slot[:, 0:kc * ncols].rearrange("p (k n) -> p k n", k=kc), self.tok(("wslot", i % self.nwslot))

    def finish(self):
        ws = []
        seen = self.seen["sp"]
        best = {}
        for ref in self.out_refs:
            k = id(ref.sem)
            if k not in best or best[k][1] < ref.val:
                best[k] = (ref.sem, ref.val)
        for k, (sem, val) in best.items():
            if seen.get(k, 0) < val:
                ws.append((sem, val))
        self.barrier()
        self.streams["sp"].append((ws, None, None))
        nc = self.nc
        regs = {"pe": "tensor", "act": "scalar", "dve": "vector", "pool": "gpsimd", "sp": "sync"}
        with nc.Block() as block:
            for eng, attr in regs.items():
                stream = self.streams[eng]

                def f(e, stream=stream):
                    for waits, fn, inc in stream:
                        fuse = FUSE_WAIT and fn is not None and len(waits) > 0
                        for s_, v_ in (waits[:-1] if fuse else waits):
                            e.wait_ge(s_, v_)
                        if fn is not None:
                            ins = fn(e)
                            if fuse:
                                ins._wait_ge(waits[-1][0], waits[-1][1])
                            if inc is not None:
                                ins.then_inc(inc[0], inc[1])
                getattr(block, attr)(f)
        self.es.close()
        return nc


def emit_rstd(P, h, cs, tbw, src_toks):
    bk, bkt = P.bank()
    for c in range(NCH):
        sq, sqt = P.rot("sq", 3, [128, TB], BF16)
        P.tt("pool", sq[:, :tbw], h[:, c, cs], h[:, c, cs], ALU.mult, r=(src_toks[c],), w=(sqt,))
        P.mm(bk[:, :tbw], P.ones[:, :], sq[:, :tbw], c == 0, c == NCH - 1, r=(sqt, P.tok("ones")), w=(bkt,))
    rstd, rt = P.rot("rstd", 2, [128, TB], F32)
    P.act(rstd[:, :tbw], bk[:, :tbw], AF.Sqrt, r=(bkt, P.tok("consts")), w=(rt,), scale=1.0 / D, bias=P.epsc[:, 0:1])
    P.op("dve", lambda e, o=rstd[:, :tbw]: e.reciprocal(out=o, in_=o), r=(rt,), w=(rt,))
    return rstd, rt


def emit_norm(P, h, gain_col, dst, dst_tokf, ntb=NTB, src_tokf=None, tbw=TB):
    if src_tokf is None:
        src_tokf = lambda c, tb: P.tok(("h", c, tb))
    for tb in range(ntb):
        cs = slice(tb * tbw, (tb + 1) * tbw)
        rstd, rt = emit_rstd(P, h, cs, tbw, [src_tokf(c, tb) for c in range(NCH)])
        for c in range(NCH):
            P.stt(dst[:, c, cs], h[:, c, cs], gain_col[:, c:c + 1], rstd[:, :tbw], ALU.mult, ALU.mult,
                  r=(src_tokf(c, tb), rt, P.tok("vecs")), w=(dst_tokf(c, tb),))


DBG = {"mlp": True, "ple": True, "mix": True}


def pipeline(items, lags):
    n = len(items)
    tot = n + max(lags)
    for step in range(tot):
        for k, lag in enumerate(lags):
            i = step - lag
            if 0 <= i < n:
                items[i][k]()


def emit_mlp_ple(P, L, w1, w2, wg, wp, pT_dram, vec):
    h = P.h
    P.begin_phase()
    xn = P.sb("xn", [128, NCH, NT], BF16, phase=True)
    hid = P.sb("hid", [128, 8, NT], BF16, phase=True)
    pT = P.sb("pT", [128, 2, NT], BF16, phase=True)
    xt = lambda c, tb: P.tok(("xn", c, tb))
    ht = lambda c, tb: P.tok(("h", c, tb))
    hidt = lambda c, tb: P.tok(("hid", c, tb))
    P.dma("pool", pT[:, :, :], pT_dram[:, :, :], r=(), w=(P.tok("pT"),))
    if DBG["mlp"]:
        emit_norm(P, h, vec("mlp_norm"), xn, xt)
    for fb in range(8 if DBG["mlp"] else 0):
        for j in range(4):
            wt, wtk = P.wnext(w1, 0, 16, fb * 1024 + j * 256, 256)
            for m in range(2):
                for tb in range(NTB):
                    cs = slice(tb * TB, (tb + 1) * TB)
                    bk, bkt = P.bank()
                    for kc in range(NCH):
                        P.mm(bk[:, :], wt[:, kc, m * 128:(m + 1) * 128], xn[:, kc, cs], kc == 0, kc == NCH - 1,
                             r=(wtk, xt(kc, tb)), w=(bkt,))
                    rl, rlt = P.rot("relu", 3, [128, TB], F32)
                    P.act(rl[:, :], bk[:, :], AF.Relu, r=(bkt,), w=(rlt,))
                    P.tt("pool", hid[:, j * 2 + m, cs], rl[:, :], rl[:, :], ALU.mult, r=(rlt,), w=(hidt(j * 2 + m, tb),))
        for j in range(4):
            wt, wtk = P.wnext(w2, fb * 1024, 8, j * 512, 512)
            for m in range(4):
                for tb in range(NTB):
                    cs = slice(tb * TB, (tb + 1) * TB)
                    bk, bkt = P.bank()
                    for kc in range(8):
                        P.mm(bk[:, :], wt[:, kc, m * 128:(m + 1) * 128], hid[:, kc, cs], kc == 0, kc == 7,
                             r=(wtk, hidt(kc, tb)), w=(bkt,))
                    c = j * 4 + m
                    P.tt("dve", h[:, c, cs], h[:, c, cs], bk[:, :], ALU.add, r=(bkt, ht(c, tb)), w=(ht(c, tb),))
    if DBG["ple"]:
        emit_norm(P, h, vec("ple_norm"), xn, xt)
    for j in range(8 if DBG["ple"] else 0):
        wt, wtk = P.wnext(wg, 0, 16, j * 256, 256)
        wq, wqk = P.wnext(wp, 0, 2, j * 256, 256)
        for m in range(2):
            for tb in range(NTB):
                cs = slice(tb * TB, (tb + 1) * TB)
                c = j * 2 + m
                bk, bkt = P.bank()
                for kc in range(NCH):
                    P.mm(bk[:, :], wt[:, kc, m * 128:(m + 1) * 128], xn[:, kc, cs], kc == 0, kc == NCH - 1,
                         r=(wtk, xt(kc, tb)), w=(bkt,))
                g, gt = P.rot("gate", 3, [128, TB], F32)
                P.act(g[:, :], bk[:, :], AF.Sigmoid, r=(bkt,), w=(gt,))
                bk2, bk2t = P.bank()
                for kc in range(2):
                    P.mm(bk2[:, :], wq[:, kc, m * 128:(m + 1) * 128], pT[:, kc, cs], kc == 0, kc == 1,
                         r=(wqk, P.tok("pT")), w=(bk2t,))
                P.tt("dve", g[:, :], g[:, :], bk2[:, :], ALU.mult, r=(gt, bk2t), w=(gt,))
                P.tt("dve", h[:, c, cs], h[:, c, cs], g[:, :], ALU.add, r=(gt, ht(c, tb)), w=(ht(c, tb),))
    P.end_phase()


def common_setup(P, nvec, alloc_h=True):
    P.setup_psum()
    P.setup_wslots()
    if alloc_h:
        P.h = P.sb("hT", [128, NCH, NT], F32)
    P.ones = P.sb("ones", [128, 128], BF16)
    P.epsc = P.sb("epsc", [128, 1], F32)
    P.vecs = P.sb("vecs", [128, nvec], F32)
    P.onesf = P.sb("onesf", [128, 128], F32)
    P.op("pool", lambda e: e.memset(P.ones[:, :], 1.0), r=(), w=(P.tok("ones"),))
    P.op("pool", lambda e: e.memset(P.onesf[:, :], 1.0), r=(), w=(P.tok("ones"),))
    P.op("pool", lambda e: e.memset(P.epsc[:, :], EPS), r=(), w=(P.tok("consts"),))


def load_h(P, hT_dram):
    for c in range(NCH):
        P.dma("sp", P.h[:, c, :], hT_dram[:, c, :], r=(), w=tuple(P.tok(("h", c, tb)) for tb in range(NTB)))


def store_h(P, out_dram):
    for c in range(NCH):
        P.dma("sp", out_dram[:, c, :], P.h[:, c, :], r=tuple(P.tok(("h", c, tb)) for tb in range(NTB)), w=(), is_out=True)


class Vecs:
    def __init__(self, names):
        self.names = list(names)

    def n(self):
        return 16 * len(self.names)

    def ap(self, P, name):
        i = self.names.index(name)
        return P.vecs[:, 16 * i:16 * (i + 1)]

    def host(self, arrs):
        return np.ascontiguousarray(np.concatenate([np.asarray(arrs[n], np.float32).reshape(16, 128).T for n in self.names], axis=1))


def run_two_pass(P, body):
    P.dry = True
    body()
    P.dry = False
    P.bi = 0
    P.rots = {}
    P.toks = {}
    body()
    assert P.wi == len(P.wplan), (P.wi, len(P.wplan))
    return P.finish()


POOL_W = (2, 4, 8, 16)


def emit_pool(P, wpool, hhalo_dram, fac_dram, vec):
    h = P.h
    P.begin_phase()
    hx = P.sb("hx", [128, NCH, 32], F32, phase=True)
    xh = P.sb("xh", [128, NCH, 32], F32, phase=True)
    fac = P.sb("fac", [128, 4, 2, 16], F32, phase=True)
    P.dma("sp", hx[:, :, :], hhalo_dram[:, :, :], r=(), w=(P.tok("hx"),))
    P.dma("sp", fac[:, :, :, :], fac_dram[:, :, :, :], r=(), w=(P.tok("fac"),))
    gain = vec("mixer_norm")
    scale = vec("pool_scale")
    emit_norm(P, hx, gain, xh, lambda c, tb: P.tok(("xh", c)), ntb=1, src_tokf=lambda c, tb: P.tok("hx"), tbw=32)
    ht = lambda c, tb: P.tok(("h", c, tb))
    for s in range(NTB):
        cs = slice(s * TB, (s + 1) * TB)
        rstd, rt = emit_rstd(P, h, cs, TB, [ht(c, s) for c in range(NCH)])
        for g in range(4):
            w = POOL_W[g]
            dg, dgt = P.rot("diffg", 2, [128, 4, TB], BF16)
            for cc in range(4):
                c = 4 * g + cc
                xe, xet = P.rot("xe", 2, [128, 16 + TB], F32)
                P.copy("pool", xe[:, 0:16], xh[:, c, s * 16:(s + 1) * 16], r=(P.tok(("xh", c)),), w=(xet,))
                P.stt(xe[:, 16:16 + TB], h[:, c, cs], gain[:, c:c + 1], rstd[:, :], ALU.mult, ALU.mult,
                      r=(ht(c, s), rt, P.tok("vecs"), xet), w=(xet,))
                cur, curt = xe, xet
                shift = 1
                for st in range(g + 1):
                    nx, nxt = P.rot("ss", 3, [128, 16 + TB], F32)
                    lo = 2 * shift - 1
                    P.tt("dve", nx[:, lo:16 + TB], cur[:, lo:16 + TB], cur[:, lo - shift:16 + TB - shift], ALU.add,
                         r=(curt,), w=(nxt,))
                    cur, curt = nx, nxt
                    shift *= 2
                P.stt(dg[:, cc, :], cur[:, 16:16 + TB], 1.0 / w, xe[:, 16:16 + TB], ALU.mult, ALU.subtract,
                      r=(curt, xet), w=(dgt,))
                t16, t16t = P.rot("t16", 2, [128, 16], F32)
                P.tt("dve", t16[:, :], cur[:, 16:32], fac[:, g, s, :], ALU.mult, r=(curt, P.tok("fac")), w=(t16t,))
                P.tt("dve", dg[:, cc, 0:16], t16[:, :], xe[:, 16:32], ALU.subtract, r=(t16t, xet, dgt), w=(dgt,))
            wt, wtk = P.wnext(wpool, g * 512, 4, 0, 512)
            for oc in range(4):
                bk, bkt = P.bank()
                for kc in range(4):
                    P.mm(bk[:, :], wt[:, kc, oc * 128:(oc + 1) * 128], dg[:, kc, :], kc == 0, kc == 3, r=(wtk, dgt), w=(bkt,))
                c = 4 * g + oc
                P.stt(h[:, c, cs], bk[:, :], scale[:, c:c + 1], h[:, c, cs], ALU.mult, ALU.add,
                      r=(bkt, ht(c, s), P.tok("vecs")), w=(ht(c, s),))
    P.end_phase()


V1 = Vecs(["mixer_norm", "pool_scale", "mlp_norm", "ple_norm"])


def build_L1():
    P = Prog()
    hT = P.din("hT", [128, NCH, NT], F32)
    hhalo = P.din("hhalo", [128, NCH, 32], F32)
    fac = P.din("poolfac", [128, 4, 2, 16], F32)
    pT = P.din("pT", [128, 2, NT], F32)
    vecs = P.din("vecs", [128, V1.n()], F32)
    wpool = P.din("pool_w", [2048, 512], F32)
    w1 = P.din("mlp_w1", [D, DFF], F32)
    w2 = P.din("mlp_w2", [DFF, D], F32)
    wg = P.din("ple_gate", [D, D], F32)
    wp = P.din("ple_proj", [256, D], F32)
    out = P.dout("hT_out", [128, NCH, NT], F32)

    def body():
        common_setup(P, V1.n())
        P.dma("sp", P.vecs[:, :], vecs[:, :], r=(), w=(P.tok("vecs"),))
        load_h(P, hT)
        vec = lambda n: V1.ap(P, n)
        if DBG["mix"]:
            emit_pool(P, wpool, hhalo, fac, vec)
        emit_mlp_ple(P, 1, w1, w2, wg, wp, pT, vec)
        store_h(P, out)
    return run_two_pass(P, body)


def to_fm(a):
    ntok, nf = a.shape
    return np.ascontiguousarray(a.T.reshape(nf // 128, 128, ntok).transpose(1, 0, 2))


def from_fm(a):
    p, nch, ntok = a.shape
    return np.ascontiguousarray(a.transpose(1, 0, 2).reshape(nch * 128, ntok).T)


def halo_rows(hfull_b, c, n):
    out = np.zeros((2 * n, hfull_b.shape[1]), np.float32)
    for s, k in enumerate(core_chunks(c)):
        if k > 0:
            out[s * n:(s + 1) * n] = hfull_b[k * TB - n:k * TB]
    return out


def pool_fac(c):
    f = np.zeros((128, 4, 2, 16), np.float32)
    for g, w in enumerate(POOL_W):
        for s, k in enumerate(core_chunks(c)):
            for t in range(16):
                cnt = min(w, t + 1) if k == 0 else w
                f[:, g, s, t] = 1.0 / cnt
    return f


def run_prog(nc, in_maps):
    if DBG.get("trace"):
        res = run_bass_kernel_spmd(nc, in_maps, core_ids=list(range(NCORES)), trace=True)
        print("EXEC_TIME_NS", res.exec_time_ns, flush=True)
        DBG["last_res"] = res
        return res.results
    res = run_bass_kernel_spmd(nc, in_maps, core_ids=list(range(NCORES)))
    return res.results


def layer1(inp, h0):
    nc = build_L1()
    vec_arrs = {"mixer_norm": inp["mixer_norm"][1], "pool_scale": inp["pool_scale"][0], "mlp_norm": inp["mlp_norm"][1],
                "ple_norm": inp["ple_norm"][1]}
    vecs = V1.host(vec_arrs)
    in_maps = []
    for c in range(NCORES):
        b = c // 4
        tk = core_tokens(c)
        in_maps.append({
            "hT": to_fm(h0[b][tk]), "hhalo": to_fm(halo_rows(h0[b], c, 16)), "poolfac": pool_fac(c),
            "pT": to_fm(inp["p"][1, b][tk]), "vecs": vecs,
            "pool_w": np.ascontiguousarray(inp["pool_w"][0].reshape(2048, 512)),
            "mlp_w1": inp["mlp_w1"][1], "mlp_w2": inp["mlp_w2"][1], "ple_gate": inp["ple_gate"][1], "ple_proj": inp["ple_proj"][1],
        })
    res = run_prog(nc, in_maps)
    h1 = np.zeros_like(h0)
    for c in range(NCORES):
        h1[c // 4][core_tokens(c)] = from_fm(res[c]["hT_out"])
    return h1


def emit_conv(P, w_in, w_o, hhalo_dram, vec):
    h = P.h
    P.begin_phase()
    hx = P.sb("hx", [128, NCH, 32], F32, phase=True)
    xh = P.sb("xh", [128, NCH, 32], BF16, phase=True)
    xn = P.sb("xn", [128, NCH, NT], BF16, phase=True)
    gT = P.sb("gT", [128, NCH, NT], BF16, phase=True)
    P.dma("sp", hx[:, :, :], hhalo_dram[:, :, :], r=(), w=(P.tok("hx"),))
    gain = vec("mixer_norm")
    xht = lambda c, tb: P.tok(("xh", c))
    xt = lambda c, tb: P.tok(("xn", c, tb))
    ht = lambda c, tb: P.tok(("h", c, tb))
    gt = lambda c, tb: P.tok(("gT", c, tb))
    emit_norm(P, hx, gain, xh, xht, ntb=1, src_tokf=lambda c, tb: P.tok("hx"), tbw=32)
    emit_norm(P, h, gain, xn, xt)
    w0, w1, w2, cb = vec("conv_w0"), vec("conv_w1"), vec("conv_w2"), vec("conv_b")
    vt = P.tok("vecs")

    def proj(wt, wtk, m, tb):
        bk, bkt = P.bank()
        cs = slice(tb * TB, (tb + 1) * TB)
        for kc in range(NCH):
            P.mm(bk[:, :], wt[:, kc, m * 128:(m + 1) * 128], xn[:, kc, cs], kc == 0, kc == NCH - 1, r=(wtk, xt(kc, tb)), w=(bkt,))
        return bk, bkt

    def projh(wt, wtk, m):
        bk, bkt = P.bank()
        for kc in range(NCH):
            P.mm(bk[:, 0:32], wt[:, kc, m * 128:(m + 1) * 128], xh[:, kc, :], kc == 0, kc == NCH - 1, r=(wtk, xht(kc, 0)), w=(bkt,))
        return bk, bkt

    for dcp in range(8):
        wt, wtk = P.wnext(w_in, 0, 16, 2048 + dcp * 256, 256)
        cg = {}
        for m in range(2):
            for tb in range(NTB):
                bk, bkt = proj(wt, wtk, m, tb)
                t, tt_ = P.rot("cg", 4, [128, TB], F32)
                P.copy("act", t[:, :], bk[:, :], r=(bkt,), w=(tt_,))
                cg[(m, tb)] = (t, tt_)
            bk, bkt = projh(wt, wtk, m)
            t, tt_ = P.rot("cgh", 2, [128, 32], F32)
            P.copy("act", t[:, :], bk[:, 0:32], r=(bkt,), w=(tt_,))
            cg[(m, "h")] = (t, tt_)
        wt, wtk = P.wnext(w_in, 0, 16, 4096 + dcp * 256, 256)
        cv = {}
        for m in range(2):
            c = dcp * 2 + m
            us = {}
            for tb in range(NTB):
                bk, bkt = proj(wt, wtk, m, tb)
                u, ut = P.rot("u", 4, [128, 16 + TB], F32)
                P.tt("dve", u[:, 16:16 + TB], cg[(m, tb)][0][:, :], bk[:, :], ALU.mult, r=(cg[(m, tb)][1], bkt), w=(ut,))
                us[tb] = (u, ut)
            bk, bkt = projh(wt, wtk, m)
            for tb in range(NTB):
                u, ut = us[tb]
                P.tt("dve", u[:, 0:16], cg[(m, "h")][0][:, tb * 16:(tb + 1) * 16], bk[:, tb * 16:(tb + 1) * 16], ALU.mult,
                     r=(cg[(m, "h")][1], bkt, ut), w=(ut,))
                v, vtk = P.rot("cv", 4, [128, TB], F32)
                P.ts("dve", v[:, :], u[:, 14:14 + TB], w0[:, c:c + 1], cb[:, c:c + 1], ALU.mult, ALU.add, r=(ut, vt), w=(vtk,))
                P.stt(v[:, :], u[:, 15:15 + TB], w1[:, c:c + 1], v[:, :], ALU.mult, ALU.add, r=(ut, vt, vtk), w=(vtk,))
                P.stt(v[:, :], u[:, 16:16 + TB], w2[:, c:c + 1], v[:, :], ALU.mult, ALU.add, r=(ut, vt, vtk), w=(vtk,))
                cv[(m, tb)] = (v, vtk)
        wt, wtk = P.wnext(w_in, 0, 16, dcp * 256, 256)
        for m in range(2):
            c = dcp * 2 + m
            for tb in range(NTB):
                bk, bkt = proj(wt, wtk, m, tb)
                P.tt("dve", gT[:, c, tb * TB:(tb + 1) * TB], bk[:, :], cv[(m, tb)][0][:, :], ALU.mult, r=(bkt, cv[(m, tb)][1]), w=(gt(c, tb),))
    emit_outproj(P, w_o, gT, gt)
    P.end_phase()


def emit_outproj(P, w_o, src, src_tokf):
    h = P.h
    ht = lambda c, tb: P.tok(("h", c, tb))
    for j in range(8):
        wt, wtk = P.wnext(w_o, 0, 16, j * 256, 256)
        for m in range(2):
            c = j * 2 + m
            for tb in range(NTB):
                cs = slice(tb * TB, (tb + 1) * TB)
                bk, bkt = P.bank()
                for kc in range(NCH):
                    P.mm(bk[:, :], wt[:, kc, m * 128:(m + 1) * 128], src[:, kc, cs], kc == 0, kc == NCH - 1,
                         r=(wtk, src_tokf(kc, tb)), w=(bkt,))
                P.tt("dve", h[:, c, cs], h[:, c, cs], bk[:, :], ALU.add, r=(bkt, ht(c, tb)), w=(ht(c, tb),))


V3 = Vecs(["mixer_norm", "conv_w0", "conv_w1", "conv_w2", "conv_b", "mlp_norm", "ple_norm"])


def build_L3():
    P = Prog()
    hT = P.din("hT", [128, NCH, NT], F32)
    hhalo = P.din("hhalo", [128, NCH, 32], F32)
    pT = P.din("pT", [128, 2, NT], F32)
    vecs = P.din("vecs", [128, V3.n()], F32)
    w_in = P.din("conv_w_in", [D, 3 * D], F32)
    w_o = P.din("conv_w_o", [D, D], F32)
    w1 = P.din("mlp_w1", [D, DFF], F32)
    w2 = P.din("mlp_w2", [DFF, D], F32)
    wg = P.din("ple_gate", [D, D], F32)
    wp = P.din("ple_proj", [256, D], F32)
    out = P.dout("hT_out", [128, NCH, NT], F32)

    def body():
        common_setup(P, V3.n())
        P.dma("sp", P.vecs[:, :], vecs[:, :], r=(), w=(P.tok("vecs"),))
        load_h(P, hT)
        vec = lambda n: V3.ap(P, n)
        if DBG["mix"]:
            emit_conv(P, w_in, w_o, hhalo, vec)
        emit_mlp_ple(P, 3, w1, w2, wg, wp, pT, vec)
        store_h(P, out)
    return run_two_pass(P, body)


def layer3(inp, h2):
    nc = build_L3()
    vec_arrs = {"mixer_norm": inp["mixer_norm"][3], "conv_w0": inp["conv_w"][0, 0], "conv_w1": inp["conv_w"][0, 1],
                "conv_w2": inp["conv_w"][0, 2], "conv_b": inp["conv_b"][0], "mlp_norm": inp["mlp_norm"][3], "ple_norm": inp["ple_norm"][3]}
    vecs = V3.host(vec_arrs)
    in_maps = []
    for c in range(NCORES):
        b = c // 4
        tk = core_tokens(c)
        in_maps.append({
            "hT": to_fm(h2[b][tk]), "hhalo": to_fm(halo_rows(h2[b], c, 16)),
            "pT": to_fm(inp["p"][3, b][tk]), "vecs": vecs,
            "conv_w_in": inp["conv_w_in"][0], "conv_w_o": inp["conv_w_o"][0],
            "mlp_w1": inp["mlp_w1"][3], "mlp_w2": inp["mlp_w2"][3], "ple_gate": inp["ple_gate"][3], "ple_proj": inp["ple_proj"][3],
        })
    res = run_prog(nc, in_maps)
    h3 = np.zeros_like(h2)
    for c in range(NCORES):
        h3[c // 4][core_tokens(c)] = from_fm(res[c]["hT_out"])
    return h3


ROPE_THETA = 500000.0
PI = float(np.pi)


def rope_consts():
    freq = np.zeros((128, 1), np.float32)
    fr = (np.float32(ROPE_THETA) ** (-np.arange(16, dtype=np.float32) * np.float32(2.0) / np.float32(32))).astype(np.float32)
    freq[0:16, 0] = fr
    freq[16:32, 0] = fr
    rotT = np.zeros((128, 128), np.float32)
    for i in range(16):
        rotT[i + 16, i] = -1.0
        rotT[i, i + 16] = 1.0
    return freq, rotT.astype(ml_dtypes.bfloat16)


def emit_rope_tables(P, posb_dram, hc):
    P.cosF = P.sb("cosF", [128, NT], F32)
    P.sinF = P.sb("sinF", [128, NT], F32)
    P.begin_phase()
    posi = P.sb("posi", [128, NT], I32, phase=True)
    ang = P.sb("ang", [128, NT], F32, phase=True)
    tk = P.tok("rope")
    P.dma("sp", posi[:, :], posb_dram[:, :], r=(), w=(tk,))
    P.copy("dve", ang[:, :], posi[:, :], r=(tk,), w=(tk,))
    P.ts("dve", ang[:, :], ang[:, :], hc[:, 2:3], None, ALU.mult, None, r=(tk, P.tok("hc")), w=(tk,))
    ki = P.sb("rope_ki", [128, NT], I32, phase=True)
    kf = P.sb("rope_kf", [128, NT], F32, phase=True)
    msk = P.sb("rope_m", [128, NT], F32, phase=True)
    for dst, shift in ((P.sinF, 0.0), (P.cosF, 0.5 * PI)):
        P.ts("dve", dst[:, :], ang[:, :], shift, None, ALU.add, None, r=(tk,), w=(tk,))
        P.ts("dve", kf[:, :], dst[:, :], 1.0 / (2.0 * PI), None, ALU.mult, None, r=(tk,), w=(tk,))
        P.copy("dve", ki[:, :], kf[:, :], r=(tk,), w=(tk,))
        P.copy("dve", kf[:, :], ki[:, :], r=(tk,), w=(tk,))
        P.stt(dst[:, :], kf[:, :], -2.0 * PI, dst[:, :], ALU.mult, ALU.add, r=(tk,), w=(tk,))
        P.ts("dve", msk[:, :], dst[:, :], PI, None, ALU.is_gt, None, r=(tk,), w=(tk,))
        P.stt(dst[:, :], msk[:, :], -2.0 * PI, dst[:, :], ALU.mult, ALU.add, r=(tk,), w=(tk,))
        P.ts("dve", msk[:, :], dst[:, :], -PI, None, ALU.is_lt, None, r=(tk,), w=(tk,))
        P.stt(dst[:, :], msk[:, :], 2.0 * PI, dst[:, :], ALU.mult, ALU.add, r=(tk,), w=(tk,))
        P.act(dst[:, :], dst[:, :], AF.Sin, r=(tk,), w=(tk,))
    P.end_phase()


def emit_headnorm(P, bk, bkt, gain1, cs, rope, hc):
    sq, sqt = P.rot("hsq", 3, [128, TB], BF16)
    P.act(sq[:, :], bk[:, :], AF.Square, r=(bkt,), w=(sqt,))
    b2, b2t = P.bank()
    P.mm(b2[:, :], P.ones[:, :], sq[:, :], True, True, r=(sqt, P.tok("ones")), w=(b2t,))
    rs, rst = P.rot("hrstd", 2, [128, TB], F32)
    P.act(rs[:, :], b2[:, :], AF.Sqrt, r=(b2t, P.tok("consts")), w=(rst,), scale=1.0 / HD, bias=P.epsc[:, 0:1])
    P.op("dve", lambda e, o=rs[:, :]: e.reciprocal(out=o, in_=o), r=(rst,), w=(rst,))
    qn, qnt = P.rot("qn", 3, [128, TB], BF16)
    P.stt(qn[:, :], bk[:, :], gain1, rs[:, :], ALU.mult, ALU.mult, r=(bkt, rst, P.tok("hc")), w=(qnt,))
    if not rope:
        return qn, qnt
    if rope == "both":
        P.qn_last = (qn, qnt)
    b3, b3t = P.bank()
    P.mm(b3[:, :], P.rotT[:, :], qn[:, :], True, True, r=(qnt, P.tok("hc")), w=(b3t,))
    t1, t1t = P.rot("rp1", 2, [128, TB], F32)
    P.tt("dve", t1[:, :], qn[:, :], P.cosF[:, cs], ALU.mult, r=(qnt, P.tok("rope")), w=(t1t,))
    t2, t2t = P.rot("rp2", 2, [128, TB], F32)
    P.tt("dve", t2[:, :], b3[:, :], P.sinF[:, cs], ALU.mult, r=(b3t, P.tok("rope")), w=(t2t,))
    P.tt("pool", t1[:, :], t1[:, :], t2[:, :], ALU.add, r=(t1t, t2t), w=(t1t,))
    return t1, t1t


def headnorm_stages(P, make_proj, gain1, cs, rope, sink):
    st = {}

    def s0():
        bk, bkt = make_proj()
        sq, sqt = P.rot("hsq", 4, [128, TB], BF16, phase=False)
        P.act(sq[:, :], bk[:, :], AF.Square, r=(bkt,), w=(sqt,))
        st.update(bk=bk, bkt=bkt, sq=sq, sqt=sqt)

    def s1():
        bk, bkt, sq, sqt = st["bk"], st["bkt"], st["sq"], st["sqt"]
        b2, b2t = P.bank()
        P.mm(b2[:, :], P.ones[:, :], sq[:, :], True, True, r=(sqt, P.tok("ones")), w=(b2t,))
        rs, rst = P.rot("hrstd", 3, [128, TB], F32, phase=False)
        P.act(rs[:, :], b2[:, :], AF.Sqrt, r=(b2t, P.tok("consts")), w=(rst,), scale=1.0 / HD, bias=P.epsc[:, 0:1])
        P.op("dve", lambda e, o=rs[:, :]: e.reciprocal(out=o, in_=o), r=(rst,), w=(rst,))
        qn, qnt = P.rot("qn", 4, [128, TB], BF16, phase=False)
        P.stt(qn[:, :], bk[:, :], gain1, rs[:, :], ALU.mult, ALU.mult, r=(bkt, rst, P.tok("hc")), w=(qnt,))
        st.update(qn=qn, qnt=qnt)

    def s2():
        qn, qnt = st["qn"], st["qnt"]
        b3, b3t = P.bank()
        P.mm(b3[:, :], P.rotT[:, :], qn[:, :], True, True, r=(qnt, P.tok("hc")), w=(b3t,))
        t1, t1t = P.rot("rp1", 3, [128, TB], F32, phase=False)
        P.tt("dve", t1[:, :], qn[:, :], P.cosF[:, cs], ALU.mult, r=(qnt, P.tok("rope")), w=(t1t,))
        t2, t2t = P.rot("rp2", 3, [128, TB], F32, phase=False)
        P.tt("dve", t2[:, :], b3[:, :], P.sinF[:, cs], ALU.mult, r=(b3t, P.tok("rope")), w=(t2t,))
        P.tt("pool", t1[:, :], t1[:, :], t2[:, :], ALU.add, r=(t1t, t2t), w=(t1t,))
        sink(t1, t1t, qn, qnt)
    return (s0, s1, s2)


def load_headconsts(P, hc_dram, rotT_dram, ncol):
    P.hc = P.sb("hc", [128, ncol], F32)
    P.rotT = P.sb("rotT", [128, 128], BF16)
    P.dma("sp", P.hc[:, :], hc_dram[:, :], r=(), w=(P.tok("hc"),))
    P.dma("sp", P.rotT[:, :], rotT_dram[:, :], r=(), w=(P.tok("hc"),))


V0A = Vecs(["mixer_norm"])


def build_L0a():
    P = Prog()
    xT = P.din("hT", [128, NCH, NT], F32)
    vecs = P.din("vecs", [128, V0A.n()], F32)
    hc_d = P.din("hc", [128, 4], F32)
    rotT_d = P.din("rotT", [128, 128], BF16)
    posb = P.din("posb", [128, NT], I32)
    wqkv = P.din("w_qkv", [D, 3 * D], F32)
    qT_o = P.dout("qT", [NH, 128, NT], BF16)
    kT_o = P.dout("kT", [NH, 128, NT], BF16)
    v_o = P.dout("v", [NT, D], BF16)
    km_o = P.dout("kmean", [128, NH, 4], BF16)

    def body():
        common_setup(P, V0A.n())
        P.dma("sp", P.vecs[:, :], vecs[:, :], r=(), w=(P.tok("vecs"),))
        load_headconsts(P, hc_d, rotT_d, 4)
        load_h(P, xT)
        emit_rope_tables(P, posb, P.hc)
        xn = P.sb("xn", [128, NCH, NT], BF16)
        km = P.sb("km", [128, NH, 4], BF16)
        xt = lambda c, tb: P.tok(("xn", c, tb))
        emit_norm(P, P.h, V0A.ap(P, "mixer_norm"), xn, xt)
        items = []
        wcur = {}
        for part in range(2):
            for j in range(8):
                for m in range(2):
                    hd = 2 * j + m
                    for tb in range(NTB):
                        cs = slice(tb * TB, (tb + 1) * TB)
                        ctx = {}

                        def make_proj(part=part, j=j, m=m, tb=tb, cs=cs):
                            if m == 0 and tb == 0:
                                wcur["w"] = P.wnext(wqkv, 0, 16, part * D + j * 256, 256)
                            wt, wtk = wcur["w"]
                            bk, bkt = P.bank()
                            for kc in range(NCH):
                                P.mm(bk[:, :], wt[:, kc, m * 128:(m + 1) * 128], xn[:, kc, cs], kc == 0, kc == NCH - 1, r=(wtk, xt(kc, tb)), w=(bkt,))
                            return bk, bkt

                        def sink(qf, qft, qn, qnt, part=part, hd=hd, tb=tb, cs=cs):
                            if tb == 0:
                                wcur[("stg", part, hd)] = P.rot("stage", 3, [128, NT], BF16, phase=False)
                            stg, stgt = wcur[("stg", part, hd)]
                            P.copy("act", stg[:, cs], qf[:, :], r=(qft, stgt), w=(stgt,))
                            if part == 1:
                                kr, krt = P.rot("kred", 2, [128, 2], F32, phase=False)
                                P.op("dve", lambda e, o=kr[:, :], i=qf[:, :].rearrange("p (b t) -> p b t", b=2): e.tensor_reduce(out=o, in_=i, axis=AX.X, op=ALU.add),
                                     r=(qft,), w=(krt,))
                                P.ts("dve", km[:, hd, tb * 2:(tb + 1) * 2], kr[:, :], 1.0 / 256.0, None, ALU.mult, None, r=(krt, P.tok("km")), w=(P.tok("km"),))
                            if tb == NTB - 1:
                                P.dma("sp", (qT_o if part == 0 else kT_o)[hd], stg[:, :], r=(stgt,), w=(), is_out=True)
                        items.append(headnorm_stages(P, make_proj, P.hc[:, part:part + 1], cs, True, sink))
        pipeline(items, (0, 1, 2))
        for j in range(8):
            wt, wtk = P.wnext(wqkv, 0, 16, 2 * D + j * 256, 256)
            vs, vst = P.rot("vstage", 2, [128, 8, 256], BF16, phase=False)
            for tt_ in range(NT // 128):
                bk, bkt = P.bank()
                for kc in range(NCH):
                    P.mm(bk[:, 0:256], xn[:, kc, tt_ * 128:(tt_ + 1) * 128], wt[:, kc, :], kc == 0, kc == NCH - 1,
                         r=(wtk, xt(kc, tt_ // 4)), w=(bkt,))
                P.copy("act", vs[:, tt_, :], bk[:, 0:256], r=(bkt, vst), w=(vst,))
            P.dma("sp", v_o[:, j * 256:(j + 1) * 256].rearrange("(t p) n -> p t n", p=128), vs[:, :, :], r=(vst,), w=(), is_out=True)
        P.dma("sp", km_o[:, :, :], km[:, :, :], r=(P.tok("km"),), w=(), is_out=True)
    return run_two_pass(P, body)


def head_consts(qg, kg):
    freq, rotT = rope_consts()
    hc = np.zeros((128, 4), np.float32)
    hc[:, 0] = qg
    hc[:, 1] = kg
    hc[:, 2] = freq[:, 0]
    hc[:, 3] = -PI
    return hc, rotT


def layer0a(inp, x=None):
    x = inp["x"] if x is None else x
    nc = build_L0a()
    vecs = V0A.host({"mixer_norm": inp["mixer_norm"][0]})
    hc, rotT = head_consts(inp["moba_q_gain"][0], inp["moba_k_gain"][0])
    in_maps = []
    for c in range(NCORES):
        b = c // 4
        tk = core_tokens(c)
        in_maps.append({"hT": to_fm(x[b][tk]), "vecs": vecs, "hc": hc, "rotT": rotT,
                        "posb": np.ascontiguousarray(np.broadcast_to(inp["positions"][b][tk].astype(np.int32)[None, :], (128, NT))),
                        "w_qkv": inp["moba_w_qkv"][0]})
    res = run_prog(nc, in_maps)
    bf = ml_dtypes.bfloat16
    q = np.zeros((B, S, NH, HD), bf)
    k = np.zeros((B, S, NH, HD), bf)
    v = np.zeros((B, S, D), bf)
    km = np.zeros((B, S // 256, NH, HD), bf)
    for c in range(NCORES):
        b = c // 4
        tk = core_tokens(c)
        q[b, tk] = res[c]["qT"].transpose(2, 0, 1)
        k[b, tk] = res[c]["kT"].transpose(2, 0, 1)
        v[b, tk] = res[c]["v"]
        kmc = res[c]["kmean"]
        for s_, ch in enumerate(core_chunks(c)):
            km[b, 2 * ch:2 * ch + 2] = kmc[:, :, 2 * s_:2 * s_ + 2].transpose(2, 1, 0)
    return {"q": q, "k": k, "v": v, "kmean": km}


def slot_lists(c):
    j = c % 4
    return [[None] * (3 - j) + list(range(0, j + 1)), [None] * j + list(range(0, 8 - j))]


def attn_consts():
    tri = np.zeros((128, 4, TB), np.float32)
    kk = np.arange(128)[:, None]
    qq = np.arange(TB)[None, :]
    for jj in range(4):
        tri[:, jj, :] = np.where(128 * jj + kk <= qq, 0.0, NEG)
    boh = np.zeros((16, 16, 128), np.float32)
    for b_ in range(16):
        boh[b_, b_, :] = 1.0
    ident = np.eye(128, dtype=np.float32)
    bf = ml_dtypes.bfloat16
    return tri.astype(bf), boh.astype(bf), ident.astype(bf), ident


def moba_masks(c):
    m1 = np.full((128, 2, 2, 16), NEG, np.float32)
    notown = np.ones((128, 2, 2, 16), np.float32)
    for s_, lst in enumerate(slot_lists(c)):
        nblk = 2 * len(lst)
        for half in range(2):
            own = nblk - 2 + half
            for blk in range(nblk):
                valid = lst[blk // 2] is not None
                m1[:, s_, half, blk] = 0.0 if (valid and blk < own) else NEG
            notown[:, s_, half, own] = 0.0
    return m1, notown


def emit_moba_attn(P, qT_d, kT_d, v_d, kmT_d, m1_d, notown_d, tri_d, boh_d, identb_d, identf_d, w_o):
    P.begin_phase()
    P.nrot = 6
    OT = P.sb("OT", [128, NCH, NT], BF16, phase=True)
    kmT = P.sb("kmT", [128, NH, 24], BF16, phase=True)
    m1 = P.sb("m1", [128, 2, 2, 16], F32, phase=True)
    notown = P.sb("notown", [128, 2, 2, 16], F32, phase=True)
    tri = P.sb("tri", [128, 4, TB], BF16, phase=True)
    boh = P.sb("boh", [16, 16, 128], BF16, phase=True)
    identb = P.sb("identb", [128, 128], BF16, phase=True)
    identf = P.sb("identf", [128, 128], F32, phase=True)
    ac = P.tok("ac")
    for dst, src, nd in ((kmT, kmT_d, 3), (m1, m1_d, 4), (notown, notown_d, 4), (tri, tri_d, 3), (boh, boh_d, 3), (identb, identb_d, 2), (identf, identf_d, 2)):
        P.dma("sp", dst[(slice(None),) * nd], src, r=(), w=(ac,))
    scale = HD ** -0.5
    units = [(h, s_) for h in range(NH) for s_ in range(2)]
    hbuf = {}
    ust = {}

    def geom(s_):
        nblk = 8 if s_ == 0 else 16
        return nblk, (0 if s_ == 0 else 8), 2 * nblk, (0 if s_ == 0 else 16)

    def pre0(u):
        h, s_ = units[u]
        if s_ == 0:
            kT, kTt = P.rot("kT", 2, [128, 48 * 128], BF16)
            vv, vvt = P.rot("vv", 2, [128, 48, 128], BF16)
            qh, qht = P.rot("qh", 2, [128, NT], BF16)
            P.dma("sp", qh[:, :], qT_d[h], r=(), w=(qht,))
            P.dma("sp", kT[:, :], kT_d[h], r=(), w=(kTt,))
            P.dma("sp", vv[:, :, :], v_d[h], r=(), w=(vvt,))
            hbuf[h] = (kT, kTt, vv, vvt, qh, qht)
        kT, kTt, vv, vvt, qh, qht = hbuf[h]
        nblk, boff, nkt, ktoff = geom(s_)
        gb, gbt = P.bank()
        for qt in range(4):
            P.mm(gb[:, qt * 16:qt * 16 + nblk], qh[:, s_ * TB + qt * 128:s_ * TB + (qt + 1) * 128], kmT[:, h, boff:boff + nblk],
                 True, True, r=(qht, ac), w=(gbt,))
        sbs = []
        for qt in range(4):
            gm, gmt = P.rot("gm", 4, [128, 16], F32)
            P.tt("dve", gm[:, 0:nblk], gb[:, qt * 16:qt * 16 + nblk], m1[:, s_, qt // 2, 0:nblk], ALU.add, r=(gbt, ac), w=(gmt,))
            mx, mxt = P.rot("mx", 4, [128, 8], F32)
            P.op("dve", lambda e, o=mx[:, :], i=gm[:, 0:nblk]: e.max(out=o, in_=i), r=(gmt,), w=(mxt,))
            sb1, sb1t = P.rot("sb1", 8, [128, 16], F32)
            P.ts("dve", sb1[:, 0:nblk], gm[:, 0:nblk], mx[:, 2:3], -NEG, ALU.is_ge, ALU.mult, r=(gmt, mxt), w=(sb1t,))
            P.stt(sb1[:, 0:nblk], sb1[:, 0:nblk], NEG, m1[:, s_, qt // 2, 0:nblk], ALU.add, ALU.add, r=(sb1t, ac), w=(sb1t,))
            P.tt("dve", sb1[:, 0:nblk], sb1[:, 0:nblk], notown[:, s_, qt // 2, 0:nblk], ALU.mult, r=(sb1t, ac), w=(sb1t,))
            sbs.append((sb1, sb1t))
        ust[u] = {"sbs": sbs}

    def pre1(u):
        h, s_ = units[u]
        nblk, boff, nkt, ktoff = geom(s_)
        tbk, tbkt = P.bank()
        for qt in range(4):
            sb1, sb1t = ust[u]["sbs"][qt]
            P.op("pe", lambda e, o=tbk[0:nblk, qt * 128:(qt + 1) * 128], i=sb1[:, 0:nblk], idn=identf[:, :]: e.transpose(out=o, in_=i, identity=idn),
                 r=(sb1t, ac), w=(tbkt,))
        selT, selTt = P.rot("selT", 2, [16, TB], BF16)
        P.copy("act", selT[0:nblk, :], tbk[0:nblk, :], r=(tbkt,), w=(selTt,))
        ust[u]["selT"] = (selT, selTt)

    def main(u):
        h, s_ = units[u]
        kT, kTt, vv, vvt, qh, qht = hbuf[h]
        nblk, boff, nkt, ktoff = geom(s_)
        selT, selTt = ust[u]["selT"]
        qs = slice(s_ * TB, (s_ + 1) * TB)
        ob, obt = P.accbank(0)
        db, dbt = P.accbank(1)
        items = []
        for kt in range(nkt):
            st = {}

            def sA(kt=kt, st=st):
                g = ktoff + kt
                if u + 1 < len(units):
                    if kt == 2:
                        pre0(u + 1)
                    if kt == nkt // 2 + 2:
                        pre1(u + 1)
                sbk, sbkt = P.bank()
                diag = kt >= nkt - 4
                P.mm(sbk[:, :], kT[:, g * 128:(g + 1) * 128], qh[:, qs], True, False, r=(kTt, qht), w=(sbkt,))
                if not DBG.get("nobias"):
                    P.mm(sbk[:, :], boh[0:nblk, kt // 2, :], selT[0:nblk, :], False, not diag, r=(ac, selTt), w=(sbkt,))
                else:
                    P.mm(sbk[:, 0:8], boh[0:nblk, kt // 2, :], selT[0:nblk, 0:8], False, not diag, r=(ac, selTt), w=(sbkt,))
                if diag:
                    P.mm(sbk[:, :], identb[:, :], tri[:, kt - (nkt - 4), :], False, True, r=(ac,), w=(sbkt,))
                st["s"] = (sbk, sbkt)

            def sB(kt=kt, st=st):
                g = ktoff + kt
                sbk, sbkt = st["s"]
                E, Et = P.rot("E", 4, [128, TB], BF16)
                if DBG.get("noexp"):
                    P.act(E[:, 0:8], sbk[:, 0:8], AF.Exp, r=(sbkt,), w=(Et,), scale=scale)
                else:
                    P.act(E[:, :], sbk[:, :], AF.Exp, r=(sbkt,), w=(Et,), scale=scale)
                P.mm(ob[:, :], vv[:, g, :], E[:, :], kt == 0, kt == nkt - 1, r=(vvt, Et), w=(obt,))
                P.mm(db[:, :], P.ones[:, :], E[:, :], kt == 0, kt == nkt - 1, r=(P.tok("ones"), Et), w=(dbt,))
            items.append((sA, sB))
        pipeline(items, (0, 2))
        rden, rdent = P.rot("rden", 2, [128, TB], F32)
        P.op("dve", lambda e, o=rden[:, :], i=db[:, :]: e.reciprocal(out=o, in_=i), r=(dbt,), w=(rdent,))
        P.tt("dve", OT[:, h, qs], ob[:, :], rden[:, :], ALU.mult, r=(obt, rdent), w=(P.tok(("OT", h, s_)),))
        del ust[u]

    pre0(0)
    pre1(0)
    for u in range(len(units)):
        main(u)
    emit_outproj(P, w_o, OT, lambda kc, tb: P.tok(("OT", kc, tb)))
    P.end_phase()


V0B = Vecs(["mlp_norm", "ple_norm"])


def build_L0b():
    P = Prog()
    xT = P.din("hT", [128, NCH, NT], F32)
    pT = P.din("pT", [128, 2, NT], F32)
    vecs = P.din("vecs", [128, V0B.n()], F32)
    qT_d = P.din("qT", [NH, 128, NT], BF16)
    kT_d = P.din("kT", [NH, 128, 48 * 128], BF16)
    v_d = P.din("v", [NH, 128, 48, 128], BF16)
    kmT_d = P.din("kmT", [128, NH, 24], BF16)
    m1_d = P.din("m1", [128, 2, 2, 16], F32)
    notown_d = P.din("notown", [128, 2, 2, 16], F32)
    tri_d = P.din("tri", [128, 4, TB], BF16)
    boh_d = P.din("boh", [16, 16, 128], BF16)
    identb_d = P.din("identb", [128, 128], BF16)
    identf_d = P.din("identf", [128, 128], F32)
    w_o = P.din("w_o", [D, D], F32)
    w1 = P.din("mlp_w1", [D, DFF], F32)
    w2 = P.din("mlp_w2", [DFF, D], F32)
    wg = P.din("ple_gate", [D, D], F32)
    wp = P.din("ple_proj", [256, D], F32)
    out = P.dout("hT_out", [128, NCH, NT], F32)

    def body():
        common_setup(P, V0B.n())
        P.dma("sp", P.vecs[:, :], vecs[:, :], r=(), w=(P.tok("vecs"),))
        load_h(P, xT)
        vec = lambda n: V0B.ap(P, n)
        if DBG["mix"]:
            emit_moba_attn(P, qT_d, kT_d, v_d, kmT_d, m1_d, notown_d, tri_d, boh_d, identb_d, identf_d, w_o)
        emit_mlp_ple(P, 0, w1, w2, wg, wp, pT, vec)
        store_h(P, out)
    return run_two_pass(P, body)


def list_tokens(lst):
    return np.concatenate([(np.arange(k * TB, (k + 1) * TB) if k is not None else np.full(TB, -1)) for k in lst])


def gather_rows(a, idx):
    out = a[np.maximum(idx, 0)]
    out[idx < 0] = 0
    return out


def layer0b(inp, x, qkv):
    nc = build_L0b()
    vecs = V0B.host({"mlp_norm": inp["mlp_norm"][0], "ple_norm": inp["ple_norm"][0]})
    tri, boh, identb, identf = attn_consts()
    in_maps = []
    for c in range(NCORES):
        b = c // 4
        tk = core_tokens(c)
        lists = slot_lists(c)
        idx = np.concatenate([list_tokens(l) for l in lists])
        kl = gather_rows(qkv["k"][b], idx)
        vl = gather_rows(qkv["v"][b], idx).reshape(48, 128, NH, HD)
        bidx = np.concatenate([np.repeat(np.array([(-1 if k is None else k) for k in l]), 2) * 2 + np.tile([0, 1], len(l)) for l in lists])
        bidx = np.where(bidx < 0, -1, bidx)
        kml = gather_rows(qkv["kmean"][b], bidx)
        m1, notown = moba_masks(c)
        in_maps.append({
            "hT": to_fm(x[b][tk]), "pT": to_fm(inp["p"][0, b][tk]), "vecs": vecs,
            "qT": np.ascontiguousarray(qkv["q"][b][tk].transpose(1, 2, 0)),
            "kT": np.ascontiguousarray(kl.transpose(1, 2, 0)),
            "v": np.ascontiguousarray(vl.transpose(2, 1, 0, 3)),
            "kmT": np.ascontiguousarray(kml.transpose(2, 1, 0)),
            "m1": m1, "notown": notown, "tri": tri, "boh": boh, "identb": identb, "identf": identf,
            "w_o": inp["moba_w_o"][0],
            "mlp_w1": inp["mlp_w1"][0], "mlp_w2": inp["mlp_w2"][0], "ple_gate": inp["ple_gate"][0], "ple_proj": inp["ple_proj"][0],
        })
    res = run_prog(nc, in_maps)
    h0 = np.zeros_like(x)
    for c in range(NCORES):
        h0[c // 4][core_tokens(c)] = from_fm(res[c]["hT_out"])
    return h0


V2A = Vecs(["mixer_norm"])
GELU_C = float(2.0 * np.sqrt(2.0 / np.pi))


def build_L2a():
    P = Prog()
    P.nwslot = 5
    P.wlive = 4
    hT = P.din("hT", [128, NCH, NT], F32)
    hhalo = P.din("hhalo", [128, NCH, 32], F32)
    vecs = P.din("vecs", [128, V2A.n()], F32)
    hc_d = P.din("hc", [128, 8], F32)
    rotT_d = P.din("rotT", [128, 128], BF16)
    identf_d = P.din("identf", [128, 128], F32)
    posb = P.din("posb", [128, NT], I32)
    posT_d = P.din("cmp_posT", [128, 2, 32], F32)
    wq = P.din("w_q", [D, D], F32)
    wkv = P.din("w_kv", [D, 3072], F32)
    wgate = P.din("w_gate", [D, 48], F32)
    cw1 = P.din("cmp_w1", [2 * 4096, 128], F32)
    cw2 = P.din("cmp_w2", [2 * 128, 128], F32)
    qc_o = P.dout("qcT", [NH, 128, NT], BF16)
    qr_o = P.dout("qrT", [NH, 128, NT], BF16)
    ks_o = P.dout("kslcT", [4, 128, NT], BF16)
    kw_o = P.dout("kwinT", [4, 128, NT], BF16)
    vs_o = P.dout("vslc", [NT, 512], BF16)
    vw_o = P.dout("vwin", [NT, 512], BF16)
    kc_o = P.dout("kcmpT", [4, 128, 64], BF16)
    vc_o = P.dout("vcmpT", [4, 128, 64], BF16)
    g_o = P.dout("gT", [48, NT], BF16)

    def body():
        common_setup(P, V2A.n())
        P.dma("sp", P.vecs[:, :], vecs[:, :], r=(), w=(P.tok("vecs"),))
        load_headconsts(P, hc_d, rotT_d, 8)
        identf = P.sb("identf", [128, 128], F32)
        posT = P.sb("posT", [128, 2, 32], BF16)
        P.dma("sp", identf[:, :], identf_d[:, :], r=(), w=(P.tok("hc"),))
        P.dma("pool", posT[:, :, :], posT_d[:, :, :], r=(), w=(P.tok("hc"),))
        load_h(P, hT)
        hx = P.sb("hx", [128, NCH, 32], F32)
        xh = P.sb("xh", [128, NCH, 32], BF16)
        P.dma("sp", hx[:, :, :], hhalo[:, :, :], r=(), w=(P.tok("hx"),))
        emit_rope_tables(P, posb, P.hc)
        xn = P.sb("xn", [128, NCH, NT], BF16)
        xt = lambda c, tb: P.tok(("xn", c, tb))
        xht = lambda c, tb: P.tok(("xh", c))
        gain = V2A.ap(P, "mixer_norm")
        emit_norm(P, hx, gain, xh, xht, ntb=1, src_tokf=lambda c, tb: P.tok("hx"), tbw=32)
        emit_norm(P, P.h, gain, xn, xt)
        hct = P.tok("hc")

        def proj(wt, wtk, m, tb):
            bk, bkt = P.bank()
            cs = slice(tb * TB, (tb + 1) * TB)
            for kc in range(NCH):
                P.mm(bk[:, :], wt[:, kc, m * 128:(m + 1) * 128], xn[:, kc, cs], kc == 0, kc == NCH - 1, r=(wtk, xt(kc, tb)), w=(bkt,))
            return bk, bkt

        items = []
        wcur = {}
        specs = [("q", wq, j * 256, 0, 2 * j + m, m, (qc_o, qr_o), j) for j in range(8) for m in range(2)]
        for idx, gcol, outd in ((2, 1, ks_o), (4, 5, kw_o)):
            specs += [("k", wkv, idx * 512 + j * 256, gcol, 2 * j + m, m, (outd,), (idx, j)) for j in range(2) for m in range(2)]
        for kind, wsrc, col0, gcol, hd, m, outs, wkey in specs:
            for tb in range(NTB):
                cs = slice(tb * TB, (tb + 1) * TB)

                def make_proj(kind=kind, wsrc=wsrc, col0=col0, m=m, tb=tb, cs=cs):
                    if m == 0 and tb == 0:
                        wcur["w"] = P.wnext(wsrc, 0, 16, col0, 256)
                    wt, wtk = wcur["w"]
                    bk, bkt = P.bank()
                    for kc in range(NCH):
                        P.mm(bk[:, :], wt[:, kc, m * 128:(m + 1) * 128], xn[:, kc, cs], kc == 0, kc == NCH - 1, r=(wtk, xt(kc, tb)), w=(bkt,))
                    return bk, bkt

                def sink(qf, qft, qn, qnt, kind=kind, hd=hd, tb=tb, cs=cs, outs=outs, wkey=wkey):
                    key = (kind, wkey, hd)
                    if tb == 0:
                        wcur[key] = (P.rot("stage", 2, [128, NT], BF16, phase=False), P.rot("stage2", 2, [128, NT], BF16, phase=False) if kind == "q" else None)
                    (sr, srt), sc_ = wcur[key]
                    P.copy("act", sr[:, cs], qf[:, :], r=(qft, srt), w=(srt,))
                    if kind == "q":
                        sc, sct = sc_
                        P.copy("pool", sc[:, cs], qn[:, :], r=(qnt, sct), w=(sct,))
                    if tb == NTB - 1:
                        if kind == "q":
                            P.dma("sp", outs[0][hd], sc_[0][:, :], r=(sc_[1],), w=(), is_out=True)
                            P.dma("sp", outs[1][hd], sr[:, :], r=(srt,), w=(), is_out=True)
                        else:
                            P.dma("sp", outs[0][hd], sr[:, :], r=(srt,), w=(), is_out=True)
                items.append(headnorm_stages(P, make_proj, P.hc[:, gcol:gcol + 1], cs, True, sink))
        pipeline(items, (0, 1, 2))
        for idx, outd in ((3, vs_o), (5, vw_o)):
            for j in range(2):
                wt, wtk = P.wnext(wkv, 0, 16, idx * 512 + j * 256, 256)
                vs, vst = P.rot("vstage", 2, [128, 8, 256], BF16, phase=False)
                for tt_ in range(NT // 128):
                    bk, bkt = P.bank()
                    for kc in range(NCH):
                        P.mm(bk[:, 0:256], xn[:, kc, tt_ * 128:(tt_ + 1) * 128], wt[:, kc, :], kc == 0, kc == NCH - 1,
                             r=(wtk, xt(kc, tt_ // 4)), w=(bkt,))
                    P.copy("act", vs[:, tt_, :], bk[:, 0:256], r=(bkt, vst), w=(vst,))
                P.dma("sp", outd[:, j * 256:(j + 1) * 256].rearrange("(t p) n -> p t n", p=128), vs[:, :, :], r=(vst,), w=(), is_out=True)
        wt, wtk = P.wnext(wgate, 0, 16, 0, 48)
        gts = P.sb("gts", [48, NT], BF16)
        for tt_ in range(NT // 128):
            bk, bkt = P.bank()
            for kc in range(NCH):
                P.mm(bk[:, 0:48], xn[:, kc, tt_ * 128:(tt_ + 1) * 128], wt[:, kc, :], kc == 0, kc == NCH - 1, r=(wtk, xt(kc, tt_ // 4)), w=(bkt,))
            gs, gst = P.rot("gsig", 2, [128, 48], F32, phase=False)
            P.act(gs[:, :], bk[:, 0:48], AF.Sigmoid, r=(bkt,), w=(gst,))
            b2, b2t = P.bank()
            P.op("pe", lambda e, o=b2[0:48, 0:128], i=gs[:, :], idn=identf[:, :]: e.transpose(out=o, in_=i, identity=idn), r=(gst, hct), w=(b2t,))
            P.copy("act", gts[:, tt_ * 128:(tt_ + 1) * 128], b2[0:48, 0:128], r=(b2t, P.tok("gts")), w=(P.tok("gts"),))
        P.dma("sp", g_o[:, :], gts[:, :], r=(P.tok("gts"),), w=(), is_out=True)
        kcs = P.sb("kcs", [128, 4, 64], BF16)
        vcs = P.sb("vcs", [128, 4, 64], BF16)
        for idx in range(2):
            w1t, w1k = P.wnext(cw1, idx * 4096, 32, 0, 128)
            w2t, w2k = P.wnext(cw2, idx * 128, 1, 0, 128)
            pbk, pbt = P.bank()
            for l in range(32):
                P.mm(pbk[:, 0:1], w1t[:, l, :], posT[:, idx, l:l + 1], l == 0, l == 31, r=(w1k, hct), w=(pbt,))
            pb, pbst = P.rot("posb", 2, [128, 1], F32, phase=False)
            P.copy("act", pb[:, :], pbk[:, 0:1], r=(pbt,), w=(pbst,))
            for j in range(2):
                wt, wtk = P.wnext(wkv, 0, 16, idx * 512 + j * 256, 256)
                for m in range(2):
                    g = 2 * j + m
                    for s_ in range(NTB):
                        raw, rawt = P.rot("craw", 2, [128, 16 + TB], BF16, phase=False)
                        bk, bkt = proj(wt, wtk, m, s_)
                        P.copy("act", raw[:, 16:16 + TB], bk[:, :], r=(bkt, rawt), w=(rawt,))
                        bh, bht = P.bank()
                        for kc in range(NCH):
                            P.mm(bh[:, 0:16], wt[:, kc, m * 128:(m + 1) * 128], xh[:, kc, s_ * 16:(s_ + 1) * 16], kc == 0, kc == NCH - 1,
                                 r=(wtk, xht(kc, 0)), w=(bht,))
                        P.copy("act", raw[:, 0:16], bh[:, 0:16], r=(bht, rawt), w=(rawt,))
                        cb, cbt = P.bank()
                        for l in range(32):
                            P.mm(cb[:, 0:32], w1t[:, l, :], raw[:, l:l + 16 * 31 + 1:16], l == 0, l == 31, r=(w1k, rawt), w=(cbt,))
                        x, xt_ = P.rot("cx", 2, [128, 32], F32, phase=False)
                        P.ts("dve", x[:, :], cb[:, 0:32], pb[:, 0:1], None, ALU.add, None, r=(cbt, pbst), w=(xt_,))
                        u, ut = P.rot("cu", 2, [128, 32], F32, phase=False)
                        P.tt("dve", u[:, :], x[:, :], x[:, :], ALU.mult, r=(xt_,), w=(ut,))
                        P.ts("dve", u[:, :], u[:, :], 0.044715, 1.0, ALU.mult, ALU.add, r=(ut,), w=(ut,))
                        P.tt("dve", u[:, :], u[:, :], x[:, :], ALU.mult, r=(ut, xt_), w=(ut,))
                        P.act(u[:, :], u[:, :], AF.Sigmoid, r=(ut,), w=(ut,), scale=GELU_C)
                        ge, get = P.rot("cge", 2, [128, 32], BF16, phase=False)
                        P.tt("dve", ge[:, :], u[:, :], x[:, :], ALU.mult, r=(ut, xt_), w=(get,))
                        ob, obt = P.bank()
                        P.mm(ob[:, 0:32], w2t[:, 0, :], ge[:, :], True, True, r=(w2k, get), w=(obt,))
                        if idx == 1:
                            P.copy("act", vcs[:, g, s_ * 32:(s_ + 1) * 32], ob[:, 0:32], r=(obt, P.tok("vcs")), w=(P.tok("vcs"),))
                        else:
                            sq, sqt = P.rot("csq", 2, [128, 32], BF16, phase=False)
                            P.act(sq[:, :], ob[:, 0:32], AF.Square, r=(obt,), w=(sqt,))
                            b2, b2t = P.bank()
                            P.mm(b2[:, 0:32], P.ones[:, :], sq[:, :], True, True, r=(sqt, P.tok("ones")), w=(b2t,))
                            rs, rst = P.rot("crs", 2, [128, 32], F32, phase=False)
                            P.act(rs[:, :], b2[:, 0:32], AF.Sqrt, r=(b2t, P.tok("consts")), w=(rst,), scale=1.0 / HD, bias=P.epsc[:, 0:1])
                            P.op("dve", lambda e, o=rs[:, :]: e.reciprocal(out=o, in_=o), r=(rst,), w=(rst,))
                            P.stt(kcs[:, g, s_ * 32:(s_ + 1) * 32], ob[:, 0:32], P.hc[:, 4:5], rs[:, :], ALU.mult, ALU.mult,
                                  r=(obt, rst, hct, P.tok("kcs")), w=(P.tok("kcs"),))
        for g in range(4):
            P.dma("sp", kc_o[g], kcs[:, g, :], r=(P.tok("kcs"),), w=(), is_out=True)
            P.dma("sp", vc_o[g], vcs[:, g, :], r=(P.tok("vcs"),), w=(), is_out=True)
    return run_two_pass(P, body)


def layer2a(inp, h1):
    nc = build_L2a()
    vecs = V2A.host({"mixer_norm": inp["mixer_norm"][2]})
    freq, rotT = rope_consts()
    hc = np.zeros((128, 8), np.float32)
    hc[:, 0] = inp["nsa_q_gain"][0]
    hc[:, 1] = inp["nsa_k_gain"][0, 1]
    hc[:, 2] = freq[:, 0]
    hc[:, 3] = -PI
    hc[:, 4] = inp["nsa_k_gain"][0, 0]
    hc[:, 5] = inp["nsa_k_gain"][0, 2]
    identf = np.eye(128, dtype=np.float32)
    posT = np.ascontiguousarray(inp["nsa_cmp_pos"][0].transpose(2, 0, 1))
    in_maps = []
    for c in range(NCORES):
        b = c // 4
        tk = core_tokens(c)
        in_maps.append({"hT": to_fm(h1[b][tk]), "hhalo": to_fm(halo_rows(h1[b], c, 16)), "vecs": vecs, "hc": hc, "rotT": rotT, "identf": identf,
                        "posb": np.ascontiguousarray(np.broadcast_to(inp["positions"][b][tk].astype(np.int32)[None, :], (128, NT))),
                        "cmp_posT": posT, "w_q": inp["nsa_w_q"][0], "w_kv": inp["nsa_w_kv"][0], "w_gate": inp["nsa_w_gate"][0],
                        "cmp_w1": np.ascontiguousarray(inp["nsa_cmp_w1"][0].reshape(2 * 4096, 128)),
                        "cmp_w2": np.ascontiguousarray(inp["nsa_cmp_w2"][0].reshape(2 * 128, 128))})
    res = run_prog(nc, in_maps)
    bf = ml_dtypes.bfloat16
    o = {"qc": np.zeros((B, S, NH, HD), bf), "qr": np.zeros((B, S, NH, HD), bf),
         "kslc": np.zeros((B, S, 4, HD), bf), "kwin": np.zeros((B, S, 4, HD), bf),
         "vslc": np.zeros((B, S, 512), bf), "vwin": np.zeros((B, S, 512), bf),
         "kcmp": np.zeros((B, 8, 32, 4, HD), bf), "vcmp": np.zeros((B, 8, 32, 4, HD), bf),
         "gT": np.zeros((B, S, 48), bf)}
    for c in range(NCORES):
        b = c // 4
        tk = core_tokens(c)
        r = res[c]
        o["qc"][b, tk] = r["qcT"].transpose(2, 0, 1)
        o["qr"][b, tk] = r["qrT"].transpose(2, 0, 1)
        o["kslc"][b, tk] = r["kslcT"].transpose(2, 0, 1)
        o["kwin"][b, tk] = r["kwinT"].transpose(2, 0, 1)
        o["vslc"][b, tk] = r["vslc"]
        o["vwin"][b, tk] = r["vwin"]
        o["gT"][b, tk] = r["gT"].T
        for s_, ch in enumerate(core_chunks(c)):
            o["kcmp"][b, ch] = r["kcmpT"][:, :, s_ * 32:(s_ + 1) * 32].transpose(2, 0, 1)
            o["vcmp"][b, ch] = r["vcmpT"][:, :, s_ * 32:(s_ + 1) * 32].transpose(2, 0, 1)
    return o


BIG = 1.0e30


def nsa_consts():
    bf = ml_dtypes.bfloat16
    kk = np.arange(128)[:, None]
    qq = np.arange(TB)[None, :]
    band = np.zeros((128, 4, TB), np.float32)
    for jj in range(4):
        band[:, jj, :] = np.where(128 * jj + kk > qq, 0.0, NEG)
    cmpdiag = np.zeros((128, TB), np.float32)
    for i in range(32):
        cmpdiag[96 + i, :] = np.where(16 * i + 15 <= np.arange(TB), 0.0, NEG)
    boh2 = np.zeros((64, 32, 128), np.float32)
    for kt in range(32):
        boh2[2 * kt, kt, 0:64] = 1.0
        boh2[2 * kt + 1, kt, 64:128] = 1.0
    sel48 = np.zeros((48, 48, 128), np.float32)
    for i in range(48):
        sel48[i, i, :] = 1.0
    ovl = np.zeros((128, 3, 64), np.float32)
    for s_, nch in enumerate((4, 8)):
        for e in range(32 * nch):
            ci, i = divmod(e, 32)
            n0, n1 = 512 * ci - 16 + 16 * i, 512 * ci + 16 + 16 * i
            for jl in range(8 * nch):
                j0, j1 = 64 * jl, 64 * jl + 64
                if n0 < j1 and j0 < n1:
                    tile = 0 if s_ == 0 else 1 + e // 128
                    ovl[e % 128, tile, jl] = 1.0
    return band.astype(bf), cmpdiag.astype(bf), boh2.astype(bf), sel48.astype(bf), ovl.astype(bf)


def nsa_core_consts(c):
    lists = slot_lists(c)
    cmppad = np.zeros((128, 3), np.float32)
    slcm = np.zeros((128, 2, 4, 3, 64), np.float32)
    winpad = np.zeros((128, 2), np.float32)
    for s_, lst in enumerate(lists):
        for e in range(32 * len(lst)):
            ci, i = divmod(e, 32)
            pad = lst[ci] is None or (lst[ci] == 0 and i == 0)
            tile = 0 if s_ == 0 else 1 + e // 128
            cmppad[e % 128, tile] = NEG if pad else 0.0
        nsel = 8 * len(lst)
        npad = sum(1 for k in lst if k is None)
        first = 8 * npad
        for qt in range(4):
            for p in range(128):
                cur = nsel - 8 + (qt * 128 + p) // 64
                for jl in range(64):
                    if jl >= nsel or lst[jl // 8] is None or jl > cur:
                        slcm[p, s_, qt, 0, jl], slcm[p, s_, qt, 1, jl], slcm[p, s_, qt, 2, jl] = 0.0, -BIG, NEG
                    elif jl == cur or jl == first:
                        slcm[p, s_, qt, 0, jl], slcm[p, s_, qt, 1, jl] = 0.0, BIG
                    else:
                        slcm[p, s_, qt, 0, jl] = 1.0
        winpad[:, s_] = NEG if core_chunks(c)[s_] == 0 else 0.0
    return cmppad, slcm, winpad


def emit_nsa_attn(P, dd, OT):
    P.begin_phase()
    P.nrot = 4
    accn = [0]

    def accpair():
        i = accn[0] % 2
        accn[0] += 1
        return P.accbank(2 * i), P.accbank(2 * i + 1)
    ph = dict(phase=True)
    tri = P.sb("tri", [128, 4, TB], BF16, **ph)
    band = P.sb("band", [128, 4, TB], BF16, **ph)
    cmpdiag = P.sb("cmpdiag", [128, TB], BF16, **ph)
    boh2 = P.sb("boh2", [64, 32, 128], BF16, **ph)
    sel48 = P.sb("sel48", [48, 48, 128], BF16, **ph)
    ovl = P.sb("ovl", [128, 3, 64], BF16, **ph)
    identb = P.sb("identb", [128, 128], BF16, **ph)
    identf = P.sb("identf", [128, 128], F32, **ph)
    cmppad = P.sb("cmppad", [128, 3], F32, **ph)
    slcm = P.sb("slcm", [128, 2, 4, 3, 64], F32, **ph)
    winpad = P.sb("winpad", [128, 2], F32, **ph)
    gT = P.sb("gTs", [48, NT], BF16, **ph)
    ac = P.tok("ac")
    for dst, name, nd in ((tri, "tri", 3), (band, "band", 3), (cmpdiag, "cmpdiag", 2), (boh2, "boh2", 3), (sel48, "sel48", 3), (ovl, "ovl", 3),
                          (identb, "identb", 2), (identf, "identf", 2), (cmppad, "cmppad", 2), (slcm, "slcm", 5), (winpad, "winpad", 2), (gT, "gT", 2)):
        P.dma("sp", dst[(slice(None),) * nd], dd[name], r=(), w=(ac,))
    scale = HD ** -0.5
    onest = P.tok("ones")
    oacc = [P.sb("oacc%d" % r, [128, TB], F32, **ph) for r in range(4)]
    psum_ = [P.sb("psumT%d" % t, [128, TB], F32, **ph) for t in range(2)]
    pb = [P.sb("pb%d" % t, [128, TB], BF16, **ph) for t in range(2)]

    def finish_branch(h, br, r, qs, ob, obt, db, dbt, first, guard):
        rden, rdent = P.rot("rden", 2, [128, TB], F32)
        if guard:
            P.ts("dve", rden[:, :], db[:, :], 1e-30, None, ALU.max, None, r=(dbt,), w=(rdent,))
            P.op("dve", lambda e, o=rden[:, :]: e.reciprocal(out=o, in_=o), r=(rdent,), w=(rdent,))
        else:
            P.op("dve", lambda e, o=rden[:, :], i=db[:, :]: e.reciprocal(out=o, in_=i), r=(dbt,), w=(rdent,))
        gb, gbt = P.bank()
        P.mm(gb[:, :], sel48[0:48, h * 3 + br, :], gT[0:48, qs], True, True, r=(ac,), w=(gbt,))
        cf, cft = P.rot("coef", 2, [128, TB], F32)
        P.tt("dve", cf[:, :], gb[:, :], rden[:, :], ALU.mult, r=(gbt, rdent), w=(cft,))
        oat = P.tok(("oacc", r))
        if first:
            P.tt("dve", oacc[r][:, :], ob[:, :], cf[:, :], ALU.mult, r=(obt, cft), w=(oat,))
        else:
            tm, tmt = P.rot("otmp", 2, [128, TB], F32)
            P.tt("dve", tm[:, :], ob[:, :], cf[:, :], ALU.mult, r=(obt, cft), w=(tmt,))
            P.tt("pool", oacc[r][:, :], oacc[r][:, :], tm[:, :], ALU.add, r=(tmt, oat), w=(oat,))
        return rden, rdent

    for g in range(4):
        ks, kst = P.rot("ks", 1, [128, 48 * 128], BF16)
        vs, vst = P.rot("vs", 1, [128, 48, 128], BF16)
        kw, kwt = P.rot("kw", 1, [128, 16 * 128], BF16)
        vw, vwt = P.rot("vw", 1, [128, 16, 128], BF16)
        kc, kct = P.rot("kc", 1, [128, 384], BF16)
        vc, vct = P.rot("vc", 1, [128, 3, 128], BF16)
        P.dma("sp", kc[:, :], dd["kcmpT"][g], r=(), w=(kct,))
        P.dma("sp", vc[:, :, :], dd["vcmp"][g], r=(), w=(vct,))
        P.dma("sp", ks[:, :], dd["kslcT"][g], r=(), w=(kst,))
        P.dma("sp", vs[:, :, :], dd["vslc"][g], r=(), w=(vst,))
        P.dma("sp", kw[:, :], dd["kwinT"][g], r=(), w=(kwt,))
        P.dma("sp", vw[:, :, :], dd["vwin"][g], r=(), w=(vwt,))
        qcs, qrs = [], []
        for r in range(4):
            h = 4 * g + r
            qc, qct = P.rot("qc", 4, [128, NT], BF16)
            qr, qrt = P.rot("qr", 4, [128, NT], BF16)
            P.dma("sp", qc[:, :], dd["qcT"][h], r=(), w=(qct,))
            P.dma("sp", qr[:, :], dd["qrT"][h], r=(), w=(qrt,))
            qcs.append((qc, qct))
            qrs.append((qr, qrt))
        for s_ in range(2):
            qs = slice(s_ * TB, (s_ + 1) * TB)
            nch = 4 if s_ == 0 else 8
            nsel = 8 * nch
            ctiles = [0] if s_ == 0 else [1, 2]
            for r in range(4):
                h = 4 * g + r
                qc, qct = qcs[r]
                (ob, obt), (db, dbt) = accpair()
                Es = []
                for ti, ct in enumerate(ctiles):
                    sbk, sbkt = P.bank()
                    last = ti == len(ctiles) - 1
                    P.mm(sbk[:, :], kc[:, ct * 128:(ct + 1) * 128], qc[:, qs], True, not last, r=(kct, qct), w=(sbkt,))
                    if last:
                        P.mm(sbk[:, :], identb[:, :], cmpdiag[:, :], False, True, r=(ac,), w=(sbkt,))
                    E, Et = P.rot("Ec", 2, [128, TB], BF16)
                    P.act(E[:, :], sbk[:, :], AF.Exp, r=(sbkt, ac), w=(Et,), scale=scale, bias=cmppad[:, ct:ct + 1])
                    P.mm(ob[:, :], vc[:, ct, :], E[:, :], ti == 0, last, r=(vct, Et), w=(obt,))
                    P.mm(db[:, :], P.ones[:, :], E[:, :], ti == 0, last, r=(onest, Et), w=(dbt,))
                    Es.append((E, Et))
                rden, rdent = finish_branch(h, 0, r, qs, ob, obt, db, dbt, True, True)
                for ti, (E, Et) in enumerate(Es):
                    pst = P.tok(("psumT", ti))
                    if r == 0:
                        P.tt("dve", psum_[ti][:, :], E[:, :], rden[:, :], ALU.mult, r=(Et, rdent), w=(pst,))
                    else:
                        tm, tmt = P.rot("otmp", 2, [128, TB], F32)
                        P.tt("dve", tm[:, :], E[:, :], rden[:, :], ALU.mult, r=(Et, rdent), w=(tmt,))
                        P.tt("pool", psum_[ti][:, :], psum_[ti][:, :], tm[:, :], ALU.add, r=(tmt, pst), w=(pst,))
            selst = {}

            def selA():
                for ti in range(len(ctiles)):
                    P.copy("act", pb[ti][:, :], psum_[ti][:, :], r=(P.tok(("psumT", ti)),), w=(P.tok(("pb", ti)),))
                sbs = []
                for qt in range(4):
                    ib, ibt = P.bank()
                    for ti, ct in enumerate(ctiles):
                        P.mm(ib[:, 0:nsel], pb[ti][:, qt * 128:(qt + 1) * 128], ovl[:, ct, 0:nsel], ti == 0, ti == len(ctiles) - 1,
                             r=(P.tok(("pb", ti)), ac), w=(ibt,))
                    im, imt = P.rot("impm", 4, [128, 64], F32)
                    P.tt("dve", im[:, 0:nsel], ib[:, 0:nsel], slcm[:, s_, qt, 0, 0:nsel], ALU.mult, r=(ibt, ac), w=(imt,))
                    P.tt("dve", im[:, 0:nsel], im[:, 0:nsel], slcm[:, s_, qt, 1, 0:nsel], ALU.add, r=(imt, ac), w=(imt,))
                    mx, mxt = P.rot("mx", 4, [128, 8], F32)
                    P.op("dve", lambda e, o=mx[:, :], i=im[:, 0:nsel]: e.max(out=o, in_=i), r=(imt,), w=(mxt,))
                    rp, rpt = P.rot("rep", 4, [128, 64], F32)
                    P.op("dve", lambda e, o=rp[:, 0:nsel], a=mx[:, :], i=im[:, 0:nsel]: e.match_replace(out=o, in_to_replace=a, in_values=i, imm_value=-2.0 * BIG),
                         r=(imt, mxt), w=(rpt,))
                    mx2, mx2t = P.rot("mx2", 4, [128, 8], F32)
                    P.op("dve", lambda e, o=mx2[:, :], i=rp[:, 0:nsel]: e.max(out=o, in_=i), r=(rpt,), w=(mx2t,))
                    sb1, sb1t = P.rot("sb1", 8, [128, 64], F32)
                    P.ts("dve", sb1[:, 0:nsel], im[:, 0:nsel], mx2[:, 7:8], -NEG, ALU.is_ge, ALU.mult, r=(imt, mx2t), w=(sb1t,))
                    P.stt(sb1[:, 0:nsel], sb1[:, 0:nsel], NEG, slcm[:, s_, qt, 2, 0:nsel], ALU.add, ALU.add, r=(sb1t, ac), w=(sb1t,))
                    sbs.append((sb1, sb1t))
                selst["sbs"] = sbs

            def selB():
                tbk, tbkt = P.bank()
                for qt in range(4):
                    sb1, sb1t = selst["sbs"][qt]
                    P.op("pe", lambda e, o=tbk[0:nsel, qt * 128:(qt + 1) * 128], i=sb1[:, 0:nsel], idn=identf[:, :]: e.transpose(out=o, in_=i, identity=idn),
                         r=(sb1t, ac), w=(tbkt,))
                selT, selTt = P.rot("selT", 2, [64, TB], BF16)
                P.copy("act", selT[0:nsel, :], tbk[0:nsel, :], r=(tbkt,), w=(selTt,))
                selst["selT"] = (selT, selTt)
            nkt = 4 * nch
            ktoff = 0 if s_ == 0 else 16
            def slc_branch(r):
                selT, selTt = selst["selT"]
                h = 4 * g + r
                qr, qrt = qrs[r]
                (ob, obt), (db, dbt) = accpair()
                items = []
                for kt in range(nkt):
                    st = {}

                    def sA(kt=kt, st=st, qr=qr, qrt=qrt):
                        gk = ktoff + kt
                        sbk, sbkt = P.bank()
                        diag = kt >= nkt - 4
                        P.mm(sbk[:, :], ks[:, gk * 128:(gk + 1) * 128], qr[:, qs], True, False, r=(kst, qrt), w=(sbkt,))
                        P.mm(sbk[:, :], boh2[0:nsel, kt, :], selT[0:nsel, :], False, not diag, r=(ac, selTt), w=(sbkt,))
                        if diag:
                            P.mm(sbk[:, :], identb[:, :], tri[:, kt - (nkt - 4), :], False, True, r=(ac,), w=(sbkt,))
                        st["s"] = (sbk, sbkt)

                    def sB(kt=kt, st=st, ob=ob, obt=obt, db=db, dbt=dbt):
                        gk = ktoff + kt
                        sbk, sbkt = st["s"]
                        E, Et = P.rot("E", 4, [128, TB], BF16)
                        P.act(E[:, :], sbk[:, :], AF.Exp, r=(sbkt,), w=(Et,), scale=scale)
                        P.mm(ob[:, :], vs[:, gk, :], E[:, :], kt == 0, kt == nkt - 1, r=(vst, Et), w=(obt,))
                        P.mm(db[:, :], P.ones[:, :], E[:, :], kt == 0, kt == nkt - 1, r=(onest, Et), w=(dbt,))
                    items.append((sA, sB))
                pipeline(items, (0, 2))
                finish_branch(h, 1, r, qs, ob, obt, db, dbt, False, False)
                P.copy("act", OT[:, h, qs], oacc[r][:, :], r=(P.tok(("oacc", r)),), w=(P.tok(("OT", h, s_)),))
            def win_branch(r):
                h = 4 * g + r
                qr, qrt = qrs[r]
                (ob, obt), (db, dbt) = accpair()
                items = []
                for kt in range(8):
                    st = {}

                    def sA(kt=kt, st=st, qr=qr, qrt=qrt):
                        gk = s_ * 8 + kt
                        sbk, sbkt = P.bank()
                        P.mm(sbk[:, :], kw[:, gk * 128:(gk + 1) * 128], qr[:, qs], True, False, r=(kwt, qrt), w=(sbkt,))
                        msk = band[:, kt, :] if kt < 4 else tri[:, kt - 4, :]
                        P.mm(sbk[:, :], identb[:, :], msk, False, True, r=(ac,), w=(sbkt,))
                        st["s"] = (sbk, sbkt)

                    def sB(kt=kt, st=st, ob=ob, obt=obt, db=db, dbt=dbt):
                        gk = s_ * 8 + kt
                        sbk, sbkt = st["s"]
                        E, Et = P.rot("E", 4, [128, TB], BF16)
                        if kt < 4:
                            P.act(E[:, :], sbk[:, :], AF.Exp, r=(sbkt, ac), w=(Et,), scale=scale, bias=winpad[:, s_:s_ + 1])
                        else:
                            P.act(E[:, :], sbk[:, :], AF.Exp, r=(sbkt,), w=(Et,), scale=scale)
                        P.mm(ob[:, :], vw[:, gk, :], E[:, :], kt == 0, kt == 7, r=(vwt, Et), w=(obt,))
                        P.mm(db[:, :], P.ones[:, :], E[:, :], kt == 0, kt == 7, r=(onest, Et), w=(dbt,))
                    items.append((sA, sB))
                pipeline(items, (0, 2))
                finish_branch(h, 2, r, qs, ob, obt, db, dbt, False, False)
            selA()
            win_branch(0)
            win_branch(1)
            selB()
            win_branch(2)
            win_branch(3)
            for r in range(4):
                slc_branch(r)
    P.end_phase()


V2B = Vecs(["mlp_norm", "ple_norm"])
L2B_IN = [("qcT", [NH, 128, NT], BF16), ("qrT", [NH, 128, NT], BF16), ("gT", [48, NT], BF16),
          ("kslcT", [4, 128, 48 * 128], BF16), ("vslc", [4, 128, 48, 128], BF16), ("kwinT", [4, 128, 16 * 128], BF16), ("vwin", [4, 128, 16, 128], BF16),
          ("kcmpT", [4, 128, 384], BF16), ("vcmp", [4, 128, 3, 128], BF16),
          ("tri", [128, 4, TB], BF16), ("band", [128, 4, TB], BF16), ("cmpdiag", [128, TB], BF16), ("boh2", [64, 32, 128], BF16),
          ("sel48", [48, 48, 128], BF16), ("ovl", [128, 3, 64], BF16), ("identb", [128, 128], BF16), ("identf", [128, 128], F32),
          ("cmppad", [128, 3], F32), ("slcm", [128, 2, 4, 3, 64], F32), ("winpad", [128, 2], F32)]


def build_L2b():
    P = Prog()
    hT = P.din("hT", [128, NCH, NT], F32)
    pT = P.din("pT", [128, 2, NT], F32)
    vecs = P.din("vecs", [128, V2B.n()], F32)
    dd = {name: P.din(name, shape, dt) for name, shape, dt in L2B_IN}
    w_o = P.din("w_o", [D, D], F32)
    w1 = P.din("mlp_w1", [D, DFF], F32)
    w2 = P.din("mlp_w2", [DFF, D], F32)
    wg = P.din("ple_gate", [D, D], F32)
    wp = P.din("ple_proj", [256, D], F32)
    out = P.dout("hT_out", [128, NCH, NT], F32)

    def body():
        common_setup(P, V2B.n(), alloc_h=False)
        P.dma("sp", P.vecs[:, :], vecs[:, :], r=(), w=(P.tok("vecs"),))
        OT = P.sb("OT", [128, NCH, NT], BF16)
        vec = lambda n: V2B.ap(P, n)
        if DBG["mix"]:
            emit_nsa_attn(P, dd, OT)
        P.h = P.sb("hT", [128, NCH, NT], F32)
        load_h(P, hT)
        if DBG["mix"]:
            P.begin_phase()
            emit_outproj(P, w_o, OT, lambda kc, tb: P.tok(("OT", kc, tb)))
            P.end_phase()
        emit_mlp_ple(P, 2, w1, w2, wg, wp, pT, vec)
        store_h(P, out)
    return run_two_pass(P, body)


def layer2b(inp, h1, a):
    nc = build_L2b()
    vecs = V2B.host({"mlp_norm": inp["mlp_norm"][2], "ple_norm": inp["ple_norm"][2]})
    tri, boh, identb, identf = attn_consts()
    band, cmpdiag, boh2, sel48, ovl = nsa_consts()
    in_maps = []
    for c in range(NCORES):
        b = c // 4
        tk = core_tokens(c)
        lists = slot_lists(c)
        idx = np.concatenate([list_tokens(l) for l in lists])
        ksl = gather_rows(a["kslc"][b], idx)
        vsl = gather_rows(a["vslc"][b], idx).reshape(48, 128, 4, HD)
        widx = np.concatenate([list_tokens([(k - 1) if k > 0 else None, k]) for k in core_chunks(c)])
        kwl = gather_rows(a["kwin"][b], widx)
        vwl = gather_rows(a["vwin"][b], widx).reshape(16, 128, 4, HD)
        cidx = np.concatenate([np.array([(-1 if k is None else k)]) for l in lists for k in l])
        kcl = gather_rows(a["kcmp"][b], cidx).reshape(384, 4, HD)
        vcl = gather_rows(a["vcmp"][b], cidx).reshape(3, 128, 4, HD)
        cmppad, slcm, winpad = nsa_core_consts(c)
        m = {
            "hT": to_fm(h1[b][tk]), "pT": to_fm(inp["p"][2, b][tk]), "vecs": vecs,
            "qcT": np.ascontiguousarray(a["qc"][b][tk].transpose(1, 2, 0)), "qrT": np.ascontiguousarray(a["qr"][b][tk].transpose(1, 2, 0)),
            "gT": np.ascontiguousarray(a["gT"][b][tk].T),
            "kslcT": np.ascontiguousarray(ksl.transpose(1, 2, 0)), "vslc": np.ascontiguousarray(vsl.transpose(2, 1, 0, 3)),
            "kwinT": np.ascontiguousarray(kwl.transpose(1, 2, 0)), "vwin": np.ascontiguousarray(vwl.transpose(2, 1, 0, 3)),
            "kcmpT": np.ascontiguousarray(kcl.transpose(1, 2, 0)), "vcmp": np.ascontiguousarray(vcl.transpose(2, 1, 0, 3)),
            "tri": tri, "band": band, "cmpdiag": cmpdiag, "boh2": boh2, "sel48": sel48, "ovl": ovl, "identb": identb, "identf": identf,
            "cmppad": cmppad, "slcm": slcm, "winpad": winpad,
            "w_o": inp["nsa_w_o"][0],
            "mlp_w1": inp["mlp_w1"][2], "mlp_w2": inp["mlp_w2"][2], "ple_gate": inp["ple_gate"][2], "ple_proj": inp["ple_proj"][2],
        }
        in_maps.append(m)
    res = run_prog(nc, in_maps)
    h2 = np.zeros_like(h1)
    for c in range(NCORES):
        h2[c // 4][core_tokens(c)] = from_fm(res[c]["hT_out"])
    return h2


def kernel(**inputs):
    inp = {k: np.asarray(v) for k, v in inputs.items()}
    x = np.ascontiguousarray(inp["x"], dtype=np.float32)
    qkv = layer0a(inp, x)
    h0 = layer0b(inp, x, qkv)
    h1 = layer1(inp, h0)
    a = layer2a(inp, h1)
    h2 = layer2b(inp, h1, a)
    h3 = layer3(inp, h2)
    return h3.astype(np.float32)
```

```python
import contextlib
import numpy as np
import ml_dtypes
import concourse.bass as bass
import concourse.mybir as mybir
from concourse.bass_utils import run_bass_kernel_spmd

F32 = mybir.dt.float32
BF16 = mybir.dt.bfloat16
I32 = mybir.dt.int32
AF = mybir.ActivationFunctionType
ALU = mybir.AluOpType
AX = mybir.AxisListType

D = 2048
NCH = 16
S = 4096
B = 2
NT = 1024
TB = 512
NTB = NT // TB
DFF = 8192
EPS = 1e-6
NCORES = 8
HD = 128
NH = 16
WSLOT = 4096
NWSLOT = 4
NEG = -30000.0
FUSE_WAIT = False


def core_chunks(c):
    j = c % 4
    return [j, 7 - j]


def core_tokens(c):
    return np.concatenate([np.arange(k * TB, (k + 1) * TB) for k in core_chunks(c)])


class Fake:
    def __getitem__(self, k):
        return self

    def rearrange(self, *a, **k):
        return self

    def ap(self):
        return self


class Ref:
    __slots__ = ("sem", "val", "eng")

    def __init__(self, sem, val, eng):
        self.sem, self.val, self.eng = sem, val, eng


class Tok:
    __slots__ = ("w", "rs")

    def __init__(self):
        self.w = None
        self.rs = {}


COMPUTE = ("pe", "act", "dve", "pool")
NDSEM = 8


class Prog:
    def __init__(self):
        self.nc = bass.Bass("TRN2", target_bir_lowering=False)
        self.es = contextlib.ExitStack()
        self.streams = {e: [] for e in COMPUTE + ("sp",)}
        self.cnt = {e: 0 for e in COMPUTE}
        self.seen = {e: {} for e in COMPUTE + ("sp",)}
        self.esem = {e: self.es.enter_context(self.nc.semaphore("sem_" + e)) for e in COMPUTE}
        self.dsem = {q: [self.es.enter_context(self.nc.semaphore("dsem_%s%d" % (q, i))) for i in range(NDSEM)]
                     for q in ("sp", "pool")}
        self.dcnt = {"sp": 0, "pool": 0}
        self.toks = {}
        self.dry = False
        self.wplan = []
        self.wi = 0
        self.wissued = 0
        self.banks = None
        self.bi = 0
        self.rots = {}
        self.dram = {}
        self.out_refs = []
        self.phase_es = None
        self.nrot = 8
        self.nwslot = NWSLOT
        self.wlive = 2

    def din(self, name, shape, dtype):
        t = self.nc.dram_tensor(name, list(shape), dtype, kind="ExternalInput").ap()
        self.dram[name] = t
        return t

    def dout(self, name, shape, dtype):
        t = self.nc.dram_tensor(name, list(shape), dtype, kind="ExternalOutput").ap()
        self.dram[name] = t
        return t

    def sb(self, name, shape, dtype, phase=False):
        if self.dry:
            return Fake()
        es = self.phase_es if (phase and self.phase_es is not None) else self.es
        self.uid = getattr(self, "uid", 0) + 1
        return es.enter_context(self.nc.sbuf_tensor("s%d_%s" % (self.uid, name), list(shape), dtype))

    def begin_phase(self):
        if self.dry:
            return
        self.barrier()
        self.phase_es = contextlib.ExitStack()
        self.rots = {k: v for k, v in self.rots.items() if not v[3]}

    def end_phase(self):
        if self.dry:
            return
        self.barrier()
        self.phase_es.close()
        self.phase_es = None
        self.rots = {k: v for k, v in self.rots.items() if not v[3]}

    def setup_psum(self):
        if self.dry:
            self.banks = [Fake() for _ in range(8)]
        else:
            self.banks = [self.es.enter_context(self.nc.psum_tensor("bank%d" % i, [128, 512], F32)) for i in range(8)]

    def bank(self):
        i = self.bi
        self.bi = (self.bi + 1) % self.nrot
        return self.banks[i], self.tok(("bank", i))

    def accbank(self, i):
        assert self.nrot + i < 8
        return self.banks[self.nrot + i], self.tok(("bank", self.nrot + i))

    def rot(self, name, n, shape, dtype, phase=True):
        if name not in self.rots:
            self.rots[name] = ([self.sb("%s_%d" % (name, i), shape, dtype, phase=phase) for i in range(n)], 0, n, phase)
        tiles, i, n_, ph = self.rots[name]
        self.rots[name] = (tiles, (i + 1) % n_, n_, ph)
        return tiles[i], self.tok((name, i))

    def tok(self, key):
        t = self.toks.get(key)
        if t is None:
            t = self.toks[key] = Tok()
        return t

    def _deps(self, eng, r, w):
        waits = {}

        def add(ref):
            if ref is None:
                return
            if ref.eng == eng and eng == "pe":
                return
            k = id(ref.sem)
            if k not in waits or waits[k][1] < ref.val:
                waits[k] = (ref.sem, ref.val)
        for t in r:
            add(t.w)
        for t in w:
            add(t.w)
            for ref in t.rs.values():
                add(ref)
        out = []
        seen = self.seen[eng]
        for k, (sem, val) in waits.items():
            if seen.get(k, 0) < val:
                seen[k] = val
                out.append((sem, val))
        return out

    def _commit(self, ref, r, w):
        for t in r:
            k = id(ref.sem)
            old = t.rs.get(k)
            if old is None or old.val < ref.val:
                t.rs[k] = ref
        for t in w:
            t.w = ref
            t.rs = {}

    def op(self, eng, fn, r=(), w=()):
        if self.dry:
            return
        waits = self._deps(eng, r, w)
        self.cnt[eng] += 1
        ref = Ref(self.esem[eng], self.cnt[eng], eng)
        self.streams[eng].append((waits, fn, (ref.sem, 1)))
        self._commit(ref, r, w)

    def dma(self, q, out, in_, r=(), w=(), is_out=False):
        if self.dry:
            return
        n = self.dcnt[q]
        self.dcnt[q] += 1
        sem = self.dsem[q][n % NDSEM]
        val = 16 * (n // NDSEM + 1)
        waits = self._deps(q, r, w)
        if n >= NDSEM:
            k = id(sem)
            if self.seen[q].get(k, 0) < val - 16:
                self.seen[q][k] = val - 16
                waits.append((sem, val - 16))
        ref = Ref(sem, val, "dma_" + q)
        self.streams[q].append((waits, (lambda e, o=out, i=in_: e.dma_start(out=o, in_=i)), (sem, 16)))
        self._commit(ref, r, w)
        if is_out:
            self.out_refs.append(ref)

    def barrier(self):
        if self.dry:
            return
        allw = [(self.esem[e], self.cnt[e]) for e in COMPUTE if self.cnt[e] > 0]
        for q in ("sp", "pool"):
            n = self.dcnt[q]
            for i in range(min(n, NDSEM)):
                last = ((n - 1 - i) // NDSEM) * NDSEM + i
                allw.append((self.dsem[q][i], 16 * (last // NDSEM + 1)))
        for e in COMPUTE + ("sp",):
            seen = self.seen[e]
            ws = []
            for sem, val in allw:
                if seen.get(id(sem), 0) < val:
                    seen[id(sem)] = val
                    ws.append((sem, val))
            if ws:
                self.streams[e].append((ws, None, None))

    def mm(self, out, lhsT, rhs, start, stop, r, w):
        self.op("pe", lambda e: e.matmul(out, lhsT, rhs, start=start, stop=stop), r, w)

    def act(self, out, in_, func, r, w, **kw):
        self.op("act", lambda e: e.activation(out=out, in_=in_, func=func, **kw), r, w)

    def tt(self, eng, out, in0, in1, op, r, w):
        self.op(eng, lambda e: e.tensor_tensor(out=out, in0=in0, in1=in1, op=op), r, w)

    def ts(self, eng, out, in0, s1, s2, op0, op1, r, w):
        if s2 is None:
            self.op(eng, lambda e: e.tensor_scalar(out=out, in0=in0, scalar1=s1, scalar2=None, op0=op0), r, w)
        else:
            self.op(eng, lambda e: e.tensor_scalar(out=out, in0=in0, scalar1=s1, scalar2=s2, op0=op0, op1=op1), r, w)

    def stt(self, out, in0, scalar, in1, op0, op1, r, w):
        self.op("dve", lambda e: e.scalar_tensor_tensor(out=out, in0=in0, scalar=scalar, in1=in1, op0=op0, op1=op1), r, w)

    def copy(self, eng, out, in_, r, w):
        if eng == "act":
            self.op("act", lambda e: e.copy(out=out, in_=in_), r, w)
        else:
            self.op(eng, lambda e: e.tensor_copy(out=out, in_=in_), r, w)

    def setup_wslots(self):
        self.wslots = [self.sb("wslot%d" % i, [128, WSLOT], BF16) for i in range(self.nwslot)]

    def wnext(self, wap, k0, kc, n0, ncols):
        assert kc * ncols <= WSLOT
        if self.dry:
            self.wplan.append((wap, k0, kc, n0, ncols))
            return Fake(), None
        i = self.wi
        self.wi += 1
        assert self.wplan[i][1:] == (k0, kc, n0, ncols), (self.wplan[i][1:], (k0, kc, n0, ncols))
        while self.wissued < min(len(self.wplan), i + self.nwslot - self.wlive + 1):
            jj = self.wissued
            wap_, k0_, kc_, n0_, nc_ = self.wplan[jj]
            slot = self.wslots[jj % self.nwslot]
            dst = slot[:, 0:kc_ * nc_].rearrange("p (k n) -> p k n", k=kc_)
            src = wap_[k0_:k0_ + kc_ * 128, n0_:n0_ + nc_].rearrange("(k p) n -> p k n", p=128)
            self.dma("pool", dst, src, r=(), w=(self.tok(("wslot", jj % self.nwslot)),))
            self.wissued += 1
        slot = self.wslots[i % self.nwslot]
        return slot[:, 0:kc * ncols].rearrange("p (k n) -> p k n", k=kc), self.tok(("wslot", i % self.nwslot))

    def finish(self):
        ws = []
        seen = self.seen["sp"]
        best = {}
        for ref in self.out_refs:
            k = id(ref.sem)
            if k not in best or best[k][1] < ref.val:
                best[k] = (ref.sem, ref.val)
        for k, (sem, val) in best.items():
            if seen.get(k, 0) < val:
                ws.append((sem, val))
        self.barrier()
        self.streams["sp"].append((ws, None, None))
        nc = self.nc
        regs = {"pe": "tensor", "act": "scalar", "dve": "vector", "pool": "gpsimd", "sp": "sync"}
        with nc.Block() as block:
            for eng, attr in regs.items():
                stream = self.streams[eng]

                def f(e, stream=stream):
                    for waits, fn, inc in stream:
                        fuse = FUSE_WAIT and fn is not None and len(waits) > 0
                        for s_, v_ in (waits[:-1] if fuse else waits):
                            e.wait_ge(s_, v_)
                        if fn is not None:
                            ins = fn(e)
                            if fuse:
                                ins._wait_ge(waits[-1][0], waits[-1][1])
                            if inc is not None:
                                ins.then_inc(inc[0], inc[1])
                getattr(block, attr)(f)
        self.es.close()
        return nc


def emit_rstd(P, h, cs, tbw, src_toks):
    bk, bkt = P.bank()
    for c in range(NCH):
        sq, sqt = P.rot("sq", 4, [128, TB], BF16)
        if c % 2 == 0:
            P.act(sq[:, :tbw], h[:, c, cs], AF.Square, r=(src_toks[c],), w=(sqt,))
        else:
            P.tt("dve", sq[:, :tbw], h[:, c, cs], h[:, c, cs], ALU.mult, r=(src_toks[c],), w=(sqt,))
        P.mm(bk[:, :tbw], P.ones[:, :], sq[:, :tbw], c == 0, c == NCH - 1, r=(sqt, P.tok("ones")), w=(bkt,))
    rstd, rt = P.rot("rstd", 2, [128, TB], F32)
    P.act(rstd[:, :tbw], bk[:, :tbw], AF.Sqrt, r=(bkt, P.tok("consts")), w=(rt,), scale=1.0 / D, bias=P.epsc[:, 0:1])
    P.op("dve", lambda e, o=rstd[:, :tbw]: e.reciprocal(out=o, in_=o), r=(rt,), w=(rt,))
    return rstd, rt


def emit_norm(P, h, gain_col, dst, dst_tokf, ntb=NTB, src_tokf=None, tbw=TB):
    if src_tokf is None:
        src_tokf = lambda c, tb: P.tok(("h", c, tb))
    for tb in range(ntb):
        cs = slice(tb * tbw, (tb + 1) * tbw)
        rstd, rt = emit_rstd(P, h, cs, tbw, [src_tokf(c, tb) for c in range(NCH)])
        for c in range(NCH):
            P.stt(dst[:, c, cs], h[:, c, cs], gain_col[:, c:c + 1], rstd[:, :tbw], ALU.mult, ALU.mult,
                  r=(src_tokf(c, tb), rt, P.tok("vecs")), w=(dst_tokf(c, tb),))


DBG = {"mlp": True, "ple": True, "mix": True}


def pipeline(items, lags):
    n = len(items)
    tot = n + max(lags)
    for step in range(tot):
        for k, lag in enumerate(lags):
            i = step - lag
            if 0 <= i < n:
                items[i][k]()


def emit_mlp_ple(P, L, w1, w2, wg, wp, pT_dram, vec):
    h = P.h
    P.begin_phase()
    xn = P.sb("xn", [128, NCH, NT], BF16, phase=True)
    hid = P.sb("hid", [128, 8, NT], BF16, phase=True)
    pT = P.sb("pT", [128, 2, NT], BF16, phase=True)
    xt = lambda c, tb: P.tok(("xn", c, tb))
    ht = lambda c, tb: P.tok(("h", c, tb))
    hidt = lambda c, tb: P.tok(("hid", c, tb))
    P.dma("pool", pT[:, :, :], pT_dram[:, :, :], r=(), w=(P.tok("pT"),))
    if DBG["mlp"]:
        emit_norm(P, h, vec("mlp_norm"), xn, xt)
    for fb in range(8 if DBG["mlp"] else 0):
        for j in range(4):
            wt, wtk = P.wnext(w1, 0, 16, fb * 1024 + j * 256, 256)
            for m in range(2):
                for tb in range(NTB):
                    cs = slice(tb * TB, (tb + 1) * TB)
                    bk, bkt = P.bank()
                    for kc in range(NCH):
                        P.mm(bk[:, :], wt[:, kc, m * 128:(m + 1) * 128], xn[:, kc, cs], kc == 0, kc == NCH - 1,
                             r=(wtk, xt(kc, tb)), w=(bkt,))
                    rl, rlt = P.rot("relu", 3, [128, TB], F32)
                    P.act(rl[:, :], bk[:, :], AF.Relu, r=(bkt,), w=(rlt,))
                    P.tt("dve", hid[:, j * 2 + m, cs], rl[:, :], rl[:, :], ALU.mult, r=(rlt,), w=(hidt(j * 2 + m, tb),))
        for j in range(4):
            wt, wtk = P.wnext(w2, fb * 1024, 8, j * 512, 512)
            for m in range(4):
                for tb in range(NTB):
                    cs = slice(tb * TB, (tb + 1) * TB)
                    bk, bkt = P.bank()
                    for kc in range(8):
                        P.mm(bk[:, :], wt[:, kc, m * 128:(m + 1) * 128], hid[:, kc, cs], kc == 0, kc == 7,
                             r=(wtk, hidt(kc, tb)), w=(bkt,))
                    c = j * 4 + m
                    P.tt("dve", h[:, c, cs], h[:, c, cs], bk[:, :], ALU.add, r=(bkt, ht(c, tb)), w=(ht(c, tb),))
    if DBG["ple"]:
        emit_norm(P, h, vec("ple_norm"), xn, xt)
    for j in range(8 if DBG["ple"] else 0):
        wt, wtk = P.wnext(wg, 0, 16, j * 256, 256)
        wq, wqk = P.wnext(wp, 0, 2, j * 256, 256)
        for m in range(2):
            for tb in range(NTB):
                cs = slice(tb * TB, (tb + 1) * TB)
                c = j * 2 + m
                bk, bkt = P.bank()
                for kc in range(NCH):
                    P.mm(bk[:, :], wt[:, kc, m * 128:(m + 1) * 128], xn[:, kc, cs], kc == 0, kc == NCH - 1,
                         r=(wtk, xt(kc, tb)), w=(bkt,))
                g, gt = P.rot("gate", 3, [128, TB], F32)
                P.act(g[:, :], bk[:, :], AF.Sigmoid, r=(bkt,), w=(gt,))
                bk2, bk2t = P.bank()
                for kc in range(2):
                    P.mm(bk2[:, :], wq[:, kc, m * 128:(m + 1) * 128], pT[:, kc, cs], kc == 0, kc == 1,
                         r=(wqk, P.tok("pT")), w=(bk2t,))
                P.tt("dve", g[:, :], g[:, :], bk2[:, :], ALU.mult, r=(gt, bk2t), w=(gt,))
                P.tt("dve", h[:, c, cs], h[:, c, cs], g[:, :], ALU.add, r=(gt, ht(c, tb)), w=(ht(c, tb),))
    P.end_phase()


def common_setup(P, nvec, alloc_h=True):
    P.setup_psum()
    P.setup_wslots()
    if alloc_h:
        P.h = P.sb("hT", [128, NCH, NT], F32)
    P.ones = P.sb("ones", [128, 128], BF16)
    P.epsc = P.sb("epsc", [128, 1], F32)
    P.vecs = P.sb("vecs", [128, nvec], F32)
    P.onesf = P.sb("onesf", [128, 128], F32)
    P.op("pool", lambda e: e.memset(P.ones[:, :], 1.0), r=(), w=(P.tok("ones"),))
    P.op("pool", lambda e: e.memset(P.onesf[:, :], 1.0), r=(), w=(P.tok("ones"),))
    P.op("pool", lambda e: e.memset(P.epsc[:, :], EPS), r=(), w=(P.tok("consts"),))


def load_h(P, hT_dram):
    for c in range(NCH):
        P.dma("sp", P.h[:, c, :], hT_dram[:, c, :], r=(), w=tuple(P.tok(("h", c, tb)) for tb in range(NTB)))


def store_h(P, out_dram):
    for c in range(NCH):
        P.dma("sp", out_dram[:, c, :], P.h[:, c, :], r=tuple(P.tok(("h", c, tb)) for tb in range(NTB)), w=(), is_out=True)


class Vecs:
    def __init__(self, names):
        self.names = list(names)

    def n(self):
        return 16 * len(self.names)

    def ap(self, P, name):
        i = self.names.index(name)
        return P.vecs[:, 16 * i:16 * (i + 1)]

    def host(self, arrs):
        return np.ascontiguousarray(np.concatenate([np.asarray(arrs[n], np.float32).reshape(16, 128).T for n in self.names], axis=1))


def run_two_pass(P, body):
    P.dry = True
    body()
    P.dry = False
    P.bi = 0
    P.rots = {}
    P.toks = {}
    body()
    assert P.wi == len(P.wplan), (P.wi, len(P.wplan))
    return P.finish()


POOL_W = (2, 4, 8, 16)


def emit_pool(P, wpool, hhalo_dram, fac_dram, vec):
    h = P.h
    P.begin_phase()
    hx = P.sb("hx", [128, NCH, 32], F32, phase=True)
    xh = P.sb("xh", [128, NCH, 32], F32, phase=True)
    fac = P.sb("fac", [128, 4, 2, 16], F32, phase=True)
    P.dma("sp", hx[:, :, :], hhalo_dram[:, :, :], r=(), w=(P.tok("hx"),))
    P.dma("sp", fac[:, :, :, :], fac_dram[:, :, :, :], r=(), w=(P.tok("fac"),))
    gain = vec("mixer_norm")
    scale = vec("pool_scale")
    emit_norm(P, hx, gain, xh, lambda c, tb: P.tok(("xh", c)), ntb=1, src_tokf=lambda c, tb: P.tok("hx"), tbw=32)
    ht = lambda c, tb: P.tok(("h", c, tb))
    for s in range(NTB):
        cs = slice(s * TB, (s + 1) * TB)
        rstd, rt = emit_rstd(P, h, cs, TB, [ht(c, s) for c in range(NCH)])
        for g in range(4):
            w = POOL_W[g]
            dg, dgt = P.rot("diffg", 2, [128, 4, TB], BF16)
            for cc in range(4):
                c = 4 * g + cc
                xe, xet = P.rot("xe", 2, [128, 16 + TB], F32)
                P.copy("pool", xe[:, 0:16], xh[:, c, s * 16:(s + 1) * 16], r=(P.tok(("xh", c)),), w=(xet,))
                P.stt(xe[:, 16:16 + TB], h[:, c, cs], gain[:, c:c + 1], rstd[:, :], ALU.mult, ALU.mult,
                      r=(ht(c, s), rt, P.tok("vecs"), xet), w=(xet,))
                cur, curt = xe, xet
                shift = 1
                for st in range(g + 1):
                    nx, nxt = P.rot("ss", 3, [128, 16 + TB], F32)
                    lo = 2 * shift - 1
                    P.tt("dve", nx[:, lo:16 + TB], cur[:, lo:16 + TB], cur[:, lo - shift:16 + TB - shift], ALU.add,
                         r=(curt,), w=(nxt,))
                    cur, curt = nx, nxt
                    shift *= 2
                P.stt(dg[:, cc, :], cur[:, 16:16 + TB], 1.0 / w, xe[:, 16:16 + TB], ALU.mult, ALU.subtract,
                      r=(curt, xet), w=(dgt,))
                t16, t16t = P.rot("t16", 2, [128, 16], F32)
                P.tt("dve", t16[:, :], cur[:, 16:32], fac[:, g, s, :], ALU.mult, r=(curt, P.tok("fac")), w=(t16t,))
                P.tt("dve", dg[:, cc, 0:16], t16[:, :], xe[:, 16:32], ALU.subtract, r=(t16t, xet, dgt), w=(dgt,))
            wt, wtk = P.wnext(wpool, g * 512, 4, 0, 512)
            for oc in range(4):
                bk, bkt = P.bank()
                for kc in range(4):
                    P.mm(bk[:, :], wt[:, kc, oc * 128:(oc + 1) * 128], dg[:, kc, :], kc == 0, kc == 3, r=(wtk, dgt), w=(bkt,))
                c = 4 * g + oc
                P.stt(h[:, c, cs], bk[:, :], scale[:, c:c + 1], h[:, c, cs], ALU.mult, ALU.add,
                      r=(bkt, ht(c, s), P.tok("vecs")), w=(ht(c, s),))
    P.end_phase()


V1 = Vecs(["mixer_norm", "pool_scale", "mlp_norm", "ple_norm"])


def build_L1():
    P = Prog()
    hT = P.din("hT", [128, NCH, NT], F32)
    hhalo = P.din("hhalo", [128, NCH, 32], F32)
    fac = P.din("poolfac", [128, 4, 2, 16], F32)
    pT = P.din("pT", [128, 2, NT], F32)
    vecs = P.din("vecs", [128, V1.n()], F32)
    wpool = P.din("pool_w", [2048, 512], F32)
    w1 = P.din("mlp_w1", [D, DFF], F32)
    w2 = P.din("mlp_w2", [DFF, D], F32)
    wg = P.din("ple_gate", [D, D], F32)
    wp = P.din("ple_proj", [256, D], F32)
    out = P.dout("hT_out", [128, NCH, NT], F32)

    def body():
        common_setup(P, V1.n())
        P.dma("sp", P.vecs[:, :], vecs[:, :], r=(), w=(P.tok("vecs"),))
        load_h(P, hT)
        vec = lambda n: V1.ap(P, n)
        if DBG["mix"]:
            emit_pool(P, wpool, hhalo, fac, vec)
        emit_mlp_ple(P, 1, w1, w2, wg, wp, pT, vec)
        store_h(P, out)
    return run_two_pass(P, body)


def to_fm(a):
    ntok, nf = a.shape
    return np.ascontiguousarray(a.T.reshape(nf // 128, 128, ntok).transpose(1, 0, 2))


def from_fm(a):
    p, nch, ntok = a.shape
    return np.ascontiguousarray(a.transpose(1, 0, 2).reshape(nch * 128, ntok).T)


def halo_rows(hfull_b, c, n):
    out = np.zeros((2 * n, hfull_b.shape[1]), np.float32)
    for s, k in enumerate(core_chunks(c)):
        if k > 0:
            out[s * n:(s + 1) * n] = hfull_b[k * TB - n:k * TB]
    return out


def pool_fac(c):
    f = np.zeros((128, 4, 2, 16), np.float32)
    for g, w in enumerate(POOL_W):
        for s, k in enumerate(core_chunks(c)):
            for t in range(16):
                cnt = min(w, t + 1) if k == 0 else w
                f[:, g, s, t] = 1.0 / cnt
    return f


def run_prog(nc, in_maps):
    if DBG.get("trace"):
        res = run_bass_kernel_spmd(nc, in_maps, core_ids=list(range(NCORES)), trace=True)
        print("EXEC_TIME_NS", res.exec_time_ns, flush=True)
        DBG["last_res"] = res
        return res.results
    res = run_bass_kernel_spmd(nc, in_maps, core_ids=list(range(NCORES)))
    return res.results


def layer1(inp, h0):
    nc = build_L1()
    vec_arrs = {"mixer_norm": inp["mixer_norm"][1], "pool_scale": inp["pool_scale"][0], "mlp_norm": inp["mlp_norm"][1],
                "ple_norm": inp["ple_norm"][1]}
    vecs = V1.host(vec_arrs)
    in_maps = []
    for c in range(NCORES):
        b = c // 4
        tk = core_tokens(c)
        in_maps.append({
            "hT": to_fm(h0[b][tk]), "hhalo": to_fm(halo_rows(h0[b], c, 16)), "poolfac": pool_fac(c),
            "pT": to_fm(inp["p"][1, b][tk]), "vecs": vecs,
            "pool_w": np.ascontiguousarray(inp["pool_w"][0].reshape(2048, 512)),
            "mlp_w1": inp["mlp_w1"][1], "mlp_w2": inp["mlp_w2"][1], "ple_gate": inp["ple_gate"][1], "ple_proj": inp["ple_proj"][1],
        })
    res = run_prog(nc, in_maps)
    h1 = np.zeros_like(h0)
    for c in range(NCORES):
        h1[c // 4][core_tokens(c)] = from_fm(res[c]["hT_out"])
    return h1


def emit_conv(P, w_in, w_o, hhalo_dram, vec):
    h = P.h
    P.begin_phase()
    hx = P.sb("hx", [128, NCH, 32], F32, phase=True)
    xh = P.sb("xh", [128, NCH, 32], BF16, phase=True)
    xn = P.sb("xn", [128, NCH, NT], BF16, phase=True)
    gT = P.sb("gT", [128, NCH, NT], BF16, phase=True)
    P.dma("sp", hx[:, :, :], hhalo_dram[:, :, :], r=(), w=(P.tok("hx"),))
    gain = vec("mixer_norm")
    xht = lambda c, tb: P.tok(("xh", c))
    xt = lambda c, tb: P.tok(("xn", c, tb))
    ht = lambda c, tb: P.tok(("h", c, tb))
    gt = lambda c, tb: P.tok(("gT", c, tb))
    emit_norm(P, hx, gain, xh, xht, ntb=1, src_tokf=lambda c, tb: P.tok("hx"), tbw=32)
    emit_norm(P, h, gain, xn, xt)
    w0, w1, w2, cb = vec("conv_w0"), vec("conv_w1"), vec("conv_w2"), vec("conv_b")
    vt = P.tok("vecs")

    def proj(wt, wtk, m, tb):
        bk, bkt = P.bank()
        cs = slice(tb * TB, (tb + 1) * TB)
        for kc in range(NCH):
            P.mm(bk[:, :], wt[:, kc, m * 128:(m + 1) * 128], xn[:, kc, cs], kc == 0, kc == NCH - 1, r=(wtk, xt(kc, tb)), w=(bkt,))
        return bk, bkt

    def projh(wt, wtk, m):
        bk, bkt = P.bank()
        for kc in range(NCH):
            P.mm(bk[:, 0:32], wt[:, kc, m * 128:(m + 1) * 128], xh[:, kc, :], kc == 0, kc == NCH - 1, r=(wtk, xht(kc, 0)), w=(bkt,))
        return bk, bkt

    for dcp in range(8):
        wt, wtk = P.wnext(w_in, 0, 16, 2048 + dcp * 256, 256)
        cg = {}
        for m in range(2):
            for tb in range(NTB):
                bk, bkt = proj(wt, wtk, m, tb)
                t, tt_ = P.rot("cg", 4, [128, TB], F32)
                P.copy("act", t[:, :], bk[:, :], r=(bkt,), w=(tt_,))
                cg[(m, tb)] = (t, tt_)
            bk, bkt = projh(wt, wtk, m)
            t, tt_ = P.rot("cgh", 2, [128, 32], F32)
            P.copy("act", t[:, :], bk[:, 0:32], r=(bkt,), w=(tt_,))
            cg[(m, "h")] = (t, tt_)
        wt, wtk = P.wnext(w_in, 0, 16, 4096 + dcp * 256, 256)
        cv = {}
        for m in range(2):
            c = dcp * 2 + m
            us = {}
            for tb in range(NTB):
                bk, bkt = proj(wt, wtk, m, tb)
                u, ut = P.rot("u", 4, [128, 16 + TB], F32)
                P.tt("dve", u[:, 16:16 + TB], cg[(m, tb)][0][:, :], bk[:, :], ALU.mult, r=(cg[(m, tb)][1], bkt), w=(ut,))
                us[tb] = (u, ut)
            bk, bkt = projh(wt, wtk, m)
            for tb in range(NTB):
                u, ut = us[tb]
                P.tt("dve", u[:, 0:16], cg[(m, "h")][0][:, tb * 16:(tb + 1) * 16], bk[:, tb * 16:(tb + 1) * 16], ALU.mult,
                     r=(cg[(m, "h")][1], bkt, ut), w=(ut,))
                v, vtk = P.rot("cv", 4, [128, TB], F32)
                P.ts("dve", v[:, :], u[:, 14:14 + TB], w0[:, c:c + 1], cb[:, c:c + 1], ALU.mult, ALU.add, r=(ut, vt), w=(vtk,))
                P.stt(v[:, :], u[:, 15:15 + TB], w1[:, c:c + 1], v[:, :], ALU.mult, ALU.add, r=(ut, vt, vtk), w=(vtk,))
                P.stt(v[:, :], u[:, 16:16 + TB], w2[:, c:c + 1], v[:, :], ALU.mult, ALU.add, r=(ut, vt, vtk), w=(vtk,))
                cv[(m, tb)] = (v, vtk)
        wt, wtk = P.wnext(w_in, 0, 16, dcp * 256, 256)
        for m in range(2):
            c = dcp * 2 + m
            for tb in range(NTB):
                bk, bkt = proj(wt, wtk, m, tb)
                P.tt("dve", gT[:, c, tb * TB:(tb + 1) * TB], bk[:, :], cv[(m, tb)][0][:, :], ALU.mult, r=(bkt, cv[(m, tb)][1]), w=(gt(c, tb),))
    emit_outproj(P, w_o, gT, gt)
    P.end_phase()


def emit_outproj(P, w_o, src, src_tokf):
    h = P.h
    ht = lambda c, tb: P.tok(("h", c, tb))
    for j in range(8):
        wt, wtk = P.wnext(w_o, 0, 16, j * 256, 256)
        for m in range(2):
            c = j * 2 + m
            for tb in range(NTB):
                cs = slice(tb * TB, (tb + 1) * TB)
                bk, bkt = P.bank()
                for kc in range(NCH):
                    P.mm(bk[:, :], wt[:, kc, m * 128:(m + 1) * 128], src[:, kc, cs], kc == 0, kc == NCH - 1,
                         r=(wtk, src_tokf(kc, tb)), w=(bkt,))
                P.tt("dve", h[:, c, cs], h[:, c, cs], bk[:, :], ALU.add, r=(bkt, ht(c, tb)), w=(ht(c, tb),))


V3 = Vecs(["mixer_norm", "conv_w0", "conv_w1", "conv_w2", "conv_b", "mlp_norm", "ple_norm"])


def build_L3():
    P = Prog()
    hT = P.din("hT", [128, NCH, NT], F32)
    hhalo = P.din("hhalo", [128, NCH, 32], F32)
    pT = P.din("pT", [128, 2, NT], F32)
    vecs = P.din("vecs", [128, V3.n()], F32)
    w_in = P.din("conv_w_in", [D, 3 * D], F32)
    w_o = P.din("conv_w_o", [D, D], F32)
    w1 = P.din("mlp_w1", [D, DFF], F32)
    w2 = P.din("mlp_w2", [DFF, D], F32)
    wg = P.din("ple_gate", [D, D], F32)
    wp = P.din("ple_proj", [256, D], F32)
    out = P.dout("hT_out", [128, NCH, NT], F32)

    def body():
        common_setup(P, V3.n())
        P.dma("sp", P.vecs[:, :], vecs[:, :], r=(), w=(P.tok("vecs"),))
        load_h(P, hT)
        vec = lambda n: V3.ap(P, n)
        if DBG["mix"]:
            emit_conv(P, w_in, w_o, hhalo, vec)
        emit_mlp_ple(P, 3, w1, w2, wg, wp, pT, vec)
        store_h(P, out)
    return run_two_pass(P, body)


def layer3(inp, h2):
    nc = build_L3()
    vec_arrs = {"mixer_norm": inp["mixer_norm"][3], "conv_w0": inp["conv_w"][0, 0], "conv_w1": inp["conv_w"][0, 1],
                "conv_w2": inp["conv_w"][0, 2], "conv_b": inp["conv_b"][0], "mlp_norm": inp["mlp_norm"][3], "ple_norm": inp["ple_norm"][3]}
    vecs = V3.host(vec_arrs)
    in_maps = []
    for c in range(NCORES):
        b = c // 4
        tk = core_tokens(c)
        in_maps.append({
            "hT": to_fm(h2[b][tk]), "hhalo": to_fm(halo_rows(h2[b], c, 16)),
            "pT": to_fm(inp["p"][3, b][tk]), "vecs": vecs,
            "conv_w_in": inp["conv_w_in"][0], "conv_w_o": inp["conv_w_o"][0],
            "mlp_w1": inp["mlp_w1"][3], "mlp_w2": inp["mlp_w2"][3], "ple_gate": inp["ple_gate"][3], "ple_proj": inp["ple_proj"][3],
        })
    res = run_prog(nc, in_maps)
    h3 = np.zeros_like(h2)
    for c in range(NCORES):
        h3[c // 4][core_tokens(c)] = from_fm(res[c]["hT_out"])
    return h3


ROPE_THETA = 500000.0
PI = float(np.pi)


def rope_consts():
    freq = np.zeros((128, 1), np.float32)
    fr = (np.float32(ROPE_THETA) ** (-np.arange(16, dtype=np.float32) * np.float32(2.0) / np.float32(32))).astype(np.float32)
    freq[0:16, 0] = fr
    freq[16:32, 0] = fr
    rotT = np.zeros((128, 128), np.float32)
    for i in range(16):
        rotT[i + 16, i] = -1.0
        rotT[i, i + 16] = 1.0
    return freq, rotT.astype(ml_dtypes.bfloat16)


def emit_rope_tables(P, posb_dram, hc):
    P.cosF = P.sb("cosF", [128, NT], F32)
    P.sinF = P.sb("sinF", [128, NT], F32)
    P.begin_phase()
    posi = P.sb("posi", [128, NT], I32, phase=True)
    ang = P.sb("ang", [128, NT], F32, phase=True)
    tk = P.tok("rope")
    P.dma("sp", posi[:, :], posb_dram[:, :], r=(), w=(tk,))
    P.copy("dve", ang[:, :], posi[:, :], r=(tk,), w=(tk,))
    P.ts("dve", ang[:, :], ang[:, :], hc[:, 2:3], None, ALU.mult, None, r=(tk, P.tok("hc")), w=(tk,))
    ki = P.sb("rope_ki", [128, NT], I32, phase=True)
    kf = P.sb("rope_kf", [128, NT], F32, phase=True)
    msk = P.sb("rope_m", [128, NT], F32, phase=True)
    for dst, shift in ((P.sinF, 0.0), (P.cosF, 0.5 * PI)):
        P.ts("dve", dst[:, :], ang[:, :], shift, None, ALU.add, None, r=(tk,), w=(tk,))
        P.ts("dve", kf[:, :], dst[:, :], 1.0 / (2.0 * PI), None, ALU.mult, None, r=(tk,), w=(tk,))
        P.copy("dve", ki[:, :], kf[:, :], r=(tk,), w=(tk,))
        P.copy("dve", kf[:, :], ki[:, :], r=(tk,), w=(tk,))
        P.stt(dst[:, :], kf[:, :], -2.0 * PI, dst[:, :], ALU.mult, ALU.add, r=(tk,), w=(tk,))
        P.ts("dve", msk[:, :], dst[:, :], PI, None, ALU.is_gt, None, r=(tk,), w=(tk,))
        P.stt(dst[:, :], msk[:, :], -2.0 * PI, dst[:, :], ALU.mult, ALU.add, r=(tk,), w=(tk,))
        P.ts("dve", msk[:, :], dst[:, :], -PI, None, ALU.is_lt, None, r=(tk,), w=(tk,))
        P.stt(dst[:, :], msk[:, :], 2.0 * PI, dst[:, :], ALU.mult, ALU.add, r=(tk,), w=(tk,))
        P.act(dst[:, :], dst[:, :], AF.Sin, r=(tk,), w=(tk,))
    P.end_phase()


def emit_headnorm(P, bk, bkt, gain1, cs, rope, hc):
    sq, sqt = P.rot("hsq", 3, [128, TB], BF16)
    P.act(sq[:, :], bk[:, :], AF.Square, r=(bkt,), w=(sqt,))
    b2, b2t = P.bank()
    P.mm(b2[:, :], P.ones[:, :], sq[:, :], True, True, r=(sqt, P.tok("ones")), w=(b2t,))
    rs, rst = P.rot("hrstd", 2, [128, TB], F32)
    P.act(rs[:, :], b2[:, :], AF.Sqrt, r=(b2t, P.tok("consts")), w=(rst,), scale=1.0 / HD, bias=P.epsc[:, 0:1])
    P.op("dve", lambda e, o=rs[:, :]: e.reciprocal(out=o, in_=o), r=(rst,), w=(rst,))
    qn, qnt = P.rot("qn", 3, [128, TB], BF16)
    P.stt(qn[:, :], bk[:, :], gain1, rs[:, :], ALU.mult, ALU.mult, r=(bkt, rst, P.tok("hc")), w=(qnt,))
    if not rope:
        return qn, qnt
    if rope == "both":
        P.qn_last = (qn, qnt)
    b3, b3t = P.bank()
    P.mm(b3[:, :], P.rotT[:, :], qn[:, :], True, True, r=(qnt, P.tok("hc")), w=(b3t,))
    t1, t1t = P.rot("rp1", 2, [128, TB], F32)
    P.tt("dve", t1[:, :], qn[:, :], P.cosF[:, cs], ALU.mult, r=(qnt, P.tok("rope")), w=(t1t,))
    t2, t2t = P.rot("rp2", 2, [128, TB], F32)
    P.tt("dve", t2[:, :], b3[:, :], P.sinF[:, cs], ALU.mult, r=(b3t, P.tok("rope")), w=(t2t,))
    P.tt("pool", t1[:, :], t1[:, :], t2[:, :], ALU.add, r=(t1t, t2t), w=(t1t,))
    return t1, t1t


def headnorm_stages(P, make_proj, gain1, cs, rope, sink):
    st = {}

    def s0():
        bk, bkt = make_proj()
        sq, sqt = P.rot("hsq", 4, [128, TB], BF16, phase=False)
        P.act(sq[:, :], bk[:, :], AF.Square, r=(bkt,), w=(sqt,))
        st.update(bk=bk, bkt=bkt, sq=sq, sqt=sqt)

    def s1():
        bk, bkt, sq, sqt = st["bk"], st["bkt"], st["sq"], st["sqt"]
        b2, b2t = P.bank()
        P.mm(b2[:, :], P.ones[:, :], sq[:, :], True, True, r=(sqt, P.tok("ones")), w=(b2t,))
        rs, rst = P.rot("hrstd", 3, [128, TB], F32, phase=False)
        P.act(rs[:, :], b2[:, :], AF.Sqrt, r=(b2t, P.tok("consts")), w=(rst,), scale=1.0 / HD, bias=P.epsc[:, 0:1])
        P.op("dve", lambda e, o=rs[:, :]: e.reciprocal(out=o, in_=o), r=(rst,), w=(rst,))
        qn, qnt = P.rot("qn", 4, [128, TB], BF16, phase=False)
        P.stt(qn[:, :], bk[:, :], gain1, rs[:, :], ALU.mult, ALU.mult, r=(bkt, rst, P.tok("hc")), w=(qnt,))
        st.update(qn=qn, qnt=qnt)

    def s2():
        qn, qnt = st["qn"], st["qnt"]
        b3, b3t = P.bank()
        P.mm(b3[:, :], P.rotT[:, :], qn[:, :], True, True, r=(qnt, P.tok("hc")), w=(b3t,))
        t1, t1t = P.rot("rp1", 3, [128, TB], F32, phase=False)
        P.tt("dve", t1[:, :], qn[:, :], P.cosF[:, cs], ALU.mult, r=(qnt, P.tok("rope")), w=(t1t,))
        t2, t2t = P.rot("rp2", 3, [128, TB], F32, phase=False)
        P.tt("dve", t2[:, :], b3[:, :], P.sinF[:, cs], ALU.mult, r=(b3t, P.tok("rope")), w=(t2t,))
        P.tt("pool", t1[:, :], t1[:, :], t2[:, :], ALU.add, r=(t1t, t2t), w=(t1t,))
        sink(t1, t1t, qn, qnt)
    return (s0, s1, s2)


def load_headconsts(P, hc_dram, rotT_dram, ncol):
    P.hc = P.sb("hc", [128, ncol], F32)
    P.rotT = P.sb("rotT", [128, 128], BF16)
    P.dma("sp", P.hc[:, :], hc_dram[:, :], r=(), w=(P.tok("hc"),))
    P.dma("sp", P.rotT[:, :], rotT_dram[:, :], r=(), w=(P.tok("hc"),))


V0A = Vecs(["mixer_norm"])


def build_L0a():
    P = Prog()
    xT = P.din("hT", [128, NCH, NT], F32)
    vecs = P.din("vecs", [128, V0A.n()], F32)
    hc_d = P.din("hc", [128, 4], F32)
    rotT_d = P.din("rotT", [128, 128], BF16)
    posb = P.din("posb", [128, NT], I32)
    wqkv = P.din("w_qkv", [D, 3 * D], F32)
    qT_o = P.dout("qT", [NH, 128, NT], BF16)
    kT_o = P.dout("kT", [NH, 128, NT], BF16)
    v_o = P.dout("v", [NT, D], BF16)
    km_o = P.dout("kmean", [128, NH, 4], BF16)

    def body():
        common_setup(P, V0A.n())
        P.dma("sp", P.vecs[:, :], vecs[:, :], r=(), w=(P.tok("vecs"),))
        load_headconsts(P, hc_d, rotT_d, 4)
        load_h(P, xT)
        emit_rope_tables(P, posb, P.hc)
        xn = P.sb("xn", [128, NCH, NT], BF16)
        km = P.sb("km", [128, NH, 4], BF16)
        xt = lambda c, tb: P.tok(("xn", c, tb))
        emit_norm(P, P.h, V0A.ap(P, "mixer_norm"), xn, xt)
        items = []
        wcur = {}
        for part in range(2):
            for j in range(8):
                for m in range(2):
                    hd = 2 * j + m
                    for tb in range(NTB):
                        cs = slice(tb * TB, (tb + 1) * TB)
                        ctx = {}

                        def make_proj(part=part, j=j, m=m, tb=tb, cs=cs):
                            if m == 0 and tb == 0:
                                wcur["w"] = P.wnext(wqkv, 0, 16, part * D + j * 256, 256)
                            wt, wtk = wcur["w"]
                            bk, bkt = P.bank()
                            for kc in range(NCH):
                                P.mm(bk[:, :], wt[:, kc, m * 128:(m + 1) * 128], xn[:, kc, cs], kc == 0, kc == NCH - 1, r=(wtk, xt(kc, tb)), w=(bkt,))
                            return bk, bkt

                        def sink(qf, qft, qn, qnt, part=part, hd=hd, tb=tb, cs=cs):
                            if tb == 0:
                                wcur[("stg", part, hd)] = P.rot("stage", 3, [128, NT], BF16, phase=False)
                            stg, stgt = wcur[("stg", part, hd)]
                            P.copy("act", stg[:, cs], qf[:, :], r=(qft, stgt), w=(stgt,))
                            if part == 1:
                                kr, krt = P.rot("kred", 2, [128, 2], F32, phase=False)
                                P.op("dve", lambda e, o=kr[:, :], i=qf[:, :].rearrange("p (b t) -> p b t", b=2): e.tensor_reduce(out=o, in_=i, axis=AX.X, op=ALU.add),
                                     r=(qft,), w=(krt,))
                                P.ts("dve", km[:, hd, tb * 2:(tb + 1) * 2], kr[:, :], 1.0 / 256.0, None, ALU.mult, None, r=(krt, P.tok("km")), w=(P.tok("km"),))
                            if tb == NTB - 1:
                                P.dma("sp", (qT_o if part == 0 else kT_o)[hd], stg[:, :], r=(stgt,), w=(), is_out=True)
                        items.append(headnorm_stages(P, make_proj, P.hc[:, part:part + 1], cs, True, sink))
        pipeline(items, (0, 1, 2))
        for j in range(8):
            wt, wtk = P.wnext(wqkv, 0, 16, 2 * D + j * 256, 256)
            vs, vst = P.rot("vstage", 2, [128, 8, 256], BF16, phase=False)
            for tt_ in range(NT // 128):
                bk, bkt = P.bank()
                for kc in range(NCH):
                    P.mm(bk[:, 0:256], xn[:, kc, tt_ * 128:(tt_ + 1) * 128], wt[:, kc, :], kc == 0, kc == NCH - 1,
                         r=(wtk, xt(kc, tt_ // 4)), w=(bkt,))
                P.copy("act", vs[:, tt_, :], bk[:, 0:256], r=(bkt, vst), w=(vst,))
            P.dma("sp", v_o[:, j * 256:(j + 1) * 256].rearrange("(t p) n -> p t n", p=128), vs[:, :, :], r=(vst,), w=(), is_out=True)
        P.dma("sp", km_o[:, :, :], km[:, :, :], r=(P.tok("km"),), w=(), is_out=True)
    return run_two_pass(P, body)


def head_consts(qg, kg):
    freq, rotT = rope_consts()
    hc = np.zeros((128, 4), np.float32)
    hc[:, 0] = qg
    hc[:, 1] = kg
    hc[:, 2] = freq[:, 0]
    hc[:, 3] = -PI
    return hc, rotT


def layer0a(inp, x=None):
    x = inp["x"] if x is None else x
    nc = build_L0a()
    vecs = V0A.host({"mixer_norm": inp["mixer_norm"][0]})
    hc, rotT = head_consts(inp["moba_q_gain"][0], inp["moba_k_gain"][0])
    in_maps = []
    for c in range(NCORES):
        b = c // 4
        tk = core_tokens(c)
        in_maps.append({"hT": to_fm(x[b][tk]), "vecs": vecs, "hc": hc, "rotT": rotT,
                        "posb": np.ascontiguousarray(np.broadcast_to(inp["positions"][b][tk].astype(np.int32)[None, :], (128, NT))),
                        "w_qkv": inp["moba_w_qkv"][0]})
    res = run_prog(nc, in_maps)
    bf = ml_dtypes.bfloat16
    q = np.zeros((B, S, NH, HD), bf)
    k = np.zeros((B, S, NH, HD), bf)
    v = np.zeros((B, S, D), bf)
    km = np.zeros((B, S // 256, NH, HD), bf)
    for c in range(NCORES):
        b = c // 4
        tk = core_tokens(c)
        q[b, tk] = res[c]["qT"].transpose(2, 0, 1)
        k[b, tk] = res[c]["kT"].transpose(2, 0, 1)
        v[b, tk] = res[c]["v"]
        kmc = res[c]["kmean"]
        for s_, ch in enumerate(core_chunks(c)):
            km[b, 2 * ch:2 * ch + 2] = kmc[:, :, 2 * s_:2 * s_ + 2].transpose(2, 1, 0)
    return {"q": q, "k": k, "v": v, "kmean": km}


def slot_lists(c):
    j = c % 4
    return [[None] * (3 - j) + list(range(0, j + 1)), [None] * j + list(range(0, 8 - j))]


def attn_consts():
    tri = np.zeros((128, 4, TB), np.float32)
    kk = np.arange(128)[:, None]
    qq = np.arange(TB)[None, :]
    for jj in range(4):
        tri[:, jj, :] = np.where(128 * jj + kk <= qq, 0.0, NEG)
    boh = np.zeros((16, 16, 128), np.float32)
    for b_ in range(16):
        boh[b_, b_, :] = 1.0
    ident = np.eye(128, dtype=np.float32)
    bf = ml_dtypes.bfloat16
    return tri.astype(bf), boh.astype(bf), ident.astype(bf), ident


def moba_masks(c):
    m1 = np.full((128, 2, 2, 16), NEG, np.float32)
    notown = np.ones((128, 2, 2, 16), np.float32)
    for s_, lst in enumerate(slot_lists(c)):
        nblk = 2 * len(lst)
        for half in range(2):
            own = nblk - 2 + half
            for blk in range(nblk):
                valid = lst[blk // 2] is not None
                m1[:, s_, half, blk] = 0.0 if (valid and blk < own) else NEG
            notown[:, s_, half, own] = 0.0
    return m1, notown


def emit_moba_attn(P, qT_d, kT_d, v_d, kmT_d, m1_d, notown_d, tri_d, boh_d, identb_d, identf_d, w_o):
    P.begin_phase()
    P.nrot = 6
    OT = P.sb("OT", [128, NCH, NT], BF16, phase=True)
    kmT = P.sb("kmT", [128, NH, 24], BF16, phase=True)
    m1 = P.sb("m1", [128, 2, 2, 16], F32, phase=True)
    notown = P.sb("notown", [128, 2, 2, 16], F32, phase=True)
    tri = P.sb("tri", [128, 4, TB], BF16, phase=True)
    boh = P.sb("boh", [16, 16, 128], BF16, phase=True)
    identb = P.sb("identb", [128, 128], BF16, phase=True)
    identf = P.sb("identf", [128, 128], F32, phase=True)
    ac = P.tok("ac")
    for dst, src, nd in ((kmT, kmT_d, 3), (m1, m1_d, 4), (notown, notown_d, 4), (tri, tri_d, 3), (boh, boh_d, 3), (identb, identb_d, 2), (identf, identf_d, 2)):
        P.dma("sp", dst[(slice(None),) * nd], src, r=(), w=(ac,))
    scale = HD ** -0.5
    units = [(h, s_) for h in range(NH) for s_ in range(2)]
    hbuf = {}
    ust = {}

    def geom(s_):
        nblk = 8 if s_ == 0 else 16
        return nblk, (0 if s_ == 0 else 8), 2 * nblk, (0 if s_ == 0 else 16)

    def pre0(u):
        h, s_ = units[u]
        if s_ == 0:
            kT, kTt = P.rot("kT", 2, [128, 48 * 128], BF16)
            vv, vvt = P.rot("vv", 2, [128, 48, 128], BF16)
            qh, qht = P.rot("qh", 2, [128, NT], BF16)
            P.dma("sp", qh[:, :], qT_d[h], r=(), w=(qht,))
            P.dma("sp", kT[:, :], kT_d[h], r=(), w=(kTt,))
            P.dma("sp", vv[:, :, :], v_d[h], r=(), w=(vvt,))
            hbuf[h] = (kT, kTt, vv, vvt, qh, qht)
        kT, kTt, vv, vvt, qh, qht = hbuf[h]
        nblk, boff, nkt, ktoff = geom(s_)
        gb, gbt = P.bank()
        for qt in range(4):
            P.mm(gb[:, qt * 16:qt * 16 + nblk], qh[:, s_ * TB + qt * 128:s_ * TB + (qt + 1) * 128], kmT[:, h, boff:boff + nblk],
                 True, True, r=(qht, ac), w=(gbt,))
        sbs = []
        for qt in range(4):
            gm, gmt = P.rot("gm", 4, [128, 16], F32)
            P.tt("dve", gm[:, 0:nblk], gb[:, qt * 16:qt * 16 + nblk], m1[:, s_, qt // 2, 0:nblk], ALU.add, r=(gbt, ac), w=(gmt,))
            mx, mxt = P.rot("mx", 4, [128, 8], F32)
            P.op("dve", lambda e, o=mx[:, :], i=gm[:, 0:nblk]: e.max(out=o, in_=i), r=(gmt,), w=(mxt,))
            sb1, sb1t = P.rot("sb1", 8, [128, 16], F32)
            P.ts("dve", sb1[:, 0:nblk], gm[:, 0:nblk], mx[:, 2:3], -NEG, ALU.is_ge, ALU.mult, r=(gmt, mxt), w=(sb1t,))
            P.stt(sb1[:, 0:nblk], sb1[:, 0:nblk], NEG, m1[:, s_, qt // 2, 0:nblk], ALU.add, ALU.add, r=(sb1t, ac), w=(sb1t,))
            P.tt("dve", sb1[:, 0:nblk], sb1[:, 0:nblk], notown[:, s_, qt // 2, 0:nblk], ALU.mult, r=(sb1t, ac), w=(sb1t,))
            sbs.append((sb1, sb1t))
        ust[u] = {"sbs": sbs}

    def pre1(u):
        h, s_ = units[u]
        nblk, boff, nkt, ktoff = geom(s_)
        tbk, tbkt = P.bank()
        for qt in range(4):
            sb1, sb1t = ust[u]["sbs"][qt]
            P.op("pe", lambda e, o=tbk[0:nblk, qt * 128:(qt + 1) * 128], i=sb1[:, 0:nblk], idn=identf[:, :]: e.transpose(out=o, in_=i, identity=idn),
                 r=(sb1t, ac), w=(tbkt,))
        selT, selTt = P.rot("selT", 2, [16, TB], BF16)
        P.copy("act", selT[0:nblk, :], tbk[0:nblk, :], r=(tbkt,), w=(selTt,))
        ust[u]["selT"] = (selT, selTt)

    def main(u):
        h, s_ = units[u]
        kT, kTt, vv, vvt, qh, qht = hbuf[h]
        nblk, boff, nkt, ktoff = geom(s_)
        selT, selTt = ust[u]["selT"]
        qs = slice(s_ * TB, (s_ + 1) * TB)
        ob, obt = P.accbank(0)
        db, dbt = P.accbank(1)
        items = []
        for kt in range(nkt):
            st = {}

            def sA(kt=kt, st=st):
                g = ktoff + kt
                if u + 1 < len(units):
                    if kt == 2:
                        pre0(u + 1)
                    if kt == nkt // 2 + 2:
                        pre1(u + 1)
                sbk, sbkt = P.bank()
                diag = kt >= nkt - 4
                P.mm(sbk[:, :], kT[:, g * 128:(g + 1) * 128], qh[:, qs], True, False, r=(kTt, qht), w=(sbkt,))
                if not DBG.get("nobias"):
                    P.mm(sbk[:, :], boh[0:nblk, kt // 2, :], selT[0:nblk, :], False, not diag, r=(ac, selTt), w=(sbkt,))
                else:
                    P.mm(sbk[:, 0:8], boh[0:nblk, kt // 2, :], selT[0:nblk, 0:8], False, not diag, r=(ac, selTt), w=(sbkt,))
                if diag:
                    P.mm(sbk[:, :], identb[:, :], tri[:, kt - (nkt - 4), :], False, True, r=(ac,), w=(sbkt,))
                st["s"] = (sbk, sbkt)

            def sB(kt=kt, st=st):
                g = ktoff + kt
                sbk, sbkt = st["s"]
                E, Et = P.rot("E", 4, [128, TB], BF16)
                if DBG.get("noexp"):
                    P.act(E[:, 0:8], sbk[:, 0:8], AF.Exp, r=(sbkt,), w=(Et,), scale=scale)
                else:
                    P.act(E[:, :], sbk[:, :], AF.Exp, r=(sbkt,), w=(Et,), scale=scale)
                P.mm(ob[:, :], vv[:, g, :], E[:, :], kt == 0, kt == nkt - 1, r=(vvt, Et), w=(obt,))
                P.mm(db[:, :], P.ones[:, :], E[:, :], kt == 0, kt == nkt - 1, r=(P.tok("ones"), Et), w=(dbt,))
            items.append((sA, sB))
        pipeline(items, (0, 2))
        rden, rdent = P.rot("rden", 2, [128, TB], F32)
        P.op("dve", lambda e, o=rden[:, :], i=db[:, :]: e.reciprocal(out=o, in_=i), r=(dbt,), w=(rdent,))
        P.tt("dve", OT[:, h, qs], ob[:, :], rden[:, :], ALU.mult, r=(obt, rdent), w=(P.tok(("OT", h, s_)),))
        del ust[u]

    pre0(0)
    pre1(0)
    for u in range(len(units)):
        main(u)
    emit_outproj(P, w_o, OT, lambda kc, tb: P.tok(("OT", kc, tb)))
    P.end_phase()


V0B = Vecs(["mlp_norm", "ple_norm"])


def build_L0b():
    P = Prog()
    xT = P.din("hT", [128, NCH, NT], F32)
    pT = P.din("pT", [128, 2, NT], F32)
    vecs = P.din("vecs", [128, V0B.n()], F32)
    qT_d = P.din("qT", [NH, 128, NT], BF16)
    kT_d = P.din("kT", [NH, 128, 48 * 128], BF16)
    v_d = P.din("v", [NH, 128, 48, 128], BF16)
    kmT_d = P.din("kmT", [128, NH, 24], BF16)
    m1_d = P.din("m1", [128, 2, 2, 16], F32)
    notown_d = P.din("notown", [128, 2, 2, 16], F32)
    tri_d = P.din("tri", [128, 4, TB], BF16)
    boh_d = P.din("boh", [16, 16, 128], BF16)
    identb_d = P.din("identb", [128, 128], BF16)
    identf_d = P.din("identf", [128, 128], F32)
    w_o = P.din("w_o", [D, D], F32)
    w1 = P.din("mlp_w1", [D, DFF], F32)
    w2 = P.din("mlp_w2", [DFF, D], F32)
    wg = P.din("ple_gate", [D, D], F32)
    wp = P.din("ple_proj", [256, D], F32)
    out = P.dout("hT_out", [128, NCH, NT], F32)

    def body():
        common_setup(P, V0B.n())
        P.dma("sp", P.vecs[:, :], vecs[:, :], r=(), w=(P.tok("vecs"),))
        load_h(P, xT)
        vec = lambda n: V0B.ap(P, n)
        if DBG["mix"]:
            emit_moba_attn(P, qT_d, kT_d, v_d, kmT_d, m1_d, notown_d, tri_d, boh_d, identb_d, identf_d, w_o)
        emit_mlp_ple(P, 0, w1, w2, wg, wp, pT, vec)
        store_h(P, out)
    return run_two_pass(P, body)


def list_tokens(lst):
    return np.concatenate([(np.arange(k * TB, (k + 1) * TB) if k is not None else np.full(TB, -1)) for k in lst])


def gather_rows(a, idx):
    out = a[np.maximum(idx, 0)]
    out[idx < 0] = 0
    return out


def layer0b(inp, x, qkv):
    nc = build_L0b()
    vecs = V0B.host({"mlp_norm": inp["mlp_norm"][0], "ple_norm": inp["ple_norm"][0]})
    tri, boh, identb, identf = attn_consts()
    in_maps = []
    for c in range(NCORES):
        b = c // 4
        tk = core_tokens(c)
        lists = slot_lists(c)
        idx = np.concatenate([list_tokens(l) for l in lists])
        kl = gather_rows(qkv["k"][b], idx)
        vl = gather_rows(qkv["v"][b], idx).reshape(48, 128, NH, HD)
        bidx = np.concatenate([np.repeat(np.array([(-1 if k is None else k) for k in l]), 2) * 2 + np.tile([0, 1], len(l)) for l in lists])
        bidx = np.where(bidx < 0, -1, bidx)
        kml = gather_rows(qkv["kmean"][b], bidx)
        m1, notown = moba_masks(c)
        in_maps.append({
            "hT": to_fm(x[b][tk]), "pT": to_fm(inp["p"][0, b][tk]), "vecs": vecs,
            "qT": np.ascontiguousarray(qkv["q"][b][tk].transpose(1, 2, 0)),
            "kT": np.ascontiguousarray(kl.transpose(1, 2, 0)),
            "v": np.ascontiguousarray(vl.transpose(2, 1, 0, 3)),
            "kmT": np.ascontiguousarray(kml.transpose(2, 1, 0)),
            "m1": m1, "notown": notown, "tri": tri, "boh": boh, "identb": identb, "identf": identf,
            "w_o": inp["moba_w_o"][0],
            "mlp_w1": inp["mlp_w1"][0], "mlp_w2": inp["mlp_w2"][0], "ple_gate": inp["ple_gate"][0], "ple_proj": inp["ple_proj"][0],
        })
    res = run_prog(nc, in_maps)
    h0 = np.zeros_like(x)
    for c in range(NCORES):
        h0[c // 4][core_tokens(c)] = from_fm(res[c]["hT_out"])
    return h0


V2A = Vecs(["mixer_norm"])
GELU_C = float(2.0 * np.sqrt(2.0 / np.pi))


def build_L2a():
    P = Prog()
    P.nwslot = 5
    P.wlive = 4
    hT = P.din("hT", [128, NCH, NT], F32)
    hhalo = P.din("hhalo", [128, NCH, 32], F32)
    vecs = P.din("vecs", [128, V2A.n()], F32)
    hc_d = P.din("hc", [128, 8], F32)
    rotT_d = P.din("rotT", [128, 128], BF16)
    identf_d = P.din("identf", [128, 128], F32)
    posb = P.din("posb", [128, NT], I32)
    posT_d = P.din("cmp_posT", [128, 2, 32], F32)
    wq = P.din("w_q", [D, D], F32)
    wkv = P.din("w_kv", [D, 3072], F32)
    wgate = P.din("w_gate", [D, 48], F32)
    cw1 = P.din("cmp_w1", [2 * 4096, 128], F32)
    cw2 = P.din("cmp_w2", [2 * 128, 128], F32)
    qc_o = P.dout("qcT", [NH, 128, NT], BF16)
    qr_o = P.dout("qrT", [NH, 128, NT], BF16)
    ks_o = P.dout("kslcT", [4, 128, NT], BF16)
    kw_o = P.dout("kwinT", [4, 128, NT], BF16)
    vs_o = P.dout("vslc", [NT, 512], BF16)
    vw_o = P.dout("vwin", [NT, 512], BF16)
    kc_o = P.dout("kcmpT", [4, 128, 64], BF16)
    vc_o = P.dout("vcmpT", [4, 128, 64], BF16)
    g_o = P.dout("gT", [48, NT], BF16)

    def body():
        common_setup(P, V2A.n())
        P.dma("sp", P.vecs[:, :], vecs[:, :], r=(), w=(P.tok("vecs"),))
        load_headconsts(P, hc_d, rotT_d, 8)
        identf = P.sb("identf", [128, 128], F32)
        posT = P.sb("posT", [128, 2, 32], BF16)
        P.dma("sp", identf[:, :], identf_d[:, :], r=(), w=(P.tok("hc"),))
        P.dma("pool", posT[:, :, :], posT_d[:, :, :], r=(), w=(P.tok("hc"),))
        load_h(P, hT)
        hx = P.sb("hx", [128, NCH, 32], F32)
        xh = P.sb("xh", [128, NCH, 32], BF16)
        P.dma("sp", hx[:, :, :], hhalo[:, :, :], r=(), w=(P.tok("hx"),))
        emit_rope_tables(P, posb, P.hc)
        xn = P.sb("xn", [128, NCH, NT], BF16)
        xt = lambda c, tb: P.tok(("xn", c, tb))
        xht = lambda c, tb: P.tok(("xh", c))
        gain = V2A.ap(P, "mixer_norm")
        emit_norm(P, hx, gain, xh, xht, ntb=1, src_tokf=lambda c, tb: P.tok("hx"), tbw=32)
        emit_norm(P, P.h, gain, xn, xt)
        hct = P.tok("hc")

        def proj(wt, wtk, m, tb):
            bk, bkt = P.bank()
            cs = slice(tb * TB, (tb + 1) * TB)
            for kc in range(NCH):
                P.mm(bk[:, :], wt[:, kc, m * 128:(m + 1) * 128], xn[:, kc, cs], kc == 0, kc == NCH - 1, r=(wtk, xt(kc, tb)), w=(bkt,))
            return bk, bkt

        items = []
        wcur = {}
        specs = [("q", wq, j * 256, 0, 2 * j + m, m, (qc_o, qr_o), j) for j in range(8) for m in range(2)]
        for idx, gcol, outd in ((2, 1, ks_o), (4, 5, kw_o)):
            specs += [("k", wkv, idx * 512 + j * 256, gcol, 2 * j + m, m, (outd,), (idx, j)) for j in range(2) for m in range(2)]
        for kind, wsrc, col0, gcol, hd, m, outs, wkey in specs:
            for tb in range(NTB):
                cs = slice(tb * TB, (tb + 1) * TB)

                def make_proj(kind=kind, wsrc=wsrc, col0=col0, m=m, tb=tb, cs=cs):
                    if m == 0 and tb == 0:
                        wcur["w"] = P.wnext(wsrc, 0, 16, col0, 256)
                    wt, wtk = wcur["w"]
                    bk, bkt = P.bank()
                    for kc in range(NCH):
                        P.mm(bk[:, :], wt[:, kc, m * 128:(m + 1) * 128], xn[:, kc, cs], kc == 0, kc == NCH - 1, r=(wtk, xt(kc, tb)), w=(bkt,))
                    return bk, bkt

                def sink(qf, qft, qn, qnt, kind=kind, hd=hd, tb=tb, cs=cs, outs=outs, wkey=wkey):
                    key = (kind, wkey, hd)
                    if tb == 0:
                        wcur[key] = (P.rot("stage", 2, [128, NT], BF16, phase=False), P.rot("stage2", 2, [128, NT], BF16, phase=False) if kind == "q" else None)
                    (sr, srt), sc_ = wcur[key]
                    P.copy("act", sr[:, cs], qf[:, :], r=(qft, srt), w=(srt,))
                    if kind == "q":
                        sc, sct = sc_
                        P.copy("pool", sc[:, cs], qn[:, :], r=(qnt, sct), w=(sct,))
                    if tb == NTB - 1:
                        if kind == "q":
                            P.dma("sp", outs[0][hd], sc_[0][:, :], r=(sc_[1],), w=(), is_out=True)
                            P.dma("sp", outs[1][hd], sr[:, :], r=(srt,), w=(), is_out=True)
                        else:
                            P.dma("sp", outs[0][hd], sr[:, :], r=(srt,), w=(), is_out=True)
                items.append(headnorm_stages(P, make_proj, P.hc[:, gcol:gcol + 1], cs, True, sink))
        pipeline(items, (0, 1, 2))
        for idx, outd in ((3, vs_o), (5, vw_o)):
            for j in range(2):
                wt, wtk = P.wnext(wkv, 0, 16, idx * 512 + j * 256, 256)
                vs, vst = P.rot("vstage", 2, [128, 8, 256], BF16, phase=False)
                for tt_ in range(NT // 128):
                    bk, bkt = P.bank()
                    for kc in range(NCH):
                        P.mm(bk[:, 0:256], xn[:, kc, tt_ * 128:(tt_ + 1) * 128], wt[:, kc, :], kc == 0, kc == NCH - 1,
                             r=(wtk, xt(kc, tt_ // 4)), w=(bkt,))
                    P.copy("act", vs[:, tt_, :], bk[:, 0:256], r=(bkt, vst), w=(vst,))
                P.dma("sp", outd[:, j * 256:(j + 1) * 256].rearrange("(t p) n -> p t n", p=128), vs[:, :, :], r=(vst,), w=(), is_out=True)
        wt, wtk = P.wnext(wgate, 0, 16, 0, 48)
        gts = P.sb("gts", [48, NT], BF16)
        for tt_ in range(NT // 128):
            bk, bkt = P.bank()
            for kc in range(NCH):
                P.mm(bk[:, 0:48], xn[:, kc, tt_ * 128:(tt_ + 1) * 128], wt[:, kc, :], kc == 0, kc == NCH - 1, r=(wtk, xt(kc, tt_ // 4)), w=(bkt,))
            gs, gst = P.rot("gsig", 2, [128, 48], F32, phase=False)
            P.act(gs[:, :], bk[:, 0:48], AF.Sigmoid, r=(bkt,), w=(gst,))
            b2, b2t = P.bank()
            P.op("pe", lambda e, o=b2[0:48, 0:128], i=gs[:, :], idn=identf[:, :]: e.transpose(out=o, in_=i, identity=idn), r=(gst, hct), w=(b2t,))
            P.copy("act", gts[:, tt_ * 128:(tt_ + 1) * 128], b2[0:48, 0:128], r=(b2t, P.tok("gts")), w=(P.tok("gts"),))
        P.dma("sp", g_o[:, :], gts[:, :], r=(P.tok("gts"),), w=(), is_out=True)
        kcs = P.sb("kcs", [128, 4, 64], BF16)
        vcs = P.sb("vcs", [128, 4, 64], BF16)
        for idx in range(2):
            w1t, w1k = P.wnext(cw1, idx * 4096, 32, 0, 128)
            w2t, w2k = P.wnext(cw2, idx * 128, 1, 0, 128)
            pbk, pbt = P.bank()
            for l in range(32):
                P.mm(pbk[:, 0:1], w1t[:, l, :], posT[:, idx, l:l + 1], l == 0, l == 31, r=(w1k, hct), w=(pbt,))
            pb, pbst = P.rot("posb", 2, [128, 1], F32, phase=False)
            P.copy("act", pb[:, :], pbk[:, 0:1], r=(pbt,), w=(pbst,))
            for j in range(2):
                wt, wtk = P.wnext(wkv, 0, 16, idx * 512 + j * 256, 256)
                for m in range(2):
                    g = 2 * j + m
                    for s_ in range(NTB):
                        raw, rawt = P.rot("craw", 2, [128, 16 + TB], BF16, phase=False)
                        bk, bkt = proj(wt, wtk, m, s_)
                        P.copy("act", raw[:, 16:16 + TB], bk[:, :], r=(bkt, rawt), w=(rawt,))
                        bh, bht = P.bank()
                        for kc in range(NCH):
                            P.mm(bh[:, 0:16], wt[:, kc, m * 128:(m + 1) * 128], xh[:, kc, s_ * 16:(s_ + 1) * 16], kc == 0, kc == NCH - 1,
                                 r=(wtk, xht(kc, 0)), w=(bht,))
                        P.copy("act", raw[:, 0:16], bh[:, 0:16], r=(bht, rawt), w=(rawt,))
                        cb, cbt = P.bank()
                        for l in range(32):
                            P.mm(cb[:, 0:32], w1t[:, l, :], raw[:, l:l + 16 * 31 + 1:16], l == 0, l == 31, r=(w1k, rawt), w=(cbt,))
                        x, xt_ = P.rot("cx", 2, [128, 32], F32, phase=False)
                        P.ts("dve", x[:, :], cb[:, 0:32], pb[:, 0:1], None, ALU.add, None, r=(cbt, pbst), w=(xt_,))
                        u, ut = P.rot("cu", 2, [128, 32], F32, phase=False)
                        P.tt("dve", u[:, :], x[:, :], x[:, :], ALU.mult, r=(xt_,), w=(ut,))
                        P.ts("dve", u[:, :], u[:, :], 0.044715, 1.0, ALU.mult, ALU.add, r=(ut,), w=(ut,))
                        P.tt("dve", u[:, :], u[:, :], x[:, :], ALU.mult, r=(ut, xt_), w=(ut,))
                        P.act(u[:, :], u[:, :], AF.Sigmoid, r=(ut,), w=(ut,), scale=GELU_C)
                        ge, get = P.rot("cge", 2, [128, 32], BF16, phase=False)
                        P.tt("dve", ge[:, :], u[:, :], x[:, :], ALU.mult, r=(ut, xt_), w=(get,))
                        ob, obt = P.bank()
                        P.mm(ob[:, 0:32], w2t[:, 0, :], ge[:, :], True, True, r=(w2k, get), w=(obt,))
                        if idx == 1:
                            P.copy("act", vcs[:, g, s_ * 32:(s_ + 1) * 32], ob[:, 0:32], r=(obt, P.tok("vcs")), w=(P.tok("vcs"),))
                        else:
                            sq, sqt = P.rot("csq", 2, [128, 32], BF16, phase=False)
                            P.act(sq[:, :], ob[:, 0:32], AF.Square, r=(obt,), w=(sqt,))
                            b2, b2t = P.bank()
                            P.mm(b2[:, 0:32], P.ones[:, :], sq[:, :], True, True, r=(sqt, P.tok("ones")), w=(b2t,))
                            rs, rst = P.rot("crs", 2, [128, 32], F32, phase=False)
                            P.act(rs[:, :], b2[:, 0:32], AF.Sqrt, r=(b2t, P.tok("consts")), w=(rst,), scale=1.0 / HD, bias=P.epsc[:, 0:1])
                            P.op("dve", lambda e, o=rs[:, :]: e.reciprocal(out=o, in_=o), r=(rst,), w=(rst,))
                            P.stt(kcs[:, g, s_ * 32:(s_ + 1) * 32], ob[:, 0:32], P.hc[:, 4:5], rs[:, :], ALU.mult, ALU.mult,
                                  r=(obt, rst, hct, P.tok("kcs")), w=(P.tok("kcs"),))
        for g in range(4):
            P.dma("sp", kc_o[g], kcs[:, g, :], r=(P.tok("kcs"),), w=(), is_out=True)
            P.dma("sp", vc_o[g], vcs[:, g, :], r=(P.tok("vcs"),), w=(), is_out=True)
    return run_two_pass(P, body)


def layer2a(inp, h1):
    nc = build_L2a()
    vecs = V2A.host({"mixer_norm": inp["mixer_norm"][2]})
    freq, rotT = rope_consts()
    hc = np.zeros((128, 8), np.float32)
    hc[:, 0] = inp["nsa_q_gain"][0]
    hc[:, 1] = inp["nsa_k_gain"][0, 1]
    hc[:, 2] = freq[:, 0]
    hc[:, 3] = -PI
    hc[:, 4] = inp["nsa_k_gain"][0, 0]
    hc[:, 5] = inp["nsa_k_gain"][0, 2]
    identf = np.eye(128, dtype=np.float32)
    posT = np.ascontiguousarray(inp["nsa_cmp_pos"][0].transpose(2, 0, 1))
    in_maps = []
    for c in range(NCORES):
        b = c // 4
        tk = core_tokens(c)
        in_maps.append({"hT": to_fm(h1[b][tk]), "hhalo": to_fm(halo_rows(h1[b], c, 16)), "vecs": vecs, "hc": hc, "rotT": rotT, "identf": identf,
                        "posb": np.ascontiguousarray(np.broadcast_to(inp["positions"][b][tk].astype(np.int32)[None, :], (128, NT))),
                        "cmp_posT": posT, "w_q": inp["nsa_w_q"][0], "w_kv": inp["nsa_w_kv"][0], "w_gate": inp["nsa_w_gate"][0],
                        "cmp_w1": np.ascontiguousarray(inp["nsa_cmp_w1"][0].reshape(2 * 4096, 128)),
                        "cmp_w2": np.ascontiguousarray(inp["nsa_cmp_w2"][0].reshape(2 * 128, 128))})
    res = run_prog(nc, in_maps)
    bf = ml_dtypes.bfloat16
    o = {"qc": np.zeros((B, S, NH, HD), bf), "qr": np.zeros((B, S, NH, HD), bf),
         "kslc": np.zeros((B, S, 4, HD), bf), "kwin": np.zeros((B, S, 4, HD), bf),
         "vslc": np.zeros((B, S, 512), bf), "vwin": np.zeros((B, S, 512), bf),
         "kcmp": np.zeros((B, 8, 32, 4, HD), bf), "vcmp": np.zeros((B, 8, 32, 4, HD), bf),
         "gT": np.zeros((B, S, 48), bf)}
    for c in range(NCORES):
        b = c // 4
        tk = core_tokens(c)
        r = res[c]
        o["qc"][b, tk] = r["qcT"].transpose(2, 0, 1)
        o["qr"][b, tk] = r["qrT"].transpose(2, 0, 1)
        o["kslc"][b, tk] = r["kslcT"].transpose(2, 0, 1)
        o["kwin"][b, tk] = r["kwinT"].transpose(2, 0, 1)
        o["vslc"][b, tk] = r["vslc"]
        o["vwin"][b, tk] = r["vwin"]
        o["gT"][b, tk] = r["gT"].T
        for s_, ch in enumerate(core_chunks(c)):
            o["kcmp"][b, ch] = r["kcmpT"][:, :, s_ * 32:(s_ + 1) * 32].transpose(2, 0, 1)
            o["vcmp"][b, ch] = r["vcmpT"][:, :, s_ * 32:(s_ + 1) * 32].transpose(2, 0, 1)
    return o


BIG = 1.0e30


def nsa_consts():
    bf = ml_dtypes.bfloat16
    kk = np.arange(128)[:, None]
    qq = np.arange(TB)[None, :]
    band = np.zeros((128, 4, TB), np.float32)
    for jj in range(4):
        band[:, jj, :] = np.where(128 * jj + kk > qq, 0.0, NEG)
    cmpdiag = np.zeros((128, TB), np.float32)
    for i in range(32):
        cmpdiag[96 + i, :] = np.where(16 * i + 15 <= np.arange(TB), 0.0, NEG)
    boh2 = np.zeros((64, 32, 128), np.float32)
    for kt in range(32):
        boh2[2 * kt, kt, 0:64] = 1.0
        boh2[2 * kt + 1, kt, 64:128] = 1.0
    sel48 = np.zeros((48, 48, 128), np.float32)
    for i in range(48):
        sel48[i, i, :] = 1.0
    ovl = np.zeros((128, 3, 64), np.float32)
    for s_, nch in enumerate((4, 8)):
        for e in range(32 * nch):
            ci, i = divmod(e, 32)
            n0, n1 = 512 * ci - 16 + 16 * i, 512 * ci + 16 + 16 * i
            for jl in range(8 * nch):
                j0, j1 = 64 * jl, 64 * jl + 64
                if n0 < j1 and j0 < n1:
                    tile = 0 if s_ == 0 else 1 + e // 128
                    ovl[e % 128, tile, jl] = 1.0
    return band.astype(bf), cmpdiag.astype(bf), boh2.astype(bf), sel48.astype(bf), ovl.astype(bf)


def nsa_core_consts(c):
    lists = slot_lists(c)
    cmppad = np.zeros((128, 3), np.float32)
    slcm = np.zeros((128, 2, 4, 3, 64), np.float32)
    winpad = np.zeros((128, 2), np.float32)
    for s_, lst in enumerate(lists):
        for e in range(32 * len(lst)):
            ci, i = divmod(e, 32)
            pad = lst[ci] is None or (lst[ci] == 0 and i == 0)
            tile = 0 if s_ == 0 else 1 + e // 128
            cmppad[e % 128, tile] = NEG if pad else 0.0
        nsel = 8 * len(lst)
        npad = sum(1 for k in lst if k is None)
        first = 8 * npad
        for qt in range(4):
            for p in range(128):
                cur = nsel - 8 + (qt * 128 + p) // 64
                for jl in range(64):
                    if jl >= nsel or lst[jl // 8] is None or jl > cur:
                        slcm[p, s_, qt, 0, jl], slcm[p, s_, qt, 1, jl], slcm[p, s_, qt, 2, jl] = 0.0, -BIG, NEG
                    elif jl == cur or jl == first:
                        slcm[p, s_, qt, 0, jl], slcm[p, s_, qt, 1, jl] = 0.0, BIG
                    else:
                        slcm[p, s_, qt, 0, jl] = 1.0
        winpad[:, s_] = NEG if core_chunks(c)[s_] == 0 else 0.0
    return cmppad, slcm, winpad


def emit_nsa_attn(P, dd, OT):
    P.begin_phase()
    P.nrot = 4
    accn = [0]

    def accpair():
        i = accn[0] % 2
        accn[0] += 1
        return P.accbank(2 * i), P.accbank(2 * i + 1)
    ph = dict(phase=True)
    tri = P.sb("tri", [128, 4, TB], BF16, **ph)
    band = P.sb("band", [128, 4, TB], BF16, **ph)
    cmpdiag = P.sb("cmpdiag", [128, TB], BF16, **ph)
    boh2 = P.sb("boh2", [64, 32, 128], BF16, **ph)
    sel48 = P.sb("sel48", [48, 48, 128], BF16, **ph)
    ovl = P.sb("ovl", [128, 3, 64], BF16, **ph)
    identb = P.sb("identb", [128, 128], BF16, **ph)
    identf = P.sb("identf", [128, 128], F32, **ph)
    cmppad = P.sb("cmppad", [128, 3], F32, **ph)
    slcm = P.sb("slcm", [128, 2, 4, 3, 64], F32, **ph)
    winpad = P.sb("winpad", [128, 2], F32, **ph)
    gT = P.sb("gTs", [48, NT], BF16, **ph)
    ac = P.tok("ac")
    for dst, name, nd in ((tri, "tri", 3), (band, "band", 3), (cmpdiag, "cmpdiag", 2), (boh2, "boh2", 3), (sel48, "sel48", 3), (ovl, "ovl", 3),
                          (identb, "identb", 2), (identf, "identf", 2), (cmppad, "cmppad", 2), (slcm, "slcm", 5), (winpad, "winpad", 2), (gT, "gT", 2)):
        P.dma("sp", dst[(slice(None),) * nd], dd[name], r=(), w=(ac,))
    scale = HD ** -0.5
    onest = P.tok("ones")
    oacc = [P.sb("oacc%d" % r, [128, TB], F32, **ph) for r in range(4)]
    psum_ = [P.sb("psumT%d" % t, [128, TB], F32, **ph) for t in range(2)]
    pb = [P.sb("pb%d" % t, [128, TB], BF16, **ph) for t in range(2)]

    def finish_branch(h, br, r, qs, ob, obt, db, dbt, first, guard):
        rden, rdent = P.rot("rden", 2, [128, TB], F32)
        if guard:
            P.ts("dve", rden[:, :], db[:, :], 1e-30, None, ALU.max, None, r=(dbt,), w=(rdent,))
            P.op("dve", lambda e, o=rden[:, :]: e.reciprocal(out=o, in_=o), r=(rdent,), w=(rdent,))
        else:
            P.op("dve", lambda e, o=rden[:, :], i=db[:, :]: e.reciprocal(out=o, in_=i), r=(dbt,), w=(rdent,))
        gb, gbt = P.bank()
        P.mm(gb[:, :], sel48[0:48, h * 3 + br, :], gT[0:48, qs], True, True, r=(ac,), w=(gbt,))
        cf, cft = P.rot("coef", 2, [128, TB], F32)
        P.tt("dve", cf[:, :], gb[:, :], rden[:, :], ALU.mult, r=(gbt, rdent), w=(cft,))
        oat = P.tok(("oacc", r))
        if first:
            P.tt("dve", oacc[r][:, :], ob[:, :], cf[:, :], ALU.mult, r=(obt, cft), w=(oat,))
        else:
            tm, tmt = P.rot("otmp", 2, [128, TB], F32)
            P.tt("dve", tm[:, :], ob[:, :], cf[:, :], ALU.mult, r=(obt, cft), w=(tmt,))
            P.tt("pool", oacc[r][:, :], oacc[r][:, :], tm[:, :], ALU.add, r=(tmt, oat), w=(oat,))
        return rden, rdent

    for g in range(4):
        ks, kst = P.rot("ks", 1, [128, 48 * 128], BF16)
        vs, vst = P.rot("vs", 1, [128, 48, 128], BF16)
        kw, kwt = P.rot("kw", 1, [128, 16 * 128], BF16)
        vw, vwt = P.rot("vw", 1, [128, 16, 128], BF16)
        kc, kct = P.rot("kc", 1, [128, 384], BF16)
        vc, vct = P.rot("vc", 1, [128, 3, 128], BF16)
        P.dma("sp", kc[:, :], dd["kcmpT"][g], r=(), w=(kct,))
        P.dma("sp", vc[:, :, :], dd["vcmp"][g], r=(), w=(vct,))
        P.dma("sp", ks[:, :], dd["kslcT"][g], r=(), w=(kst,))
        P.dma("sp", vs[:, :, :], dd["vslc"][g], r=(), w=(vst,))
        P.dma("sp", kw[:, :], dd["kwinT"][g], r=(), w=(kwt,))
        P.dma("sp", vw[:, :, :], dd["vwin"][g], r=(), w=(vwt,))
        qcs, qrs = [], []
        for r in range(4):
            h = 4 * g + r
            qc, qct = P.rot("qc", 4, [128, NT], BF16)
            qr, qrt = P.rot("qr", 4, [128, NT], BF16)
            P.dma("sp", qc[:, :], dd["qcT"][h], r=(), w=(qct,))
            P.dma("sp", qr[:, :], dd["qrT"][h], r=(), w=(qrt,))
            qcs.append((qc, qct))
            qrs.append((qr, qrt))
        for s_ in range(2):
            qs = slice(s_ * TB, (s_ + 1) * TB)
            nch = 4 if s_ == 0 else 8
            nsel = 8 * nch
            ctiles = [0] if s_ == 0 else [1, 2]
            for r in range(4):
                h = 4 * g + r
                qc, qct = qcs[r]
                (ob, obt), (db, dbt) = accpair()
                Es = []
                for ti, ct in enumerate(ctiles):
                    sbk, sbkt = P.bank()
                    last = ti == len(ctiles) - 1
                    P.mm(sbk[:, :], kc[:, ct * 128:(ct + 1) * 128], qc[:, qs], True, not last, r=(kct, qct), w=(sbkt,))
                    if last:
                        P.mm(sbk[:, :], identb[:, :], cmpdiag[:, :], False, True, r=(ac,), w=(sbkt,))
                    E, Et = P.rot("Ec", 2, [128, TB], BF16)
                    P.act(E[:, :], sbk[:, :], AF.Exp, r=(sbkt, ac), w=(Et,), scale=scale, bias=cmppad[:, ct:ct + 1])
                    P.mm(ob[:, :], vc[:, ct, :], E[:, :], ti == 0, last, r=(vct, Et), w=(obt,))
                    P.mm(db[:, :], P.ones[:, :], E[:, :], ti == 0, last, r=(onest, Et), w=(dbt,))
                    Es.append((E, Et))
                rden, rdent = finish_branch(h, 0, r, qs, ob, obt, db, dbt, True, True)
                for ti, (E, Et) in enumerate(Es):
                    pst = P.tok(("psumT", ti))
                    if r == 0:
                        P.tt("dve", psum_[ti][:, :], E[:, :], rden[:, :], ALU.mult, r=(Et, rdent), w=(pst,))
                    else:
                        tm, tmt = P.rot("otmp", 2, [128, TB], F32)
                        P.tt("dve", tm[:, :], E[:, :], rden[:, :], ALU.mult, r=(Et, rdent), w=(tmt,))
                        P.tt("pool", psum_[ti][:, :], psum_[ti][:, :], tm[:, :], ALU.add, r=(tmt, pst), w=(pst,))
            selst = {}

            def selA():
                for ti in range(len(ctiles)):
                    P.copy("act", pb[ti][:, :], psum_[ti][:, :], r=(P.tok(("psumT", ti)),), w=(P.tok(("pb", ti)),))
                sbs = []
                for qt in range(4):
                    ib, ibt = P.bank()
                    for ti, ct in enumerate(ctiles):
                        P.mm(ib[:, 0:nsel], pb[ti][:, qt * 128:(qt + 1) * 128], ovl[:, ct, 0:nsel], ti == 0, ti == len(ctiles) - 1,
                             r=(P.tok(("pb", ti)), ac), w=(ibt,))
                    im, imt = P.rot("impm", 4, [128, 64], F32)
                    P.tt("dve", im[:, 0:nsel], ib[:, 0:nsel], slcm[:, s_, qt, 0, 0:nsel], ALU.mult, r=(ibt, ac), w=(imt,))
                    P.tt("dve", im[:, 0:nsel], im[:, 0:nsel], slcm[:, s_, qt, 1, 0:nsel], ALU.add, r=(imt, ac), w=(imt,))
                    mx, mxt = P.rot("mx", 4, [128, 8], F32)
                    P.op("dve", lambda e, o=mx[:, :], i=im[:, 0:nsel]: e.max(out=o, in_=i), r=(imt,), w=(mxt,))
                    rp, rpt = P.rot("rep", 4, [128, 64], F32)
                    P.op("dve", lambda e, o=rp[:, 0:nsel], a=mx[:, :], i=im[:, 0:nsel]: e.match_replace(out=o, in_to_replace=a, in_values=i, imm_value=-2.0 * BIG),
                         r=(imt, mxt), w=(rpt,))
                    mx2, mx2t = P.rot("mx2", 4, [128, 8], F32)
                    P.op("dve", lambda e, o=mx2[:, :], i=rp[:, 0:nsel]: e.max(out=o, in_=i), r=(rpt,), w=(mx2t,))
                    sb1, sb1t = P.rot("sb1", 8, [128, 64], F32)
                    P.ts("dve", sb1[:, 0:nsel], im[:, 0:nsel], mx2[:, 7:8], -NEG, ALU.is_ge, ALU.mult, r=(imt, mx2t), w=(sb1t,))
                    P.stt(sb1[:, 0:nsel], sb1[:, 0:nsel], NEG, slcm[:, s_, qt, 2, 0:nsel], ALU.add, ALU.add, r=(sb1t, ac), w=(sb1t,))
                    sbs.append((sb1, sb1t))
                selst["sbs"] = sbs

            def selB():
                tbk, tbkt = P.bank()
                for qt in range(4):
                    sb1, sb1t = selst["sbs"][qt]
                    P.op("pe", lambda e, o=tbk[0:nsel, qt * 128:(qt + 1) * 128], i=sb1[:, 0:nsel], idn=identf[:, :]: e.transpose(out=o, in_=i, identity=idn),
                         r=(sb1t, ac), w=(tbkt,))
                selT, selTt = P.rot("selT", 2, [64, TB], BF16)
                P.copy("act", selT[0:nsel, :], tbk[0:nsel, :], r=(tbkt,), w=(selTt,))
                selst["selT"] = (selT, selTt)
            nkt = 4 * nch
            ktoff = 0 if s_ == 0 else 16
            def slc_branch(r):
                selT, selTt = selst["selT"]
                h = 4 * g + r
                qr, qrt = qrs[r]
                (ob, obt), (db, dbt) = accpair()
                items = []
                for kt in range(nkt):
                    st = {}

                    def sA(kt=kt, st=st, qr=qr, qrt=qrt):
                        gk = ktoff + kt
                        sbk, sbkt = P.bank()
                        diag = kt >= nkt - 4
                        P.mm(sbk[:, :], ks[:, gk * 128:(gk + 1) * 128], qr[:, qs], True, False, r=(kst, qrt), w=(sbkt,))
                        P.mm(sbk[:, :], boh2[0:nsel, kt, :], selT[0:nsel, :], False, not diag, r=(ac, selTt), w=(sbkt,))
                        if diag:
                            P.mm(sbk[:, :], identb[:, :], tri[:, kt - (nkt - 4), :], False, True, r=(ac,), w=(sbkt,))
                        st["s"] = (sbk, sbkt)

                    def sB(kt=kt, st=st, ob=ob, obt=obt, db=db, dbt=dbt):
                        gk = ktoff + kt
                        sbk, sbkt = st["s"]
                        E, Et = P.rot("E", 4, [128, TB], BF16)
                        P.act(E[:, :], sbk[:, :], AF.Exp, r=(sbkt,), w=(Et,), scale=scale)
                        P.mm(ob[:, :], vs[:, gk, :], E[:, :], kt == 0, kt == nkt - 1, r=(vst, Et), w=(obt,))
                        P.mm(db[:, :], P.ones[:, :], E[:, :], kt == 0, kt == nkt - 1, r=(onest, Et), w=(dbt,))
                    items.append((sA, sB))
                pipeline(items, (0, 2))
                finish_branch(h, 1, r, qs, ob, obt, db, dbt, False, False)
                P.copy("act", OT[:, h, qs], oacc[r][:, :], r=(P.tok(("oacc", r)),), w=(P.tok(("OT", h, s_)),))
            def win_branch(r):
                h = 4 * g + r
                qr, qrt = qrs[r]
                (ob, obt), (db, dbt) = accpair()
                items = []
                for kt in range(8):
                    st = {}

                    def sA(kt=kt, st=st, qr=qr, qrt=qrt):
                        gk = s_ * 8 + kt
                        sbk, sbkt = P.bank()
                        P.mm(sbk[:, :], kw[:, gk * 128:(gk + 1) * 128], qr[:, qs], True, False, r=(kwt, qrt), w=(sbkt,))
                        msk = band[:, kt, :] if kt < 4 else tri[:, kt - 4, :]
                        P.mm(sbk[:, :], identb[:, :], msk, False, True, r=(ac,), w=(sbkt,))
                        st["s"] = (sbk, sbkt)

                    def sB(kt=kt, st=st, ob=ob, obt=obt, db=db, dbt=dbt):
                        gk = s_ * 8 + kt
                        sbk, sbkt = st["s"]
                        E, Et = P.rot("E", 4, [128, TB], BF16)
                        if kt < 4:
                            P.act(E[:, :], sbk[:, :], AF.Exp, r=(sbkt, ac), w=(Et,), scale=scale, bias=winpad[:, s_:s_ + 1])
                        else:
                            P.act(E[:, :], sbk[:, :], AF.Exp, r=(sbkt,), w=(Et,), scale=scale)
                        P.mm(ob[:, :], vw[:, gk, :], E[:, :], kt == 0, kt == 7, r=(vwt, Et), w=(obt,))
                        P.mm(db[:, :], P.ones[:, :], E[:, :], kt == 0, kt == 7, r=(onest, Et), w=(dbt,))
                    items.append((sA, sB))
                pipeline(items, (0, 2))
                finish_branch(h, 2, r, qs, ob, obt, db, dbt, False, False)
            selA()
            win_branch(0)
            win_branch(1)
            selB()
            win_branch(2)
            win_branch(3)
            for r in range(4):
                slc_branch(r)
    P.end_phase()


V2B = Vecs(["mlp_norm", "ple_norm"])
L2B_IN = [("qcT", [NH, 128, NT], BF16), ("qrT", [NH, 128, NT], BF16), ("gT", [48, NT], BF16),
          ("kslcT", [4, 128, 48 * 128], BF16), ("vslc", [4, 128, 48, 128], BF16), ("kwinT", [4, 128, 16 * 128], BF16), ("vwin", [4, 128, 16, 128], BF16),
          ("kcmpT", [4, 128, 384], BF16), ("vcmp", [4, 128, 3, 128], BF16),
          ("tri", [128, 4, TB], BF16), ("band", [128, 4, TB], BF16), ("cmpdiag", [128, TB], BF16), ("boh2", [64, 32, 128], BF16),
          ("sel48", [48, 48, 128], BF16), ("ovl", [128, 3, 64], BF16), ("identb", [128, 128], BF16), ("identf", [128, 128], F32),
          ("cmppad", [128, 3], F32), ("slcm", [128, 2, 4, 3, 64], F32), ("winpad", [128, 2], F32)]


def build_L2b():
    P = Prog()
    hT = P.din("hT", [128, NCH, NT], F32)
    pT = P.din("pT", [128, 2, NT], F32)
    vecs = P.din("vecs", [128, V2B.n()], F32)
    dd = {name: P.din(name, shape, dt) for name, shape, dt in L2B_IN}
    w_o = P.din("w_o", [D, D], F32)
    w1 = P.din("mlp_w1", [D, DFF], F32)
    w2 = P.din("mlp_w2", [DFF, D], F32)
    wg = P.din("ple_gate", [D, D], F32)
    wp = P.din("ple_proj", [256, D], F32)
    out = P.dout("hT_out", [128, NCH, NT], F32)

    def body():
        common_setup(P, V2B.n(), alloc_h=False)
        P.dma("sp", P.vecs[:, :], vecs[:, :], r=(), w=(P.tok("vecs"),))
        OT = P.sb("OT", [128, NCH, NT], BF16)
        vec = lambda n: V2B.ap(P, n)
        if DBG["mix"]:
            emit_nsa_attn(P, dd, OT)
        P.h = P.sb("hT", [128, NCH, NT], F32)
        load_h(P, hT)
        if DBG["mix"]:
            P.begin_phase()
            emit_outproj(P, w_o, OT, lambda kc, tb: P.tok(("OT", kc, tb)))
            P.end_phase()
        emit_mlp_ple(P, 2, w1, w2, wg, wp, pT, vec)
        store_h(P, out)
    return run_two_pass(P, body)


def layer2b(inp, h1, a):
    nc = build_L2b()
    vecs = V2B.host({"mlp_norm": inp["mlp_norm"][2], "ple_norm": inp["ple_norm"][2]})
    tri, boh, identb, identf = attn_consts()
    band, cmpdiag, boh2, sel48, ovl = nsa_consts()
    in_maps = []
    for c in range(NCORES):
        b = c // 4
        tk = core_tokens(c)
        lists = slot_lists(c)
        idx = np.concatenate([list_tokens(l) for l in lists])
        ksl = gather_rows(a["kslc"][b], idx)
        vsl = gather_rows(a["vslc"][b], idx).reshape(48, 128, 4, HD)
        widx = np.concatenate([list_tokens([(k - 1) if k > 0 else None, k]) for k in core_chunks(c)])
        kwl = gather_rows(a["kwin"][b], widx)
        vwl = gather_rows(a["vwin"][b], widx).reshape(16, 128, 4, HD)
        cidx = np.concatenate([np.array([(-1 if k is None else k)]) for l in lists for k in l])
        kcl = gather_rows(a["kcmp"][b], cidx).reshape(384, 4, HD)
        vcl = gather_rows(a["vcmp"][b], cidx).reshape(3, 128, 4, HD)
        cmppad, slcm, winpad = nsa_core_consts(c)
        m = {
            "hT": to_fm(h1[b][tk]), "pT": to_fm(inp["p"][2, b][tk]), "vecs": vecs,
            "qcT": np.ascontiguousarray(a["qc"][b][tk].transpose(1, 2, 0)), "qrT": np.ascontiguousarray(a["qr"][b][tk].transpose(1, 2, 0)),
            "gT": np.ascontiguousarray(a["gT"][b][tk].T),
            "kslcT": np.ascontiguousarray(ksl.transpose(1, 2, 0)), "vslc": np.ascontiguousarray(vsl.transpose(2, 1, 0, 3)),
            "kwinT": np.ascontiguousarray(kwl.transpose(1, 2, 0)), "vwin": np.ascontiguousarray(vwl.transpose(2, 1, 0, 3)),
            "kcmpT": np.ascontiguousarray(kcl.transpose(1, 2, 0)), "vcmp": np.ascontiguousarray(vcl.transpose(2, 1, 0, 3)),
            "tri": tri, "band": band, "cmpdiag": cmpdiag, "boh2": boh2, "sel48": sel48, "ovl": ovl, "identb": identb, "identf": identf,
            "cmppad": cmppad, "slcm": slcm, "winpad": winpad,
            "w_o": inp["nsa_w_o"][0],
            "mlp_w1": inp["mlp_w1"][2], "mlp_w2": inp["mlp_w2"][2], "ple_gate": inp["ple_gate"][2], "ple_proj": inp["ple_proj"][2],
        }
        in_maps.append(m)
    res = run_prog(nc, in_maps)
    h2 = np.zeros_like(h1)
    for c in range(NCORES):
        h2[c // 4][core_tokens(c)] = from_fm(res[c]["hT_out"])
    return h2


def kernel(**inputs):
    inp = {k: np.asarray(v) for k, v in inputs.items()}
    x = np.ascontiguousarray(inp["x"], dtype=np.float32)
    qkv = layer0a(inp, x)
    h0 = layer0b(inp, x, qkv)
    h1 = layer1(inp, h0)
    a = layer2a(inp, h1)
    h2 = layer2b(inp, h1, a)
    h3 = layer3(inp, h2)
    return h3.astype(np.float32)
```
